# Optimizing a Trainium2 kernel written in Bass

```python
import math
import jax
import jax.numpy as jnp
from jax import lax
import numpy as np

D_MODEL = 1024
BATCH = 4
SEQ = 8192
DEPTH = 2

MOBA_HEADS = 8
MOBA_HEAD_DIM = 64
MOBA_WIDTH = MOBA_HEADS * MOBA_HEAD_DIM
MOBA_BLOCK = 256
MOBA_TOPK = 3
MOBA_QCHUNK = 128
HGRN_HEADS = 4
HGRN_KDIM = 128
HGRN_VDIM = 128
HGRN_WIDTH = HGRN_HEADS * HGRN_VDIM
HGRN_CHUNK = 64
LOG_DECAY_MASK = -1e4
EXP_CLIP = 30.0
REL_BUCKETS = 32
REL_MAX_DIST = 2048
PLE_DIM = 256
RMS_EPS = 1e-6
NEG_INF = -1e30
IN_COLS = (MOBA_WIDTH,) * 4 + (HGRN_HEADS * HGRN_KDIM,) * 2 + (HGRN_WIDTH,) * 2 + (D_MODEL,) * 2
IN_WIDTH = sum(IN_COLS)

kernel_name = 'hybrid_moba_hgrn2_gated_merge'


def rmsnorm(x, gain):
    xf = x.astype(jnp.float32)
    return xf * lax.rsqrt(jnp.mean(xf * xf, axis=-1, keepdims=True) + RMS_EPS) * gain.astype(jnp.float32)


def t5_bucket(rel):
    n = jnp.maximum(rel, 0)
    max_exact = REL_BUCKETS // 2
    nf = jnp.maximum(n, max_exact).astype(jnp.float32)
    large = max_exact + (jnp.log(nf / max_exact) / math.log(REL_MAX_DIST / max_exact)
                         * (REL_BUCKETS - max_exact)).astype(jnp.int32)
    large = jnp.minimum(large, REL_BUCKETS - 1)
    return jnp.where(n < max_exact, n, large)


def moba_attention(q, k, v, rel_bias):
    b, h, s, dh = q.shape
    nb = -(-s // MOBA_BLOCK)
    s_pad = nb * MOBA_BLOCK
    pad = ((0, 0), (0, 0), (0, s_pad - s), (0, 0))
    q, k, v = jnp.pad(q, pad), jnp.pad(k, pad), jnp.pad(v, pad)
    k_blk = k.reshape(b, h, nb, MOBA_BLOCK, dh)
    v_blk = v.reshape(b, h, nb, MOBA_BLOCK, dh)
    k_mean = jnp.mean(k_blk, axis=3)
    q_blk = jnp.arange(s_pad) // MOBA_BLOCK
    gate = jnp.einsum('bhsd,bhnd->bhsn', q, k_mean)
    fully_past = jnp.arange(nb)[None, :] < q_blk[:, None]
    gate = jnp.where(fully_past, gate, NEG_INF)
    topk = min(MOBA_TOPK, nb)
    _, sel = lax.top_k(gate, topk)

    bias_table = rel_bias.astype(jnp.float32).T
    b_ix = jnp.arange(b)[:, None, None]
    h_ix = jnp.arange(h)[None, :, None]
    h4 = jnp.arange(h)[None, :, None, None]
    offs = jnp.arange(MOBA_BLOCK)
    scale = dh ** -0.5
    nqc = s_pad // MOBA_QCHUNK
    q_c = jnp.moveaxis(q.reshape(b, h, nqc, MOBA_QCHUNK, dh), 2, 0)
    sel_c = jnp.moveaxis(sel.reshape(b, h, nqc, MOBA_QCHUNK, topk), 2, 0)

    def chunk(args):
        qc, sc, c = args
        qpos = c * MOBA_QCHUNK + jnp.arange(MOBA_QCHUNK)
        blk = (c * MOBA_QCHUNK) // MOBA_BLOCK
        logits = []
        for j in range(topk):
            idx = sc[..., j]
            kg = k_blk[b_ix, h_ix, idx]
            rel = qpos[:, None] - (idx[..., None] * MOBA_BLOCK + offs)
            lg = jnp.einsum('bhqd,bhqkd->bhqk', qc, kg) * scale + bias_table[h4, t5_bucket(rel)]
            logits.append(jnp.where(j < blk, lg, NEG_INF))
        k_own = lax.dynamic_index_in_dim(k_blk, blk, axis=2, keepdims=False)
        v_own = lax.dynamic_index_in_dim(v_blk, blk, axis=2, keepdims=False)
        rel_own = qpos[:, None] - (blk * MOBA_BLOCK + offs)[None, :]
        lg_own = jnp.einsum('bhqd,bhkd->bhqk', qc, k_own) * scale + bias_table[:, t5_bucket(rel_own)]
        logits.append(jnp.where(rel_own >= 0, lg_own, NEG_INF))
        probs = jax.nn.softmax(jnp.concatenate(logits, axis=-1), axis=-1)
        out = jnp.einsum('bhqk,bhkd->bhqd', probs[..., topk * MOBA_BLOCK:], v_own)
        for j in range(topk):
            vg = v_blk[b_ix, h_ix, sc[..., j]]
            out = out + jnp.einsum('bhqk,bhqkd->bhqd', probs[..., j * MOBA_BLOCK:(j + 1) * MOBA_BLOCK], vg)
        return out

    out = lax.map(chunk, (q_c, sel_c, jnp.arange(nqc)))
    return jnp.moveaxis(out, 0, 2).reshape(b, h, s_pad, dh)[:, :, :s]


def hgrn2_chunkwise(q, k, v, g):
    b, s, h, dk = q.shape
    dv = v.shape[-1]
    nc = s // HGRN_CHUNK

    def chunks(t):
        return t.reshape(b, nc, HGRN_CHUNK, h, t.shape[-1]).transpose(1, 0, 3, 2, 4)

    causal = jnp.tril(jnp.ones((HGRN_CHUNK, HGRN_CHUNK), dtype=bool))

    def step(state, inp):
        qc, kc, vc, gc = inp
        cum = jnp.cumsum(gc, axis=2)
        o_inter = jnp.einsum('bhtk,bhkv->bhtv', qc * jnp.exp(cum), state)
        diff = jnp.where(causal[:, :, None], cum[:, :, :, None, :] - cum[:, :, None, :, :], LOG_DECAY_MASK)
        scores = jnp.einsum('bhtk,bhsk,bhtsk->bhts', qc, kc, jnp.exp(diff))
        o_intra = jnp.einsum('bhts,bhsv->bhtv', scores, vc)
        last = cum[:, :, -1:, :]
        state = state * jnp.exp(last[:, :, 0, :, None]) + jnp.einsum('bhsk,bhsv->bhkv', kc * jnp.exp(last - cum), vc)
        return state, o_inter + o_intra

    state0 = jnp.zeros((b, h, dk, dv), jnp.float32)
    _, out = lax.scan(step, state0, (chunks(q), chunks(k), chunks(v), chunks(g)))
    return out.transpose(1, 0, 3, 2, 4).reshape(b, s, h, dv)


def setup_inputs(seed: int = 0) -> dict:
    key = jax.random.key(seed)
    ks = jax.random.split(key, 14)
    f32 = jnp.float32
    nrm = lambda k, shp: jax.random.normal(k, shp, f32)
    return {
        'x': nrm(ks[0], (BATCH, SEQ, D_MODEL)),
        'p': nrm(ks[1], (DEPTH, BATCH, SEQ, PLE_DIM)),
        'norm_gain': 1.0 + 0.05 * nrm(ks[2], (DEPTH, D_MODEL)),
        'w_in': nrm(ks[3], (DEPTH, D_MODEL, IN_WIDTH)) * D_MODEL ** -0.5,
        'q_norm_gain': 1.0 + 0.05 * nrm(ks[4], (DEPTH, MOBA_HEAD_DIM)),
        'k_norm_gain': 1.0 + 0.05 * nrm(ks[5], (DEPTH, MOBA_HEAD_DIM)),
        'rel_bias': 0.5 * nrm(ks[6], (REL_BUCKETS, MOBA_HEADS)),
        'hgrn_lb_logits': 0.1 * nrm(ks[7], (DEPTH, HGRN_HEADS * HGRN_KDIM)),
        'hgrn_out_gain': 1.0 + 0.05 * nrm(ks[8], (DEPTH, HGRN_WIDTH)),
        'w_up_a': nrm(ks[9], (DEPTH, MOBA_WIDTH, D_MODEL)) * MOBA_WIDTH ** -0.5,
        'w_up_b': nrm(ks[10], (DEPTH, HGRN_WIDTH, D_MODEL)) * HGRN_WIDTH ** -0.5,
        'w_out': nrm(ks[11], (DEPTH, D_MODEL, D_MODEL)) * D_MODEL ** -0.5,
        'w_ple': nrm(ks[12], (DEPTH, PLE_DIM, D_MODEL)) * PLE_DIM ** -0.5,
        'w_ple_gate': nrm(ks[13], (DEPTH, D_MODEL, D_MODEL)) * D_MODEL ** -0.5,
    }


def reference(x, p, norm_gain, w_in, q_norm_gain, k_norm_gain, rel_bias, hgrn_lb_logits,
              hgrn_out_gain, w_up_a, w_up_b, w_out, w_ple, w_ple_gate):
    f32 = jnp.float32
    dt = x.dtype
    b, s, _ = x.shape
    split_pts = []
    acc = 0
    for w in IN_COLS[:-1]:
        acc += w
        split_pts.append(acc)
    lb_sm = jax.nn.softmax(hgrn_lb_logits.astype(f32), axis=0)
    lower_bounds = jnp.cumsum(lb_sm, axis=0) - lb_sm[0:1]

    h = x
    for i in range(DEPTH):
        xn = rmsnorm(h, norm_gain[i]).astype(dt)
        proj = xn @ w_in[i]
        aq, ak, av, ag, bq, bf, bi, bg, gate_a, gate_b = jnp.split(proj, split_pts, axis=-1)

        def heads(t):
            return t.reshape(b, s, MOBA_HEADS, MOBA_HEAD_DIM).transpose(0, 2, 1, 3)
        qa = rmsnorm(heads(aq), q_norm_gain[i])
        ka = rmsnorm(heads(ak), k_norm_gain[i])
        va = heads(av).astype(f32)
        ya = moba_attention(qa, ka, va, rel_bias).transpose(0, 2, 1, 3).reshape(b, s, MOBA_WIDTH)
        ya = (ya * jax.nn.silu(ag.astype(f32))).astype(dt) @ w_up_a[i]

        lb = lower_bounds[i].reshape(HGRN_HEADS, HGRN_KDIM)
        fpre = bf.astype(f32).reshape(b, s, HGRN_HEADS, HGRN_KDIM)
        log_f = jax.nn.log_sigmoid(fpre) + jnp.log1p(lb * jnp.exp(jnp.minimum(-fpre, EXP_CLIP)))
        kb = (1.0 - lb) * jax.nn.sigmoid(-fpre)
        qb = jax.nn.silu(bq.astype(f32)).reshape(b, s, HGRN_HEADS, HGRN_KDIM)
        vb = bi.astype(f32).reshape(b, s, HGRN_HEADS, HGRN_VDIM)
        yb = hgrn2_chunkwise(qb, kb, vb, log_f)
        yb = rmsnorm(yb, hgrn_out_gain[i].reshape(HGRN_HEADS, HGRN_VDIM)).reshape(b, s, HGRN_WIDTH)
        yb = (yb * jax.nn.silu(bg.astype(f32))).astype(dt) @ w_up_b[i]

        merged = jax.nn.sigmoid(gate_a.astype(f32)) * ya.astype(f32) + jax.nn.sigmoid(gate_b.astype(f32)) * yb.astype(f32)
        h = h + merged.astype(dt) @ w_out[i]

        ple = (p[i] @ w_ple[i]).astype(f32) * jax.nn.sigmoid((h @ w_ple_gate[i]).astype(f32))
        h = h + ple.astype(dt)
    return h
```

```python
import math
from contextlib import ExitStack

import numpy as np
import ml_dtypes
import concourse.bass as bass
import concourse.mybir as mybir
from concourse.bass_utils import run_bass_kernel_spmd

F32 = mybir.dt.float32
BF16 = mybir.dt.bfloat16
AF = mybir.ActivationFunctionType
ALU = mybir.AluOpType
AX = mybir.AxisListType

SEM_CHUNK = 20000
D = 1024
NEG = -30000.0
STRIP = 2432
EPS = 1e-6


class Buf:
    __slots__ = ("name", "last_writer", "dma_writers", "readers", "dma_readers")

    def __init__(self, name):
        self.name = name
        self.clear()

    def clear(self):
        self.last_writer = None
        self.dma_writers = []
        self.readers = {}
        self.dma_readers = []


class Sched:
    def __init__(self, nc, stack):
        self.nc = nc
        self.stack = stack
        self.ops = []
        self.eng = {"pe": nc.tensor, "act": nc.scalar, "dve": nc.vector,
                    "pool": nc.gpsimd, "sp": nc.sync}
        self.nsem = 0
        self.eng_count = {}
        self.eng_sems = {}
        self.dma_pool = []
        self.dma_free = []
        self.key_slot = {}
        self.waited = {}
        self.total = 0

    def new_sem(self, name):
        self.nsem += 1
        return self.stack.enter_context(self.nc.semaphore(name))

    def op(self, eng, fn, reads=(), writes=()):
        self.ops.append(["c", eng, fn, tuple(reads), tuple(writes), None, 1])

    def dma(self, queue, fn, reads=(), writes=(), key=None, inc=16):
        assert key is not None
        self.ops.append(["d", queue, fn, tuple(reads), tuple(writes), key, inc])

    def emit(self):
        ops = self.ops
        n = len(ops)
        deps = [None] * n
        signaling = [False] * n
        last_of_eng = {}
        for i, o in enumerate(ops):
            d = set()
            isd = o[0] == "d"
            for b in o[3]:
                if b.last_writer is not None:
                    d.add(b.last_writer)
                d.update(b.dma_writers)
            for b in o[4]:
                if b.last_writer is not None:
                    d.add(b.last_writer)
                d.update(b.readers.values())
                d.update(b.dma_readers)
                if not isd:
                    d.update(b.dma_writers)
            d.discard(i)
            for b in o[3]:
                if isd:
                    b.dma_readers.append(i)
                else:
                    b.readers[o[1]] = i
            for b in o[4]:
                if isd:
                    b.dma_writers.append(i)
                else:
                    b.last_writer = i
                    b.dma_writers = []
                    b.readers = {}
                    b.dma_readers = []
            deps[i] = d
            for j in d:
                signaling[j] = True
            if o[0] == "c":
                last_of_eng[o[1]] = i
        for e, i in last_of_eng.items():
            signaling[i] = True
        sig = [None] * n
        for i, o in enumerate(ops):
            if o[0] == "c":
                if not signaling[i]:
                    continue
                e = o[1]
                c = self.eng_count.get(e, 0)
                k = c // SEM_CHUNK
                if (e, k) not in self.eng_sems:
                    self.eng_sems[(e, k)] = self.new_sem(f"s_{e}_{k}")
                self.eng_count[e] = c + 1
                sig[i] = (self.eng_sems[(e, k)], c - k * SEM_CHUNK + 1, 1, ("e", e, k))
            else:
                key = o[5]
                if key not in self.key_slot:
                    if self.dma_free:
                        s = self.dma_free.pop()
                    else:
                        self.dma_pool.append([self.new_sem(f"d_{len(self.dma_pool)}"), 0])
                        s = len(self.dma_pool) - 1
                    self.key_slot[key] = s
                s = self.key_slot[key]
                self.dma_pool[s][1] += o[6]
                sig[i] = (self.dma_pool[s][0], self.dma_pool[s][1], o[6], ("k", s))
        waited = self.waited
        for i, o in enumerate(ops):
            e = o[1]
            engine = self.eng[e]
            w = waited.setdefault(e, {})
            for j in sorted(deps[i]):
                pj = ops[j]
                if pj[0] == "c" and pj[1] == "pe" and e == "pe" and o[0] == "c":
                    continue
                sem, val, _, sid = sig[j]
                if w.get(sid, 0) >= val:
                    continue
                w[sid] = val
                engine.wait_ge(sem, val)
            ins = o[2](engine)
            if sig[i] is not None:
                ins.then_inc(sig[i][0], sig[i][2])
        finals = []
        for (e, k), sem in self.eng_sems.items():
            c = self.eng_count.get(e, 0)
            if c // SEM_CHUNK == k and c - k * SEM_CHUNK > 0:
                finals.append((sem, c - k * SEM_CHUNK, ("e", e, k)))
            elif c // SEM_CHUNK > k:
                finals.append((sem, SEM_CHUNK, ("e", e, k)))
        for s, (sem, c) in enumerate(self.dma_pool):
            if c > 0:
                finals.append((sem, c, ("k", s)))
        for e in ("sp", "pe", "act", "dve", "pool"):
            w = waited.setdefault(e, {})
            for sem, val, sid in finals:
                if w.get(sid, 0) >= val:
                    continue
                w[sid] = val
                self.eng[e].wait_ge(sem, val)
        for o in ops:
            for b in o[3] + o[4]:
                b.clear()
        self.key_slot = {}
        self.dma_free = list(range(len(self.dma_pool)))
        self.total += n
        self.ops = []
        return n


class Ring:
    def __init__(self, items):
        self.items = items
        self.i = 0

    def next(self):
        it = self.items[self.i % len(self.items)]
        self.i += 1
        return it


def t5_bucket_np(rel):
    n = np.maximum(rel, 0)
    nf = np.maximum(n, 16).astype(np.float32)
    large = 16 + (np.log(nf / np.float32(16)) / np.float32(math.log(128)) * np.float32(16)).astype(np.int32)
    large = np.minimum(large, 31)
    return np.where(n < 16, n, large)


def build(S, L=2, debug=False, groups=None):
    NT, NB, NG, NCH = S // 128, S // 256, S // 512, S // 64
    SH, NGO = S // 2, S // 1024
    HA, HB = 4, 2
    if groups is None:
        groups = [[0, 1], [2, 3], [4, 5], [6, 7]]
    nc = bass.Bass("TRN2", target_bir_lowering=False)

    def din(name, shape, dt=F32):
        return nc.dram_tensor(name, list(shape), dt, kind="ExternalInput").ap()

    def dscr(name, shape, dt=BF16, out=False):
        kind = "Internal"
        return nc.dram_tensor(name, list(shape), dt, kind=kind).ap()

    xT = din("xT", [D, S])
    xT_own = din("xT_own", [D, SH])
    pT = din("pT", [L, 256, SH])
    w_in = din("w_in", [L, D, 4096])
    w_up_a = din("w_up_a", [L, 512, D])
    w_up_b = din("w_up_b", [L, 512, D])
    w_out = din("w_out", [L, D, D])
    w_ple = din("w_ple", [L, 256, D])
    w_pg = din("w_pg", [L, D, D])
    g_norm = din("g_norm", [128, L, 8])
    g_q = din("g_q", [128, L])
    g_k = din("g_k", [128, L])
    g_o = din("g_o", [128, L, HB])
    lbl = din("lbl", [128, L, HB])
    strip_raw = din("strip_raw", [128, HA, STRIP])
    c31 = din("c31", [128, HA])
    c_ident = din("c_ident", [128, 128], BF16)
    c_blk = din("c_blk", [128, 128], BF16)
    c_onehot = din("c_onehot", [32, S], BF16)
    c_causal = din("c_causal", [64, 64])
    c_scan = din("c_scan", [128, 512])
    c_fut = din("c_fut", [128, NT, 32], BF16)
    c_neg = din("c_neg", [128, NT, 32], BF16)

    hT_out = nc.dram_tensor("hT_out", [D, SH], F32, kind="ExternalOutput").ap()
    h_mid = dscr("h_mid", [D, SH], F32)
    s_qn = dscr("s_qn", [256, S])
    s_kn = dscr("s_kn", [256, S])
    s_v = dscr("s_v", [S, 256])
    s_sag = dscr("s_sag", [256, S])
    s_qd = dscr("s_qd", [256, S])
    s_kd = dscr("s_kd", [256, S])
    s_ke = dscr("s_ke", [256, S])
    s_vb = dscr("s_vb", [S, 256])
    s_sbg = dscr("s_sbg", [256, S])
    s_sga = dscr("s_sga", [D, SH])
    s_sgb = dscr("s_sgb", [D, SH])
    XS = dscr("XS", [4, 128, S], out=True)
    XG = dscr("XG", [4, 2 * 128, S], out=True)
    XO = dscr("XO", [4, 2 * 128, SH])
    PIECE = max(512, SH // 4)
    NPC = SH // PIECE
    HS = dscr("HS", [NPC, D, PIECE])
    HG = dscr("HG", [NPC, 2 * D, PIECE])

    off_own = (nc.sync.partition_id() % 2) * SH

    with ExitStack() as outer:
        Sc = Sched(nc, outer)
        uniq = [0]

        def mk(stack):
            uniq[0] += 1
            tag = f"u{uniq[0]}_"

            def sb(name, shape, dt=F32):
                return stack.enter_context(nc.sbuf_tensor(tag + name, list(shape), dt))

            def ps(name, shape, dt=F32):
                return stack.enter_context(nc.psum_tensor(tag + name, list(shape), dt))
            return sb, ps

        def MM(out, lhsT, rhs, start, stop, r, w):
            Sc.op("pe", lambda e: e.matmul(out, lhsT=lhsT, rhs=rhs, start=start, stop=stop), r, w)

        def ACT(out, in_, func, r, w, bias=0.0, scale=1.0, eng="act"):
            Sc.op(eng, lambda e: e.activation(out=out, in_=in_, func=func, bias=bias, scale=scale), r, w)

        def TT(eng, out, in0, in1, op, r, w):
            Sc.op(eng, lambda e: e.tensor_tensor(out=out, in0=in0, in1=in1, op=op), r, w)

        def TSC(eng, out, in0, s1, s2, op0, op1, r, w):
            if s2 is None:
                Sc.op(eng, lambda e: e.tensor_scalar(out=out, in0=in0, scalar1=s1, scalar2=None, op0=op0), r, w)
            else:
                Sc.op(eng, lambda e: e.tensor_scalar(out=out, in0=in0, scalar1=s1, scalar2=s2, op0=op0, op1=op1), r, w)

        def STT(eng, out, in0, scalar, in1, op0, op1, r, w):
            Sc.op(eng, lambda e: e.scalar_tensor_tensor(out=out, in0=in0, scalar=scalar, in1=in1, op0=op0, op1=op1), r, w)

        def CP(eng, out, in_, r, w):
            if eng == "act":
                Sc.op("act", lambda e: e.activation(out=out, in_=in_, func=AF.Copy), r, w)
            else:
                Sc.op(eng, lambda e: e.tensor_copy(out=out, in_=in_), r, w)

        def LD(out, in_, w, key, r=(), q="sp"):
            Sc.dma(q, lambda e: e.dma_start(out=out, in_=in_), r, w, key)

        def ST(out, in_, r, key, w=(), q="pool"):
            Sc.dma(q, lambda e: e.dma_start(out=out, in_=in_), r, w, key)

        def AG(out, in_, key, r=(), w=()):
            Sc.dma("pool", lambda e: e.collective_compute(
                "AllGather", ALU.bypass, replica_groups=groups, ins=[in_], outs=[out]), r, w, key, inc=1)

        sbP, _ = mk(outer)
        ident = sbP("ident", [128, 128], BF16); b_ident = Buf("ident")
        blk = sbP("blk", [128, 128], BF16); b_blk = Buf("blk")
        ones = sbP("ones", [128, 128], BF16); b_ones = Buf("ones")
        gn = sbP("gn", [128, L, 8]); b_gn = Buf("gn")
        gq = sbP("gq", [128, L]); b_gq = Buf("gq")
        gk = sbP("gk", [128, L]); b_gk = Buf("gk")
        go = sbP("go", [128, L, HB]); b_go = Buf("go")
        lb = sbP("lb", [128, L, HB]); b_lb = Buf("lb")
        oml = sbP("oml", [128, L, HB]); b_oml = Buf("oml")
        lbe = sbP("lbe", [128, L, HB]); b_lbe = Buf("lbe")
        lbs = sbP("lbs", [128, HB]); b_lbs = Buf("lbs")
        KS = sbP("KS", [128, 2, NB]); b_KS = Buf("KS")
        EL = sbP("EL", [128, HB, NCH]); b_EL = Buf("EL")
        caus = sbP("caus", [64, 64]); b_caus = Buf("caus")
        scanm = sbP("scanm", [128, 512]); b_scanm = Buf("scanm")

        with ExitStack() as st:
            sb, ps = mk(st)
            LD(ident[:], c_ident, [b_ident], b_ident)
            LD(blk[:], c_blk, [b_blk], b_blk)
            Sc.op("pool", lambda e: e.memset(ones[:], 1.0), (), [b_ones])
            LD(gn[:], g_norm, [b_gn], b_gn)
            LD(gq[:], g_q, [b_gq], b_gq)
            LD(gk[:], g_k, [b_gk], b_gk)
            LD(go[:], g_o, [b_go], b_go)
            LD(lbe[:], lbl, [b_lbe], b_lbe)
            LD(caus[:], c_causal, [b_caus], b_caus)
            LD(scanm[:], c_scan, [b_scanm], b_scanm)
            TSC("dve", gq[:], gq[:], 0.125, None, ALU.mult, None, [b_gq], [b_gq])
            ACT(lbe[:], lbe[:], AF.Exp, [b_lbe], [b_lbe])
            CP("dve", lbs[:], lbe[:, 0, :], [b_lbe], [b_lbs])
            for l in range(1, L):
                TT("dve", lbs[:], lbs[:], lbe[:, l, :], ALU.add, [b_lbs, b_lbe], [b_lbs])
            Sc.op("dve", lambda e: e.reciprocal(out=lbs[:], in_=lbs[:]), [b_lbs], [b_lbs])
            Sc.op("dve", lambda e: e.memset(lb[:, 0, :], 0.0), (), [b_lb])
            for l in range(1, L):
                TT("dve", lb[:, l, :], lb[:, l - 1, :], lbe[:, l, :], ALU.add, [b_lb, b_lbe], [b_lb])
            for l in range(1, L):
                TT("dve", lb[:, l, :], lb[:, l, :], lbs[:], ALU.mult, [b_lb, b_lbs], [b_lb])
            for l in range(L):
                TSC("dve", oml[:, l, :], lb[:, l, :], -1.0, 1.0, ALU.mult, ALU.add, [b_lb], [b_oml])
            Sc.emit()

        for l in range(L):
            first, last = (l == 0), (l == L - 1)
            h_own = xT_own if first else h_mid
            h_dst = hT_out if last else h_mid

            with ExitStack() as st:
                sb, ps = mk(st)
                Wb = sb("Wb", [128, 8, 4096], BF16); b_Wb = Buf("Wb")
                wst = [(sb(f"wst{i}", [128, 8, 128]), Buf(f"wst{i}")) for i in range(2)]
                wv = w_in[l].rearrange("(c p) n -> p c n", p=128)
                for i in range(32):
                    t, b = wst[i % 2]
                    LD(t[:], wv[:, :, i * 128:(i + 1) * 128], [b], b)
                    CP(("act", "dve", "pool")[i % 3], Wb[:, :, i * 128:(i + 1) * 128], t[:], [b], [b_Wb])
                HDT = F32 if first else BF16
                H = sb("H", [128, 8, 512], HDT); b_H = Buf("H")
                H2 = sb("H2", [128, 8, 512]); b_H2 = Buf("H2")
                SQ = sb("SQ", [128, 8, 512], BF16); b_SQ = Buf("SQ")
                XNs = [(sb(f"XN{i}", [128, 8, 512], BF16), Buf(f"XN{i}")) for i in range(2)]
                rstd = sb("rstd", [128, 512]); b_rstd = Buf("rstd")
                stage = Ring([(sb(f"stg{i}", [128, 512], BF16), Buf(f"stg{i}")) for i in range(6)])
                f32r = Ring([(sb(f"f32r{i}", [128, 512]), Buf(f"f32r{i}")) for i in range(8)])
                QB = [(sb(f"QB{i}", [128, 512]), Buf(f"QB{i}")) for i in range(HB)]
                SG = [(sb(f"SG{i}", [128, 512]), Buf(f"SG{i}")) for i in range(HB)]
                sqq = Ring([(sb(f"sqq{i}", [128, 512], BF16), Buf(f"sqq{i}")) for i in range(2)])
                p_ss = ps("p_ss", [128, 512]); b_pss = Buf("p_ss")
                PS = Ring([(ps(f"PSa{i}", [128, 512]), Buf(f"PSa{i}")) for i in range(5)])
                BS = Ring([(ps(f"BSa{i}", [128, 512]), Buf(f"BSa{i}")) for i in range(2)])
                Sc.op("dve", lambda e: e.memset(KS[:], 0.0), (), [b_KS])

                def load_h(g):
                    if first:
                        LD(H[:], xT.rearrange("(c p) t -> p c t", p=128)[:, :, g * 512:(g + 1) * 512], [b_H], b_H)
                    else:
                        half, tok = g // NGO, (g % NGO) * 512
                        q, col = tok // PIECE, tok % PIECE
                        src = HG[q, half * D:(half + 1) * D, col:col + 512].rearrange("(c p) t -> p c t", p=128)
                        LD(H[:], src, [b_H], b_H)

                def norm_part1(g):
                    load_h(g)
                    ACT(SQ[:].rearrange("p c t -> p (c t)"), H[:].rearrange("p c t -> p (c t)"),
                        AF.Square, [b_H], [b_SQ])

                def norm_part2(Hs, bH, XN, b_XN):
                    for c in range(8):
                        MM(p_ss[:], ones[:], SQ[:, c, :], c == 0, c == 7, [b_ones, b_SQ], [b_pss])
                    ACT(rstd[:], p_ss[:], AF.Ln, [b_pss], [b_rstd], bias=EPS, scale=1.0 / D)
                    ACT(rstd[:], rstd[:], AF.Exp, [b_rstd], [b_rstd], scale=-0.5)
                    for c in range(8):
                        STT("dve", XN[:, c, :], Hs[:, c, :], gn[:, l, c:c + 1], rstd[:],
                            ALU.mult, ALU.mult, [bH, b_gn, b_rstd], [b_XN])

                norm_part1(0)
                norm_part2(H, b_H, *XNs[0])
                for g in range(NG):
                    XN, b_XN = XNs[g % 2]

                    def proj_fm(c0):
                        P, bP = PS.next()
                        for c in range(8):
                            MM(P[:], Wb[:, c, c0:c0 + 128], XN[:, c, :], c == 0, c == 7, [b_Wb, b_XN], [bP])
                        return P, bP

                    def store_fm(dst, j, t, b):
                        ST(dst[j * 128:(j + 1) * 128, g * 512:(g + 1) * 512], t[:], [b], b)

                    def qk_tail(j, P, bP, s2, bs2):
                        isk = j >= 2
                        Bp, bB = BS.next()
                        MM(Bp[:], blk[:], s2[:], True, True, [b_blk, bs2], [bB])
                        r1, br1 = f32r.next()
                        ACT(r1[:], Bp[:], AF.Ln, [bB], [br1], bias=EPS, scale=1.0 / 64)
                        ACT(r1[:], r1[:], AF.Exp, [br1], [br1], scale=-0.5)
                        o, bo = stage.next()
                        gg = gk if isk else gq
                        STT("dve", o[:], P[:], gg[:, l:l + 1], r1[:], ALU.mult, ALU.mult,
                            [bP, b_gk if isk else b_gq, br1], [bo])
                        if isk:
                            Sc.op("dve", lambda e, o=o, j=j, g=g: e.tensor_reduce(
                                out=KS[:, j - 2, 2 * g:2 * g + 2],
                                in_=o[:].rearrange("p (b t) -> p b t", t=256),
                                axis=AX.X, op=ALU.add), [bo], [b_KS])
                        store_fm(s_kn if isk else s_qn, j % 2, o, bo)

                    pend = None
                    for j in range(4):
                        P, bP = proj_fm(j * 128)
                        s2, bs2 = sqq.next()
                        ACT(s2[:], P[:], AF.Square, [bP], [bs2])
                        if pend is not None:
                            qk_tail(*pend)
                        pend = (j, P, bP, s2, bs2)
                    firstv = True
                    for (c0, dst) in ((512, s_v), (1536, s_vb)):
                        for tt in range(4):
                            P, bP = PS.next()
                            for c in range(8):
                                MM(P[:, 0:256], XN[:, c, tt * 128:(tt + 1) * 128], Wb[:, c, c0:c0 + 256],
                                   c == 0, c == 7, [b_XN, b_Wb], [bP])
                            if firstv:
                                qk_tail(*pend)
                                firstv = False
                            o, bo = stage.next()
                            CP("act", o[:, 0:256], P[:, 0:256], [bP], [bo])
                            ST(dst[g * 512 + tt * 128:g * 512 + (tt + 1) * 128, :], o[:, 0:256], [bo], bo)
                    if g + 1 < NG:
                        norm_part1(g + 1)
                    for jj in range(HB):
                        P, bP = proj_fm(1280 + jj * 128)
                        ACT(SG[jj][0][:], P[:], AF.Sigmoid, [bP], [SG[jj][1]])
                    if g + 1 < NG:
                        norm_part2(H, b_H, *XNs[(g + 1) % 2])
                    for (c00, dst) in ((768, s_sag), (1792, s_sbg)):
                        for jj in range(2):
                            P, bP = proj_fm(c00 + jj * 128)
                            o, bo = stage.next()
                            ACT(o[:], P[:], AF.Silu, [bP], [bo])
                            store_fm(dst, jj, o, bo)
                    for jj in range(HB):
                        P, bP = proj_fm(1024 + jj * 128)
                        ACT(QB[jj][0][:], P[:], AF.Silu, [bP], [QB[jj][1]])
                    for jj in range(HB):
                        sg, bsg = SG[jj]
                        f, bf_ = f32r.next()
                        TSC("dve", f[:], sg[:], oml[:, l, jj:jj + 1], lb[:, l, jj:jj + 1], ALU.mult, ALU.add,
                            [bsg, b_oml, b_lb], [bf_])
                        gl, bgl = f32r.next()
                        ACT(gl[:], f[:], AF.Ln, [bf_], [bgl])
                        cum, bcum = f32r.next()
                        Sc.op("dve", lambda e, cum=cum, gl=gl: e.tensor_tensor_scan(
                            out=cum[:], data0=scanm[:], data1=gl[:], initial=0.0,
                            op0=ALU.mult, op1=ALU.add), [bgl, b_scanm], [bcum])
                        ec, bec = f32r.next()
                        ACT(ec[:], cum[:], AF.Exp, [bcum], [bec])
                        ACT(gl[:], cum[:], AF.Exp, [bcum], [bgl], scale=-1.0)
                        CP("pool", EL[:, jj, g * 8:(g + 1) * 8],
                           ec[:].rearrange("p (c t) -> p c t", t=64)[:, :, 63], [bec], [b_EL])
                        o, bo = stage.next()
                        TT("pool", o[:], QB[jj][0][:], ec[:], ALU.mult, [QB[jj][1], bec], [bo])
                        store_fm(s_qd, jj, o, bo)
                        TSC("dve", f[:], f[:], -1.0, 1.0, ALU.mult, ALU.add, [bf_], [bf_])
                        TT("dve", f[:], f[:], gl[:], ALU.mult, [bf_, bgl], [bf_])
                        o, bo = stage.next()
                        CP("act", o[:], f[:], [bf_], [bo])
                        store_fm(s_kd, jj, o, bo)
                        o, bo = stage.next()
                        TT("pool", o[:].rearrange("p (c t) -> p c t", t=64),
                           f[:].rearrange("p (c t) -> p c t", t=64),
                           EL[:, jj, g * 8:(g + 1) * 8].unsqueeze(2).to_broadcast([128, 8, 64]),
                           ALU.mult, [bf_, b_EL], [bo])
                        store_fm(s_ke, jj, o, bo)
                hov = h_own.rearrange("(c p) t -> p c t", p=128)
                def gate_norm(g):
                    XN, b_XN = XNs[g % 2]
                    LD(H2[:], hov[:, :, g * 512:(g + 1) * 512], [b_H2], b_H2)
                    ACT(SQ[:].rearrange("p c t -> p (c t)"), H2[:].rearrange("p c t -> p (c t)"),
                        AF.Square, [b_H2], [b_SQ])
                    norm_part2(H2, b_H2, XN, b_XN)

                gate_norm(0)
                for g in range(NGO):
                    XN, b_XN = XNs[g % 2]
                    for jj in range(16):
                        if jj == 6 and g + 1 < NGO:
                            gate_norm(g + 1)
                        P, bP = PS.next()
                        for c in range(8):
                            MM(P[:], Wb[:, c, 2048 + jj * 128:2048 + (jj + 1) * 128], XN[:, c, :], c == 0, c == 7,
                               [b_Wb, b_XN], [bP])
                        o, bo = stage.next()
                        ACT(o[:], P[:], AF.Sigmoid, [bP], [bo])
                        dst = s_sga if jj < 8 else s_sgb
                        ST(dst[(jj % 8) * 128:(jj % 8 + 1) * 128, g * 512:(g + 1) * 512], o[:], [bo], bo)
                Sc.emit()

            with ExitStack() as st:
                sb, ps = mk(st)
                QAs = [(sb(f"QA{i}", [128, S], BF16), Buf(f"QA{i}")) for i in range(2)]
                KAs = [(sb(f"KA{i}", [128, S], BF16), Buf(f"KA{i}")) for i in range(2)]
                VAs = [(sb(f"VA{i}", [128, NT, 128], BF16), Buf(f"VA{i}")) for i in range(2)]
                SAGr = Ring([(sb(f"SAG{i}", [64, 512], BF16), Buf(f"SAG{i}")) for i in range(3)])
                YGr = Ring([(sb(f"YG{i}", [64, 512], BF16), Buf(f"YG{i}")) for i in range(3)])
                FUT = sb("FUT", [128, NT, 32], BF16); b_FUT = Buf("FUT")
                NEGP = sb("NEGP", [128, NT, 32], BF16); b_NEGP = Buf("NEGP")
                KMh = sb("KMh", [64, 32], BF16); b_KMh = Buf("KMh")
                NTB = min(NT, 16)
                Gs = sb("Gs", [128, NTB, 32]); b_Gs = Buf("Gs")
                thr = sb("thr", [128, NTB, 8]); b_thr = Buf("thr")
                nsel = sb("nsel", [128, NTB, 32]); b_nsel = Buf("nsel")
                MBp = sb("MBp", [128, NTB, 128], BF16); b_MBp = Buf("MBp")
                PT = Ring([(sb(f"PT{i}", [128, 1024], BF16), Buf(f"PT{i}")) for i in range(3)])
                rden = sb("rden", [64, 512]); b_rden = Buf("rden")
                yh = sb("yh", [64, 512]); b_yh = Buf("yh")
                STp = Ring([(ps(f"STp{i}", [128, 1024]), Buf(f"STp{i}")) for i in range(2)])
                Op = Ring([(ps(f"Op{i}", [128, 512]), Buf(f"Op{i}")) for i in range(2)])
                Gp = ps("Gp", [128, 16, 32]); b_Gp = Buf("Gp")
                MTp = ps("MTp", [128, 512]); b_MTp = Buf("MTp")
                LD(FUT[:], c_fut, [b_FUT], b_FUT)
                LD(NEGP[:], c_neg, [b_NEGP], b_NEGP)
                for i in range(2):
                    LD(KAs[i][0][64:96, :], c_onehot, [KAs[i][1]], KAs[i][1])
                    Sc.op("pool", lambda e, i=i: e.memset(VAs[i][0][:, :, 64:128], 1.0), (), [VAs[i][1]])
                Sc.op("pool", lambda e: e.memset(MBp[:], 0.0), (), [b_MBp])
                TS_ = sb("TS", [128, HA, STRIP], BF16); b_TS = Buf("TS")
                c31s = sb("c31s", [128, HA]); b_c31 = Buf("c31s")
                LD(c31s[:], c31, [b_c31], b_c31)
                SPC = STRIP // 4
                stgs = [(sb(f"stripstg{i}", [128, SPC]), Buf(f"stripstg{i}")) for i in range(2)]
                for h in range(HA):
                    for q4 in range(4):
                        stg, b_stg = stgs[(h * 4 + q4) % 2]
                        LD(stg[:], strip_raw[:, h, q4 * SPC:(q4 + 1) * SPC], [b_stg], b_stg)
                        TSC("dve", TS_[:, h, q4 * SPC:(q4 + 1) * SPC], stg[:], c31s[:, h:h + 1], None, ALU.subtract,
                            None, [b_stg, b_c31], [b_TS])

                def head_loads(h):
                    sl = h % 2
                    QA, b_QA = QAs[sl]; KA, b_KA = KAs[sl]; VA, b_VA = VAs[sl]
                    LD(QA[0:64, :], s_qn[h * 64:(h + 1) * 64, :], [b_QA], b_QA)
                    LD(KA[0:64, :], s_kn[h * 64:(h + 1) * 64, :], [b_KA], b_KA)
                    vsrc = s_v[:, h * 64:(h + 1) * 64].rearrange("(t p) d -> p t d", p=128)
                    nvs = max(1, NT // 8)
                    for i in range(0, NT, nvs):
                        LD(VA[:, i:i + nvs, 0:64], vsrc[:, i:i + nvs, :], [b_VA], b_VA)

                def gate_stage(h, tb, stage_):
                    sl = h % 2
                    QA, b_QA = QAs[sl]
                    cq, po = h // 2, 64 * (h % 2)
                    if stage_ == 0:
                        if tb == 0:
                            TSC("dve", KMh[0:64, 0:NB], KS[po:po + 64, cq, :], 1.0 / 256, None, ALU.mult, None,
                                [b_KS], [b_KMh])
                        for t in range(NTB):
                            MM(Gp[:, t, 0:NB], QA[0:64, (tb + t) * 128:(tb + t + 1) * 128], KMh[0:64, 0:NB],
                               True, True, [b_QA, b_KMh], [b_Gp])
                    elif stage_ == 1:
                        if NB < 32:
                            Sc.op("dve", lambda e: e.memset(Gs[:], NEG), (), [b_Gs])
                        TT("dve", Gs[:, :, 0:NB], Gp[:, 0:NTB, 0:NB], FUT[:, tb:tb + NTB, 0:NB], ALU.add,
                           [b_Gp, b_FUT], [b_Gs])
                        for t in range(NTB):
                            Sc.op("dve", lambda e, t=t: e.max(out=thr[:, t, :], in_=Gs[:, t, :]), [b_Gs], [b_thr])
                        TT("dve", nsel[:], Gs[:], thr[:, :, 2:3].to_broadcast([128, NTB, 32]), ALU.is_lt,
                           [b_Gs, b_thr], [b_nsel])
                        TT("pool", MBp[:, :, 64:96], nsel[:], NEGP[:, tb:tb + NTB, :], ALU.mult,
                           [b_nsel, b_NEGP], [b_MBp])
                    else:
                        t4 = (stage_ - 2) * 4
                        for t in range(4):
                            MM(MTp[:, t * 128:(t + 1) * 128], MBp[:, t4 + t, :], ident[:], True, True,
                               [b_MBp, b_ident], [b_MTp])
                        c0 = (tb + t4) * 128
                        CP("act", QA[64:96, c0:c0 + 512], MTp[64:96, :], [b_MTp], [b_QA])

                NST = 2 + NTB // 4
                gate_sched = [(tb, st_) for tb in range(0, NT, NTB) for st_ in range(NST)]

                def head_gate(h):
                    for (tb, st_) in gate_sched:
                        gate_stage(h, tb, st_)

                head_loads(0)
                head_gate(0)
                for h in range(HA):
                    sl = h % 2
                    QA, b_QA = QAs[sl]; KA, b_KA = KAs[sl]; VA, b_VA = VAs[sl]
                    pairs = [(g, kp) for g in range(NG) for kp in range(2 * g + 2)]
                    slots = {}

                    def emit_qk(i):
                        g, kp = pairs[i]
                        Sp, bS = STp.next()
                        for u in range(2):
                            kt = 2 * kp + u
                            delta = 512 * g - 128 * kt
                            near = delta <= 1536
                            MM(Sp[:, u * 512:(u + 1) * 512], KA[0:96, kt * 128:(kt + 1) * 128],
                               QA[0:96, g * 512:(g + 1) * 512], True, not near, [b_KA, b_QA], [bS])
                            if near:
                                MM(Sp[:, u * 512:(u + 1) * 512], ident[:],
                                   TS_[:, h, delta + 384:delta + 384 + 512], False, True, [b_ident, b_TS], [bS])
                        slots[i] = (Sp, bS)

                    emit_qk(0)
                    O, bO = None, None
                    gate_i0 = len(pairs) // 4
                    gate_step = max(1, (len(pairs) - gate_i0 - 2) // len(gate_sched))
                    for i, (g, kp) in enumerate(pairs):
                        npair = 2 * g + 2
                        if kp == 0:
                            O, bO = Op.next()
                            SAG, b_SAG = SAGr.next()
                            LD(SAG[:], s_sag[h * 64:(h + 1) * 64, g * 512:(g + 1) * 512], [b_SAG], b_SAG)
                        if i + 1 < len(pairs):
                            emit_qk(i + 1)
                        if i == 0 and h + 1 < HA:
                            head_loads(h + 1)
                        if h + 1 < HA and i >= gate_i0 and (i - gate_i0) % gate_step == 0:
                            kq = (i - gate_i0) // gate_step
                            if kq < len(gate_sched):
                                gate_stage(h + 1, *gate_sched[kq])
                        Sp, bS = slots.pop(i)
                        P_, bPt = PT.next()
                        ACT(P_[:], Sp[:], AF.Exp, [bS], [bPt])
                        for u in range(2):
                            kt = 2 * kp + u
                            MM(O[:], VA[:, kt, :], P_[:, u * 512:(u + 1) * 512], kt == 0, kt == 2 * npair - 1,
                               [b_VA, bPt], [bO])
                        if kp == npair - 1:
                            Sc.op("dve", lambda e, O=O: e.reciprocal(out=rden[:], in_=O[64:128, :]), [bO], [b_rden])
                            TT("dve", yh[:], O[0:64, :], rden[:], ALU.mult, [bO, b_rden], [b_yh])
                            YG, b_YG = YGr.next()
                            TT("pool", YG[:], yh[:], SAG[:], ALU.mult, [b_yh, b_SAG], [b_YG])
                            ST(XS[h // 2, (h % 2) * 64:(h % 2) * 64 + 64, g * 512:(g + 1) * 512], YG[:], [b_YG], b_YG)
                Sc.emit()

            with ExitStack() as st:
                sb, ps = mk(st)
                QDs = [(sb(f"QD{i}", [128, S], BF16), Buf(f"QD{i}")) for i in range(HB)]
                KDs = [(sb(f"KD{i}", [128, S], BF16), Buf(f"KD{i}")) for i in range(HB)]
                KEr = Ring([(sb(f"KE{i}", [128, 512], BF16), Buf(f"KE{i}")) for i in range(4)])
                VBr = Ring([(sb(f"VB{i}", [64, 8, 128], BF16), Buf(f"VB{i}")) for i in range(4)])
                SBGr = Ring([(sb(f"SBG{i}", [128, 512], BF16), Buf(f"SBG{i}")) for i in range(4)])
                YBr = Ring([(sb(f"YB{i}", [128, 512], BF16), Buf(f"YB{i}")) for i in range(4)])
                KTr = Ring([(sb(f"KT{i}", [64, 8, 128], BF16), Buf(f"KT{i}")) for i in range(4)])
                ATs = Ring([(sb(f"ATs{i}", [64, 64], BF16), Buf(f"ATs{i}")) for i in range(4)])
                Sbs = [[(sb(f"Sb{hh}_{i}", [128, 128], BF16), Buf(f"Sb{hh}_{i}")) for i in range(2)] for hh in range(HB)]
                osq = sb("osq", [128, 512], BF16); b_osq = Buf("osq")
                ort = sb("ort", [128, 512]); b_ort = Buf("ort")
                ors = sb("ors", [128, 512]); b_ors = Buf("ors")
                ybf = sb("ybf", [128, 512]); b_ybf = Buf("ybf")
                KTp = [(ps(f"KTp{i}", [64, 8, 128], BF16), Buf(f"KTp{i}")) for i in range(HB)]
                OHp = [Ring([(ps(f"OHp{hh}_{i}", [128, 512]), Buf(f"OHp{hh}_{i}")) for i in range(2)]) for hh in range(HB)]
                MISC = ps("MISC", [128, 512])
                ATp = [(MISC[0:64, hh * 64:(hh + 1) * 64], Buf(f"ATp{hh}")) for hh in range(HB)]
                dSp = [(MISC[:, 128 + hh * 128:256 + hh * 128], Buf(f"dSp{hh}")) for hh in range(HB)]
                NSp = ps("NSp", [128, 512]); b_NSp = Buf("NSp")
                for k in range(2):
                    AG(XG[k], XS[k], Buf(f"agx{k}"))
                for hh in range(HB):
                    rows = slice(hh * 128, (hh + 1) * 128)
                    LD(QDs[hh][0][:], s_qd[rows, :], [QDs[hh][1]], QDs[hh][1])
                    LD(KDs[hh][0][:], s_kd[rows, :], [KDs[hh][1]], KDs[hh][1])
                    Sc.op("dve", lambda e, hh=hh: e.memset(Sbs[hh][0][0][:], 0.0), (), [Sbs[hh][0][1]])
                cur = [0] * HB

                def group_loads(g):
                    out = []
                    for hh in range(HB):
                        rows = slice(hh * 128, (hh + 1) * 128)
                        KE, bKE = KEr.next()
                        LD(KE[:], s_ke[rows, g * 512:(g + 1) * 512], [bKE], bKE)
                        VB, bVB = VBr.next()
                        vsrc = s_vb[g * 512:(g + 1) * 512, hh * 128:(hh + 1) * 128].rearrange("(c s) v -> s c v", s=64)
                        LD(VB[:], vsrc, [bVB], bVB)
                        SBG, bSBG = SBGr.next()
                        LD(SBG[:], s_sbg[rows, g * 512:(g + 1) * 512], [bSBG], bSBG)
                        out.append((KE, bKE, VB, bVB, SBG, bSBG))
                    return out

                nxt = group_loads(0)
                for g in range(NG):
                    gl_ = nxt
                    if g + 1 < NG:
                        nxt = group_loads(g + 1)
                    KTl, OHl = [], []
                    for hh in range(HB):
                        KE, bKE, VB, bVB, SBG, bSBG = gl_[hh]
                        KTps, bKTp = KTp[hh]
                        for c in range(8):
                            Sc.op("pe", lambda e, KTps=KTps, c=c, KE=KE: e.transpose(
                                out=KTps[:, c, :], in_=KE[:, c * 64:(c + 1) * 64], identity=ident[:]),
                                [bKE, b_ident], [bKTp])
                        KTs, bKT = KTr.next()
                        CP("act", KTs[:], KTps[:], [bKTp], [bKT])
                        KTl.append((KTs, bKT))
                        OHl.append(OHp[hh].next())
                    for c in range(8):
                        for hh in range(HB):
                            KE, bKE, VB, bVB, SBG, bSBG = gl_[hh]
                            QD, b_QD = QDs[hh]; KD, b_KD = KDs[hh]
                            KTs, bKT = KTl[hh]; OH, bOH = OHl[hh]
                            ch = g * 8 + c
                            cs = slice(ch * 64, (ch + 1) * 64)
                            A, bA = ATp[hh]
                            MM(A, KD[:, cs], QD[:, cs], True, True, [b_KD, b_QD], [bA])
                            As, bAs = ATs.next()
                            TT("dve", As[:], A, caus[:], ALU.mult, [bA, b_caus], [bAs])
                            S0, bS0 = Sbs[hh][cur[hh]]
                            S1, bS1 = Sbs[hh][1 - cur[hh]]
                            MM(OH[:, c * 64:(c + 1) * 64], VB[:, c, :], As[:], True, False, [bVB, bAs], [bOH])
                            MM(OH[:, c * 64:(c + 1) * 64], S0[:], QD[:, cs], False, True, [bS0, b_QD], [bOH])
                            dS, bdS = dSp[hh]
                            MM(dS, KTs[:, c, :], VB[:, c, :], True, True, [bKT, bVB], [bdS])
                            STT("dve", S1[:], S0[:], EL[:, hh, ch:ch + 1], dS, ALU.mult, ALU.add,
                                [bS0, b_EL, bdS], [bS1])
                            cur[hh] = 1 - cur[hh]
                    for hh in range(HB):
                        KE, bKE, VB, bVB, SBG, bSBG = gl_[hh]
                        OH, bOH = OHl[hh]
                        ACT(osq[:], OH[:], AF.Square, [bOH], [b_osq])
                        MM(NSp[:], ones[:], osq[:], True, True, [b_ones, b_osq], [b_NSp])
                        ACT(ort[:], NSp[:], AF.Ln, [b_NSp], [b_ort], bias=EPS, scale=1.0 / 128)
                        ACT(ors[:], ort[:], AF.Exp, [b_ort], [b_ors], scale=-0.5)
                        STT("dve", ybf[:], OH[:], go[:, l, hh:hh + 1], ors[:], ALU.mult, ALU.mult,
                            [bOH, b_go, b_ors], [b_ybf])
                        YB, bYB = YBr.next()
                        TT("pool", YB[:], ybf[:], SBG[:], ALU.mult, [b_ybf, bSBG], [bYB])
                        ST(XS[2 + hh, :, g * 512:(g + 1) * 512], YB[:], [bYB], bYB)
                Sc.emit()

            with ExitStack() as st:
                sb, ps = mk(st)
                WA = sb("WA", [128, 4, D], BF16); b_WA = Buf("WA")
                WB_ = sb("WB", [128, 4, D], BF16); b_WB = Buf("WB")
                WO = sb("WO", [128, 8, D], BF16); b_WO = Buf("WO")
                WP = sb("WP", [128, 2, D], BF16); b_WP = Buf("WP")
                WG = sb("WG", [128, 8, D], BF16); b_WG = Buf("WG")
                wst = [(sb(f"wstc{i}", [128, 2, 512]), Buf(f"wstc{i}")) for i in range(2)]
                k = 0
                for (dst, bd, src, nch) in ((WA, b_WA, w_up_a[l], 4), (WB_, b_WB, w_up_b[l], 4),
                                            (WO, b_WO, w_out[l], 8), (WP, b_WP, w_ple[l], 2),
                                            (WG, b_WG, w_pg[l], 8)):
                    sv = src.rearrange("(c p) n -> p c n", p=128)
                    for c2 in range(0, nch, 2):
                        for n2 in range(0, D, 512):
                            t, b = wst[k % 2]
                            LD(t[:], sv[:, c2:c2 + 2, n2:n2 + 512], [b], b)
                            CP(("act", "dve", "pool")[k % 3], dst[:, c2:c2 + 2, n2:n2 + 512], t[:], [b], [bd])
                            k += 1
                INS = []
                for i in range(2):
                    INS.append(dict(
                        YAG=(sb(f"YAG{i}", [128, 4, 512], BF16), Buf(f"YAG{i}")),
                        YBG=(sb(f"YBG{i}", [128, 4, 512], BF16), Buf(f"YBG{i}")),
                        SGA=(sb(f"SGA{i}", [128, 8, 512], BF16), Buf(f"SGA{i}")),
                        SGB=(sb(f"SGB{i}", [128, 8, 512], BF16), Buf(f"SGB{i}")),
                        PB=(sb(f"PB{i}", [128, 2, 512], BF16), Buf(f"PB{i}"))))
                H = sb("Hc", [128, 8, 512]); b_H = Buf("Hc")
                PF = sb("PF", [128, 2, 512]); b_PF = Buf("PF")
                MG = sb("MG", [128, 8, 512], BF16); b_MG = Buf("MG")
                HM = sb("HM", [128, 8, 512]); b_HM = Buf("HM")
                HMb = sb("HMb", [128, 8, 512], BF16); b_HMb = Buf("HMb")
                HN = Ring([(sb(f"HN{i}", [128, 512]), Buf(f"HN{i}")) for i in range(2)])
                HNb = Ring([(sb(f"HNb{i}", [128, 512], BF16), Buf(f"HNb{i}")) for i in range(2)])
                tmp = Ring([(sb(f"tmp{i}", [128, 512]), Buf(f"tmp{i}")) for i in range(4)])
                PS = Ring([(ps(f"PSc{i}", [128, 512]), Buf(f"PSc{i}")) for i in range(8)])
                hv = h_own.rearrange("(c p) t -> p c t", p=128)
                hd = h_dst.rearrange("(c p) t -> p c t", p=128)
                pv = pT[l].rearrange("(c p) t -> p c t", p=128)
                b_XG = [Buf(f"XG{k}") for k in range(4)]
                for k in (2, 3):
                    AG(XG[k], XS[k], Buf(f"agx{k}"), w=[b_XG[k]])
                b_XO = Buf("XO")
                for kx_ in range(4):
                    for rk in range(2):
                        LD(XO[kx_, rk * 128:(rk + 1) * 128, :], XG[kx_, rk * 128:(rk + 1) * 128, bass.ds(off_own, SH)],
                           [b_XO], b_XO, r=[b_XG[kx_]])
                b_HS = [Buf(f"HSp{q}") for q in range(NPC)]

                def loads(g):
                    I = INS[g % 2]
                    ts_ = slice(g * 512, (g + 1) * 512)
                    for c in range(4):
                        rk, kk = c // 2, c % 2
                        LD(I["YAG"][0][:, c, :], XO[kk, rk * 128:(rk + 1) * 128, ts_], [I["YAG"][1]], I["YAG"][1], r=[b_XO])
                        LD(I["YBG"][0][:, c, :], XO[2 + kk, rk * 128:(rk + 1) * 128, ts_], [I["YBG"][1]], I["YBG"][1], r=[b_XO])
                    LD(I["SGA"][0][:], s_sga.rearrange("(c p) t -> p c t", p=128)[:, :, ts_], [I["SGA"][1]], I["SGA"][1])
                    LD(I["SGB"][0][:], s_sgb.rearrange("(c p) t -> p c t", p=128)[:, :, ts_], [I["SGB"][1]], I["SGB"][1])
                    LD(PF[:], pv[:, :, ts_], [b_PF], b_PF)
                    CP("act", I["PB"][0][:], PF[:], [b_PF], [I["PB"][1]])

                loads(0)
                LD(H[:], hv[:, :, 0:512], [b_H], b_H)
                for g in range(NGO):
                    ts_ = slice(g * 512, (g + 1) * 512)
                    I = INS[g % 2]
                    YAG, b_YAG = I["YAG"]; YBG, b_YBG = I["YBG"]; SGA, b_SGA = I["SGA"]; SGB, b_SGB = I["SGB"]
                    PB, b_PB = I["PB"]
                    if g + 1 < NGO:
                        loads(g + 1)
                    for j in range(8):
                        Pa, bPa = PS.next()
                        for c in range(4):
                            MM(Pa[:], WA[:, c, j * 128:(j + 1) * 128], YAG[:, c, :], c == 0, c == 3,
                               [b_WA, b_YAG], [bPa])
                        Pb, bPb = PS.next()
                        for c in range(4):
                            MM(Pb[:], WB_[:, c, j * 128:(j + 1) * 128], YBG[:, c, :], c == 0, c == 3,
                               [b_WB, b_YBG], [bPb])
                        t1, bt1 = tmp.next()
                        TT("dve", t1[:], Pa[:], SGA[:, j, :], ALU.mult, [bPa, b_SGA], [bt1])
                        t2, bt2 = tmp.next()
                        TT("dve", t2[:], Pb[:], SGB[:, j, :], ALU.mult, [bPb, b_SGB], [bt2])
                        TT("pool", MG[:, j, :], t1[:], t2[:], ALU.add, [bt1, bt2], [b_MG])
                    for j in range(8):
                        Po, bPo = PS.next()
                        for c in range(8):
                            MM(Po[:], WO[:, c, j * 128:(j + 1) * 128], MG[:, c, :], c == 0, c == 7,
                               [b_WO, b_MG], [bPo])
                        TT("dve", HM[:, j, :], Po[:], H[:, j, :], ALU.add, [bPo, b_H], [b_HM])
                        CP("act", HMb[:, j, :], HM[:, j, :], [b_HM], [b_HMb])
                    if g + 1 < NGO:
                        LD(H[:], hv[:, :, (g + 1) * 512:(g + 2) * 512], [b_H], b_H)
                    q, col = (g * 512) // PIECE, (g * 512) % PIECE
                    for j in range(8):
                        Pp, bPp = PS.next()
                        for c in range(2):
                            MM(Pp[:], WP[:, c, j * 128:(j + 1) * 128], PB[:, c, :], c == 0, c == 1,
                               [b_WP, b_PB], [bPp])
                        Pg, bPg = PS.next()
                        for c in range(8):
                            MM(Pg[:], WG[:, c, j * 128:(j + 1) * 128], HMb[:, c, :], c == 0, c == 7,
                               [b_WG, b_HMb], [bPg])
                        sg, bsg = tmp.next()
                        ACT(sg[:], Pg[:], AF.Sigmoid, [bPg], [bsg])
                        t1, bt1 = tmp.next()
                        TT("dve", t1[:], Pp[:], sg[:], ALU.mult, [bPp, bsg], [bt1])
                        hn, bhn = HN.next()
                        TT("pool", hn[:], t1[:], HM[:, j, :], ALU.add, [bt1, b_HM], [bhn])
                        ST(hd[:, j, ts_], hn[:], [bhn], bhn, q="sp")
                        if not last:
                            hb, bhb = HNb.next()
                            CP("act", hb[:], hn[:], [bhn], [bhb])
                            ST(HS[q, j * 128:(j + 1) * 128, col:col + 512], hb[:], [bhb], bhb, w=[b_HS[q]], q="sp")
                    if not last and (g + 1) * 512 % PIECE == 0:
                        AG(HG[q], HS[q], Buf(f"agh{q}"), r=[b_HS[q]])
                Sc.emit()

        print("total ops", Sc.total, "sems", Sc.nsem)
    return nc


def host_consts(S):
    NT = S // 128
    bf = ml_dtypes.bfloat16
    c = {}
    c["c_ident"] = np.eye(128, dtype=np.float32).astype(bf)
    blk = np.zeros((128, 128), np.float32)
    blk[:64, :64] = 1.0
    blk[64:, 64:] = 1.0
    c["c_blk"] = blk.astype(bf)
    oh = np.zeros((32, S), np.float32)
    for n in range(S // 256):
        oh[n, n * 256:(n + 1) * 256] = 1.0
    c["c_onehot"] = oh.astype(bf)
    c["c_causal"] = np.triu(np.ones((64, 64), np.float32))
    sm = np.ones((128, 512), np.float32)
    sm[:, ::64] = 0.0
    c["c_scan"] = sm
    fut = np.zeros((NT, 32), np.float32)
    neg = np.full((NT, 32), NEG, np.float32)
    for t in range(NT):
        b = t // 2
        fut[t, b:] = NEG
        neg[t, b] = 0.0
    c["c_fut"] = np.ascontiguousarray(np.broadcast_to(fut[None], (128, NT, 32))).astype(bf)
    c["c_neg"] = np.ascontiguousarray(np.broadcast_to(neg[None], (128, NT, 32))).astype(bf)
    return c


def host_strips(rel_bias, heads):
    i = np.arange(128)[:, None]
    u = np.arange(STRIP)[None, :]
    rel = u - 384 - i
    bucket = t5_bucket_np(rel)
    strip = np.empty((128, len(heads), STRIP), np.float32)
    for k, h in enumerate(heads):
        g = rel_bias[:, h][bucket]
        strip[:, k, :] = np.where(rel >= 0, g, np.float32(NEG))
    c31 = np.ascontiguousarray(np.broadcast_to(rel_bias[31, heads][None, :], (128, len(heads)))).astype(np.float32)
    return strip, c31


def host_inputs(b, r, S, x, p, norm_gain, w_in, q_norm_gain, k_norm_gain, rel_bias, hgrn_lb_logits,
                hgrn_out_gain, w_up_a, w_up_b, w_out, w_ple, w_ple_gate, consts):
    L = w_in.shape[0]
    SH = S // 2
    m = dict(consts)
    m["xT"] = np.ascontiguousarray(x[b, :S].T)
    m["xT_own"] = np.ascontiguousarray(x[b, r * SH:(r + 1) * SH].T)
    m["pT"] = np.ascontiguousarray(np.transpose(p[:, b, r * SH:(r + 1) * SH, :], (0, 2, 1)))
    cols = []
    for blk0 in range(0, 4096, 512):
        cols.append(np.arange(blk0 + r * 256, blk0 + (r + 1) * 256))
    cols.append(np.arange(4096, 6144))
    cols = np.concatenate(cols)
    m["w_in"] = np.ascontiguousarray(w_in[:, :, cols])
    m["w_up_a"] = w_up_a
    m["w_up_b"] = w_up_b
    m["w_out"] = w_out
    m["w_ple"] = w_ple
    m["w_pg"] = w_ple_gate
    m["g_norm"] = np.ascontiguousarray(np.transpose(norm_gain.reshape(L, 8, 128), (2, 0, 1)))
    m["g_q"] = np.ascontiguousarray(np.concatenate([q_norm_gain, q_norm_gain], axis=1).T)
    m["g_k"] = np.ascontiguousarray(np.concatenate([k_norm_gain, k_norm_gain], axis=1).T)
    m["g_o"] = np.ascontiguousarray(np.transpose(hgrn_out_gain.reshape(L, 4, 128)[:, 2 * r:2 * r + 2], (2, 0, 1)))
    m["lbl"] = np.ascontiguousarray(np.transpose(hgrn_lb_logits.reshape(L, 4, 128)[:, 2 * r:2 * r + 2], (2, 0, 1)))
    strip, c31 = host_strips(rel_bias, list(range(4 * r, 4 * r + 4)))
    m["strip_raw"] = strip
    m["c31"] = c31
    return m


_NC_CACHE = {}


def kernel(x, p, norm_gain, w_in, q_norm_gain, k_norm_gain, rel_bias, hgrn_lb_logits,
           hgrn_out_gain, w_up_a, w_up_b, w_out, w_ple, w_ple_gate):
    args = [np.asarray(a, dtype=np.float32) for a in (
        x, p, norm_gain, w_in, q_norm_gain, k_norm_gain, rel_bias, hgrn_lb_logits,
        hgrn_out_gain, w_up_a, w_up_b, w_out, w_ple, w_ple_gate)]
    x = args[0]
    B, S, _ = x.shape
    if S not in _NC_CACHE:
        _NC_CACHE[S] = build(S)
    nc = _NC_CACHE[S]
    consts = host_consts(S)
    in_maps = [host_inputs(i // 2, i % 2, S, *args, consts) for i in range(2 * B)]
    res = run_bass_kernel_spmd(nc, in_maps, core_ids=list(range(2 * B)))
    SH = S // 2
    out = np.empty((B, S, D), np.float32)
    for i in range(2 * B):
        out[i // 2, (i % 2) * SH:(i % 2 + 1) * SH, :] = res.results[i]["hT_out"].T
    return out
```

```python
import math
from contextlib import ExitStack

import numpy as np
import ml_dtypes
import concourse.bass as bass
import concourse.mybir as mybir
from concourse.bass_utils import run_bass_kernel_spmd

F32 = mybir.dt.float32
BF16 = mybir.dt.bfloat16
AF = mybir.ActivationFunctionType
ALU = mybir.AluOpType
AX = mybir.AxisListType

SEM_CHUNK = 20000
D = 1024
NEG = -30000.0
STRIP = 2432
EPS = 1e-6


class Buf:
    __slots__ = ("name", "last_writer", "dma_writers", "readers", "dma_readers")

    def __init__(self, name):
        self.name = name
        self.clear()

    def clear(self):
        self.last_writer = None
        self.dma_writers = []
        self.readers = {}
        self.dma_readers = []


class Sched:
    def __init__(self, nc, stack):
        self.nc = nc
        self.stack = stack
        self.ops = []
        self.eng = {"pe": nc.tensor, "act": nc.scalar, "dve": nc.vector,
                    "pool": nc.gpsimd, "sp": nc.sync}
        self.nsem = 0
        self.eng_count = {}
        self.eng_sems = {}
        self.dma_pool = []
        self.dma_free = []
        self.key_slot = {}
        self.waited = {}
        self.total = 0

    def new_sem(self, name):
        self.nsem += 1
        return self.stack.enter_context(self.nc.semaphore(name))

    def op(self, eng, fn, reads=(), writes=()):
        self.ops.append(["c", eng, fn, tuple(reads), tuple(writes), None, 1])

    def dma(self, queue, fn, reads=(), writes=(), key=None, inc=16):
        assert key is not None
        self.ops.append(["d", queue, fn, tuple(reads), tuple(writes), key, inc])

    def emit(self):
        ops = self.ops
        n = len(ops)
        deps = [None] * n
        signaling = [False] * n
        last_of_eng = {}
        for i, o in enumerate(ops):
            d = set()
            isd = o[0] == "d"
            for b in o[3]:
                if b.last_writer is not None:
                    d.add(b.last_writer)
                d.update(b.dma_writers)
            for b in o[4]:
                if b.last_writer is not None:
                    d.add(b.last_writer)
                d.update(b.readers.values())
                d.update(b.dma_readers)
                if not isd:
                    d.update(b.dma_writers)
            d.discard(i)
            for b in o[3]:
                if isd:
                    b.dma_readers.append(i)
                else:
                    b.readers[o[1]] = i
            for b in o[4]:
                if isd:
                    b.dma_writers.append(i)
                else:
                    b.last_writer = i
                    b.dma_writers = []
                    b.readers = {}
                    b.dma_readers = []
            deps[i] = d
            for j in d:
                signaling[j] = True
            if o[0] == "c":
                last_of_eng[o[1]] = i
        for e, i in last_of_eng.items():
            signaling[i] = True
        sig = [None] * n
        for i, o in enumerate(ops):
            if o[0] == "c":
                if not signaling[i]:
                    continue
                e = o[1]
                c = self.eng_count.get(e, 0)
                k = c // SEM_CHUNK
                if (e, k) not in self.eng_sems:
                    self.eng_sems[(e, k)] = self.new_sem(f"s_{e}_{k}")
                self.eng_count[e] = c + 1
                sig[i] = (self.eng_sems[(e, k)], c - k * SEM_CHUNK + 1, 1, ("e", e, k))
            else:
                key = o[5]
                if key not in self.key_slot:
                    if self.dma_free:
                        s = self.dma_free.pop()
                    else:
                        self.dma_pool.append([self.new_sem(f"d_{len(self.dma_pool)}"), 0])
                        s = len(self.dma_pool) - 1
                    self.key_slot[key] = s
                s = self.key_slot[key]
                self.dma_pool[s][1] += o[6]
                sig[i] = (self.dma_pool[s][0], self.dma_pool[s][1], o[6], ("k", s))
        waited = self.waited
        for i, o in enumerate(ops):
            e = o[1]
            engine = self.eng[e]
            w = waited.setdefault(e, {})
            for j in sorted(deps[i]):
                pj = ops[j]
                if pj[0] == "c" and pj[1] == "pe" and e == "pe" and o[0] == "c":
                    continue
                sem, val, _, sid = sig[j]
                if w.get(sid, 0) >= val:
                    continue
                w[sid] = val
                engine.wait_ge(sem, val)
            ins = o[2](engine)
            if sig[i] is not None:
                ins.then_inc(sig[i][0], sig[i][2])
        finals = []
        for (e, k), sem in self.eng_sems.items():
            c = self.eng_count.get(e, 0)
            if c // SEM_CHUNK == k and c - k * SEM_CHUNK > 0:
                finals.append((sem, c - k * SEM_CHUNK, ("e", e, k)))
            elif c // SEM_CHUNK > k:
                finals.append((sem, SEM_CHUNK, ("e", e, k)))
        for s, (sem, c) in enumerate(self.dma_pool):
            if c > 0:
                finals.append((sem, c, ("k", s)))
        for e in ("sp", "pe", "act", "dve", "pool"):
            w = waited.setdefault(e, {})
            for sem, val, sid in finals:
                if w.get(sid, 0) >= val:
                    continue
                w[sid] = val
                self.eng[e].wait_ge(sem, val)
        for o in ops:
            for b in o[3] + o[4]:
                b.clear()
        self.key_slot = {}
        self.dma_free = list(range(len(self.dma_pool)))
        self.total += n
        self.ops = []
        return n


class Ring:
    def __init__(self, items):
        self.items = items
        self.i = 0

    def next(self):
        it = self.items[self.i % len(self.items)]
        self.i += 1
        return it


def t5_bucket_np(rel):
    n = np.maximum(rel, 0)
    nf = np.maximum(n, 16).astype(np.float32)
    large = 16 + (np.log(nf / np.float32(16)) / np.float32(math.log(128)) * np.float32(16)).astype(np.int32)
    large = np.minimum(large, 31)
    return np.where(n < 16, n, large)


def build(S, L=2, debug=False, groups=None):
    NT, NB, NG, NCH = S // 128, S // 256, S // 512, S // 64
    SH, NGO = S // 2, S // 1024
    HA, HB = 4, 2
    if groups is None:
        groups = [[0, 1], [2, 3], [4, 5], [6, 7]]
    nc = bass.Bass("TRN2", target_bir_lowering=False)

    def din(name, shape, dt=F32):
        return nc.dram_tensor(name, list(shape), dt, kind="ExternalInput").ap()

    def dscr(name, shape, dt=BF16, out=False):
        kind = "Internal"
        return nc.dram_tensor(name, list(shape), dt, kind=kind).ap()

    xT = din("xT", [D, S])
    xT_own = din("xT_own", [D, SH])
    pT = din("pT", [L, 256, SH])
    w_in = din("w_in", [L, D, 4096])
    w_up_a = din("w_up_a", [L, 512, D])
    w_up_b = din("w_up_b", [L, 512, D])
    w_out = din("w_out", [L, D, D])
    w_ple = din("w_ple", [L, 256, D])
    w_pg = din("w_pg", [L, D, D])
    g_norm = din("g_norm", [128, L, 8])
    g_q = din("g_q", [128, L])
    g_k = din("g_k", [128, L])
    g_o = din("g_o", [128, L, HB])
    lbl = din("lbl", [128, L, HB])
    strip_raw = din("strip_raw", [128, HA, STRIP])
    c31 = din("c31", [128, HA])
    c_ident = din("c_ident", [128, 128], BF16)
    c_blk = din("c_blk", [128, 128], BF16)
    c_onehot = din("c_onehot", [32, S], BF16)
    c_causal = din("c_causal", [64, 64])
    c_scan = din("c_scan", [128, 512])
    c_fut = din("c_fut", [128, NT, 32], BF16)
    c_neg = din("c_neg", [128, NT, 32], BF16)

    hT_out = nc.dram_tensor("hT_out", [D, SH], F32, kind="ExternalOutput").ap()
    h_mid = dscr("h_mid", [D, SH], F32)
    s_qn = dscr("s_qn", [256, S])
    s_kn = dscr("s_kn", [256, S])
    s_v = dscr("s_v", [S, 256])
    s_sag = dscr("s_sag", [256, S])
    s_qd = dscr("s_qd", [256, S])
    s_kd = dscr("s_kd", [256, S])
    s_ke = dscr("s_ke", [256, S])
    s_vb = dscr("s_vb", [S, 256])
    s_sbg = dscr("s_sbg", [256, S])
    s_sga = dscr("s_sga", [D, SH])
    s_sgb = dscr("s_sgb", [D, SH])
    XS = dscr("XS", [4, 128, S], out=True)
    XG = dscr("XG", [4, 2 * 128, S], out=True)
    XO = dscr("XO", [4, 2 * 128, SH])
    PIECE = max(512, SH // 4)
    NPC = SH // PIECE
    HS = dscr("HS", [NPC, D, PIECE])
    HG = dscr("HG", [NPC, 2 * D, PIECE])

    off_own = (nc.sync.partition_id() % 2) * SH

    with ExitStack() as outer:
        Sc = Sched(nc, outer)
        uniq = [0]

        def mk(stack):
            uniq[0] += 1
            tag = f"u{uniq[0]}_"

            def sb(name, shape, dt=F32):
                return stack.enter_context(nc.sbuf_tensor(tag + name, list(shape), dt))

            def ps(name, shape, dt=F32):
                return stack.enter_context(nc.psum_tensor(tag + name, list(shape), dt))
            return sb, ps

        def MM(out, lhsT, rhs, start, stop, r, w):
            Sc.op("pe", lambda e: e.matmul(out, lhsT=lhsT, rhs=rhs, start=start, stop=stop), r, w)

        def ACT(out, in_, func, r, w, bias=0.0, scale=1.0, eng="act"):
            Sc.op(eng, lambda e: e.activation(out=out, in_=in_, func=func, bias=bias, scale=scale), r, w)

        def TT(eng, out, in0, in1, op, r, w):
            Sc.op(eng, lambda e: e.tensor_tensor(out=out, in0=in0, in1=in1, op=op), r, w)

        def TSC(eng, out, in0, s1, s2, op0, op1, r, w):
            if s2 is None:
                Sc.op(eng, lambda e: e.tensor_scalar(out=out, in0=in0, scalar1=s1, scalar2=None, op0=op0), r, w)
            else:
                Sc.op(eng, lambda e: e.tensor_scalar(out=out, in0=in0, scalar1=s1, scalar2=s2, op0=op0, op1=op1), r, w)

        def STT(eng, out, in0, scalar, in1, op0, op1, r, w):
            Sc.op(eng, lambda e: e.scalar_tensor_tensor(out=out, in0=in0, scalar=scalar, in1=in1, op0=op0, op1=op1), r, w)

        def CP(eng, out, in_, r, w):
            if eng == "act":
                Sc.op("act", lambda e: e.activation(out=out, in_=in_, func=AF.Copy), r, w)
            else:
                Sc.op(eng, lambda e: e.tensor_copy(out=out, in_=in_), r, w)

        def LD(out, in_, w, key, r=(), q="sp"):
            Sc.dma(q, lambda e: e.dma_start(out=out, in_=in_), r, w, key)

        def ST(out, in_, r, key, w=(), q="pool"):
            Sc.dma(q, lambda e: e.dma_start(out=out, in_=in_), r, w, key)

        def AG(out, in_, key, r=(), w=()):
            Sc.dma("pool", lambda e: e.collective_compute(
                "AllGather", ALU.bypass, replica_groups=groups, ins=[in_], outs=[out]), r, w, key, inc=1)

        sbP, _ = mk(outer)
        ident = sbP("ident", [128, 128], BF16); b_ident = Buf("ident")
        blk = sbP("blk", [128, 128], BF16); b_blk = Buf("blk")
        ones = sbP("ones", [128, 128], BF16); b_ones = Buf("ones")
        gn = sbP("gn", [128, L, 8]); b_gn = Buf("gn")
        gq = sbP("gq", [128, L]); b_gq = Buf("gq")
        gk = sbP("gk", [128, L]); b_gk = Buf("gk")
        go = sbP("go", [128, L, HB]); b_go = Buf("go")
        lb = sbP("lb", [128, L, HB]); b_lb = Buf("lb")
        oml = sbP("oml", [128, L, HB]); b_oml = Buf("oml")
        lbe = sbP("lbe", [128, L, HB]); b_lbe = Buf("lbe")
        lbs = sbP("lbs", [128, HB]); b_lbs = Buf("lbs")
        KS = sbP("KS", [128, 2, NB]); b_KS = Buf("KS")
        EL = sbP("EL", [128, HB, NCH]); b_EL = Buf("EL")
        caus = sbP("caus", [64, 64]); b_caus = Buf("caus")
        scanm = sbP("scanm", [128, 512]); b_scanm = Buf("scanm")

        with ExitStack() as st:
            sb, ps = mk(st)
            LD(ident[:], c_ident, [b_ident], b_ident)
            LD(blk[:], c_blk, [b_blk], b_blk)
            Sc.op("pool", lambda e: e.memset(ones[:], 1.0), (), [b_ones])
            LD(gn[:], g_norm, [b_gn], b_gn)
            LD(gq[:], g_q, [b_gq], b_gq)
            LD(gk[:], g_k, [b_gk], b_gk)
            LD(go[:], g_o, [b_go], b_go)
            LD(lbe[:], lbl, [b_lbe], b_lbe)
            LD(caus[:], c_causal, [b_caus], b_caus)
            LD(scanm[:], c_scan, [b_scanm], b_scanm)
            TSC("dve", gq[:], gq[:], 0.125, None, ALU.mult, None, [b_gq], [b_gq])
            ACT(lbe[:], lbe[:], AF.Exp, [b_lbe], [b_lbe])
            CP("dve", lbs[:], lbe[:, 0, :], [b_lbe], [b_lbs])
            for l in range(1, L):
                TT("dve", lbs[:], lbs[:], lbe[:, l, :], ALU.add, [b_lbs, b_lbe], [b_lbs])
            Sc.op("dve", lambda e: e.reciprocal(out=lbs[:], in_=lbs[:]), [b_lbs], [b_lbs])
            Sc.op("dve", lambda e: e.memset(lb[:, 0, :], 0.0), (), [b_lb])
            for l in range(1, L):
                TT("dve", lb[:, l, :], lb[:, l - 1, :], lbe[:, l, :], ALU.add, [b_lb, b_lbe], [b_lb])
            for l in range(1, L):
                TT("dve", lb[:, l, :], lb[:, l, :], lbs[:], ALU.mult, [b_lb, b_lbs], [b_lb])
            for l in range(L):
                TSC("dve", oml[:, l, :], lb[:, l, :], -1.0, 1.0, ALU.mult, ALU.add, [b_lb], [b_oml])
            Sc.emit()

        for l in range(L):
            first, last = (l == 0), (l == L - 1)
            h_own = xT_own if first else h_mid
            h_dst = hT_out if last else h_mid

            with ExitStack() as st:
                sb, ps = mk(st)
                Wb = sb("Wb", [128, 8, 4096], BF16); b_Wb = Buf("Wb")
                wst = [(sb(f"wst{i}", [128, 8, 128]), Buf(f"wst{i}")) for i in range(2)]
                wv = w_in[l].rearrange("(c p) n -> p c n", p=128)
                for i in range(32):
                    t, b = wst[i % 2]
                    LD(t[:], wv[:, :, i * 128:(i + 1) * 128], [b], b)
                    CP(("act", "dve", "pool")[i % 3], Wb[:, :, i * 128:(i + 1) * 128], t[:], [b], [b_Wb])
                HDT = F32 if first else BF16
                H = sb("H", [128, 8, 512], HDT); b_H = Buf("H")
                H2 = sb("H2", [128, 8, 512]); b_H2 = Buf("H2")
                SQ = sb("SQ", [128, 8, 512], BF16); b_SQ = Buf("SQ")
                XNs = [(sb(f"XN{i}", [128, 8, 512], BF16), Buf(f"XN{i}")) for i in range(2)]
                rstd = sb("rstd", [128, 512]); b_rstd = Buf("rstd")
                stage = Ring([(sb(f"stg{i}", [128, 512], BF16), Buf(f"stg{i}")) for i in range(6)])
                f32r = Ring([(sb(f"f32r{i}", [128, 512]), Buf(f"f32r{i}")) for i in range(8)])
                QB = [(sb(f"QB{i}", [128, 512]), Buf(f"QB{i}")) for i in range(HB)]
                SG = [(sb(f"SG{i}", [128, 512]), Buf(f"SG{i}")) for i in range(HB)]
                sqq = Ring([(sb(f"sqq{i}", [128, 512], BF16), Buf(f"sqq{i}")) for i in range(2)])
                p_ss = ps("p_ss", [128, 512]); b_pss = Buf("p_ss")
                PS = Ring([(ps(f"PSa{i}", [128, 512]), Buf(f"PSa{i}")) for i in range(5)])
                BS = Ring([(ps(f"BSa{i}", [128, 512]), Buf(f"BSa{i}")) for i in range(2)])
                Sc.op("dve", lambda e: e.memset(KS[:], 0.0), (), [b_KS])

                def load_h(g):
                    if first:
                        LD(H[:], xT.rearrange("(c p) t -> p c t", p=128)[:, :, g * 512:(g + 1) * 512], [b_H], b_H)
                    else:
                        half, tok = g // NGO, (g % NGO) * 512
                        q, col = tok // PIECE, tok % PIECE
                        src = HG[q, half * D:(half + 1) * D, col:col + 512].rearrange("(c p) t -> p c t", p=128)
                        LD(H[:], src, [b_H], b_H)

                def norm_part1(g):
                    load_h(g)
                    ACT(SQ[:].rearrange("p c t -> p (c t)"), H[:].rearrange("p c t -> p (c t)"),
                        AF.Square, [b_H], [b_SQ])

                def norm_part2(Hs, bH, XN, b_XN):
                    for c in range(8):
                        MM(p_ss[:], ones[:], SQ[:, c, :], c == 0, c == 7, [b_ones, b_SQ], [b_pss])
                    ACT(rstd[:], p_ss[:], AF.Ln, [b_pss], [b_rstd], bias=EPS, scale=1.0 / D)
                    ACT(rstd[:], rstd[:], AF.Exp, [b_rstd], [b_rstd], scale=-0.5)
                    for c in range(8):
                        STT("dve", XN[:, c, :], Hs[:, c, :], gn[:, l, c:c + 1], rstd[:],
                            ALU.mult, ALU.mult, [bH, b_gn, b_rstd], [b_XN])

                norm_part1(0)
                norm_part2(H, b_H, *XNs[0])
                for g in range(NG):
                    XN, b_XN = XNs[g % 2]

                    def proj_fm(c0):
                        P, bP = PS.next()
                        for c in range(8):
                            MM(P[:], Wb[:, c, c0:c0 + 128], XN[:, c, :], c == 0, c == 7, [b_Wb, b_XN], [bP])
                        return P, bP

                    def store_fm(dst, j, t, b):
                        ST(dst[j * 128:(j + 1) * 128, g * 512:(g + 1) * 512], t[:], [b], b)

                    def qk_tail(j, P, bP, s2, bs2):
                        isk = j >= 2
                        Bp, bB = BS.next()
                        MM(Bp[:], blk[:], s2[:], True, True, [b_blk, bs2], [bB])
                        r1, br1 = f32r.next()
                        ACT(r1[:], Bp[:], AF.Ln, [bB], [br1], bias=EPS, scale=1.0 / 64)
                        ACT(r1[:], r1[:], AF.Exp, [br1], [br1], scale=-0.5)
                        o, bo = stage.next()
                        gg = gk if isk else gq
                        STT("dve", o[:], P[:], gg[:, l:l + 1], r1[:], ALU.mult, ALU.mult,
                            [bP, b_gk if isk else b_gq, br1], [bo])
                        if isk:
                            Sc.op("dve", lambda e, o=o, j=j, g=g: e.tensor_reduce(
                                out=KS[:, j - 2, 2 * g:2 * g + 2],
                                in_=o[:].rearrange("p (b t) -> p b t", t=256),
                                axis=AX.X, op=ALU.add), [bo], [b_KS])
                        store_fm(s_kn if isk else s_qn, j % 2, o, bo)

                    pend = None
                    for j in range(4):
                        P, bP = proj_fm(j * 128)
                        s2, bs2 = sqq.next()
                        ACT(s2[:], P[:], AF.Square, [bP], [bs2])
                        if pend is not None:
                            qk_tail(*pend)
                        pend = (j, P, bP, s2, bs2)
                    firstv = True
                    for (c0, dst) in ((512, s_v), (1536, s_vb)):
                        for tt in range(4):
                            P, bP = PS.next()
                            for c in range(8):
                                MM(P[:, 0:256], XN[:, c, tt * 128:(tt + 1) * 128], Wb[:, c, c0:c0 + 256],
                                   c == 0, c == 7, [b_XN, b_Wb], [bP])
                            if firstv:
                                qk_tail(*pend)
                                firstv = False
                            o, bo = stage.next()
                            CP("act", o[:, 0:256], P[:, 0:256], [bP], [bo])
                            ST(dst[g * 512 + tt * 128:g * 512 + (tt + 1) * 128, :], o[:, 0:256], [bo], bo)
                    if g + 1 < NG:
                        norm_part1(g + 1)
                    for jj in range(HB):
                        P, bP = proj_fm(1280 + jj * 128)
                        ACT(SG[jj][0][:], P[:], AF.Sigmoid, [bP], [SG[jj][1]])
                    if g + 1 < NG:
                        norm_part2(H, b_H, *XNs[(g + 1) % 2])
                    for (c00, dst) in ((768, s_sag), (1792, s_sbg)):
                        for jj in range(2):
                            P, bP = proj_fm(c00 + jj * 128)
                            o, bo = stage.next()
                            ACT(o[:], P[:], AF.Silu, [bP], [bo])
                            store_fm(dst, jj, o, bo)
                    for jj in range(HB):
                        P, bP = proj_fm(1024 + jj * 128)
                        ACT(QB[jj][0][:], P[:], AF.Silu, [bP], [QB[jj][1]])
                    for jj in range(HB):
                        sg, bsg = SG[jj]
                        f, bf_ = f32r.next()
                        TSC("dve", f[:], sg[:], oml[:, l, jj:jj + 1], lb[:, l, jj:jj + 1], ALU.mult, ALU.add,
                            [bsg, b_oml, b_lb], [bf_])
                        gl, bgl = f32r.next()
                        ACT(gl[:], f[:], AF.Ln, [bf_], [bgl])
                        cum, bcum = f32r.next()
                        Sc.op("dve", lambda e, cum=cum, gl=gl: e.tensor_tensor_scan(
                            out=cum[:], data0=scanm[:], data1=gl[:], initial=0.0,
                            op0=ALU.mult, op1=ALU.add), [bgl, b_scanm], [bcum])
                        ec, bec = f32r.next()
                        ACT(ec[:], cum[:], AF.Exp, [bcum], [bec])
                        ACT(gl[:], cum[:], AF.Exp, [bcum], [bgl], scale=-1.0)
                        CP("pool", EL[:, jj, g * 8:(g + 1) * 8],
                           ec[:].rearrange("p (c t) -> p c t", t=64)[:, :, 63], [bec], [b_EL])
                        o, bo = stage.next()
                        TT("pool", o[:], QB[jj][0][:], ec[:], ALU.mult, [QB[jj][1], bec], [bo])
                        store_fm(s_qd, jj, o, bo)
                        TSC("dve", f[:], f[:], -1.0, 1.0, ALU.mult, ALU.add, [bf_], [bf_])
                        TT("dve", f[:], f[:], gl[:], ALU.mult, [bf_, bgl], [bf_])
                        o, bo = stage.next()
                        CP("act", o[:], f[:], [bf_], [bo])
                        store_fm(s_kd, jj, o, bo)
                        o, bo = stage.next()
                        TT("pool", o[:].rearrange("p (c t) -> p c t", t=64),
                           f[:].rearrange("p (c t) -> p c t", t=64),
                           EL[:, jj, g * 8:(g + 1) * 8].unsqueeze(2).to_broadcast([128, 8, 64]),
                           ALU.mult, [bf_, b_EL], [bo])
                        store_fm(s_ke, jj, o, bo)
                hov = h_own.rearrange("(c p) t -> p c t", p=128)
                def gate_norm(g):
                    XN, b_XN = XNs[g % 2]
                    LD(H2[:], hov[:, :, g * 512:(g + 1) * 512], [b_H2], b_H2)
                    ACT(SQ[:].rearrange("p c t -> p (c t)"), H2[:].rearrange("p c t -> p (c t)"),
                        AF.Square, [b_H2], [b_SQ])
                    norm_part2(H2, b_H2, XN, b_XN)

                gate_norm(0)
                for g in range(NGO):
                    XN, b_XN = XNs[g % 2]
                    for jj in range(16):
                        if jj == 6 and g + 1 < NGO:
                            gate_norm(g + 1)
                        P, bP = PS.next()
                        for c in range(8):
                            MM(P[:], Wb[:, c, 2048 + jj * 128:2048 + (jj + 1) * 128], XN[:, c, :], c == 0, c == 7,
                               [b_Wb, b_XN], [bP])
                        o, bo = stage.next()
                        ACT(o[:], P[:], AF.Sigmoid, [bP], [bo])
                        dst = s_sga if jj < 8 else s_sgb
                        ST(dst[(jj % 8) * 128:(jj % 8 + 1) * 128, g * 512:(g + 1) * 512], o[:], [bo], bo)
                Sc.emit()

            with ExitStack() as st:
                sb, ps = mk(st)
                QAs = [(sb(f"QA{i}", [128, S], BF16), Buf(f"QA{i}")) for i in range(2)]
                KAs = [(sb(f"KA{i}", [128, S], BF16), Buf(f"KA{i}")) for i in range(2)]
                VAs = [(sb(f"VA{i}", [128, NT, 128], BF16), Buf(f"VA{i}")) for i in range(2)]
                SAGr = Ring([(sb(f"SAG{i}", [64, 512], BF16), Buf(f"SAG{i}")) for i in range(3)])
                YGr = Ring([(sb(f"YG{i}", [64, 512], BF16), Buf(f"YG{i}")) for i in range(3)])
                FUT = sb("FUT", [128, NT, 32], BF16); b_FUT = Buf("FUT")
                NEGP = sb("NEGP", [128, NT, 32], BF16); b_NEGP = Buf("NEGP")
                KMh = sb("KMh", [64, 32], BF16); b_KMh = Buf("KMh")
                NTB = min(NT, 16)
                Gs = sb("Gs", [128, NTB, 32]); b_Gs = Buf("Gs")
                thr = sb("thr", [128, NTB, 8]); b_thr = Buf("thr")
                nsel = sb("nsel", [128, NTB, 32]); b_nsel = Buf("nsel")
                MBp = sb("MBp", [128, NTB, 128], BF16); b_MBp = Buf("MBp")
                PT = Ring([(sb(f"PT{i}", [128, 1024], BF16), Buf(f"PT{i}")) for i in range(3)])
                rden = sb("rden", [64, 512]); b_rden = Buf("rden")
                yh = sb("yh", [64, 512]); b_yh = Buf("yh")
                STp = Ring([(ps(f"STp{i}", [128, 1024]), Buf(f"STp{i}")) for i in range(2)])
                Op = Ring([(ps(f"Op{i}", [128, 512]), Buf(f"Op{i}")) for i in range(2)])
                Gp = ps("Gp", [128, 16, 32]); b_Gp = Buf("Gp")
                MTp = ps("MTp", [128, 512]); b_MTp = Buf("MTp")
                LD(FUT[:], c_fut, [b_FUT], b_FUT)
                LD(NEGP[:], c_neg, [b_NEGP], b_NEGP)
                for i in range(2):
                    LD(KAs[i][0][64:96, :], c_onehot, [KAs[i][1]], KAs[i][1])
                    Sc.op("pool", lambda e, i=i: e.memset(VAs[i][0][:, :, 64:128], 1.0), (), [VAs[i][1]])
                Sc.op("pool", lambda e: e.memset(MBp[:], 0.0), (), [b_MBp])
                TS_ = sb("TS", [128, HA, STRIP], BF16); b_TS = Buf("TS")
                c31s = sb("c31s", [128, HA]); b_c31 = Buf("c31s")
                LD(c31s[:], c31, [b_c31], b_c31)
                SPC = STRIP // 4
                stgs = [(sb(f"stripstg{i}", [128, SPC]), Buf(f"stripstg{i}")) for i in range(2)]
                for h in range(HA):
                    for q4 in range(4):
                        stg, b_stg = stgs[(h * 4 + q4) % 2]
                        LD(stg[:], strip_raw[:, h, q4 * SPC:(q4 + 1) * SPC], [b_stg], b_stg)
                        TSC("dve", TS_[:, h, q4 * SPC:(q4 + 1) * SPC], stg[:], c31s[:, h:h + 1], None, ALU.subtract,
                            None, [b_stg, b_c31], [b_TS])

                def head_loads(h):
                    sl = h % 2
                    QA, b_QA = QAs[sl]; KA, b_KA = KAs[sl]; VA, b_VA = VAs[sl]
                    LD(QA[0:64, :], s_qn[h * 64:(h + 1) * 64, :], [b_QA], b_QA)
                    LD(KA[0:64, :], s_kn[h * 64:(h + 1) * 64, :], [b_KA], b_KA)
                    vsrc = s_v[:, h * 64:(h + 1) * 64].rearrange("(t p) d -> p t d", p=128)
                    nvs = max(1, NT // 8)
                    for i in range(0, NT, nvs):
                        LD(VA[:, i:i + nvs, 0:64], vsrc[:, i:i + nvs, :], [b_VA], b_VA)

                def gate_stage(h, tb, stage_):
                    sl = h % 2
                    QA, b_QA = QAs[sl]
                    cq, po = h // 2, 64 * (h % 2)
                    if stage_ == 0:
                        if tb == 0:
                            TSC("dve", KMh[0:64, 0:NB], KS[po:po + 64, cq, :], 1.0 / 256, None, ALU.mult, None,
                                [b_KS], [b_KMh])
                        for t in range(NTB):
                            MM(Gp[:, t, 0:NB], QA[0:64, (tb + t) * 128:(tb + t + 1) * 128], KMh[0:64, 0:NB],
                               True, True, [b_QA, b_KMh], [b_Gp])
                    elif stage_ == 1:
                        if NB < 32:
                            Sc.op("dve", lambda e: e.memset(Gs[:], NEG), (), [b_Gs])
                        TT("dve", Gs[:, :, 0:NB], Gp[:, 0:NTB, 0:NB], FUT[:, tb:tb + NTB, 0:NB], ALU.add,
                           [b_Gp, b_FUT], [b_Gs])
                        for t in range(NTB):
                            Sc.op("dve", lambda e, t=t: e.max(out=thr[:, t, :], in_=Gs[:, t, :]), [b_Gs], [b_thr])
                        TT("dve", nsel[:], Gs[:], thr[:, :, 2:3].to_broadcast([128, NTB, 32]), ALU.is_lt,
                           [b_Gs, b_thr], [b_nsel])
                        TT("pool", MBp[:, :, 64:96], nsel[:], NEGP[:, tb:tb + NTB, :], ALU.mult,
                           [b_nsel, b_NEGP], [b_MBp])
                    else:
                        t4 = (stage_ - 2) * 4
                        for t in range(4):
                            MM(MTp[:, t * 128:(t + 1) * 128], MBp[:, t4 + t, :], ident[:], True, True,
                               [b_MBp, b_ident], [b_MTp])
                        c0 = (tb + t4) * 128
                        CP("act", QA[64:96, c0:c0 + 512], MTp[64:96, :], [b_MTp], [b_QA])

                NST = 2 + NTB // 4
                gate_sched = [(tb, st_) for tb in range(0, NT, NTB) for st_ in range(NST)]

                def head_gate(h):
                    for (tb, st_) in gate_sched:
                        gate_stage(h, tb, st_)

                head_loads(0)
                head_gate(0)
                for h in range(HA):
                    sl = h % 2
                    QA, b_QA = QAs[sl]; KA, b_KA = KAs[sl]; VA, b_VA = VAs[sl]
                    pairs = [(g, kp) for g in range(NG) for kp in range(2 * g + 2)]
                    slots = {}

                    def emit_qk(i):
                        g, kp = pairs[i]
                        Sp, bS = STp.next()
                        for u in range(2):
                            kt = 2 * kp + u
                            delta = 512 * g - 128 * kt
                            near = delta <= 1536
                            MM(Sp[:, u * 512:(u + 1) * 512], KA[0:96, kt * 128:(kt + 1) * 128],
                               QA[0:96, g * 512:(g + 1) * 512], True, not near, [b_KA, b_QA], [bS])
                            if near:
                                MM(Sp[:, u * 512:(u + 1) * 512], ident[:],
                                   TS_[:, h, delta + 384:delta + 384 + 512], False, True, [b_ident, b_TS], [bS])
                        slots[i] = (Sp, bS)

                    emit_qk(0)
                    O, bO = None, None
                    gate_i0 = len(pairs) // 4
                    gate_step = max(1, (len(pairs) - gate_i0 - 2) // len(gate_sched))
                    for i, (g, kp) in enumerate(pairs):
                        npair = 2 * g + 2
                        if kp == 0:
                            O, bO = Op.next()
                            SAG, b_SAG = SAGr.next()
                            LD(SAG[:], s_sag[h * 64:(h + 1) * 64, g * 512:(g + 1) * 512], [b_SAG], b_SAG)
                        if i + 1 < len(pairs):
                            emit_qk(i + 1)
                        if i == 0 and h + 1 < HA:
                            head_loads(h + 1)
                        if h + 1 < HA and i >= gate_i0 and (i - gate_i0) % gate_step == 0:
                            kq = (i - gate_i0) // gate_step
                            if kq < len(gate_sched):
                                gate_stage(h + 1, *gate_sched[kq])
                        Sp, bS = slots.pop(i)
                        P_, bPt = PT.next()
                        ACT(P_[:], Sp[:], AF.Exp, [bS], [bPt])
                        for u in range(2):
                            kt = 2 * kp + u
                            MM(O[:], VA[:, kt, :], P_[:, u * 512:(u + 1) * 512], kt == 0, kt == 2 * npair - 1,
                               [b_VA, bPt], [bO])
                        if kp == npair - 1:
                            Sc.op("dve", lambda e, O=O: e.reciprocal(out=rden[:], in_=O[64:128, :]), [bO], [b_rden])
                            TT("dve", yh[:], O[0:64, :], rden[:], ALU.mult, [bO, b_rden], [b_yh])
                            YG, b_YG = YGr.next()
                            TT("pool", YG[:], yh[:], SAG[:], ALU.mult, [b_yh, b_SAG], [b_YG])
                            ST(XS[h // 2, (h % 2) * 64:(h % 2) * 64 + 64, g * 512:(g + 1) * 512], YG[:], [b_YG], b_YG)
                Sc.emit()

            with ExitStack() as st:
                sb, ps = mk(st)
                QDs = [(sb(f"QD{i}", [128, S], BF16), Buf(f"QD{i}")) for i in range(HB)]
                KDs = [(sb(f"KD{i}", [128, S], BF16), Buf(f"KD{i}")) for i in range(HB)]
                KEr = Ring([(sb(f"KE{i}", [128, 512], BF16), Buf(f"KE{i}")) for i in range(4)])
                VBr = Ring([(sb(f"VB{i}", [64, 8, 128], BF16), Buf(f"VB{i}")) for i in range(4)])
                SBGr = Ring([(sb(f"SBG{i}", [128, 512], BF16), Buf(f"SBG{i}")) for i in range(4)])
                YBr = Ring([(sb(f"YB{i}", [128, 512], BF16), Buf(f"YB{i}")) for i in range(4)])
                KTr = Ring([(sb(f"KT{i}", [64, 8, 128], BF16), Buf(f"KT{i}")) for i in range(4)])
                ATs = Ring([(sb(f"ATs{i}", [64, 64], BF16), Buf(f"ATs{i}")) for i in range(4)])
                Sbs = [[(sb(f"Sb{hh}_{i}", [128, 128], BF16), Buf(f"Sb{hh}_{i}")) for i in range(2)] for hh in range(HB)]
                osq = sb("osq", [128, 512], BF16); b_osq = Buf("osq")
                ort = sb("ort", [128, 512]); b_ort = Buf("ort")
                ors = sb("ors", [128, 512]); b_ors = Buf("ors")
                ybf = sb("ybf", [128, 512]); b_ybf = Buf("ybf")
                KT1 = (ps("KTp", [64, 8, 128], BF16), Buf("KTp"))
                KTp = [KT1] * HB
                OHp = [Ring([(ps(f"OHp{hh}_{i}", [128, 512]), Buf(f"OHp{hh}_{i}")) for i in range(1)]) for hh in range(HB)]
                Abk = [ps(f"Abk{hh}", [128, 512]) for hh in range(HB)]
                dSbk = [ps(f"dSbk{hh}", [128, 512]) for hh in range(HB)]
                ATp = [(Abk[hh][0:64, 0:64], Buf(f"ATp{hh}")) for hh in range(HB)]
                dSp = [(dSbk[hh][:, 0:128], Buf(f"dSp{hh}")) for hh in range(HB)]
                NSp = ps("NSp", [128, 512]); b_NSp = Buf("NSp")
                for k in range(2):
                    AG(XG[k], XS[k], Buf(f"agx{k}"))
                for hh in range(HB):
                    rows = slice(hh * 128, (hh + 1) * 128)
                    LD(QDs[hh][0][:], s_qd[rows, :], [QDs[hh][1]], QDs[hh][1])
                    LD(KDs[hh][0][:], s_kd[rows, :], [KDs[hh][1]], KDs[hh][1])
                    Sc.op("dve", lambda e, hh=hh: e.memset(Sbs[hh][0][0][:], 0.0), (), [Sbs[hh][0][1]])
                cur = [0] * HB

                def group_loads(g):
                    out = []
                    for hh in range(HB):
                        rows = slice(hh * 128, (hh + 1) * 128)
                        KE, bKE = KEr.next()
                        LD(KE[:], s_ke[rows, g * 512:(g + 1) * 512], [bKE], bKE)
                        VB, bVB = VBr.next()
                        vsrc = s_vb[g * 512:(g + 1) * 512, hh * 128:(hh + 1) * 128].rearrange("(c s) v -> s c v", s=64)
                        LD(VB[:], vsrc, [bVB], bVB)
                        SBG, bSBG = SBGr.next()
                        LD(SBG[:], s_sbg[rows, g * 512:(g + 1) * 512], [bSBG], bSBG)
                        out.append((KE, bKE, VB, bVB, SBG, bSBG))
                    return out

                nxt = group_loads(0)
                for g in range(NG):
                    gl_ = nxt
                    if g + 1 < NG:
                        nxt = group_loads(g + 1)
                    KTl, OHl = [], []
                    for hh in range(HB):
                        KE, bKE, VB, bVB, SBG, bSBG = gl_[hh]
                        KTps, bKTp = KTp[hh]
                        for c in range(8):
                            Sc.op("pe", lambda e, KTps=KTps, c=c, KE=KE: e.transpose(
                                out=KTps[:, c, :], in_=KE[:, c * 64:(c + 1) * 64], identity=ident[:]),
                                [bKE, b_ident], [bKTp])
                        KTs, bKT = KTr.next()
                        CP("act", KTs[:], KTps[:], [bKTp], [bKT])
                        KTl.append((KTs, bKT))
                        OHl.append(OHp[hh].next())
                    for c in range(8):
                        ch = g * 8 + c
                        cs = slice(ch * 64, (ch + 1) * 64)
                        Asl = []
                        for hh in range(HB):
                            QD, b_QD = QDs[hh]; KD, b_KD = KDs[hh]
                            A, bA = ATp[hh]
                            MM(A, KD[:, cs], QD[:, cs], True, True, [b_KD, b_QD], [bA])
                            As, bAs = ATs.next()
                            TT("dve", As[:], A, caus[:], ALU.mult, [bA, b_caus], [bAs])
                            Asl.append((As, bAs))
                        for hh in range(HB):
                            KE, bKE, VB, bVB, SBG, bSBG = gl_[hh]
                            QD, b_QD = QDs[hh]
                            KTs, bKT = KTl[hh]; OH, bOH = OHl[hh]
                            As, bAs = Asl[hh]
                            S0, bS0 = Sbs[hh][cur[hh]]
                            S1, bS1 = Sbs[hh][1 - cur[hh]]
                            MM(OH[:, c * 64:(c + 1) * 64], VB[:, c, :], As[:], True, False, [bVB, bAs], [bOH])
                            MM(OH[:, c * 64:(c + 1) * 64], S0[:], QD[:, cs], False, True, [bS0, b_QD], [bOH])
                            dS, bdS = dSp[hh]
                            MM(dS, KTs[:, c, :], VB[:, c, :], True, True, [bKT, bVB], [bdS])
                            STT("dve", S1[:], S0[:], EL[:, hh, ch:ch + 1], dS, ALU.mult, ALU.add,
                                [bS0, b_EL, bdS], [bS1])
                            cur[hh] = 1 - cur[hh]
                    for hh in range(HB):
                        KE, bKE, VB, bVB, SBG, bSBG = gl_[hh]
                        OH, bOH = OHl[hh]
                        ACT(osq[:], OH[:], AF.Square, [bOH], [b_osq])
                        MM(NSp[:], ones[:], osq[:], True, True, [b_ones, b_osq], [b_NSp])
                        ACT(ort[:], NSp[:], AF.Ln, [b_NSp], [b_ort], bias=EPS, scale=1.0 / 128)
                        ACT(ors[:], ort[:], AF.Exp, [b_ort], [b_ors], scale=-0.5)
                        STT("dve", ybf[:], OH[:], go[:, l, hh:hh + 1], ors[:], ALU.mult, ALU.mult,
                            [bOH, b_go, b_ors], [b_ybf])
                        YB, bYB = YBr.next()
                        TT("pool", YB[:], ybf[:], SBG[:], ALU.mult, [b_ybf, bSBG], [bYB])
                        ST(XS[2 + hh, :, g * 512:(g + 1) * 512], YB[:], [bYB], bYB)
                Sc.emit()

            with ExitStack() as st:
                sb, ps = mk(st)
                WA = sb("WA", [128, 4, D], BF16); b_WA = Buf("WA")
                WB_ = sb("WB", [128, 4, D], BF16); b_WB = Buf("WB")
                WO = sb("WO", [128, 8, D], BF16); b_WO = Buf("WO")
                WP = sb("WP", [128, 2, D], BF16); b_WP = Buf("WP")
                WG = sb("WG", [128, 8, D], BF16); b_WG = Buf("WG")
                wst = [(sb(f"wstc{i}", [128, 2, 512]), Buf(f"wstc{i}")) for i in range(2)]
                k = 0
                for (dst, bd, src, nch) in ((WA, b_WA, w_up_a[l], 4), (WB_, b_WB, w_up_b[l], 4),
                                            (WO, b_WO, w_out[l], 8), (WP, b_WP, w_ple[l], 2),
                                            (WG, b_WG, w_pg[l], 8)):
                    sv = src.rearrange("(c p) n -> p c n", p=128)
                    for c2 in range(0, nch, 2):
                        for n2 in range(0, D, 512):
                            t, b = wst[k % 2]
                            LD(t[:], sv[:, c2:c2 + 2, n2:n2 + 512], [b], b)
                            CP(("act", "dve", "pool")[k % 3], dst[:, c2:c2 + 2, n2:n2 + 512], t[:], [b], [bd])
                            k += 1
                INS = []
                for i in range(2):
                    INS.append(dict(
                        YAG=(sb(f"YAG{i}", [128, 4, 512], BF16), Buf(f"YAG{i}")),
                        YBG=(sb(f"YBG{i}", [128, 4, 512], BF16), Buf(f"YBG{i}")),
                        SGA=(sb(f"SGA{i}", [128, 8, 512], BF16), Buf(f"SGA{i}")),
                        SGB=(sb(f"SGB{i}", [128, 8, 512], BF16), Buf(f"SGB{i}")),
                        PB=(sb(f"PB{i}", [128, 2, 512], BF16), Buf(f"PB{i}"))))
                H = sb("Hc", [128, 8, 512]); b_H = Buf("Hc")
                PF = sb("PF", [128, 2, 512]); b_PF = Buf("PF")
                MG = sb("MG", [128, 8, 512], BF16); b_MG = Buf("MG")
                HM = sb("HM", [128, 8, 512]); b_HM = Buf("HM")
                HMb = sb("HMb", [128, 8, 512], BF16); b_HMb = Buf("HMb")
                HN = Ring([(sb(f"HN{i}", [128, 512]), Buf(f"HN{i}")) for i in range(2)])
                HNb = Ring([(sb(f"HNb{i}", [128, 512], BF16), Buf(f"HNb{i}")) for i in range(2)])
                tmp = Ring([(sb(f"tmp{i}", [128, 512]), Buf(f"tmp{i}")) for i in range(4)])
                PS = Ring([(ps(f"PSc{i}", [128, 512]), Buf(f"PSc{i}")) for i in range(8)])
                hv = h_own.rearrange("(c p) t -> p c t", p=128)
                hd = h_dst.rearrange("(c p) t -> p c t", p=128)
                pv = pT[l].rearrange("(c p) t -> p c t", p=128)
                b_XG = [Buf(f"XG{k}") for k in range(4)]
                for k in (2, 3):
                    AG(XG[k], XS[k], Buf(f"agx{k}"), w=[b_XG[k]])
                b_XO = Buf("XO")
                for kx_ in range(4):
                    for rk in range(2):
                        LD(XO[kx_, rk * 128:(rk + 1) * 128, :], XG[kx_, rk * 128:(rk + 1) * 128, bass.ds(off_own, SH)],
                           [b_XO], b_XO, r=[b_XG[kx_]])
                b_HS = [Buf(f"HSp{q}") for q in range(NPC)]

                def loads(g):
                    I = INS[g % 2]
                    ts_ = slice(g * 512, (g + 1) * 512)
                    for c in range(4):
                        rk, kk = c // 2, c % 2
                        LD(I["YAG"][0][:, c, :], XO[kk, rk * 128:(rk + 1) * 128, ts_], [I["YAG"][1]], I["YAG"][1], r=[b_XO])
                        LD(I["YBG"][0][:, c, :], XO[2 + kk, rk * 128:(rk + 1) * 128, ts_], [I["YBG"][1]], I["YBG"][1], r=[b_XO])
                    LD(I["SGA"][0][:], s_sga.rearrange("(c p) t -> p c t", p=128)[:, :, ts_], [I["SGA"][1]], I["SGA"][1])
                    LD(I["SGB"][0][:], s_sgb.rearrange("(c p) t -> p c t", p=128)[:, :, ts_], [I["SGB"][1]], I["SGB"][1])
                    LD(PF[:], pv[:, :, ts_], [b_PF], b_PF)
                    CP("act", I["PB"][0][:], PF[:], [b_PF], [I["PB"][1]])

                loads(0)
                LD(H[:], hv[:, :, 0:512], [b_H], b_H)
                for g in range(NGO):
                    ts_ = slice(g * 512, (g + 1) * 512)
                    I = INS[g % 2]
                    YAG, b_YAG = I["YAG"]; YBG, b_YBG = I["YBG"]; SGA, b_SGA = I["SGA"]; SGB, b_SGB = I["SGB"]
                    PB, b_PB = I["PB"]
                    if g + 1 < NGO:
                        loads(g + 1)
                    for j in range(8):
                        Pa, bPa = PS.next()
                        for c in range(4):
                            MM(Pa[:], WA[:, c, j * 128:(j + 1) * 128], YAG[:, c, :], c == 0, c == 3,
                               [b_WA, b_YAG], [bPa])
                        Pb, bPb = PS.next()
                        for c in range(4):
                            MM(Pb[:], WB_[:, c, j * 128:(j + 1) * 128], YBG[:, c, :], c == 0, c == 3,
                               [b_WB, b_YBG], [bPb])
                        t1, bt1 = tmp.next()
                        TT("dve", t1[:], Pa[:], SGA[:, j, :], ALU.mult, [bPa, b_SGA], [bt1])
                        t2, bt2 = tmp.next()
                        TT("dve", t2[:], Pb[:], SGB[:, j, :], ALU.mult, [bPb, b_SGB], [bt2])
                        TT("pool", MG[:, j, :], t1[:], t2[:], ALU.add, [bt1, bt2], [b_MG])
                    for j in range(8):
                        Po, bPo = PS.next()
                        for c in range(8):
                            MM(Po[:], WO[:, c, j * 128:(j + 1) * 128], MG[:, c, :], c == 0, c == 7,
                               [b_WO, b_MG], [bPo])
                        TT("dve", HM[:, j, :], Po[:], H[:, j, :], ALU.add, [bPo, b_H], [b_HM])
                        CP("act", HMb[:, j, :], HM[:, j, :], [b_HM], [b_HMb])
                    if g + 1 < NGO:
                        LD(H[:], hv[:, :, (g + 1) * 512:(g + 2) * 512], [b_H], b_H)
                    q, col = (g * 512) // PIECE, (g * 512) % PIECE
                    for j in range(8):
                        Pp, bPp = PS.next()
                        for c in range(2):
                            MM(Pp[:], WP[:, c, j * 128:(j + 1) * 128], PB[:, c, :], c == 0, c == 1,
                               [b_WP, b_PB], [bPp])
                        Pg, bPg = PS.next()
                        for c in range(8):
                            MM(Pg[:], WG[:, c, j * 128:(j + 1) * 128], HMb[:, c, :], c == 0, c == 7,
                               [b_WG, b_HMb], [bPg])
                        sg, bsg = tmp.next()
                        ACT(sg[:], Pg[:], AF.Sigmoid, [bPg], [bsg])
                        t1, bt1 = tmp.next()
                        TT("dve", t1[:], Pp[:], sg[:], ALU.mult, [bPp, bsg], [bt1])
                        hn, bhn = HN.next()
                        TT("pool", hn[:], t1[:], HM[:, j, :], ALU.add, [bt1, b_HM], [bhn])
                        ST(hd[:, j, ts_], hn[:], [bhn], bhn, q="sp")
                        if not last:
                            hb, bhb = HNb.next()
                            CP("act", hb[:], hn[:], [bhn], [bhb])
                            ST(HS[q, j * 128:(j + 1) * 128, col:col + 512], hb[:], [bhb], bhb, w=[b_HS[q]], q="sp")
                    if not last and (g + 1) * 512 % PIECE == 0:
                        AG(HG[q], HS[q], Buf(f"agh{q}"), r=[b_HS[q]])
                Sc.emit()

        print("total ops", Sc.total, "sems", Sc.nsem)
    return nc


def host_consts(S):
    NT = S // 128
    bf = ml_dtypes.bfloat16
    c = {}
    c["c_ident"] = np.eye(128, dtype=np.float32).astype(bf)
    blk = np.zeros((128, 128), np.float32)
    blk[:64, :64] = 1.0
    blk[64:, 64:] = 1.0
    c["c_blk"] = blk.astype(bf)
    oh = np.zeros((32, S), np.float32)
    for n in range(S // 256):
        oh[n, n * 256:(n + 1) * 256] = 1.0
    c["c_onehot"] = oh.astype(bf)
    c["c_causal"] = np.triu(np.ones((64, 64), np.float32))
    sm = np.ones((128, 512), np.float32)
    sm[:, ::64] = 0.0
    c["c_scan"] = sm
    fut = np.zeros((NT, 32), np.float32)
    neg = np.full((NT, 32), NEG, np.float32)
    for t in range(NT):
        b = t // 2
        fut[t, b:] = NEG
        neg[t, b] = 0.0
    c["c_fut"] = np.ascontiguousarray(np.broadcast_to(fut[None], (128, NT, 32))).astype(bf)
    c["c_neg"] = np.ascontiguousarray(np.broadcast_to(neg[None], (128, NT, 32))).astype(bf)
    return c


def host_strips(rel_bias, heads):
    i = np.arange(128)[:, None]
    u = np.arange(STRIP)[None, :]
    rel = u - 384 - i
    bucket = t5_bucket_np(rel)
    strip = np.empty((128, len(heads), STRIP), np.float32)
    for k, h in enumerate(heads):
        g = rel_bias[:, h][bucket]
        strip[:, k, :] = np.where(rel >= 0, g, np.float32(NEG))
    c31 = np.ascontiguousarray(np.broadcast_to(rel_bias[31, heads][None, :], (128, len(heads)))).astype(np.float32)
    return strip, c31


def host_inputs(b, r, S, x, p, norm_gain, w_in, q_norm_gain, k_norm_gain, rel_bias, hgrn_lb_logits,
                hgrn_out_gain, w_up_a, w_up_b, w_out, w_ple, w_ple_gate, consts):
    L = w_in.shape[0]
    SH = S // 2
    m = dict(consts)
    m["xT"] = np.ascontiguousarray(x[b, :S].T)
    m["xT_own"] = np.ascontiguousarray(x[b, r * SH:(r + 1) * SH].T)
    m["pT"] = np.ascontiguousarray(np.transpose(p[:, b, r * SH:(r + 1) * SH, :], (0, 2, 1)))
    cols = []
    for blk0 in range(0, 4096, 512):
        cols.append(np.arange(blk0 + r * 256, blk0 + (r + 1) * 256))
    cols.append(np.arange(4096, 6144))
    cols = np.concatenate(cols)
    m["w_in"] = np.ascontiguousarray(w_in[:, :, cols])
    m["w_up_a"] = w_up_a
    m["w_up_b"] = w_up_b
    m["w_out"] = w_out
    m["w_ple"] = w_ple
    m["w_pg"] = w_ple_gate
    m["g_norm"] = np.ascontiguousarray(np.transpose(norm_gain.reshape(L, 8, 128), (2, 0, 1)))
    m["g_q"] = np.ascontiguousarray(np.concatenate([q_norm_gain, q_norm_gain], axis=1).T)
    m["g_k"] = np.ascontiguousarray(np.concatenate([k_norm_gain, k_norm_gain], axis=1).T)
    m["g_o"] = np.ascontiguousarray(np.transpose(hgrn_out_gain.reshape(L, 4, 128)[:, 2 * r:2 * r + 2], (2, 0, 1)))
    m["lbl"] = np.ascontiguousarray(np.transpose(hgrn_lb_logits.reshape(L, 4, 128)[:, 2 * r:2 * r + 2], (2, 0, 1)))
    strip, c31 = host_strips(rel_bias, list(range(4 * r, 4 * r + 4)))
    m["strip_raw"] = strip
    m["c31"] = c31
    return m


_NC_CACHE = {}


def kernel(x, p, norm_gain, w_in, q_norm_gain, k_norm_gain, rel_bias, hgrn_lb_logits,
           hgrn_out_gain, w_up_a, w_up_b, w_out, w_ple, w_ple_gate):
    args = [np.asarray(a, dtype=np.float32) for a in (
        x, p, norm_gain, w_in, q_norm_gain, k_norm_gain, rel_bias, hgrn_lb_logits,
        hgrn_out_gain, w_up_a, w_up_b, w_out, w_ple, w_ple_gate)]
    x = args[0]
    B, S, _ = x.shape
    if S not in _NC_CACHE:
        _NC_CACHE[S] = build(S)
    nc = _NC_CACHE[S]
    consts = host_consts(S)
    in_maps = [host_inputs(i // 2, i % 2, S, *args, consts) for i in range(2 * B)]
    res = run_bass_kernel_spmd(nc, in_maps, core_ids=list(range(2 * B)))
    SH = S // 2
    out = np.empty((B, S, D), np.float32)
    for i in range(2 * B):
        out[i // 2, (i % 2) * SH:(i % 2 + 1) * SH, :] = res.results[i]["hT_out"].T
    return out
```

```python
import math
from contextlib import ExitStack

import numpy as np
import ml_dtypes
import concourse.bass as bass
import concourse.mybir as mybir
from concourse.bass_utils import run_bass_kernel_spmd

F32 = mybir.dt.float32
BF16 = mybir.dt.bfloat16
AF = mybir.ActivationFunctionType
ALU = mybir.AluOpType
AX = mybir.AxisListType

SEM_CHUNK = 20000
D = 1024
NEG = -30000.0
STRIP = 2432
EPS = 1e-6


class Buf:
    __slots__ = ("name", "last_writer", "dma_writers", "readers", "dma_readers")

    def __init__(self, name):
        self.name = name
        self.clear()

    def clear(self):
        self.last_writer = None
        self.dma_writers = []
        self.readers = {}
        self.dma_readers = []


class Sched:
    def __init__(self, nc, stack):
        self.nc = nc
        self.stack = stack
        self.ops = []
        self.eng = {"pe": nc.tensor, "act": nc.scalar, "dve": nc.vector,
                    "pool": nc.gpsimd, "sp": nc.sync}
        self.nsem = 0
        self.eng_count = {}
        self.eng_sems = {}
        self.dma_pool = []
        self.dma_free = []
        self.key_slot = {}
        self.waited = {}
        self.total = 0

    def new_sem(self, name):
        self.nsem += 1
        return self.stack.enter_context(self.nc.semaphore(name))

    def op(self, eng, fn, reads=(), writes=()):
        self.ops.append(["c", eng, fn, tuple(reads), tuple(writes), None, 1])

    def dma(self, queue, fn, reads=(), writes=(), key=None, inc=16):
        assert key is not None
        self.ops.append(["d", queue, fn, tuple(reads), tuple(writes), key, inc])

    def emit(self):
        ops = self.ops
        n = len(ops)
        deps = [None] * n
        signaling = [False] * n
        last_of_eng = {}
        for i, o in enumerate(ops):
            d = set()
            isd = o[0] == "d"
            for b in o[3]:
                if b.last_writer is not None:
                    d.add(b.last_writer)
                d.update(b.dma_writers)
            for b in o[4]:
                if b.last_writer is not None:
                    d.add(b.last_writer)
                d.update(b.readers.values())
                d.update(b.dma_readers)
                if not isd:
                    d.update(b.dma_writers)
            d.discard(i)
            for b in o[3]:
                if isd:
                    b.dma_readers.append(i)
                else:
                    b.readers[o[1]] = i
            for b in o[4]:
                if isd:
                    b.dma_writers.append(i)
                else:
                    b.last_writer = i
                    b.dma_writers = []
                    b.readers = {}
                    b.dma_readers = []
            deps[i] = d
            for j in d:
                signaling[j] = True
            if o[0] == "c":
                last_of_eng[o[1]] = i
        for e, i in last_of_eng.items():
            signaling[i] = True
        sig = [None] * n
        for i, o in enumerate(ops):
            if o[0] == "c":
                if not signaling[i]:
                    continue
                e = o[1]
                c = self.eng_count.get(e, 0)
                k = c // SEM_CHUNK
                if (e, k) not in self.eng_sems:
                    self.eng_sems[(e, k)] = self.new_sem(f"s_{e}_{k}")
                self.eng_count[e] = c + 1
                sig[i] = (self.eng_sems[(e, k)], c - k * SEM_CHUNK + 1, 1, ("e", e, k))
            else:
                key = o[5]
                if key not in self.key_slot:
                    if self.dma_free:
                        s = self.dma_free.pop()
                    else:
                        self.dma_pool.append([self.new_sem(f"d_{len(self.dma_pool)}"), 0])
                        s = len(self.dma_pool) - 1
                    self.key_slot[key] = s
                s = self.key_slot[key]
                self.dma_pool[s][1] += o[6]
                sig[i] = (self.dma_pool[s][0], self.dma_pool[s][1], o[6], ("k", s))
        waited = self.waited
        for i, o in enumerate(ops):
            e = o[1]
            engine = self.eng[e]
            w = waited.setdefault(e, {})
            for j in sorted(deps[i]):
                pj = ops[j]
                if pj[0] == "c" and pj[1] == "pe" and e == "pe" and o[0] == "c":
                    continue
                sem, val, _, sid = sig[j]
                if w.get(sid, 0) >= val:
                    continue
                w[sid] = val
                engine.wait_ge(sem, val)
            ins = o[2](engine)
            if sig[i] is not None:
                ins.then_inc(sig[i][0], sig[i][2])
        finals = []
        for (e, k), sem in self.eng_sems.items():
            c = self.eng_count.get(e, 0)
            if c // SEM_CHUNK == k and c - k * SEM_CHUNK > 0:
                finals.append((sem, c - k * SEM_CHUNK, ("e", e, k)))
            elif c // SEM_CHUNK > k:
                finals.append((sem, SEM_CHUNK, ("e", e, k)))
        for s, (sem, c) in enumerate(self.dma_pool):
            if c > 0:
                finals.append((sem, c, ("k", s)))
        for e in ("sp", "pe", "act", "dve", "pool"):
            w = waited.setdefault(e, {})
            for sem, val, sid in finals:
                if w.get(sid, 0) >= val:
                    continue
                w[sid] = val
                self.eng[e].wait_ge(sem, val)
        for o in ops:
            for b in o[3] + o[4]:
                b.clear()
        self.key_slot = {}
        self.dma_free = list(range(len(self.dma_pool)))
        self.total += n
        self.ops = []
        return n


class Ring:
    def __init__(self, items):
        self.items = items
        self.i = 0

    def next(self):
        it = self.items[self.i % len(self.items)]
        self.i += 1
        return it


def t5_bucket_np(rel):
    n = np.maximum(rel, 0)
    nf = np.maximum(n, 16).astype(np.float32)
    large = 16 + (np.log(nf / np.float32(16)) / np.float32(math.log(128)) * np.float32(16)).astype(np.int32)
    large = np.minimum(large, 31)
    return np.where(n < 16, n, large)


def build(S, L=2, debug=False, groups=None):
    NT, NB, NG, NCH = S // 128, S // 256, S // 512, S // 64
    SH, NGO = S // 2, S // 1024
    HA, HB = 4, 2
    if groups is None:
        groups = [[0, 1], [2, 3], [4, 5], [6, 7]]
    nc = bass.Bass("TRN2", target_bir_lowering=False)

    def din(name, shape, dt=F32):
        return nc.dram_tensor(name, list(shape), dt, kind="ExternalInput").ap()

    def dscr(name, shape, dt=BF16, out=False):
        kind = "Internal"
        return nc.dram_tensor(name, list(shape), dt, kind=kind).ap()

    xT = din("xT", [D, S])
    xT_own = din("xT_own", [D, SH])
    pT = din("pT", [L, 256, SH])
    w_in = din("w_in", [L, D, 4096])
    w_up_a = din("w_up_a", [L, 512, D])
    w_up_b = din("w_up_b", [L, 512, D])
    w_out = din("w_out", [L, D, D])
    w_ple = din("w_ple", [L, 256, D])
    w_pg = din("w_pg", [L, D, D])
    g_norm = din("g_norm", [128, L, 8])
    g_q = din("g_q", [128, L])
    g_k = din("g_k", [128, L])
    g_o = din("g_o", [128, L, HB])
    lbl = din("lbl", [128, L, HB])
    strip_raw = din("strip_raw", [128, HA, STRIP])
    c31 = din("c31", [128, HA])
    c_ident = din("c_ident", [128, 128], BF16)
    c_blk = din("c_blk", [128, 128], BF16)
    c_onehot = din("c_onehot", [32, S], BF16)
    c_causal = din("c_causal", [64, 64])
    c_scan = din("c_scan", [128, 512])
    c_fut = din("c_fut", [128, NT, 32], BF16)
    c_neg = din("c_neg", [128, NT, 32], BF16)

    hT_out = nc.dram_tensor("hT_out", [D, SH], F32, kind="ExternalOutput").ap()
    h_mid = dscr("h_mid", [D, SH], F32)
    s_qn = dscr("s_qn", [256, S])
    s_kn = dscr("s_kn", [256, S])
    s_v = dscr("s_v", [S, 256])
    s_sag = dscr("s_sag", [256, S])
    s_qd = dscr("s_qd", [256, S])
    s_kd = dscr("s_kd", [256, S])
    s_ke = dscr("s_ke", [256, S])
    s_vb = dscr("s_vb", [S, 256])
    s_sbg = dscr("s_sbg", [256, S])
    s_sga = dscr("s_sga", [D, SH])
    s_sgb = dscr("s_sgb", [D, SH])
    XS = dscr("XS", [4, 128, S], out=True)
    XG = dscr("XG", [4, 2 * 128, S], out=True)
    XO = dscr("XO", [4, 2 * 128, SH])
    PIECE = max(512, SH // 4)
    NPC = SH // PIECE
    HS = dscr("HS", [NPC, D, PIECE])
    HG = dscr("HG", [NPC, 2 * D, PIECE])

    off_own = (nc.sync.partition_id() % 2) * SH

    with ExitStack() as outer:
        Sc = Sched(nc, outer)
        uniq = [0]

        def mk(stack):
            uniq[0] += 1
            tag = f"u{uniq[0]}_"

            def sb(name, shape, dt=F32):
                return stack.enter_context(nc.sbuf_tensor(tag + name, list(shape), dt))

            def ps(name, shape, dt=F32):
                return stack.enter_context(nc.psum_tensor(tag + name, list(shape), dt))
            return sb, ps

        def MM(out, lhsT, rhs, start, stop, r, w):
            Sc.op("pe", lambda e: e.matmul(out, lhsT=lhsT, rhs=rhs, start=start, stop=stop), r, w)

        def ACT(out, in_, func, r, w, bias=0.0, scale=1.0, eng="act"):
            Sc.op(eng, lambda e: e.activation(out=out, in_=in_, func=func, bias=bias, scale=scale), r, w)

        def TT(eng, out, in0, in1, op, r, w):
            Sc.op(eng, lambda e: e.tensor_tensor(out=out, in0=in0, in1=in1, op=op), r, w)

        def TSC(eng, out, in0, s1, s2, op0, op1, r, w):
            if s2 is None:
                Sc.op(eng, lambda e: e.tensor_scalar(out=out, in0=in0, scalar1=s1, scalar2=None, op0=op0), r, w)
            else:
                Sc.op(eng, lambda e: e.tensor_scalar(out=out, in0=in0, scalar1=s1, scalar2=s2, op0=op0, op1=op1), r, w)

        def STT(eng, out, in0, scalar, in1, op0, op1, r, w):
            Sc.op(eng, lambda e: e.scalar_tensor_tensor(out=out, in0=in0, scalar=scalar, in1=in1, op0=op0, op1=op1), r, w)

        def CP(eng, out, in_, r, w):
            if eng == "act":
                Sc.op("act", lambda e: e.activation(out=out, in_=in_, func=AF.Copy), r, w)
            else:
                Sc.op(eng, lambda e: e.tensor_copy(out=out, in_=in_), r, w)

        def LD(out, in_, w, key, r=(), q="sp"):
            Sc.dma(q, lambda e: e.dma_start(out=out, in_=in_), r, w, key)

        def ST(out, in_, r, key, w=(), q="pool"):
            Sc.dma(q, lambda e: e.dma_start(out=out, in_=in_), r, w, key)

        def AG(out, in_, key, r=(), w=()):
            Sc.dma("pool", lambda e: e.collective_compute(
                "AllGather", ALU.bypass, replica_groups=groups, ins=[in_], outs=[out]), r, w, key, inc=1)

        sbP, _ = mk(outer)
        ident = sbP("ident", [128, 128], BF16); b_ident = Buf("ident")
        blk = sbP("blk", [128, 128], BF16); b_blk = Buf("blk")
        ones = sbP("ones", [128, 128], BF16); b_ones = Buf("ones")
        gn = sbP("gn", [128, L, 8]); b_gn = Buf("gn")
        gq = sbP("gq", [128, L]); b_gq = Buf("gq")
        gk = sbP("gk", [128, L]); b_gk = Buf("gk")
        go = sbP("go", [128, L, HB]); b_go = Buf("go")
        lb = sbP("lb", [128, L, HB]); b_lb = Buf("lb")
        oml = sbP("oml", [128, L, HB]); b_oml = Buf("oml")
        lbe = sbP("lbe", [128, L, HB]); b_lbe = Buf("lbe")
        lbs = sbP("lbs", [128, HB]); b_lbs = Buf("lbs")
        KS = sbP("KS", [128, 2, NB]); b_KS = Buf("KS")
        EL = sbP("EL", [128, HB, NCH]); b_EL = Buf("EL")
        caus = sbP("caus", [64, 64]); b_caus = Buf("caus")
        scanm = sbP("scanm", [128, 512]); b_scanm = Buf("scanm")

        with ExitStack() as st:
            sb, ps = mk(st)
            LD(ident[:], c_ident, [b_ident], b_ident)
            LD(blk[:], c_blk, [b_blk], b_blk)
            Sc.op("pool", lambda e: e.memset(ones[:], 1.0), (), [b_ones])
            LD(gn[:], g_norm, [b_gn], b_gn)
            LD(gq[:], g_q, [b_gq], b_gq)
            LD(gk[:], g_k, [b_gk], b_gk)
            LD(go[:], g_o, [b_go], b_go)
            LD(lbe[:], lbl, [b_lbe], b_lbe)
            LD(caus[:], c_causal, [b_caus], b_caus)
            LD(scanm[:], c_scan, [b_scanm], b_scanm)
            TSC("dve", gq[:], gq[:], 0.125, None, ALU.mult, None, [b_gq], [b_gq])
            ACT(lbe[:], lbe[:], AF.Exp, [b_lbe], [b_lbe])
            CP("dve", lbs[:], lbe[:, 0, :], [b_lbe], [b_lbs])
            for l in range(1, L):
                TT("dve", lbs[:], lbs[:], lbe[:, l, :], ALU.add, [b_lbs, b_lbe], [b_lbs])
            Sc.op("dve", lambda e: e.reciprocal(out=lbs[:], in_=lbs[:]), [b_lbs], [b_lbs])
            Sc.op("dve", lambda e: e.memset(lb[:, 0, :], 0.0), (), [b_lb])
            for l in range(1, L):
                TT("dve", lb[:, l, :], lb[:, l - 1, :], lbe[:, l, :], ALU.add, [b_lb, b_lbe], [b_lb])
            for l in range(1, L):
                TT("dve", lb[:, l, :], lb[:, l, :], lbs[:], ALU.mult, [b_lb, b_lbs], [b_lb])
            for l in range(L):
                TSC("dve", oml[:, l, :], lb[:, l, :], -1.0, 1.0, ALU.mult, ALU.add, [b_lb], [b_oml])
            Sc.emit()

        for l in range(L):
            first, last = (l == 0), (l == L - 1)
            h_own = xT_own if first else h_mid
            h_dst = hT_out if last else h_mid

            with ExitStack() as st:
                sb, ps = mk(st)
                Wb = sb("Wb", [128, 8, 4096], BF16)
                b_Wbs = [Buf(f"Wb{i}") for i in range(32)]
                wst = [(sb(f"wst{i}", [128, 8, 128]), Buf(f"wst{i}")) for i in range(2)]
                wv = w_in[l].rearrange("(c p) n -> p c n", p=128)
                for i in range(32):
                    t, b = wst[i % 2]
                    LD(t[:], wv[:, :, i * 128:(i + 1) * 128], [b], b)
                    CP(("act", "dve", "pool")[i % 3], Wb[:, :, i * 128:(i + 1) * 128], t[:], [b], [b_Wbs[i]])
                HDT = F32 if first else BF16
                H = sb("H", [128, 8, 512], HDT); b_H = Buf("H")
                H2 = sb("H2", [128, 8, 512]); b_H2 = Buf("H2")
                SQ = sb("SQ", [128, 8, 512], BF16); b_SQ = Buf("SQ")
                XNs = [(sb(f"XN{i}", [128, 8, 512], BF16), Buf(f"XN{i}")) for i in range(2)]
                rstd = sb("rstd", [128, 512]); b_rstd = Buf("rstd")
                stage = Ring([(sb(f"stg{i}", [128, 512], BF16), Buf(f"stg{i}")) for i in range(6)])
                f32r = Ring([(sb(f"f32r{i}", [128, 512]), Buf(f"f32r{i}")) for i in range(8)])
                QB = [(sb(f"QB{i}", [128, 512]), Buf(f"QB{i}")) for i in range(HB)]
                SG = [(sb(f"SG{i}", [128, 512]), Buf(f"SG{i}")) for i in range(HB)]
                sqq = Ring([(sb(f"sqq{i}", [128, 512], BF16), Buf(f"sqq{i}")) for i in range(2)])
                p_ss = ps("p_ss", [128, 512]); b_pss = Buf("p_ss")
                PS = Ring([(ps(f"PSa{i}", [128, 512]), Buf(f"PSa{i}")) for i in range(5)])
                BS = Ring([(ps(f"BSa{i}", [128, 512]), Buf(f"BSa{i}")) for i in range(2)])
                Sc.op("dve", lambda e: e.memset(KS[:], 0.0), (), [b_KS])

                def load_h(g):
                    if first:
                        LD(H[:], xT.rearrange("(c p) t -> p c t", p=128)[:, :, g * 512:(g + 1) * 512], [b_H], b_H)
                    else:
                        half, tok = g // NGO, (g % NGO) * 512
                        q, col = tok // PIECE, tok % PIECE
                        src = HG[q, half * D:(half + 1) * D, col:col + 512].rearrange("(c p) t -> p c t", p=128)
                        LD(H[:], src, [b_H], b_H)

                def norm_part1(g):
                    load_h(g)
                    ACT(SQ[:].rearrange("p c t -> p (c t)"), H[:].rearrange("p c t -> p (c t)"),
                        AF.Square, [b_H], [b_SQ])

                def norm_part2(Hs, bH, XN, b_XN):
                    for c in range(8):
                        MM(p_ss[:], ones[:], SQ[:, c, :], c == 0, c == 7, [b_ones, b_SQ], [b_pss])
                    ACT(rstd[:], p_ss[:], AF.Ln, [b_pss], [b_rstd], bias=EPS, scale=1.0 / D)
                    ACT(rstd[:], rstd[:], AF.Exp, [b_rstd], [b_rstd], scale=-0.5)
                    for c in range(8):
                        STT("dve", XN[:, c, :], Hs[:, c, :], gn[:, l, c:c + 1], rstd[:],
                            ALU.mult, ALU.mult, [bH, b_gn, b_rstd], [b_XN])

                def decay_chain(g):
                        for jj in range(HB):
                            sg, bsg = SG[jj]
                            f, bf_ = f32r.next()
                            TSC("dve", f[:], sg[:], oml[:, l, jj:jj + 1], lb[:, l, jj:jj + 1], ALU.mult, ALU.add,
                                [bsg, b_oml, b_lb], [bf_])
                            gl, bgl = f32r.next()
                            ACT(gl[:], f[:], AF.Ln, [bf_], [bgl])
                            cum, bcum = f32r.next()
                            Sc.op("dve", lambda e, cum=cum, gl=gl: e.tensor_tensor_scan(
                                out=cum[:], data0=scanm[:], data1=gl[:], initial=0.0,
                                op0=ALU.mult, op1=ALU.add), [bgl, b_scanm], [bcum])
                            ec, bec = f32r.next()
                            ACT(ec[:], cum[:], AF.Exp, [bcum], [bec])
                            ACT(gl[:], cum[:], AF.Exp, [bcum], [bgl], scale=-1.0)
                            CP("pool", EL[:, jj, g * 8:(g + 1) * 8],
                               ec[:].rearrange("p (c t) -> p c t", t=64)[:, :, 63], [bec], [b_EL])
                            o, bo = stage.next()
                            TT("pool", o[:], QB[jj][0][:], ec[:], ALU.mult, [QB[jj][1], bec], [bo])
                            ST(s_qd[jj * 128:(jj + 1) * 128, g * 512:(g + 1) * 512], o[:], [bo], bo)
                            TSC("dve", f[:], f[:], -1.0, 1.0, ALU.mult, ALU.add, [bf_], [bf_])
                            TT("dve", f[:], f[:], gl[:], ALU.mult, [bf_, bgl], [bf_])
                            o, bo = stage.next()
                            CP("act", o[:], f[:], [bf_], [bo])
                            ST(s_kd[jj * 128:(jj + 1) * 128, g * 512:(g + 1) * 512], o[:], [bo], bo)
                            o, bo = stage.next()
                            TT("pool", o[:].rearrange("p (c t) -> p c t", t=64),
                               f[:].rearrange("p (c t) -> p c t", t=64),
                               EL[:, jj, g * 8:(g + 1) * 8].unsqueeze(2).to_broadcast([128, 8, 64]),
                               ALU.mult, [bf_, b_EL], [bo])
                            ST(s_ke[jj * 128:(jj + 1) * 128, g * 512:(g + 1) * 512], o[:], [bo], bo)

                norm_part1(0)
                norm_part2(H, b_H, *XNs[0])
                for g in range(NG):
                    XN, b_XN = XNs[g % 2]

                    def proj_fm(c0):
                        P, bP = PS.next()
                        for c in range(8):
                            MM(P[:], Wb[:, c, c0:c0 + 128], XN[:, c, :], c == 0, c == 7,
                               [b_Wbs[c0 // 128], b_XN], [bP])
                        return P, bP

                    def store_fm(dst, j, t, b):
                        ST(dst[j * 128:(j + 1) * 128, g * 512:(g + 1) * 512], t[:], [b], b)

                    def qk_tail(j, P, bP, s2, bs2):
                        isk = j >= 2
                        Bp, bB = BS.next()
                        MM(Bp[:], blk[:], s2[:], True, True, [b_blk, bs2], [bB])
                        r1, br1 = f32r.next()
                        ACT(r1[:], Bp[:], AF.Ln, [bB], [br1], bias=EPS, scale=1.0 / 64)
                        ACT(r1[:], r1[:], AF.Exp, [br1], [br1], scale=-0.5)
                        o, bo = stage.next()
                        gg = gk if isk else gq
                        STT("dve", o[:], P[:], gg[:, l:l + 1], r1[:], ALU.mult, ALU.mult,
                            [bP, b_gk if isk else b_gq, br1], [bo])
                        if isk:
                            Sc.op("dve", lambda e, o=o, j=j, g=g: e.tensor_reduce(
                                out=KS[:, j - 2, 2 * g:2 * g + 2],
                                in_=o[:].rearrange("p (b t) -> p b t", t=256),
                                axis=AX.X, op=ALU.add), [bo], [b_KS])
                        store_fm(s_kn if isk else s_qn, j % 2, o, bo)

                    pend = None
                    for j in range(4):
                        P, bP = proj_fm(j * 128)
                        s2, bs2 = sqq.next()
                        ACT(s2[:], P[:], AF.Square, [bP], [bs2])
                        if pend is not None:
                            qk_tail(*pend)
                        pend = (j, P, bP, s2, bs2)
                    if g > 0:
                        decay_chain(g - 1)
                    firstv = True
                    for (c0, dst) in ((512, s_v), (1536, s_vb)):
                        for tt in range(4):
                            P, bP = PS.next()
                            for c in range(8):
                                MM(P[:, 0:256], XN[:, c, tt * 128:(tt + 1) * 128], Wb[:, c, c0:c0 + 256],
                                   c == 0, c == 7, [b_XN, b_Wbs[c0 // 128], b_Wbs[c0 // 128 + 1]], [bP])
                            if firstv:
                                qk_tail(*pend)
                                firstv = False
                            o, bo = stage.next()
                            CP("act", o[:, 0:256], P[:, 0:256], [bP], [bo])
                            ST(dst[g * 512 + tt * 128:g * 512 + (tt + 1) * 128, :], o[:, 0:256], [bo], bo)
                    if g + 1 < NG:
                        norm_part1(g + 1)
                    for jj in range(HB):
                        P, bP = proj_fm(1280 + jj * 128)
                        ACT(SG[jj][0][:], P[:], AF.Sigmoid, [bP], [SG[jj][1]])
                    if g + 1 < NG:
                        norm_part2(H, b_H, *XNs[(g + 1) % 2])
                    for (c00, dst) in ((768, s_sag), (1792, s_sbg)):
                        for jj in range(2):
                            P, bP = proj_fm(c00 + jj * 128)
                            o, bo = stage.next()
                            ACT(o[:], P[:], AF.Silu, [bP], [bo])
                            store_fm(dst, jj, o, bo)
                    for jj in range(HB):
                        P, bP = proj_fm(1024 + jj * 128)
                        ACT(QB[jj][0][:], P[:], AF.Silu, [bP], [QB[jj][1]])
                decay_chain(NG - 1)
                hov = h_own.rearrange("(c p) t -> p c t", p=128)
                def gate_norm(g):
                    XN, b_XN = XNs[g % 2]
                    LD(H2[:], hov[:, :, g * 512:(g + 1) * 512], [b_H2], b_H2)
                    ACT(SQ[:].rearrange("p c t -> p (c t)"), H2[:].rearrange("p c t -> p (c t)"),
                        AF.Square, [b_H2], [b_SQ])
                    norm_part2(H2, b_H2, XN, b_XN)

                gate_norm(0)
                for g in range(NGO):
                    XN, b_XN = XNs[g % 2]
                    for jj in range(16):
                        if jj == 6 and g + 1 < NGO:
                            gate_norm(g + 1)
                        P, bP = PS.next()
                        for c in range(8):
                            MM(P[:], Wb[:, c, 2048 + jj * 128:2048 + (jj + 1) * 128], XN[:, c, :], c == 0, c == 7,
                               [b_Wbs[16 + jj], b_XN], [bP])
                        o, bo = stage.next()
                        ACT(o[:], P[:], AF.Sigmoid, [bP], [bo])
                        dst = s_sga if jj < 8 else s_sgb
                        ST(dst[(jj % 8) * 128:(jj % 8 + 1) * 128, g * 512:(g + 1) * 512], o[:], [bo], bo)
                Sc.emit()

            with ExitStack() as st:
                sb, ps = mk(st)
                QAs = [(sb(f"QA{i}", [128, S], BF16), Buf(f"QA{i}")) for i in range(2)]
                KAs = [(sb(f"KA{i}", [128, S], BF16), Buf(f"KA{i}")) for i in range(2)]
                VAs = [(sb(f"VA{i}", [128, NT, 128], BF16), Buf(f"VA{i}")) for i in range(2)]
                SAGr = Ring([(sb(f"SAG{i}", [64, 512], BF16), Buf(f"SAG{i}")) for i in range(3)])
                YGr = Ring([(sb(f"YG{i}", [64, 512], BF16), Buf(f"YG{i}")) for i in range(3)])
                FUT = sb("FUT", [128, NT, 32], BF16); b_FUT = Buf("FUT")
                NEGP = sb("NEGP", [128, NT, 32], BF16); b_NEGP = Buf("NEGP")
                KMh = sb("KMh", [64, 32], BF16); b_KMh = Buf("KMh")
                NTB = min(NT, 16)
                Gs = sb("Gs", [128, NTB, 32]); b_Gs = Buf("Gs")
                thr = sb("thr", [128, NTB, 8]); b_thr = Buf("thr")
                nsel = sb("nsel", [128, NTB, 32]); b_nsel = Buf("nsel")
                MBp = sb("MBp", [128, NTB, 128], BF16); b_MBp = Buf("MBp")
                PT = Ring([(sb(f"PT{i}", [128, 1024], BF16), Buf(f"PT{i}")) for i in range(3)])
                rden = sb("rden", [64, 512]); b_rden = Buf("rden")
                yh = sb("yh", [64, 512]); b_yh = Buf("yh")
                STp = Ring([(ps(f"STp{i}", [128, 1024]), Buf(f"STp{i}")) for i in range(2)])
                Op = Ring([(ps(f"Op{i}", [128, 512]), Buf(f"Op{i}")) for i in range(2)])
                Gp = ps("Gp", [128, 16, 32]); b_Gp = Buf("Gp")
                MTp = ps("MTp", [128, 512]); b_MTp = Buf("MTp")
                LD(FUT[:], c_fut, [b_FUT], b_FUT)
                LD(NEGP[:], c_neg, [b_NEGP], b_NEGP)
                for i in range(2):
                    LD(KAs[i][0][64:96, :], c_onehot, [KAs[i][1]], KAs[i][1])
                    Sc.op("pool", lambda e, i=i: e.memset(VAs[i][0][:, :, 64:128], 1.0), (), [VAs[i][1]])
                Sc.op("pool", lambda e: e.memset(MBp[:], 0.0), (), [b_MBp])
                TS_ = sb("TS", [128, HA, STRIP], BF16); b_TS = Buf("TS")
                c31s = sb("c31s", [128, HA]); b_c31 = Buf("c31s")
                LD(c31s[:], c31, [b_c31], b_c31)
                SPC = STRIP // 4
                stgs = [(sb(f"stripstg{i}", [128, SPC]), Buf(f"stripstg{i}")) for i in range(2)]
                for h in range(HA):
                    for q4 in range(4):
                        stg, b_stg = stgs[(h * 4 + q4) % 2]
                        LD(stg[:], strip_raw[:, h, q4 * SPC:(q4 + 1) * SPC], [b_stg], b_stg)
                        TSC("dve", TS_[:, h, q4 * SPC:(q4 + 1) * SPC], stg[:], c31s[:, h:h + 1], None, ALU.subtract,
                            None, [b_stg, b_c31], [b_TS])

                def head_loads(h):
                    sl = h % 2
                    QA, b_QA = QAs[sl]; KA, b_KA = KAs[sl]; VA, b_VA = VAs[sl]
                    LD(QA[0:64, :], s_qn[h * 64:(h + 1) * 64, :], [b_QA], b_QA)
                    LD(KA[0:64, :], s_kn[h * 64:(h + 1) * 64, :], [b_KA], b_KA)
                    vsrc = s_v[:, h * 64:(h + 1) * 64].rearrange("(t p) d -> p t d", p=128)
                    nvs = max(1, NT // 8)
                    for i in range(0, NT, nvs):
                        LD(VA[:, i:i + nvs, 0:64], vsrc[:, i:i + nvs, :], [b_VA], b_VA)

                def gate_stage(h, tb, stage_):
                    sl = h % 2
                    QA, b_QA = QAs[sl]
                    cq, po = h // 2, 64 * (h % 2)
                    if stage_ == 0:
                        if tb == 0:
                            TSC("dve", KMh[0:64, 0:NB], KS[po:po + 64, cq, :], 1.0 / 256, None, ALU.mult, None,
                                [b_KS], [b_KMh])
                        for t in range(NTB):
                            MM(Gp[:, t, 0:NB], QA[0:64, (tb + t) * 128:(tb + t + 1) * 128], KMh[0:64, 0:NB],
                               True, True, [b_QA, b_KMh], [b_Gp])
                    elif stage_ == 1:
                        if NB < 32:
                            Sc.op("dve", lambda e: e.memset(Gs[:], NEG), (), [b_Gs])
                        TT("dve", Gs[:, :, 0:NB], Gp[:, 0:NTB, 0:NB], FUT[:, tb:tb + NTB, 0:NB], ALU.add,
                           [b_Gp, b_FUT], [b_Gs])
                        for t in range(NTB):
                            Sc.op("dve", lambda e, t=t: e.max(out=thr[:, t, :], in_=Gs[:, t, :]), [b_Gs], [b_thr])
                        TT("dve", nsel[:], Gs[:], thr[:, :, 2:3].to_broadcast([128, NTB, 32]), ALU.is_lt,
                           [b_Gs, b_thr], [b_nsel])
                        TT("pool", MBp[:, :, 64:96], nsel[:], NEGP[:, tb:tb + NTB, :], ALU.mult,
                           [b_nsel, b_NEGP], [b_MBp])
                    else:
                        t4 = (stage_ - 2) * 4
                        for t in range(4):
                            MM(MTp[:, t * 128:(t + 1) * 128], MBp[:, t4 + t, :], ident[:], True, True,
                               [b_MBp, b_ident], [b_MTp])
                        c0 = (tb + t4) * 128
                        CP("act", QA[64:96, c0:c0 + 512], MTp[64:96, :], [b_MTp], [b_QA])

                NST = 2 + NTB // 4
                gate_sched = [(tb, st_) for tb in range(0, NT, NTB) for st_ in range(NST)]

                def head_gate(h):
                    for (tb, st_) in gate_sched:
                        gate_stage(h, tb, st_)

                head_loads(0)
                head_gate(0)
                for h in range(HA):
                    sl = h % 2
                    QA, b_QA = QAs[sl]; KA, b_KA = KAs[sl]; VA, b_VA = VAs[sl]
                    pairs = [(g, kp) for g in range(NG) for kp in range(2 * g + 2)]
                    slots = {}

                    def emit_qk(i):
                        g, kp = pairs[i]
                        Sp, bS = STp.next()
                        for u in range(2):
                            kt = 2 * kp + u
                            delta = 512 * g - 128 * kt
                            near = delta <= 1536
                            MM(Sp[:, u * 512:(u + 1) * 512], KA[0:96, kt * 128:(kt + 1) * 128],
                               QA[0:96, g * 512:(g + 1) * 512], True, not near, [b_KA, b_QA], [bS])
                            if near:
                                MM(Sp[:, u * 512:(u + 1) * 512], ident[:],
                                   TS_[:, h, delta + 384:delta + 384 + 512], False, True, [b_ident, b_TS], [bS])
                        slots[i] = (Sp, bS)

                    emit_qk(0)
                    O, bO = None, None
                    gate_i0 = len(pairs) // 4
                    gate_step = max(1, (len(pairs) - gate_i0 - 2) // len(gate_sched))
                    for i, (g, kp) in enumerate(pairs):
                        npair = 2 * g + 2
                        if kp == 0:
                            O, bO = Op.next()
                            SAG, b_SAG = SAGr.next()
                            LD(SAG[:], s_sag[h * 64:(h + 1) * 64, g * 512:(g + 1) * 512], [b_SAG], b_SAG)
                        if i + 1 < len(pairs):
                            emit_qk(i + 1)
                        if i == 0 and h + 1 < HA:
                            head_loads(h + 1)
                        if h + 1 < HA and i >= gate_i0 and (i - gate_i0) % gate_step == 0:
                            kq = (i - gate_i0) // gate_step
                            if kq < len(gate_sched):
                                gate_stage(h + 1, *gate_sched[kq])
                        Sp, bS = slots.pop(i)
                        P_, bPt = PT.next()
                        ACT(P_[:], Sp[:], AF.Exp, [bS], [bPt])
                        for u in range(2):
                            kt = 2 * kp + u
                            MM(O[:], VA[:, kt, :], P_[:, u * 512:(u + 1) * 512], kt == 0, kt == 2 * npair - 1,
                               [b_VA, bPt], [bO])
                        if kp == npair - 1:
                            Sc.op("dve", lambda e, O=O: e.reciprocal(out=rden[:], in_=O[64:128, :]), [bO], [b_rden])
                            TT("dve", yh[:], O[0:64, :], rden[:], ALU.mult, [bO, b_rden], [b_yh])
                            YG, b_YG = YGr.next()
                            TT("pool", YG[:], yh[:], SAG[:], ALU.mult, [b_yh, b_SAG], [b_YG])
                            ST(XS[h // 2, (h % 2) * 64:(h % 2) * 64 + 64, g * 512:(g + 1) * 512], YG[:], [b_YG], b_YG)
                Sc.emit()

            with ExitStack() as st:
                sb, ps = mk(st)
                QDs = [(sb(f"QD{i}", [128, S], BF16), Buf(f"QD{i}")) for i in range(HB)]
                KDs = [(sb(f"KD{i}", [128, S], BF16), Buf(f"KD{i}")) for i in range(HB)]
                KEr = Ring([(sb(f"KE{i}", [128, 512], BF16), Buf(f"KE{i}")) for i in range(4)])
                VBr = Ring([(sb(f"VB{i}", [64, 8, 128], BF16), Buf(f"VB{i}")) for i in range(4)])
                SBGr = Ring([(sb(f"SBG{i}", [128, 512], BF16), Buf(f"SBG{i}")) for i in range(4)])
                YBr = Ring([(sb(f"YB{i}", [128, 512], BF16), Buf(f"YB{i}")) for i in range(4)])
                KTr = Ring([(sb(f"KT{i}", [64, 8, 128], BF16), Buf(f"KT{i}")) for i in range(4)])
                ATs = Ring([(sb(f"ATs{i}", [64, 64], BF16), Buf(f"ATs{i}")) for i in range(4)])
                Sbs = [[(sb(f"Sb{hh}_{i}", [128, 128], BF16), Buf(f"Sb{hh}_{i}")) for i in range(2)] for hh in range(HB)]
                osq = sb("osq", [128, 512], BF16); b_osq = Buf("osq")
                ort = sb("ort", [128, 512]); b_ort = Buf("ort")
                ors = sb("ors", [128, 512]); b_ors = Buf("ors")
                ybf = sb("ybf", [128, 512]); b_ybf = Buf("ybf")
                KT1 = (ps("KTp", [64, 8, 128], BF16), Buf("KTp"))
                KTp = [KT1] * HB
                OHp = [Ring([(ps(f"OHp{hh}_{i}", [128, 512]), Buf(f"OHp{hh}_{i}")) for i in range(1)]) for hh in range(HB)]
                Abk = [ps(f"Abk{hh}", [128, 512]) for hh in range(HB)]
                dSbk = [ps(f"dSbk{hh}", [128, 512]) for hh in range(HB)]
                ATp = [(Abk[hh][0:64, 0:64], Buf(f"ATp{hh}")) for hh in range(HB)]
                dSp = [(dSbk[hh][:, 0:128], Buf(f"dSp{hh}")) for hh in range(HB)]
                NSp = ps("NSp", [128, 512]); b_NSp = Buf("NSp")
                for k in range(2):
                    AG(XG[k], XS[k], Buf(f"agx{k}"))
                for hh in range(HB):
                    rows = slice(hh * 128, (hh + 1) * 128)
                    LD(QDs[hh][0][:], s_qd[rows, :], [QDs[hh][1]], QDs[hh][1])
                    LD(KDs[hh][0][:], s_kd[rows, :], [KDs[hh][1]], KDs[hh][1])
                    Sc.op("dve", lambda e, hh=hh: e.memset(Sbs[hh][0][0][:], 0.0), (), [Sbs[hh][0][1]])
                cur = [0] * HB

                def group_loads(g):
                    out = []
                    for hh in range(HB):
                        rows = slice(hh * 128, (hh + 1) * 128)
                        KE, bKE = KEr.next()
                        LD(KE[:], s_ke[rows, g * 512:(g + 1) * 512], [bKE], bKE)
                        VB, bVB = VBr.next()
                        vsrc = s_vb[g * 512:(g + 1) * 512, hh * 128:(hh + 1) * 128].rearrange("(c s) v -> s c v", s=64)
                        LD(VB[:], vsrc, [bVB], bVB)
                        SBG, bSBG = SBGr.next()
                        LD(SBG[:], s_sbg[rows, g * 512:(g + 1) * 512], [bSBG], bSBG)
                        out.append((KE, bKE, VB, bVB, SBG, bSBG))
                    return out

                nxt = group_loads(0)
                for g in range(NG):
                    gl_ = nxt
                    if g + 1 < NG:
                        nxt = group_loads(g + 1)
                    KTl, OHl = [], []
                    for hh in range(HB):
                        KE, bKE, VB, bVB, SBG, bSBG = gl_[hh]
                        KTps, bKTp = KTp[hh]
                        for c in range(8):
                            Sc.op("pe", lambda e, KTps=KTps, c=c, KE=KE: e.transpose(
                                out=KTps[:, c, :], in_=KE[:, c * 64:(c + 1) * 64], identity=ident[:]),
                                [bKE, b_ident], [bKTp])
                        KTs, bKT = KTr.next()
                        CP("act", KTs[:], KTps[:], [bKTp], [bKT])
                        KTl.append((KTs, bKT))
                        OHl.append(OHp[hh].next())
                    for c in range(8):
                        ch = g * 8 + c
                        cs = slice(ch * 64, (ch + 1) * 64)
                        Asl = []
                        for hh in range(HB):
                            QD, b_QD = QDs[hh]; KD, b_KD = KDs[hh]
                            A, bA = ATp[hh]
                            MM(A, KD[:, cs], QD[:, cs], True, True, [b_KD, b_QD], [bA])
                            As, bAs = ATs.next()
                            TT("dve", As[:], A, caus[:], ALU.mult, [bA, b_caus], [bAs])
                            Asl.append((As, bAs))
                        for hh in range(HB):
                            KE, bKE, VB, bVB, SBG, bSBG = gl_[hh]
                            QD, b_QD = QDs[hh]
                            KTs, bKT = KTl[hh]; OH, bOH = OHl[hh]
                            As, bAs = Asl[hh]
                            S0, bS0 = Sbs[hh][cur[hh]]
                            S1, bS1 = Sbs[hh][1 - cur[hh]]
                            MM(OH[:, c * 64:(c + 1) * 64], VB[:, c, :], As[:], True, False, [bVB, bAs], [bOH])
                            MM(OH[:, c * 64:(c + 1) * 64], S0[:], QD[:, cs], False, True, [bS0, b_QD], [bOH])
                            dS, bdS = dSp[hh]
                            MM(dS, KTs[:, c, :], VB[:, c, :], True, True, [bKT, bVB], [bdS])
                            STT("dve", S1[:], S0[:], EL[:, hh, ch:ch + 1], dS, ALU.mult, ALU.add,
                                [bS0, b_EL, bdS], [bS1])
                            cur[hh] = 1 - cur[hh]
                    for hh in range(HB):
                        KE, bKE, VB, bVB, SBG, bSBG = gl_[hh]
                        OH, bOH = OHl[hh]
                        ACT(osq[:], OH[:], AF.Square, [bOH], [b_osq])
                        MM(NSp[:], ones[:], osq[:], True, True, [b_ones, b_osq], [b_NSp])
                        ACT(ort[:], NSp[:], AF.Ln, [b_NSp], [b_ort], bias=EPS, scale=1.0 / 128)
                        ACT(ors[:], ort[:], AF.Exp, [b_ort], [b_ors], scale=-0.5)
                        STT("dve", ybf[:], OH[:], go[:, l, hh:hh + 1], ors[:], ALU.mult, ALU.mult,
                            [bOH, b_go, b_ors], [b_ybf])
                        YB, bYB = YBr.next()
                        TT("pool", YB[:], ybf[:], SBG[:], ALU.mult, [b_ybf, bSBG], [bYB])
                        ST(XS[2 + hh, :, g * 512:(g + 1) * 512], YB[:], [bYB], bYB)
                Sc.emit()

            with ExitStack() as st:
                sb, ps = mk(st)
                WA = sb("WA", [128, 4, D], BF16)
                WB_ = sb("WB", [128, 4, D], BF16)
                WO = sb("WO", [128, 8, D], BF16)
                WP = sb("WP", [128, 2, D], BF16)
                WG = sb("WG", [128, 8, D], BF16)
                bW = {}

                def wtok(name, c, j):
                    return bW[(name, (c // 2) * 2, (j // 4) * 512)]
                wst = [(sb(f"wstc{i}", [128, 2, 512]), Buf(f"wstc{i}")) for i in range(2)]
                k = 0
                for (dst, nm, src, nch) in ((WA, "WA", w_up_a[l], 4), (WB_, "WB", w_up_b[l], 4),
                                            (WO, "WO", w_out[l], 8), (WP, "WP", w_ple[l], 2),
                                            (WG, "WG", w_pg[l], 8)):
                    sv = src.rearrange("(c p) n -> p c n", p=128)
                    for c2 in range(0, nch, 2):
                        for n2 in range(0, D, 512):
                            t, b = wst[k % 2]
                            bW[(nm, c2, n2)] = Buf(f"{nm}_{c2}_{n2}")
                            LD(t[:], sv[:, c2:c2 + 2, n2:n2 + 512], [b], b)
                            CP(("act", "dve", "pool")[k % 3], dst[:, c2:c2 + 2, n2:n2 + 512], t[:], [b],
                               [bW[(nm, c2, n2)]])
                            k += 1
                INS = []
                for i in range(2):
                    INS.append(dict(
                        YAG=(sb(f"YAG{i}", [128, 4, 512], BF16), Buf(f"YAG{i}")),
                        YBG=(sb(f"YBG{i}", [128, 4, 512], BF16), Buf(f"YBG{i}")),
                        SGA=(sb(f"SGA{i}", [128, 8, 512], BF16), Buf(f"SGA{i}")),
                        SGB=(sb(f"SGB{i}", [128, 8, 512], BF16), Buf(f"SGB{i}")),
                        PB=(sb(f"PB{i}", [128, 2, 512], BF16), Buf(f"PB{i}"))))
                H = sb("Hc", [128, 8, 512]); b_H = Buf("Hc")
                PF = sb("PF", [128, 2, 512]); b_PF = Buf("PF")
                MG = sb("MG", [128, 8, 512], BF16); b_MG = Buf("MG")
                HM = sb("HM", [128, 8, 512]); b_HM = Buf("HM")
                HMb = sb("HMb", [128, 8, 512], BF16); b_HMb = Buf("HMb")
                HN = Ring([(sb(f"HN{i}", [128, 512]), Buf(f"HN{i}")) for i in range(2)])
                HNb = Ring([(sb(f"HNb{i}", [128, 512], BF16), Buf(f"HNb{i}")) for i in range(2)])
                tmp = Ring([(sb(f"tmp{i}", [128, 512]), Buf(f"tmp{i}")) for i in range(4)])
                PS = Ring([(ps(f"PSc{i}", [128, 512]), Buf(f"PSc{i}")) for i in range(8)])
                hv = h_own.rearrange("(c p) t -> p c t", p=128)
                hd = h_dst.rearrange("(c p) t -> p c t", p=128)
                pv = pT[l].rearrange("(c p) t -> p c t", p=128)
                b_XG = [Buf(f"XG{k}") for k in range(4)]
                for k in (2, 3):
                    AG(XG[k], XS[k], Buf(f"agx{k}"), w=[b_XG[k]])
                NXP = 4 if SH >= 2048 else 1
                XPW = SH // NXP
                b_XOs = [Buf(f"XO{i}") for i in range(NXP)]
                for xp in range(NXP):
                    for kx_ in range(4):
                        LD(XO[kx_, :, xp * XPW:(xp + 1) * XPW], XG[kx_, :, bass.ds(off_own + xp * XPW, XPW)],
                           [b_XOs[xp]], b_XOs[xp], r=[b_XG[kx_]])
                b_HS = [Buf(f"HSp{q}") for q in range(NPC)]

                def loads(g):
                    I = INS[g % 2]
                    ts_ = slice(g * 512, (g + 1) * 512)
                    for c in range(4):
                        rk, kk = c // 2, c % 2
                        bxo = b_XOs[(g * 512) // XPW]
                        LD(I["YAG"][0][:, c, :], XO[kk, rk * 128:(rk + 1) * 128, ts_], [I["YAG"][1]], I["YAG"][1], r=[bxo])
                        LD(I["YBG"][0][:, c, :], XO[2 + kk, rk * 128:(rk + 1) * 128, ts_], [I["YBG"][1]], I["YBG"][1], r=[bxo])
                    LD(I["SGA"][0][:], s_sga.rearrange("(c p) t -> p c t", p=128)[:, :, ts_], [I["SGA"][1]], I["SGA"][1])
                    LD(I["SGB"][0][:], s_sgb.rearrange("(c p) t -> p c t", p=128)[:, :, ts_], [I["SGB"][1]], I["SGB"][1])
                    LD(PF[:], pv[:, :, ts_], [b_PF], b_PF)
                    CP("act", I["PB"][0][:], PF[:], [b_PF], [I["PB"][1]])

                loads(0)
                LD(H[:], hv[:, :, 0:512], [b_H], b_H)
                for g in range(NGO):
                    ts_ = slice(g * 512, (g + 1) * 512)
                    I = INS[g % 2]
                    YAG, b_YAG = I["YAG"]; YBG, b_YBG = I["YBG"]; SGA, b_SGA = I["SGA"]; SGB, b_SGB = I["SGB"]
                    PB, b_PB = I["PB"]
                    if g + 1 < NGO:
                        loads(g + 1)
                    for j in range(8):
                        Pa, bPa = PS.next()
                        for c in range(4):
                            MM(Pa[:], WA[:, c, j * 128:(j + 1) * 128], YAG[:, c, :], c == 0, c == 3,
                               [wtok("WA", c, j), b_YAG], [bPa])
                        Pb, bPb = PS.next()
                        for c in range(4):
                            MM(Pb[:], WB_[:, c, j * 128:(j + 1) * 128], YBG[:, c, :], c == 0, c == 3,
                               [wtok("WB", c, j), b_YBG], [bPb])
                        t1, bt1 = tmp.next()
                        TT("dve", t1[:], Pa[:], SGA[:, j, :], ALU.mult, [bPa, b_SGA], [bt1])
                        t2, bt2 = tmp.next()
                        TT("dve", t2[:], Pb[:], SGB[:, j, :], ALU.mult, [bPb, b_SGB], [bt2])
                        TT("pool", MG[:, j, :], t1[:], t2[:], ALU.add, [bt1, bt2], [b_MG])
                    for j in range(8):
                        Po, bPo = PS.next()
                        for c in range(8):
                            MM(Po[:], WO[:, c, j * 128:(j + 1) * 128], MG[:, c, :], c == 0, c == 7,
                               [wtok("WO", c, j), b_MG], [bPo])
                        TT("dve", HM[:, j, :], Po[:], H[:, j, :], ALU.add, [bPo, b_H], [b_HM])
                        CP("act", HMb[:, j, :], HM[:, j, :], [b_HM], [b_HMb])
                    if g + 1 < NGO:
                        LD(H[:], hv[:, :, (g + 1) * 512:(g + 2) * 512], [b_H], b_H)
                    q, col = (g * 512) // PIECE, (g * 512) % PIECE
                    for j in range(8):
                        Pp, bPp = PS.next()
                        for c in range(2):
                            MM(Pp[:], WP[:, c, j * 128:(j + 1) * 128], PB[:, c, :], c == 0, c == 1,
                               [wtok("WP", c, j), b_PB], [bPp])
                        Pg, bPg = PS.next()
                        for c in range(8):
                            MM(Pg[:], WG[:, c, j * 128:(j + 1) * 128], HMb[:, c, :], c == 0, c == 7,
                               [wtok("WG", c, j), b_HMb], [bPg])
                        sg, bsg = tmp.next()
                        ACT(sg[:], Pg[:], AF.Sigmoid, [bPg], [bsg])
                        t1, bt1 = tmp.next()
                        TT("dve", t1[:], Pp[:], sg[:], ALU.mult, [bPp, bsg], [bt1])
                        hn, bhn = HN.next()
                        TT("pool", hn[:], t1[:], HM[:, j, :], ALU.add, [bt1, b_HM], [bhn])
                        ST(hd[:, j, ts_], hn[:], [bhn], bhn, q="sp")
                        if not last:
                            hb, bhb = HNb.next()
                            CP("act", hb[:], hn[:], [bhn], [bhb])
                            ST(HS[q, j * 128:(j + 1) * 128, col:col + 512], hb[:], [bhb], bhb, w=[b_HS[q]], q="sp")
                    if not last and (g + 1) * 512 % PIECE == 0:
                        AG(HG[q], HS[q], Buf(f"agh{q}"), r=[b_HS[q]])
                Sc.emit()

        print("total ops", Sc.total, "sems", Sc.nsem)
    return nc


def host_consts(S):
    NT = S // 128
    bf = ml_dtypes.bfloat16
    c = {}
    c["c_ident"] = np.eye(128, dtype=np.float32).astype(bf)
    blk = np.zeros((128, 128), np.float32)
    blk[:64, :64] = 1.0
    blk[64:, 64:] = 1.0
    c["c_blk"] = blk.astype(bf)
    oh = np.zeros((32, S), np.float32)
    for n in range(S // 256):
        oh[n, n * 256:(n + 1) * 256] = 1.0
    c["c_onehot"] = oh.astype(bf)
    c["c_causal"] = np.triu(np.ones((64, 64), np.float32))
    sm = np.ones((128, 512), np.float32)
    sm[:, ::64] = 0.0
    c["c_scan"] = sm
    fut = np.zeros((NT, 32), np.float32)
    neg = np.full((NT, 32), NEG, np.float32)
    for t in range(NT):
        b = t // 2
        fut[t, b:] = NEG
        neg[t, b] = 0.0
    c["c_fut"] = np.ascontiguousarray(np.broadcast_to(fut[None], (128, NT, 32))).astype(bf)
    c["c_neg"] = np.ascontiguousarray(np.broadcast_to(neg[None], (128, NT, 32))).astype(bf)
    return c


def host_strips(rel_bias, heads):
    i = np.arange(128)[:, None]
    u = np.arange(STRIP)[None, :]
    rel = u - 384 - i
    bucket = t5_bucket_np(rel)
    strip = np.empty((128, len(heads), STRIP), np.float32)
    for k, h in enumerate(heads):
        g = rel_bias[:, h][bucket]
        strip[:, k, :] = np.where(rel >= 0, g, np.float32(NEG))
    c31 = np.ascontiguousarray(np.broadcast_to(rel_bias[31, heads][None, :], (128, len(heads)))).astype(np.float32)
    return strip, c31


def host_inputs(b, r, S, x, p, norm_gain, w_in, q_norm_gain, k_norm_gain, rel_bias, hgrn_lb_logits,
                hgrn_out_gain, w_up_a, w_up_b, w_out, w_ple, w_ple_gate, consts):
    L = w_in.shape[0]
    SH = S // 2
    m = dict(consts)
    m["xT"] = np.ascontiguousarray(x[b, :S].T)
    m["xT_own"] = np.ascontiguousarray(x[b, r * SH:(r + 1) * SH].T)
    m["pT"] = np.ascontiguousarray(np.transpose(p[:, b, r * SH:(r + 1) * SH, :], (0, 2, 1)))
    cols = []
    for blk0 in range(0, 4096, 512):
        cols.append(np.arange(blk0 + r * 256, blk0 + (r + 1) * 256))
    cols.append(np.arange(4096, 6144))
    cols = np.concatenate(cols)
    m["w_in"] = np.ascontiguousarray(w_in[:, :, cols])
    m["w_up_a"] = w_up_a
    m["w_up_b"] = w_up_b
    m["w_out"] = w_out
    m["w_ple"] = w_ple
    m["w_pg"] = w_ple_gate
    m["g_norm"] = np.ascontiguousarray(np.transpose(norm_gain.reshape(L, 8, 128), (2, 0, 1)))
    m["g_q"] = np.ascontiguousarray(np.concatenate([q_norm_gain, q_norm_gain], axis=1).T)
    m["g_k"] = np.ascontiguousarray(np.concatenate([k_norm_gain, k_norm_gain], axis=1).T)
    m["g_o"] = np.ascontiguousarray(np.transpose(hgrn_out_gain.reshape(L, 4, 128)[:, 2 * r:2 * r + 2], (2, 0, 1)))
    m["lbl"] = np.ascontiguousarray(np.transpose(hgrn_lb_logits.reshape(L, 4, 128)[:, 2 * r:2 * r + 2], (2, 0, 1)))
    strip, c31 = host_strips(rel_bias, list(range(4 * r, 4 * r + 4)))
    m["strip_raw"] = strip
    m["c31"] = c31
    return m


_NC_CACHE = {}


def kernel(x, p, norm_gain, w_in, q_norm_gain, k_norm_gain, rel_bias, hgrn_lb_logits,
           hgrn_out_gain, w_up_a, w_up_b, w_out, w_ple, w_ple_gate):
    args = [np.asarray(a, dtype=np.float32) for a in (
        x, p, norm_gain, w_in, q_norm_gain, k_norm_gain, rel_bias, hgrn_lb_logits,
        hgrn_out_gain, w_up_a, w_up_b, w_out, w_ple, w_ple_gate)]
    x = args[0]
    B, S, _ = x.shape
    if S not in _NC_CACHE:
        _NC_CACHE[S] = build(S)
    nc = _NC_CACHE[S]
    consts = host_consts(S)
    in_maps = [host_inputs(i // 2, i % 2, S, *args, consts) for i in range(2 * B)]
    res = run_bass_kernel_spmd(nc, in_maps, core_ids=list(range(2 * B)))
    SH = S // 2
    out = np.empty((B, S, D), np.float32)
    for i in range(2 * B):
        out[i // 2, (i % 2) * SH:(i % 2 + 1) * SH, :] = res.results[i]["hT_out"].T
    return out
```

```python
import math
from contextlib import ExitStack

import numpy as np
import ml_dtypes
import concourse.bass as bass
import concourse.mybir as mybir
from concourse.bass_utils import run_bass_kernel_spmd

F32 = mybir.dt.float32
BF16 = mybir.dt.bfloat16
AF = mybir.ActivationFunctionType
ALU = mybir.AluOpType
AX = mybir.AxisListType

SEM_CHUNK = 20000
D = 1024
NEG = -30000.0
STRIP = 2432
EPS = 1e-6


class Buf:
    __slots__ = ("name", "last_writer", "dma_writers", "readers", "dma_readers")

    def __init__(self, name):
        self.name = name
        self.clear()

    def clear(self):
        self.last_writer = None
        self.dma_writers = []
        self.readers = {}
        self.dma_readers = []


class Sched:
    def __init__(self, nc, stack):
        self.nc = nc
        self.stack = stack
        self.ops = []
        self.eng = {"pe": nc.tensor, "act": nc.scalar, "dve": nc.vector,
                    "pool": nc.gpsimd, "sp": nc.sync}
        self.nsem = 0
        self.eng_count = {}
        self.eng_sems = {}
        self.dma_pool = []
        self.dma_free = []
        self.key_slot = {}
        self.waited = {}
        self.total = 0

    def new_sem(self, name):
        self.nsem += 1
        return self.stack.enter_context(self.nc.semaphore(name))

    def op(self, eng, fn, reads=(), writes=()):
        self.ops.append(["c", eng, fn, tuple(reads), tuple(writes), None, 1])

    def dma(self, queue, fn, reads=(), writes=(), key=None, inc=16):
        assert key is not None
        self.ops.append(["d", queue, fn, tuple(reads), tuple(writes), key, inc])

    def emit(self):
        ops = self.ops
        n = len(ops)
        deps = [None] * n
        signaling = [False] * n
        last_of_eng = {}
        for i, o in enumerate(ops):
            d = set()
            isd = o[0] == "d"
            for b in o[3]:
                if b.last_writer is not None:
                    d.add(b.last_writer)
                d.update(b.dma_writers)
            for b in o[4]:
                if b.last_writer is not None:
                    d.add(b.last_writer)
                d.update(b.readers.values())
                d.update(b.dma_readers)
                if not isd:
                    d.update(b.dma_writers)
            d.discard(i)
            for b in o[3]:
                if isd:
                    b.dma_readers.append(i)
                else:
                    b.readers[o[1]] = i
            for b in o[4]:
                if isd:
                    b.dma_writers.append(i)
                else:
                    b.last_writer = i
                    b.dma_writers = []
                    b.readers = {}
                    b.dma_readers = []
            deps[i] = d
            for j in d:
                signaling[j] = True
            if o[0] == "c":
                last_of_eng[o[1]] = i
        for e, i in last_of_eng.items():
            signaling[i] = True
        sig = [None] * n
        for i, o in enumerate(ops):
            if o[0] == "c":
                if not signaling[i]:
                    continue
                e = o[1]
                c = self.eng_count.get(e, 0)
                k = c // SEM_CHUNK
                if (e, k) not in self.eng_sems:
                    self.eng_sems[(e, k)] = self.new_sem(f"s_{e}_{k}")
                self.eng_count[e] = c + 1
                sig[i] = (self.eng_sems[(e, k)], c - k * SEM_CHUNK + 1, 1, ("e", e, k))
            else:
                key = o[5]
                if key not in self.key_slot:
                    if self.dma_free:
                        s = self.dma_free.pop()
                    else:
                        self.dma_pool.append([self.new_sem(f"d_{len(self.dma_pool)}"), 0])
                        s = len(self.dma_pool) - 1
                    self.key_slot[key] = s
                s = self.key_slot[key]
                self.dma_pool[s][1] += o[6]
                sig[i] = (self.dma_pool[s][0], self.dma_pool[s][1], o[6], ("k", s))
        waited = self.waited
        for i, o in enumerate(ops):
            e = o[1]
            engine = self.eng[e]
            w = waited.setdefault(e, {})
            for j in sorted(deps[i]):
                pj = ops[j]
                if pj[0] == "c" and pj[1] == "pe" and e == "pe" and o[0] == "c":
                    continue
                sem, val, _, sid = sig[j]
                if w.get(sid, 0) >= val:
                    continue
                w[sid] = val
                engine.wait_ge(sem, val)
            ins = o[2](engine)
            if sig[i] is not None:
                ins.then_inc(sig[i][0], sig[i][2])
        finals = []
        for (e, k), sem in self.eng_sems.items():
            c = self.eng_count.get(e, 0)
            if c // SEM_CHUNK == k and c - k * SEM_CHUNK > 0:
                finals.append((sem, c - k * SEM_CHUNK, ("e", e, k)))
            elif c // SEM_CHUNK > k:
                finals.append((sem, SEM_CHUNK, ("e", e, k)))
        for s, (sem, c) in enumerate(self.dma_pool):
            if c > 0:
                finals.append((sem, c, ("k", s)))
        for e in ("sp", "pe", "act", "dve", "pool"):
            w = waited.setdefault(e, {})
            for sem, val, sid in finals:
                if w.get(sid, 0) >= val:
                    continue
                w[sid] = val
                self.eng[e].wait_ge(sem, val)
        for o in ops:
            for b in o[3] + o[4]:
                b.clear()
        self.key_slot = {}
        self.dma_free = list(range(len(self.dma_pool)))
        self.total += n
        self.ops = []
        return n


class Ring:
    def __init__(self, items):
        self.items = items
        self.i = 0

    def next(self):
        it = self.items[self.i % len(self.items)]
        self.i += 1
        return it


def t5_bucket_np(rel):
    n = np.maximum(rel, 0)
    nf = np.maximum(n, 16).astype(np.float32)
    large = 16 + (np.log(nf / np.float32(16)) / np.float32(math.log(128)) * np.float32(16)).astype(np.int32)
    large = np.minimum(large, 31)
    return np.where(n < 16, n, large)


def build(S, L=2, debug=False, groups=None):
    NT, NB, NG, NCH = S // 128, S // 256, S // 512, S // 64
    SH, NGO = S // 2, S // 1024
    HA, HB = 4, 2
    if groups is None:
        groups = [[0, 1], [2, 3], [4, 5], [6, 7]]
    nc = bass.Bass("TRN2", target_bir_lowering=False)

    def din(name, shape, dt=F32):
        return nc.dram_tensor(name, list(shape), dt, kind="ExternalInput").ap()

    def dscr(name, shape, dt=BF16, out=False):
        kind = "Internal"
        return nc.dram_tensor(name, list(shape), dt, kind=kind).ap()

    xT = din("xT", [D, S])
    xT_own = din("xT_own", [D, SH])
    pT = din("pT", [L, 256, SH])
    w_in = din("w_in", [L, D, 4096])
    w_up_a = din("w_up_a", [L, 512, D])
    w_up_b = din("w_up_b", [L, 512, D])
    w_out = din("w_out", [L, D, D])
    w_ple = din("w_ple", [L, 256, D])
    w_pg = din("w_pg", [L, D, D])
    g_norm = din("g_norm", [128, L, 8])
    g_q = din("g_q", [128, L])
    g_k = din("g_k", [128, L])
    g_o = din("g_o", [128, L, HB])
    lbl = din("lbl", [128, L, HB])
    strip_raw = din("strip_raw", [128, HA, STRIP])
    c31 = din("c31", [128, HA])
    c_ident = din("c_ident", [128, 128], BF16)
    c_blk = din("c_blk", [128, 128], BF16)
    c_onehot = din("c_onehot", [32, S], BF16)
    c_causal = din("c_causal", [64, 64])
    c_scan = din("c_scan", [128, 512])
    c_fut = din("c_fut", [128, NT, 32], BF16)
    c_neg = din("c_neg", [128, NT, 32], BF16)

    hT_out = nc.dram_tensor("hT_out", [D, SH], F32, kind="ExternalOutput").ap()
    h_mid = dscr("h_mid", [D, SH], F32)
    s_qn = dscr("s_qn", [256, S])
    s_kn = dscr("s_kn", [256, S])
    s_v = dscr("s_v", [S, 256])
    s_sag = dscr("s_sag", [256, S])
    s_qd = dscr("s_qd", [256, S])
    s_kd = dscr("s_kd", [256, S])
    s_ke = dscr("s_ke", [256, S])
    s_vb = dscr("s_vb", [S, 256])
    s_sbg = dscr("s_sbg", [256, S])
    s_sga = dscr("s_sga", [D, SH])
    s_sgb = dscr("s_sgb", [D, SH])
    XS = dscr("XS", [4, 128, S], out=True)
    XG = dscr("XG", [4, 2 * 128, S], out=True)
    XO = dscr("XO", [4, 2 * 128, SH])
    PIECE = max(512, SH // 4)
    NPC = SH // PIECE
    HS = dscr("HS", [NPC, D, PIECE])
    HG = dscr("HG", [NPC, 2 * D, PIECE])

    off_own = (nc.sync.partition_id() % 2) * SH

    with ExitStack() as outer:
        Sc = Sched(nc, outer)
        uniq = [0]

        def mk(stack):
            uniq[0] += 1
            tag = f"u{uniq[0]}_"

            def sb(name, shape, dt=F32):
                return stack.enter_context(nc.sbuf_tensor(tag + name, list(shape), dt))

            def ps(name, shape, dt=F32):
                return stack.enter_context(nc.psum_tensor(tag + name, list(shape), dt))
            return sb, ps

        def MM(out, lhsT, rhs, start, stop, r, w):
            Sc.op("pe", lambda e: e.matmul(out, lhsT=lhsT, rhs=rhs, start=start, stop=stop), r, w)

        def ACT(out, in_, func, r, w, bias=0.0, scale=1.0, eng="act"):
            Sc.op(eng, lambda e: e.activation(out=out, in_=in_, func=func, bias=bias, scale=scale), r, w)

        def TT(eng, out, in0, in1, op, r, w):
            Sc.op(eng, lambda e: e.tensor_tensor(out=out, in0=in0, in1=in1, op=op), r, w)

        def TSC(eng, out, in0, s1, s2, op0, op1, r, w):
            if s2 is None:
                Sc.op(eng, lambda e: e.tensor_scalar(out=out, in0=in0, scalar1=s1, scalar2=None, op0=op0), r, w)
            else:
                Sc.op(eng, lambda e: e.tensor_scalar(out=out, in0=in0, scalar1=s1, scalar2=s2, op0=op0, op1=op1), r, w)

        def STT(eng, out, in0, scalar, in1, op0, op1, r, w):
            Sc.op(eng, lambda e: e.scalar_tensor_tensor(out=out, in0=in0, scalar=scalar, in1=in1, op0=op0, op1=op1), r, w)

        def CP(eng, out, in_, r, w):
            if eng == "act":
                Sc.op("act", lambda e: e.activation(out=out, in_=in_, func=AF.Copy), r, w)
            else:
                Sc.op(eng, lambda e: e.tensor_copy(out=out, in_=in_), r, w)

        def LD(out, in_, w, key, r=(), q="sp"):
            Sc.dma(q, lambda e: e.dma_start(out=out, in_=in_), r, w, key)

        def ST(out, in_, r, key, w=(), q="pool"):
            Sc.dma(q, lambda e: e.dma_start(out=out, in_=in_), r, w, key)

        def AG(out, in_, key, r=(), w=()):
            Sc.dma("pool", lambda e: e.collective_compute(
                "AllGather", ALU.bypass, replica_groups=groups, ins=[in_], outs=[out]), r, w, key, inc=1)

        sbP, _ = mk(outer)
        ident = sbP("ident", [128, 128], BF16); b_ident = Buf("ident")
        blk = sbP("blk", [128, 128], BF16); b_blk = Buf("blk")
        ones = sbP("ones", [128, 128], BF16); b_ones = Buf("ones")
        gn = sbP("gn", [128, L, 8]); b_gn = Buf("gn")
        gq = sbP("gq", [128, L]); b_gq = Buf("gq")
        gk = sbP("gk", [128, L]); b_gk = Buf("gk")
        go = sbP("go", [128, L, HB]); b_go = Buf("go")
        lb = sbP("lb", [128, L, HB]); b_lb = Buf("lb")
        oml = sbP("oml", [128, L, HB]); b_oml = Buf("oml")
        lbe = sbP("lbe", [128, L, HB]); b_lbe = Buf("lbe")
        lbs = sbP("lbs", [128, HB]); b_lbs = Buf("lbs")
        KS = sbP("KS", [128, 2, NB]); b_KS = Buf("KS")
        EL = sbP("EL", [128, HB, NCH]); b_EL = Buf("EL")
        caus = sbP("caus", [64, 64]); b_caus = Buf("caus")
        scanm = sbP("scanm", [128, 512]); b_scanm = Buf("scanm")

        with ExitStack() as st:
            sb, ps = mk(st)
            LD(ident[:], c_ident, [b_ident], b_ident)
            LD(blk[:], c_blk, [b_blk], b_blk)
            Sc.op("pool", lambda e: e.memset(ones[:], 1.0), (), [b_ones])
            LD(gn[:], g_norm, [b_gn], b_gn)
            LD(gq[:], g_q, [b_gq], b_gq)
            LD(gk[:], g_k, [b_gk], b_gk)
            LD(go[:], g_o, [b_go], b_go)
            LD(lbe[:], lbl, [b_lbe], b_lbe)
            LD(caus[:], c_causal, [b_caus], b_caus)
            LD(scanm[:], c_scan, [b_scanm], b_scanm)
            TSC("dve", gq[:], gq[:], 0.125, None, ALU.mult, None, [b_gq], [b_gq])
            ACT(lbe[:], lbe[:], AF.Exp, [b_lbe], [b_lbe])
            CP("dve", lbs[:], lbe[:, 0, :], [b_lbe], [b_lbs])
            for l in range(1, L):
                TT("dve", lbs[:], lbs[:], lbe[:, l, :], ALU.add, [b_lbs, b_lbe], [b_lbs])
            Sc.op("dve", lambda e: e.reciprocal(out=lbs[:], in_=lbs[:]), [b_lbs], [b_lbs])
            Sc.op("dve", lambda e: e.memset(lb[:, 0, :], 0.0), (), [b_lb])
            for l in range(1, L):
                TT("dve", lb[:, l, :], lb[:, l - 1, :], lbe[:, l, :], ALU.add, [b_lb, b_lbe], [b_lb])
            for l in range(1, L):
                TT("dve", lb[:, l, :], lb[:, l, :], lbs[:], ALU.mult, [b_lb, b_lbs], [b_lb])
            for l in range(L):
                TSC("dve", oml[:, l, :], lb[:, l, :], -1.0, 1.0, ALU.mult, ALU.add, [b_lb], [b_oml])
            Sc.emit()

        for l in range(L):
            first, last = (l == 0), (l == L - 1)
            h_own = xT_own if first else h_mid
            h_dst = hT_out if last else h_mid

            with ExitStack() as st:
                sb, ps = mk(st)
                Wb = sb("Wb", [128, 8, 4096], BF16)
                b_Wbs = [Buf(f"Wb{i}") for i in range(32)]
                wst = [(sb(f"wst{i}", [128, 8, 128]), Buf(f"wst{i}")) for i in range(2)]
                wv = w_in[l].rearrange("(c p) n -> p c n", p=128)
                for i in range(32):
                    t, b = wst[i % 2]
                    LD(t[:], wv[:, :, i * 128:(i + 1) * 128], [b], b)
                    CP(("act", "dve", "pool")[i % 3], Wb[:, :, i * 128:(i + 1) * 128], t[:], [b], [b_Wbs[i]])
                HDT = F32 if first else BF16
                H = sb("H", [128, 8, 512], HDT); b_H = Buf("H")
                H2 = sb("H2", [128, 8, 512]); b_H2 = Buf("H2")
                SQ = sb("SQ", [128, 8, 512], BF16); b_SQ = Buf("SQ")
                XNs = [(sb(f"XN{i}", [128, 8, 512], BF16), Buf(f"XN{i}")) for i in range(2)]
                rstd = sb("rstd", [128, 512]); b_rstd = Buf("rstd")
                stage = Ring([(sb(f"stg{i}", [128, 512], BF16), Buf(f"stg{i}")) for i in range(6)])
                f32r = Ring([(sb(f"f32r{i}", [128, 512]), Buf(f"f32r{i}")) for i in range(8)])
                QB = [(sb(f"QB{i}", [128, 512]), Buf(f"QB{i}")) for i in range(HB)]
                SG = [(sb(f"SG{i}", [128, 512]), Buf(f"SG{i}")) for i in range(HB)]
                sqq = Ring([(sb(f"sqq{i}", [128, 512], BF16), Buf(f"sqq{i}")) for i in range(2)])
                p_ss = ps("p_ss", [128, 512]); b_pss = Buf("p_ss")
                PS = Ring([(ps(f"PSa{i}", [128, 512]), Buf(f"PSa{i}")) for i in range(5)])
                BS = Ring([(ps(f"BSa{i}", [128, 512]), Buf(f"BSa{i}")) for i in range(2)])
                Sc.op("dve", lambda e: e.memset(KS[:], 0.0), (), [b_KS])

                def load_h(g):
                    if first:
                        LD(H[:], xT.rearrange("(c p) t -> p c t", p=128)[:, :, g * 512:(g + 1) * 512], [b_H], b_H)
                    else:
                        half, tok = g // NGO, (g % NGO) * 512
                        q, col = tok // PIECE, tok % PIECE
                        src = HG[q, half * D:(half + 1) * D, col:col + 512].rearrange("(c p) t -> p c t", p=128)
                        LD(H[:], src, [b_H], b_H)

                def norm_part1(g):
                    load_h(g)
                    ACT(SQ[:].rearrange("p c t -> p (c t)"), H[:].rearrange("p c t -> p (c t)"),
                        AF.Square, [b_H], [b_SQ])

                def norm_part2(Hs, bH, XN, b_XN):
                    for c in range(8):
                        MM(p_ss[:], ones[:], SQ[:, c, :], c == 0, c == 7, [b_ones, b_SQ], [b_pss])
                    ACT(rstd[:], p_ss[:], AF.Ln, [b_pss], [b_rstd], bias=EPS, scale=1.0 / D)
                    ACT(rstd[:], rstd[:], AF.Exp, [b_rstd], [b_rstd], scale=-0.5)
                    for c in range(8):
                        STT("dve", XN[:, c, :], Hs[:, c, :], gn[:, l, c:c + 1], rstd[:],
                            ALU.mult, ALU.mult, [bH, b_gn, b_rstd], [b_XN])

                chain_state = {}

                def decay_stage(g, jj, stg_):
                    if stg_ == 0:
                        sg, bsg = SG[jj]
                        f, bf_ = f32r.next()
                        TSC("dve", f[:], sg[:], oml[:, l, jj:jj + 1], lb[:, l, jj:jj + 1], ALU.mult, ALU.add,
                            [bsg, b_oml, b_lb], [bf_])
                        gl, bgl = f32r.next()
                        ACT(gl[:], f[:], AF.Ln, [bf_], [bgl])
                        cum, bcum = f32r.next()
                        Sc.op("dve", lambda e, cum=cum, gl=gl: e.tensor_tensor_scan(
                            out=cum[:], data0=scanm[:], data1=gl[:], initial=0.0,
                            op0=ALU.mult, op1=ALU.add), [bgl, b_scanm], [bcum])
                        TSC("dve", f[:], f[:], -1.0, 1.0, ALU.mult, ALU.add, [bf_], [bf_])
                        chain_state[(g, jj)] = (f, bf_, gl, bgl, cum, bcum)
                    elif stg_ == 1:
                        f, bf_, gl, bgl, cum, bcum = chain_state[(g, jj)]
                        ec, bec = f32r.next()
                        ACT(ec[:], cum[:], AF.Exp, [bcum], [bec])
                        ACT(gl[:], cum[:], AF.Exp, [bcum], [bgl], scale=-1.0)
                        CP("pool", EL[:, jj, g * 8:(g + 1) * 8],
                           ec[:].rearrange("p (c t) -> p c t", t=64)[:, :, 63], [bec], [b_EL])
                        o, bo = stage.next()
                        TT("pool", o[:], QB[jj][0][:], ec[:], ALU.mult, [QB[jj][1], bec], [bo])
                        ST(s_qd[jj * 128:(jj + 1) * 128, g * 512:(g + 1) * 512], o[:], [bo], bo)
                        TT("dve", f[:], f[:], gl[:], ALU.mult, [bf_, bgl], [bf_])
                    else:
                        f, bf_, gl, bgl, cum, bcum = chain_state.pop((g, jj))
                        o, bo = stage.next()
                        CP("act", o[:], f[:], [bf_], [bo])
                        ST(s_kd[jj * 128:(jj + 1) * 128, g * 512:(g + 1) * 512], o[:], [bo], bo)
                        o, bo = stage.next()
                        TT("pool", o[:].rearrange("p (c t) -> p c t", t=64),
                           f[:].rearrange("p (c t) -> p c t", t=64),
                           EL[:, jj, g * 8:(g + 1) * 8].unsqueeze(2).to_broadcast([128, 8, 64]),
                           ALU.mult, [bf_, b_EL], [bo])
                        ST(s_ke[jj * 128:(jj + 1) * 128, g * 512:(g + 1) * 512], o[:], [bo], bo)

                def decay_chain(g):
                    for st_ in range(3):
                        for jj in range(HB):
                            decay_stage(g, jj, st_)

                norm_part1(0)
                norm_part2(H, b_H, *XNs[0])
                for g in range(NG):
                    XN, b_XN = XNs[g % 2]

                    def proj_fm(c0):
                        P, bP = PS.next()
                        for c in range(8):
                            MM(P[:], Wb[:, c, c0:c0 + 128], XN[:, c, :], c == 0, c == 7,
                               [b_Wbs[c0 // 128], b_XN], [bP])
                        return P, bP

                    def store_fm(dst, j, t, b):
                        ST(dst[j * 128:(j + 1) * 128, g * 512:(g + 1) * 512], t[:], [b], b)

                    def qk_tail(j, P, bP, s2, bs2):
                        isk = j >= 2
                        Bp, bB = BS.next()
                        MM(Bp[:], blk[:], s2[:], True, True, [b_blk, bs2], [bB])
                        r1, br1 = f32r.next()
                        ACT(r1[:], Bp[:], AF.Ln, [bB], [br1], bias=EPS, scale=1.0 / 64)
                        ACT(r1[:], r1[:], AF.Exp, [br1], [br1], scale=-0.5)
                        o, bo = stage.next()
                        gg = gk if isk else gq
                        STT("dve", o[:], P[:], gg[:, l:l + 1], r1[:], ALU.mult, ALU.mult,
                            [bP, b_gk if isk else b_gq, br1], [bo])
                        if isk:
                            Sc.op("dve", lambda e, o=o, j=j, g=g: e.tensor_reduce(
                                out=KS[:, j - 2, 2 * g:2 * g + 2],
                                in_=o[:].rearrange("p (b t) -> p b t", t=256),
                                axis=AX.X, op=ALU.add), [bo], [b_KS])
                        store_fm(s_kn if isk else s_qn, j % 2, o, bo)

                    pend = None
                    for j in range(4):
                        P, bP = proj_fm(j * 128)
                        s2, bs2 = sqq.next()
                        ACT(s2[:], P[:], AF.Square, [bP], [bs2])
                        if pend is not None:
                            qk_tail(*pend)
                        pend = (j, P, bP, s2, bs2)
                    firstv = True
                    vi = 0
                    chain_sched = [(jj, st_) for st_ in range(3) for jj in range(HB)]
                    for (c0, dst) in ((512, s_v), (1536, s_vb)):
                        for tt in range(4):
                            if g > 0 and 1 <= vi <= len(chain_sched):
                                decay_stage(g - 1, *chain_sched[vi - 1])
                            vi += 1
                            P, bP = PS.next()
                            for c in range(8):
                                MM(P[:, 0:256], XN[:, c, tt * 128:(tt + 1) * 128], Wb[:, c, c0:c0 + 256],
                                   c == 0, c == 7, [b_XN, b_Wbs[c0 // 128], b_Wbs[c0 // 128 + 1]], [bP])
                            if firstv:
                                qk_tail(*pend)
                                firstv = False
                            o, bo = stage.next()
                            CP("act", o[:, 0:256], P[:, 0:256], [bP], [bo])
                            ST(dst[g * 512 + tt * 128:g * 512 + (tt + 1) * 128, :], o[:, 0:256], [bo], bo)
                    if g + 1 < NG:
                        norm_part1(g + 1)
                    for jj in range(HB):
                        P, bP = proj_fm(1280 + jj * 128)
                        ACT(SG[jj][0][:], P[:], AF.Sigmoid, [bP], [SG[jj][1]])
                    if g + 1 < NG:
                        norm_part2(H, b_H, *XNs[(g + 1) % 2])
                    for (c00, dst) in ((768, s_sag), (1792, s_sbg)):
                        for jj in range(2):
                            P, bP = proj_fm(c00 + jj * 128)
                            o, bo = stage.next()
                            ACT(o[:], P[:], AF.Silu, [bP], [bo])
                            store_fm(dst, jj, o, bo)
                    for jj in range(HB):
                        P, bP = proj_fm(1024 + jj * 128)
                        ACT(QB[jj][0][:], P[:], AF.Silu, [bP], [QB[jj][1]])
                decay_chain(NG - 1)
                hov = h_own.rearrange("(c p) t -> p c t", p=128)
                def gate_norm(g):
                    XN, b_XN = XNs[g % 2]
                    LD(H2[:], hov[:, :, g * 512:(g + 1) * 512], [b_H2], b_H2)
                    ACT(SQ[:].rearrange("p c t -> p (c t)"), H2[:].rearrange("p c t -> p (c t)"),
                        AF.Square, [b_H2], [b_SQ])
                    norm_part2(H2, b_H2, XN, b_XN)

                gate_norm(0)
                for g in range(NGO):
                    XN, b_XN = XNs[g % 2]
                    for jj in range(16):
                        if jj == 6 and g + 1 < NGO:
                            gate_norm(g + 1)
                        P, bP = PS.next()
                        for c in range(8):
                            MM(P[:], Wb[:, c, 2048 + jj * 128:2048 + (jj + 1) * 128], XN[:, c, :], c == 0, c == 7,
                               [b_Wbs[16 + jj], b_XN], [bP])
                        o, bo = stage.next()
                        ACT(o[:], P[:], AF.Sigmoid, [bP], [bo])
                        dst = s_sga if jj < 8 else s_sgb
                        ST(dst[(jj % 8) * 128:(jj % 8 + 1) * 128, g * 512:(g + 1) * 512], o[:], [bo], bo)
                Sc.emit()

            with ExitStack() as st:
                sb, ps = mk(st)
                QAs = [(sb(f"QA{i}", [128, S], BF16), Buf(f"QA{i}")) for i in range(2)]
                KAs = [(sb(f"KA{i}", [128, S], BF16), Buf(f"KA{i}")) for i in range(2)]
                VAs = [(sb(f"VA{i}", [128, NT, 128], BF16), Buf(f"VA{i}")) for i in range(2)]
                SAGr = Ring([(sb(f"SAG{i}", [64, 512], BF16), Buf(f"SAG{i}")) for i in range(3)])
                YGr = Ring([(sb(f"YG{i}", [64, 512], BF16), Buf(f"YG{i}")) for i in range(3)])
                FUT = sb("FUT", [128, NT, 32], BF16); b_FUT = Buf("FUT")
                NEGP = sb("NEGP", [128, NT, 32], BF16); b_NEGP = Buf("NEGP")
                KMh = sb("KMh", [64, 32], BF16); b_KMh = Buf("KMh")
                NTB = min(NT, 16)
                Gs = sb("Gs", [128, NTB, 32]); b_Gs = Buf("Gs")
                thr = sb("thr", [128, NTB, 8]); b_thr = Buf("thr")
                nsel = sb("nsel", [128, NTB, 32]); b_nsel = Buf("nsel")
                MBp = sb("MBp", [128, NTB, 128], BF16); b_MBp = Buf("MBp")
                PT = Ring([(sb(f"PT{i}", [128, 1024], BF16), Buf(f"PT{i}")) for i in range(3)])
                rden = sb("rden", [64, 512]); b_rden = Buf("rden")
                yh = sb("yh", [64, 512]); b_yh = Buf("yh")
                STp = Ring([(ps(f"STp{i}", [128, 1024]), Buf(f"STp{i}")) for i in range(2)])
                Op = Ring([(ps(f"Op{i}", [128, 512]), Buf(f"Op{i}")) for i in range(2)])
                Gp = ps("Gp", [128, 16, 32]); b_Gp = Buf("Gp")
                MTp = ps("MTp", [128, 512]); b_MTp = Buf("MTp")
                LD(FUT[:], c_fut, [b_FUT], b_FUT)
                LD(NEGP[:], c_neg, [b_NEGP], b_NEGP)
                for i in range(2):
                    LD(KAs[i][0][64:96, :], c_onehot, [KAs[i][1]], KAs[i][1])
                    Sc.op("pool", lambda e, i=i: e.memset(VAs[i][0][:, :, 64:128], 1.0), (), [VAs[i][1]])
                Sc.op("pool", lambda e: e.memset(MBp[:], 0.0), (), [b_MBp])
                TS_ = sb("TS", [128, HA, STRIP], BF16); b_TS = Buf("TS")
                c31s = sb("c31s", [128, HA]); b_c31 = Buf("c31s")
                LD(c31s[:], c31, [b_c31], b_c31)
                SPC = STRIP // 4
                stgs = [(sb(f"stripstg{i}", [128, SPC]), Buf(f"stripstg{i}")) for i in range(2)]
                for h in range(HA):
                    for q4 in range(4):
                        stg, b_stg = stgs[(h * 4 + q4) % 2]
                        LD(stg[:], strip_raw[:, h, q4 * SPC:(q4 + 1) * SPC], [b_stg], b_stg)
                        TSC("dve", TS_[:, h, q4 * SPC:(q4 + 1) * SPC], stg[:], c31s[:, h:h + 1], None, ALU.subtract,
                            None, [b_stg, b_c31], [b_TS])

                def head_loads(h):
                    sl = h % 2
                    QA, b_QA = QAs[sl]; KA, b_KA = KAs[sl]; VA, b_VA = VAs[sl]
                    LD(QA[0:64, :], s_qn[h * 64:(h + 1) * 64, :], [b_QA], b_QA)
                    LD(KA[0:64, :], s_kn[h * 64:(h + 1) * 64, :], [b_KA], b_KA)
                    vsrc = s_v[:, h * 64:(h + 1) * 64].rearrange("(t p) d -> p t d", p=128)
                    nvs = max(1, NT // 8)
                    for i in range(0, NT, nvs):
                        LD(VA[:, i:i + nvs, 0:64], vsrc[:, i:i + nvs, :], [b_VA], b_VA)

                def gate_stage(h, tb, stage_):
                    sl = h % 2
                    QA, b_QA = QAs[sl]
                    cq, po = h // 2, 64 * (h % 2)
                    if stage_ == 0:
                        if tb == 0:
                            TSC("dve", KMh[0:64, 0:NB], KS[po:po + 64, cq, :], 1.0 / 256, None, ALU.mult, None,
                                [b_KS], [b_KMh])
                        for t in range(NTB):
                            MM(Gp[:, t, 0:NB], QA[0:64, (tb + t) * 128:(tb + t + 1) * 128], KMh[0:64, 0:NB],
                               True, True, [b_QA, b_KMh], [b_Gp])
                    elif stage_ == 1:
                        if NB < 32:
                            Sc.op("dve", lambda e: e.memset(Gs[:], NEG), (), [b_Gs])
                        TT("dve", Gs[:, :, 0:NB], Gp[:, 0:NTB, 0:NB], FUT[:, tb:tb + NTB, 0:NB], ALU.add,
                           [b_Gp, b_FUT], [b_Gs])
                        for t in range(NTB):
                            Sc.op("dve", lambda e, t=t: e.max(out=thr[:, t, :], in_=Gs[:, t, :]), [b_Gs], [b_thr])
                        TT("dve", nsel[:], Gs[:], thr[:, :, 2:3].to_broadcast([128, NTB, 32]), ALU.is_lt,
                           [b_Gs, b_thr], [b_nsel])
                        TT("pool", MBp[:, :, 64:96], nsel[:], NEGP[:, tb:tb + NTB, :], ALU.mult,
                           [b_nsel, b_NEGP], [b_MBp])
                    else:
                        t4 = (stage_ - 2) * 4
                        for t in range(4):
                            MM(MTp[:, t * 128:(t + 1) * 128], MBp[:, t4 + t, :], ident[:], True, True,
                               [b_MBp, b_ident], [b_MTp])
                        c0 = (tb + t4) * 128
                        CP("act", QA[64:96, c0:c0 + 512], MTp[64:96, :], [b_MTp], [b_QA])

                NST = 2 + NTB // 4
                gate_sched = [(tb, st_) for tb in range(0, NT, NTB) for st_ in range(NST)]

                def head_gate(h):
                    for (tb, st_) in gate_sched:
                        gate_stage(h, tb, st_)

                head_loads(0)
                head_gate(0)
                for h in range(HA):
                    sl = h % 2
                    QA, b_QA = QAs[sl]; KA, b_KA = KAs[sl]; VA, b_VA = VAs[sl]
                    pairs = [(g, kp) for g in range(NG) for kp in range(2 * g + 2)]
                    slots = {}

                    def emit_qk(i):
                        g, kp = pairs[i]
                        Sp, bS = STp.next()
                        for u in range(2):
                            kt = 2 * kp + u
                            delta = 512 * g - 128 * kt
                            near = delta <= 1536
                            MM(Sp[:, u * 512:(u + 1) * 512], KA[0:96, kt * 128:(kt + 1) * 128],
                               QA[0:96, g * 512:(g + 1) * 512], True, not near, [b_KA, b_QA], [bS])
                            if near:
                                MM(Sp[:, u * 512:(u + 1) * 512], ident[:],
                                   TS_[:, h, delta + 384:delta + 384 + 512], False, True, [b_ident, b_TS], [bS])
                        slots[i] = (Sp, bS)

                    emit_qk(0)
                    O, bO = None, None
                    gate_i0 = len(pairs) // 4
                    gate_step = max(1, (len(pairs) - gate_i0 - 2) // len(gate_sched))
                    for i, (g, kp) in enumerate(pairs):
                        npair = 2 * g + 2
                        if kp == 0:
                            O, bO = Op.next()
                            SAG, b_SAG = SAGr.next()
                            LD(SAG[:], s_sag[h * 64:(h + 1) * 64, g * 512:(g + 1) * 512], [b_SAG], b_SAG)
                        if i + 1 < len(pairs):
                            emit_qk(i + 1)
                        if i == 0 and h + 1 < HA:
                            head_loads(h + 1)
                        if h + 1 < HA and i >= gate_i0 and (i - gate_i0) % gate_step == 0:
                            kq = (i - gate_i0) // gate_step
                            if kq < len(gate_sched):
                                gate_stage(h + 1, *gate_sched[kq])
                        Sp, bS = slots.pop(i)
                        P_, bPt = PT.next()
                        ACT(P_[:], Sp[:], AF.Exp, [bS], [bPt])
                        for u in range(2):
                            kt = 2 * kp + u
                            MM(O[:], VA[:, kt, :], P_[:, u * 512:(u + 1) * 512], kt == 0, kt == 2 * npair - 1,
                               [b_VA, bPt], [bO])
                        if kp == npair - 1:
                            Sc.op("dve", lambda e, O=O: e.reciprocal(out=rden[:], in_=O[64:128, :]), [bO], [b_rden])
                            TT("dve", yh[:], O[0:64, :], rden[:], ALU.mult, [bO, b_rden], [b_yh])
                            YG, b_YG = YGr.next()
                            TT("pool", YG[:], yh[:], SAG[:], ALU.mult, [b_yh, b_SAG], [b_YG])
                            ST(XS[h // 2, (h % 2) * 64:(h % 2) * 64 + 64, g * 512:(g + 1) * 512], YG[:], [b_YG], b_YG)
                Sc.emit()

            with ExitStack() as st:
                sb, ps = mk(st)
                QDs = [(sb(f"QD{i}", [128, S], BF16), Buf(f"QD{i}")) for i in range(HB)]
                KDs = [(sb(f"KD{i}", [128, S], BF16), Buf(f"KD{i}")) for i in range(HB)]
                KEr = Ring([(sb(f"KE{i}", [128, 512], BF16), Buf(f"KE{i}")) for i in range(4)])
                VBr = Ring([(sb(f"VB{i}", [64, 8, 128], BF16), Buf(f"VB{i}")) for i in range(4)])
                SBGr = Ring([(sb(f"SBG{i}", [128, 512], BF16), Buf(f"SBG{i}")) for i in range(4)])
                YBr = Ring([(sb(f"YB{i}", [128, 512], BF16), Buf(f"YB{i}")) for i in range(4)])
                KTr = Ring([(sb(f"KT{i}", [64, 8, 128], BF16), Buf(f"KT{i}")) for i in range(4)])
                ATs = Ring([(sb(f"ATs{i}", [64, 64], BF16), Buf(f"ATs{i}")) for i in range(4)])
                Sbs = [[(sb(f"Sb{hh}_{i}", [128, 128], BF16), Buf(f"Sb{hh}_{i}")) for i in range(2)] for hh in range(HB)]
                osq = sb("osq", [128, 512], BF16); b_osq = Buf("osq")
                ort = sb("ort", [128, 512]); b_ort = Buf("ort")
                ors = sb("ors", [128, 512]); b_ors = Buf("ors")
                ybf = sb("ybf", [128, 512]); b_ybf = Buf("ybf")
                KT1 = (ps("KTp", [64, 8, 128], BF16), Buf("KTp"))
                KTp = [KT1] * HB
                OHp = [Ring([(ps(f"OHp{hh}_{i}", [128, 512]), Buf(f"OHp{hh}_{i}")) for i in range(1)]) for hh in range(HB)]
                Abk = [ps(f"Abk{hh}", [128, 512]) for hh in range(HB)]
                dSbk = [ps(f"dSbk{hh}", [128, 512]) for hh in range(HB)]
                ATp = [(Abk[hh][0:64, 0:64], Buf(f"ATp{hh}")) for hh in range(HB)]
                dSp = [(dSbk[hh][:, 0:128], Buf(f"dSp{hh}")) for hh in range(HB)]
                NSp = ps("NSp", [128, 512]); b_NSp = Buf("NSp")
                for k in range(2):
                    AG(XG[k], XS[k], Buf(f"agx{k}"))
                for hh in range(HB):
                    rows = slice(hh * 128, (hh + 1) * 128)
                    LD(QDs[hh][0][:], s_qd[rows, :], [QDs[hh][1]], QDs[hh][1])
                    LD(KDs[hh][0][:], s_kd[rows, :], [KDs[hh][1]], KDs[hh][1])
                    Sc.op("dve", lambda e, hh=hh: e.memset(Sbs[hh][0][0][:], 0.0), (), [Sbs[hh][0][1]])
                cur = [0] * HB

                def group_loads(g):
                    out = []
                    for hh in range(HB):
                        rows = slice(hh * 128, (hh + 1) * 128)
                        KE, bKE = KEr.next()
                        LD(KE[:], s_ke[rows, g * 512:(g + 1) * 512], [bKE], bKE)
                        VB, bVB = VBr.next()
                        vsrc = s_vb[g * 512:(g + 1) * 512, hh * 128:(hh + 1) * 128].rearrange("(c s) v -> s c v", s=64)
                        LD(VB[:], vsrc, [bVB], bVB)
                        SBG, bSBG = SBGr.next()
                        LD(SBG[:], s_sbg[rows, g * 512:(g + 1) * 512], [bSBG], bSBG)
                        out.append((KE, bKE, VB, bVB, SBG, bSBG))
                    return out

                nxt = group_loads(0)
                for g in range(NG):
                    gl_ = nxt
                    if g + 1 < NG:
                        nxt = group_loads(g + 1)
                    KTl, OHl = [], []
                    for hh in range(HB):
                        KE, bKE, VB, bVB, SBG, bSBG = gl_[hh]
                        KTps, bKTp = KTp[hh]
                        for c in range(8):
                            Sc.op("pe", lambda e, KTps=KTps, c=c, KE=KE: e.transpose(
                                out=KTps[:, c, :], in_=KE[:, c * 64:(c + 1) * 64], identity=ident[:]),
                                [bKE, b_ident], [bKTp])
                        KTs, bKT = KTr.next()
                        CP("act", KTs[:], KTps[:], [bKTp], [bKT])
                        KTl.append((KTs, bKT))
                        OHl.append(OHp[hh].next())
                    for c in range(8):
                        ch = g * 8 + c
                        cs = slice(ch * 64, (ch + 1) * 64)
                        Asl = []
                        for hh in range(HB):
                            QD, b_QD = QDs[hh]; KD, b_KD = KDs[hh]
                            A, bA = ATp[hh]
                            MM(A, KD[:, cs], QD[:, cs], True, True, [b_KD, b_QD], [bA])
                            As, bAs = ATs.next()
                            TT("dve", As[:], A, caus[:], ALU.mult, [bA, b_caus], [bAs])
                            Asl.append((As, bAs))
                        for hh in range(HB):
                            KE, bKE, VB, bVB, SBG, bSBG = gl_[hh]
                            QD, b_QD = QDs[hh]
                            KTs, bKT = KTl[hh]; OH, bOH = OHl[hh]
                            As, bAs = Asl[hh]
                            S0, bS0 = Sbs[hh][cur[hh]]
                            S1, bS1 = Sbs[hh][1 - cur[hh]]
                            MM(OH[:, c * 64:(c + 1) * 64], VB[:, c, :], As[:], True, False, [bVB, bAs], [bOH])
                            MM(OH[:, c * 64:(c + 1) * 64], S0[:], QD[:, cs], False, True, [bS0, b_QD], [bOH])
                            dS, bdS = dSp[hh]
                            MM(dS, KTs[:, c, :], VB[:, c, :], True, True, [bKT, bVB], [bdS])
                            STT("dve", S1[:], S0[:], EL[:, hh, ch:ch + 1], dS, ALU.mult, ALU.add,
                                [bS0, b_EL, bdS], [bS1])
                            cur[hh] = 1 - cur[hh]
                    for hh in range(HB):
                        KE, bKE, VB, bVB, SBG, bSBG = gl_[hh]
                        OH, bOH = OHl[hh]
                        ACT(osq[:], OH[:], AF.Square, [bOH], [b_osq])
                        MM(NSp[:], ones[:], osq[:], True, True, [b_ones, b_osq], [b_NSp])
                        ACT(ort[:], NSp[:], AF.Ln, [b_NSp], [b_ort], bias=EPS, scale=1.0 / 128)
                        ACT(ors[:], ort[:], AF.Exp, [b_ort], [b_ors], scale=-0.5)
                        STT("dve", ybf[:], OH[:], go[:, l, hh:hh + 1], ors[:], ALU.mult, ALU.mult,
                            [bOH, b_go, b_ors], [b_ybf])
                        YB, bYB = YBr.next()
                        TT("pool", YB[:], ybf[:], SBG[:], ALU.mult, [b_ybf, bSBG], [bYB])
                        ST(XS[2 + hh, :, g * 512:(g + 1) * 512], YB[:], [bYB], bYB)
                Sc.emit()

            with ExitStack() as st:
                sb, ps = mk(st)
                WA = sb("WA", [128, 4, D], BF16)
                WB_ = sb("WB", [128, 4, D], BF16)
                WO = sb("WO", [128, 8, D], BF16)
                WP = sb("WP", [128, 2, D], BF16)
                WG = sb("WG", [128, 8, D], BF16)
                bW = {}

                def wtok(name, c, j):
                    return bW[(name, (c // 2) * 2, (j // 4) * 512)]
                wst = [(sb(f"wstc{i}", [128, 2, 512]), Buf(f"wstc{i}")) for i in range(2)]
                k = 0
                for (dst, nm, src, nch) in ((WA, "WA", w_up_a[l], 4), (WB_, "WB", w_up_b[l], 4),
                                            (WO, "WO", w_out[l], 8), (WP, "WP", w_ple[l], 2),
                                            (WG, "WG", w_pg[l], 8)):
                    sv = src.rearrange("(c p) n -> p c n", p=128)
                    for c2 in range(0, nch, 2):
                        for n2 in range(0, D, 512):
                            t, b = wst[k % 2]
                            bW[(nm, c2, n2)] = Buf(f"{nm}_{c2}_{n2}")
                            LD(t[:], sv[:, c2:c2 + 2, n2:n2 + 512], [b], b)
                            CP(("act", "dve", "pool")[k % 3], dst[:, c2:c2 + 2, n2:n2 + 512], t[:], [b],
                               [bW[(nm, c2, n2)]])
                            k += 1
                INS = []
                for i in range(2):
                    INS.append(dict(
                        YAG=(sb(f"YAG{i}", [128, 4, 512], BF16), Buf(f"YAG{i}")),
                        YBG=(sb(f"YBG{i}", [128, 4, 512], BF16), Buf(f"YBG{i}")),
                        SGA=(sb(f"SGA{i}", [128, 8, 512], BF16), Buf(f"SGA{i}")),
                        SGB=(sb(f"SGB{i}", [128, 8, 512], BF16), Buf(f"SGB{i}")),
                        PB=(sb(f"PB{i}", [128, 2, 512], BF16), Buf(f"PB{i}"))))
                H = sb("Hc", [128, 8, 512]); b_H = Buf("Hc")
                PF = sb("PF", [128, 2, 512]); b_PF = Buf("PF")
                MG = sb("MG", [128, 8, 512], BF16); b_MG = Buf("MG")
                HM = sb("HM", [128, 8, 512]); b_HM = Buf("HM")
                HMb = sb("HMb", [128, 8, 512], BF16); b_HMb = Buf("HMb")
                HN = Ring([(sb(f"HN{i}", [128, 512]), Buf(f"HN{i}")) for i in range(2)])
                HNb = Ring([(sb(f"HNb{i}", [128, 512], BF16), Buf(f"HNb{i}")) for i in range(2)])
                tmp = Ring([(sb(f"tmp{i}", [128, 512]), Buf(f"tmp{i}")) for i in range(4)])
                PS = Ring([(ps(f"PSc{i}", [128, 512]), Buf(f"PSc{i}")) for i in range(8)])
                hv = h_own.rearrange("(c p) t -> p c t", p=128)
                hd = h_dst.rearrange("(c p) t -> p c t", p=128)
                pv = pT[l].rearrange("(c p) t -> p c t", p=128)
                b_XG = [Buf(f"XG{k}") for k in range(4)]
                for k in (2, 3):
                    AG(XG[k], XS[k], Buf(f"agx{k}"), w=[b_XG[k]])
                NXP = 4 if SH >= 2048 else 1
                XPW = SH // NXP
                b_XOs = [Buf(f"XO{i}") for i in range(NXP)]
                for xp in range(NXP):
                    for kx_ in range(4):
                        LD(XO[kx_, :, xp * XPW:(xp + 1) * XPW], XG[kx_, :, bass.ds(off_own + xp * XPW, XPW)],
                           [b_XOs[xp]], b_XOs[xp], r=[b_XG[kx_]])
                b_HS = [Buf(f"HSp{q}") for q in range(NPC)]

                def loads(g):
                    I = INS[g % 2]
                    ts_ = slice(g * 512, (g + 1) * 512)
                    for c in range(4):
                        rk, kk = c // 2, c % 2
                        bxo = b_XOs[(g * 512) // XPW]
                        LD(I["YAG"][0][:, c, :], XO[kk, rk * 128:(rk + 1) * 128, ts_], [I["YAG"][1]], I["YAG"][1], r=[bxo])
                        LD(I["YBG"][0][:, c, :], XO[2 + kk, rk * 128:(rk + 1) * 128, ts_], [I["YBG"][1]], I["YBG"][1], r=[bxo])
                    LD(I["SGA"][0][:], s_sga.rearrange("(c p) t -> p c t", p=128)[:, :, ts_], [I["SGA"][1]], I["SGA"][1])
                    LD(I["SGB"][0][:], s_sgb.rearrange("(c p) t -> p c t", p=128)[:, :, ts_], [I["SGB"][1]], I["SGB"][1])
                    LD(PF[:], pv[:, :, ts_], [b_PF], b_PF)
                    CP("act", I["PB"][0][:], PF[:], [b_PF], [I["PB"][1]])

                loads(0)
                LD(H[:], hv[:, :, 0:512], [b_H], b_H)
                for g in range(NGO):
                    ts_ = slice(g * 512, (g + 1) * 512)
                    I = INS[g % 2]
                    YAG, b_YAG = I["YAG"]; YBG, b_YBG = I["YBG"]; SGA, b_SGA = I["SGA"]; SGB, b_SGB = I["SGB"]
                    PB, b_PB = I["PB"]
                    if g + 1 < NGO:
                        loads(g + 1)
                    for j in range(8):
                        Pa, bPa = PS.next()
                        for c in range(4):
                            MM(Pa[:], WA[:, c, j * 128:(j + 1) * 128], YAG[:, c, :], c == 0, c == 3,
                               [wtok("WA", c, j), b_YAG], [bPa])
                        Pb, bPb = PS.next()
                        for c in range(4):
                            MM(Pb[:], WB_[:, c, j * 128:(j + 1) * 128], YBG[:, c, :], c == 0, c == 3,
                               [wtok("WB", c, j), b_YBG], [bPb])
                        t1, bt1 = tmp.next()
                        TT("dve", t1[:], Pa[:], SGA[:, j, :], ALU.mult, [bPa, b_SGA], [bt1])
                        t2, bt2 = tmp.next()
                        TT("dve", t2[:], Pb[:], SGB[:, j, :], ALU.mult, [bPb, b_SGB], [bt2])
                        TT("pool", MG[:, j, :], t1[:], t2[:], ALU.add, [bt1, bt2], [b_MG])
                    for j in range(8):
                        Po, bPo = PS.next()
                        for c in range(8):
                            MM(Po[:], WO[:, c, j * 128:(j + 1) * 128], MG[:, c, :], c == 0, c == 7,
                               [wtok("WO", c, j), b_MG], [bPo])
                        TT("dve", HM[:, j, :], Po[:], H[:, j, :], ALU.add, [bPo, b_H], [b_HM])
                        CP("act", HMb[:, j, :], HM[:, j, :], [b_HM], [b_HMb])
                    if g + 1 < NGO:
                        LD(H[:], hv[:, :, (g + 1) * 512:(g + 2) * 512], [b_H], b_H)
                    q, col = (g * 512) // PIECE, (g * 512) % PIECE
                    for j in range(8):
                        Pp, bPp = PS.next()
                        for c in range(2):
                            MM(Pp[:], WP[:, c, j * 128:(j + 1) * 128], PB[:, c, :], c == 0, c == 1,
                               [wtok("WP", c, j), b_PB], [bPp])
                        Pg, bPg = PS.next()
                        for c in range(8):
                            MM(Pg[:], WG[:, c, j * 128:(j + 1) * 128], HMb[:, c, :], c == 0, c == 7,
                               [wtok("WG", c, j), b_HMb], [bPg])
                        sg, bsg = tmp.next()
                        ACT(sg[:], Pg[:], AF.Sigmoid, [bPg], [bsg])
                        t1, bt1 = tmp.next()
                        TT("dve", t1[:], Pp[:], sg[:], ALU.mult, [bPp, bsg], [bt1])
                        hn, bhn = HN.next()
                        TT("pool", hn[:], t1[:], HM[:, j, :], ALU.add, [bt1, b_HM], [bhn])
                        ST(hd[:, j, ts_], hn[:], [bhn], bhn, q="sp")
                        if not last:
                            hb, bhb = HNb.next()
                            CP("act", hb[:], hn[:], [bhn], [bhb])
                            ST(HS[q, j * 128:(j + 1) * 128, col:col + 512], hb[:], [bhb], bhb, w=[b_HS[q]], q="sp")
                    if not last and (g + 1) * 512 % PIECE == 0:
                        AG(HG[q], HS[q], Buf(f"agh{q}"), r=[b_HS[q]])
                Sc.emit()

        print("total ops", Sc.total, "sems", Sc.nsem)
    return nc


def host_consts(S):
    NT = S // 128
    bf = ml_dtypes.bfloat16
    c = {}
    c["c_ident"] = np.eye(128, dtype=np.float32).astype(bf)
    blk = np.zeros((128, 128), np.float32)
    blk[:64, :64] = 1.0
    blk[64:, 64:] = 1.0
    c["c_blk"] = blk.astype(bf)
    oh = np.zeros((32, S), np.float32)
    for n in range(S // 256):
        oh[n, n * 256:(n + 1) * 256] = 1.0
    c["c_onehot"] = oh.astype(bf)
    c["c_causal"] = np.triu(np.ones((64, 64), np.float32))
    sm = np.ones((128, 512), np.float32)
    sm[:, ::64] = 0.0
    c["c_scan"] = sm
    fut = np.zeros((NT, 32), np.float32)
    neg = np.full((NT, 32), NEG, np.float32)
    for t in range(NT):
        b = t // 2
        fut[t, b:] = NEG
        neg[t, b] = 0.0
    c["c_fut"] = np.ascontiguousarray(np.broadcast_to(fut[None], (128, NT, 32))).astype(bf)
    c["c_neg"] = np.ascontiguousarray(np.broadcast_to(neg[None], (128, NT, 32))).astype(bf)
    return c


def host_strips(rel_bias, heads):
    i = np.arange(128)[:, None]
    u = np.arange(STRIP)[None, :]
    rel = u - 384 - i
    bucket = t5_bucket_np(rel)
    strip = np.empty((128, len(heads), STRIP), np.float32)
    for k, h in enumerate(heads):
        g = rel_bias[:, h][bucket]
        strip[:, k, :] = np.where(rel >= 0, g, np.float32(NEG))
    c31 = np.ascontiguousarray(np.broadcast_to(rel_bias[31, heads][None, :], (128, len(heads)))).astype(np.float32)
    return strip, c31


def host_inputs(b, r, S, x, p, norm_gain, w_in, q_norm_gain, k_norm_gain, rel_bias, hgrn_lb_logits,
                hgrn_out_gain, w_up_a, w_up_b, w_out, w_ple, w_ple_gate, consts):
    L = w_in.shape[0]
    SH = S // 2
    m = dict(consts)
    m["xT"] = np.ascontiguousarray(x[b, :S].T)
    m["xT_own"] = np.ascontiguousarray(x[b, r * SH:(r + 1) * SH].T)
    m["pT"] = np.ascontiguousarray(np.transpose(p[:, b, r * SH:(r + 1) * SH, :], (0, 2, 1)))
    cols = []
    for blk0 in range(0, 4096, 512):
        cols.append(np.arange(blk0 + r * 256, blk0 + (r + 1) * 256))
    cols.append(np.arange(4096, 6144))
    cols = np.concatenate(cols)
    m["w_in"] = np.ascontiguousarray(w_in[:, :, cols])
    m["w_up_a"] = w_up_a
    m["w_up_b"] = w_up_b
    m["w_out"] = w_out
    m["w_ple"] = w_ple
    m["w_pg"] = w_ple_gate
    m["g_norm"] = np.ascontiguousarray(np.transpose(norm_gain.reshape(L, 8, 128), (2, 0, 1)))
    m["g_q"] = np.ascontiguousarray(np.concatenate([q_norm_gain, q_norm_gain], axis=1).T)
    m["g_k"] = np.ascontiguousarray(np.concatenate([k_norm_gain, k_norm_gain], axis=1).T)
    m["g_o"] = np.ascontiguousarray(np.transpose(hgrn_out_gain.reshape(L, 4, 128)[:, 2 * r:2 * r + 2], (2, 0, 1)))
    m["lbl"] = np.ascontiguousarray(np.transpose(hgrn_lb_logits.reshape(L, 4, 128)[:, 2 * r:2 * r + 2], (2, 0, 1)))
    strip, c31 = host_strips(rel_bias, list(range(4 * r, 4 * r + 4)))
    m["strip_raw"] = strip
    m["c31"] = c31
    return m


_NC_CACHE = {}


def kernel(x, p, norm_gain, w_in, q_norm_gain, k_norm_gain, rel_bias, hgrn_lb_logits,
           hgrn_out_gain, w_up_a, w_up_b, w_out, w_ple, w_ple_gate):
    args = [np.asarray(a, dtype=np.float32) for a in (
        x, p, norm_gain, w_in, q_norm_gain, k_norm_gain, rel_bias, hgrn_lb_logits,
        hgrn_out_gain, w_up_a, w_up_b, w_out, w_ple, w_ple_gate)]
    x = args[0]
    B, S, _ = x.shape
    if S not in _NC_CACHE:
        _NC_CACHE[S] = build(S)
    nc = _NC_CACHE[S]
    consts = host_consts(S)
    in_maps = [host_inputs(i // 2, i % 2, S, *args, consts) for i in range(2 * B)]
    res = run_bass_kernel_spmd(nc, in_maps, core_ids=list(range(2 * B)))
    SH = S // 2
    out = np.empty((B, S, D), np.float32)
    for i in range(2 * B):
        out[i // 2, (i % 2) * SH:(i % 2 + 1) * SH, :] = res.results[i]["hT_out"].T
    return out
```

```python
import math
from contextlib import ExitStack

import numpy as np
import ml_dtypes
import concourse.bass as bass
import concourse.mybir as mybir
from concourse.bass_utils import run_bass_kernel_spmd

F32 = mybir.dt.float32
BF16 = mybir.dt.bfloat16
AF = mybir.ActivationFunctionType
ALU = mybir.AluOpType
AX = mybir.AxisListType

SEM_CHUNK = 20000
D = 1024
NEG = -30000.0
STRIP = 2432
EPS = 1e-6


class Buf:
    __slots__ = ("name", "last_writer", "dma_writers", "readers", "dma_readers")

    def __init__(self, name):
        self.name = name
        self.clear()

    def clear(self):
        self.last_writer = None
        self.dma_writers = []
        self.readers = {}
        self.dma_readers = []


class Sched:
    def __init__(self, nc, stack):
        self.nc = nc
        self.stack = stack
        self.ops = []
        self.eng = {"pe": nc.tensor, "act": nc.scalar, "dve": nc.vector,
                    "pool": nc.gpsimd, "sp": nc.sync}
        self.nsem = 0
        self.eng_count = {}
        self.eng_sems = {}
        self.dma_pool = []
        self.dma_free = []
        self.key_slot = {}
        self.waited = {}
        self.total = 0

    def new_sem(self, name):
        self.nsem += 1
        return self.stack.enter_context(self.nc.semaphore(name))

    def op(self, eng, fn, reads=(), writes=()):
        self.ops.append(["c", eng, fn, tuple(reads), tuple(writes), None, 1])

    def dma(self, queue, fn, reads=(), writes=(), key=None, inc=16):
        assert key is not None
        self.ops.append(["d", queue, fn, tuple(reads), tuple(writes), key, inc])

    def emit(self):
        ops = self.ops
        n = len(ops)
        deps = [None] * n
        signaling = [False] * n
        last_of_eng = {}
        for i, o in enumerate(ops):
            d = set()
            isd = o[0] == "d"
            for b in o[3]:
                if b.last_writer is not None:
                    d.add(b.last_writer)
                d.update(b.dma_writers)
            for b in o[4]:
                if b.last_writer is not None:
                    d.add(b.last_writer)
                d.update(b.readers.values())
                d.update(b.dma_readers)
                if not isd:
                    d.update(b.dma_writers)
            d.discard(i)
            for b in o[3]:
                if isd:
                    b.dma_readers.append(i)
                else:
                    b.readers[o[1]] = i
            for b in o[4]:
                if isd:
                    b.dma_writers.append(i)
                else:
                    b.last_writer = i
                    b.dma_writers = []
                    b.readers = {}
                    b.dma_readers = []
            deps[i] = d
            for j in d:
                signaling[j] = True
            if o[0] == "c":
                last_of_eng[o[1]] = i
        for e, i in last_of_eng.items():
            signaling[i] = True
        sig = [None] * n
        for i, o in enumerate(ops):
            if o[0] == "c":
                if not signaling[i]:
                    continue
                e = o[1]
                c = self.eng_count.get(e, 0)
                k = c // SEM_CHUNK
                if (e, k) not in self.eng_sems:
                    self.eng_sems[(e, k)] = self.new_sem(f"s_{e}_{k}")
                self.eng_count[e] = c + 1
                sig[i] = (self.eng_sems[(e, k)], c - k * SEM_CHUNK + 1, 1, ("e", e, k))
            else:
                key = o[5]
                if key not in self.key_slot:
                    if self.dma_free:
                        s = self.dma_free.pop()
                    else:
                        self.dma_pool.append([self.new_sem(f"d_{len(self.dma_pool)}"), 0])
                        s = len(self.dma_pool) - 1
                    self.key_slot[key] = s
                s = self.key_slot[key]
                self.dma_pool[s][1] += o[6]
                sig[i] = (self.dma_pool[s][0], self.dma_pool[s][1], o[6], ("k", s))
        waited = self.waited
        for i, o in enumerate(ops):
            e = o[1]
            engine = self.eng[e]
            w = waited.setdefault(e, {})
            for j in sorted(deps[i]):
                pj = ops[j]
                if pj[0] == "c" and pj[1] == "pe" and e == "pe" and o[0] == "c":
                    continue
                sem, val, _, sid = sig[j]
                if w.get(sid, 0) >= val:
                    continue
                w[sid] = val
                engine.wait_ge(sem, val)
            ins = o[2](engine)
            if sig[i] is not None:
                ins.then_inc(sig[i][0], sig[i][2])
        finals = []
        for (e, k), sem in self.eng_sems.items():
            c = self.eng_count.get(e, 0)
            if c // SEM_CHUNK == k and c - k * SEM_CHUNK > 0:
                finals.append((sem, c - k * SEM_CHUNK, ("e", e, k)))
            elif c // SEM_CHUNK > k:
                finals.append((sem, SEM_CHUNK, ("e", e, k)))
        for s, (sem, c) in enumerate(self.dma_pool):
            if c > 0:
                finals.append((sem, c, ("k", s)))
        for e in ("sp", "pe", "act", "dve", "pool"):
            w = waited.setdefault(e, {})
            for sem, val, sid in finals:
                if w.get(sid, 0) >= val:
                    continue
                w[sid] = val
                self.eng[e].wait_ge(sem, val)
        for o in ops:
            for b in o[3] + o[4]:
                b.clear()
        self.key_slot = {}
        self.dma_free = list(range(len(self.dma_pool)))
        self.total += n
        self.ops = []
        return n


class Ring:
    def __init__(self, items):
        self.items = items
        self.i = 0

    def next(self):
        it = self.items[self.i % len(self.items)]
        self.i += 1
        return it


def t5_bucket_np(rel):
    n = np.maximum(rel, 0)
    nf = np.maximum(n, 16).astype(np.float32)
    large = 16 + (np.log(nf / np.float32(16)) / np.float32(math.log(128)) * np.float32(16)).astype(np.int32)
    large = np.minimum(large, 31)
    return np.where(n < 16, n, large)


def build(S, L=2, debug=False, groups=None):
    NT, NB, NG, NCH = S // 128, S // 256, S // 512, S // 64
    SH, NGO = S // 2, S // 1024
    HA, HB = 4, 2
    if groups is None:
        groups = [[0, 1], [2, 3], [4, 5], [6, 7]]
    nc = bass.Bass("TRN2", target_bir_lowering=False)

    def din(name, shape, dt=F32):
        return nc.dram_tensor(name, list(shape), dt, kind="ExternalInput").ap()

    def dscr(name, shape, dt=BF16, out=False):
        kind = "Internal"
        return nc.dram_tensor(name, list(shape), dt, kind=kind).ap()

    xT = din("xT", [D, S])
    xT_own = din("xT_own", [D, SH])
    pT = din("pT", [L, 256, SH])
    w_in = din("w_in", [L, D, 4096])
    w_up_a = din("w_up_a", [L, 512, D])
    w_up_b = din("w_up_b", [L, 512, D])
    w_out = din("w_out", [L, D, D])
    w_ple = din("w_ple", [L, 256, D])
    w_pg = din("w_pg", [L, D, D])
    g_norm = din("g_norm", [128, L, 8])
    g_q = din("g_q", [128, L])
    g_k = din("g_k", [128, L])
    g_o = din("g_o", [128, L, HB])
    lbl = din("lbl", [128, L, HB])
    strip_raw = din("strip_raw", [128, HA, STRIP])
    c31 = din("c31", [128, HA])
    c_ident = din("c_ident", [128, 128], BF16)
    c_blk = din("c_blk", [128, 128], BF16)
    c_onehot = din("c_onehot", [32, S], BF16)
    c_causal = din("c_causal", [64, 64])
    c_scan = din("c_scan", [128, 512])
    c_fut = din("c_fut", [128, NT, 32], BF16)
    c_neg = din("c_neg", [128, NT, 32], BF16)

    hT_out = nc.dram_tensor("hT_out", [D, SH], F32, kind="ExternalOutput").ap()
    h_mid = dscr("h_mid", [D, SH], F32)
    s_qn = dscr("s_qn", [256, S])
    s_kn = dscr("s_kn", [256, S])
    s_v = dscr("s_v", [S, 256])
    s_sag = dscr("s_sag", [256, S])
    s_qd = dscr("s_qd", [256, S])
    s_kd = dscr("s_kd", [256, S])
    s_ke = dscr("s_ke", [256, S])
    s_vb = dscr("s_vb", [S, 256])
    s_sbg = dscr("s_sbg", [256, S])
    s_sga = dscr("s_sga", [D, SH])
    s_sgb = dscr("s_sgb", [D, SH])
    XS = dscr("XS", [4, 128, S], out=True)
    XG = dscr("XG", [4, 2 * 128, S], out=True)
    XO = dscr("XO", [4, 2 * 128, SH])
    PIECE = max(512, SH // 4)
    NPC = SH // PIECE
    HS = dscr("HS", [NPC, D, PIECE])
    HG = dscr("HG", [NPC, 2 * D, PIECE])

    off_own = (nc.sync.partition_id() % 2) * SH

    with ExitStack() as outer:
        Sc = Sched(nc, outer)
        uniq = [0]

        def mk(stack):
            uniq[0] += 1
            tag = f"u{uniq[0]}_"

            def sb(name, shape, dt=F32):
                return stack.enter_context(nc.sbuf_tensor(tag + name, list(shape), dt))

            def ps(name, shape, dt=F32):
                return stack.enter_context(nc.psum_tensor(tag + name, list(shape), dt))
            return sb, ps

        def MM(out, lhsT, rhs, start, stop, r, w):
            Sc.op("pe", lambda e: e.matmul(out, lhsT=lhsT, rhs=rhs, start=start, stop=stop), r, w)

        def ACT(out, in_, func, r, w, bias=0.0, scale=1.0, eng="act"):
            Sc.op(eng, lambda e: e.activation(out=out, in_=in_, func=func, bias=bias, scale=scale), r, w)

        def TT(eng, out, in0, in1, op, r, w):
            Sc.op(eng, lambda e: e.tensor_tensor(out=out, in0=in0, in1=in1, op=op), r, w)

        def TSC(eng, out, in0, s1, s2, op0, op1, r, w):
            if s2 is None:
                Sc.op(eng, lambda e: e.tensor_scalar(out=out, in0=in0, scalar1=s1, scalar2=None, op0=op0), r, w)
            else:
                Sc.op(eng, lambda e: e.tensor_scalar(out=out, in0=in0, scalar1=s1, scalar2=s2, op0=op0, op1=op1), r, w)

        def STT(eng, out, in0, scalar, in1, op0, op1, r, w):
            Sc.op(eng, lambda e: e.scalar_tensor_tensor(out=out, in0=in0, scalar=scalar, in1=in1, op0=op0, op1=op1), r, w)

        def CP(eng, out, in_, r, w):
            if eng == "act":
                Sc.op("act", lambda e: e.activation(out=out, in_=in_, func=AF.Copy), r, w)
            else:
                Sc.op(eng, lambda e: e.tensor_copy(out=out, in_=in_), r, w)

        def LD(out, in_, w, key, r=(), q="sp"):
            Sc.dma(q, lambda e: e.dma_start(out=out, in_=in_), r, w, key)

        def ST(out, in_, r, key, w=(), q="pool"):
            Sc.dma(q, lambda e: e.dma_start(out=out, in_=in_), r, w, key)

        def AG(out, in_, key, r=(), w=()):
            Sc.dma("pool", lambda e: e.collective_compute(
                "AllGather", ALU.bypass, replica_groups=groups, ins=[in_], outs=[out]), r, w, key, inc=1)

        sbP, _ = mk(outer)
        ident = sbP("ident", [128, 128], BF16); b_ident = Buf("ident")
        blk = sbP("blk", [128, 128], BF16); b_blk = Buf("blk")
        ones = sbP("ones", [128, 128], BF16); b_ones = Buf("ones")
        gn = sbP("gn", [128, L, 8]); b_gn = Buf("gn")
        gq = sbP("gq", [128, L]); b_gq = Buf("gq")
        gk = sbP("gk", [128, L]); b_gk = Buf("gk")
        go = sbP("go", [128, L, HB]); b_go = Buf("go")
        lb = sbP("lb", [128, L, HB]); b_lb = Buf("lb")
        oml = sbP("oml", [128, L, HB]); b_oml = Buf("oml")
        lbe = sbP("lbe", [128, L, HB]); b_lbe = Buf("lbe")
        lbs = sbP("lbs", [128, HB]); b_lbs = Buf("lbs")
        KS = sbP("KS", [128, 2, NB]); b_KS = Buf("KS")
        EL = sbP("EL", [128, HB, NCH]); b_EL = Buf("EL")
        caus = sbP("caus", [64, 64]); b_caus = Buf("caus")
        scanm = sbP("scanm", [128, 512]); b_scanm = Buf("scanm")

        with ExitStack() as st:
            sb, ps = mk(st)
            LD(ident[:], c_ident, [b_ident], b_ident)
            LD(blk[:], c_blk, [b_blk], b_blk)
            Sc.op("pool", lambda e: e.memset(ones[:], 1.0), (), [b_ones])
            LD(gn[:], g_norm, [b_gn], b_gn)
            LD(gq[:], g_q, [b_gq], b_gq)
            LD(gk[:], g_k, [b_gk], b_gk)
            LD(go[:], g_o, [b_go], b_go)
            LD(lbe[:], lbl, [b_lbe], b_lbe)
            LD(caus[:], c_causal, [b_caus], b_caus)
            LD(scanm[:], c_scan, [b_scanm], b_scanm)
            TSC("dve", gq[:], gq[:], 0.125, None, ALU.mult, None, [b_gq], [b_gq])
            ACT(lbe[:], lbe[:], AF.Exp, [b_lbe], [b_lbe])
            CP("dve", lbs[:], lbe[:, 0, :], [b_lbe], [b_lbs])
            for l in range(1, L):
                TT("dve", lbs[:], lbs[:], lbe[:, l, :], ALU.add, [b_lbs, b_lbe], [b_lbs])
            Sc.op("dve", lambda e: e.reciprocal(out=lbs[:], in_=lbs[:]), [b_lbs], [b_lbs])
            Sc.op("dve", lambda e: e.memset(lb[:, 0, :], 0.0), (), [b_lb])
            for l in range(1, L):
                TT("dve", lb[:, l, :], lb[:, l - 1, :], lbe[:, l, :], ALU.add, [b_lb, b_lbe], [b_lb])
            for l in range(1, L):
                TT("dve", lb[:, l, :], lb[:, l, :], lbs[:], ALU.mult, [b_lb, b_lbs], [b_lb])
            for l in range(L):
                TSC("dve", oml[:, l, :], lb[:, l, :], -1.0, 1.0, ALU.mult, ALU.add, [b_lb], [b_oml])
            Sc.emit()

        for l in range(L):
            first, last = (l == 0), (l == L - 1)
            h_own = xT_own if first else h_mid
            h_dst = hT_out if last else h_mid

            with ExitStack() as st:
                sb, ps = mk(st)
                Wb = sb("Wb", [128, 8, 4096], BF16)
                b_Wbs = [Buf(f"Wb{i}") for i in range(32)]
                wst = [(sb(f"wst{i}", [128, 8, 128]), Buf(f"wst{i}")) for i in range(2)]
                wv = w_in[l].rearrange("(c p) n -> p c n", p=128)
                for i in range(32):
                    t, b = wst[i % 2]
                    LD(t[:], wv[:, :, i * 128:(i + 1) * 128], [b], b)
                    CP(("act", "dve", "pool")[i % 3], Wb[:, :, i * 128:(i + 1) * 128], t[:], [b], [b_Wbs[i]])
                HDT = F32 if first else BF16
                H = sb("H", [128, 8, 512], HDT); b_H = Buf("H")
                H2 = sb("H2", [128, 8, 512]); b_H2 = Buf("H2")
                SQ = sb("SQ", [128, 8, 512], BF16); b_SQ = Buf("SQ")
                XNs = [(sb(f"XN{i}", [128, 8, 512], BF16), Buf(f"XN{i}")) for i in range(2)]
                rstd = sb("rstd", [128, 512]); b_rstd = Buf("rstd")
                stage = Ring([(sb(f"stg{i}", [128, 512], BF16), Buf(f"stg{i}")) for i in range(6)])
                f32r = Ring([(sb(f"f32r{i}", [128, 512]), Buf(f"f32r{i}")) for i in range(8)])
                QB = [(sb(f"QB{i}", [128, 512]), Buf(f"QB{i}")) for i in range(HB)]
                SG = [(sb(f"SG{i}", [128, 512]), Buf(f"SG{i}")) for i in range(HB)]
                sqq = Ring([(sb(f"sqq{i}", [128, 512], BF16), Buf(f"sqq{i}")) for i in range(2)])
                p_ss = ps("p_ss", [128, 512]); b_pss = Buf("p_ss")
                PS = Ring([(ps(f"PSa{i}", [128, 512]), Buf(f"PSa{i}")) for i in range(5)])
                BS = Ring([(ps(f"BSa{i}", [128, 512]), Buf(f"BSa{i}")) for i in range(2)])
                Sc.op("dve", lambda e: e.memset(KS[:], 0.0), (), [b_KS])

                def load_h(g):
                    if first:
                        LD(H[:], xT.rearrange("(c p) t -> p c t", p=128)[:, :, g * 512:(g + 1) * 512], [b_H], b_H)
                    else:
                        half, tok = g // NGO, (g % NGO) * 512
                        q, col = tok // PIECE, tok % PIECE
                        src = HG[q, half * D:(half + 1) * D, col:col + 512].rearrange("(c p) t -> p c t", p=128)
                        LD(H[:], src, [b_H], b_H)

                def norm_part1(g):
                    load_h(g)
                    ACT(SQ[:].rearrange("p c t -> p (c t)"), H[:].rearrange("p c t -> p (c t)"),
                        AF.Square, [b_H], [b_SQ])

                def norm_part2(Hs, bH, XN, b_XN):
                    for c in range(8):
                        MM(p_ss[:], ones[:], SQ[:, c, :], c == 0, c == 7, [b_ones, b_SQ], [b_pss])
                    ACT(rstd[:], p_ss[:], AF.Ln, [b_pss], [b_rstd], bias=EPS, scale=1.0 / D)
                    ACT(rstd[:], rstd[:], AF.Exp, [b_rstd], [b_rstd], scale=-0.5)
                    for c in range(8):
                        STT("dve", XN[:, c, :], Hs[:, c, :], gn[:, l, c:c + 1], rstd[:],
                            ALU.mult, ALU.mult, [bH, b_gn, b_rstd], [b_XN])

                def decay_chain(g):
                        for jj in range(HB):
                            sg, bsg = SG[jj]
                            f, bf_ = f32r.next()
                            TSC("dve", f[:], sg[:], oml[:, l, jj:jj + 1], lb[:, l, jj:jj + 1], ALU.mult, ALU.add,
                                [bsg, b_oml, b_lb], [bf_])
                            gl, bgl = f32r.next()
                            ACT(gl[:], f[:], AF.Ln, [bf_], [bgl])
                            cum, bcum = f32r.next()
                            Sc.op("dve", lambda e, cum=cum, gl=gl: e.tensor_tensor_scan(
                                out=cum[:], data0=scanm[:], data1=gl[:], initial=0.0,
                                op0=ALU.mult, op1=ALU.add), [bgl, b_scanm], [bcum])
                            ec, bec = f32r.next()
                            ACT(ec[:], cum[:], AF.Exp, [bcum], [bec])
                            ACT(gl[:], cum[:], AF.Exp, [bcum], [bgl], scale=-1.0)
                            CP("pool", EL[:, jj, g * 8:(g + 1) * 8],
                               ec[:].rearrange("p (c t) -> p c t", t=64)[:, :, 63], [bec], [b_EL])
                            o, bo = stage.next()
                            TT("pool", o[:], QB[jj][0][:], ec[:], ALU.mult, [QB[jj][1], bec], [bo])
                            ST(s_qd[jj * 128:(jj + 1) * 128, g * 512:(g + 1) * 512], o[:], [bo], bo)
                            TSC("dve", f[:], f[:], -1.0, 1.0, ALU.mult, ALU.add, [bf_], [bf_])
                            TT("dve", f[:], f[:], gl[:], ALU.mult, [bf_, bgl], [bf_])
                            o, bo = stage.next()
                            CP("act", o[:], f[:], [bf_], [bo])
                            ST(s_kd[jj * 128:(jj + 1) * 128, g * 512:(g + 1) * 512], o[:], [bo], bo)
                            o, bo = stage.next()
                            TT("pool", o[:].rearrange("p (c t) -> p c t", t=64),
                               f[:].rearrange("p (c t) -> p c t", t=64),
                               EL[:, jj, g * 8:(g + 1) * 8].unsqueeze(2).to_broadcast([128, 8, 64]),
                               ALU.mult, [bf_, b_EL], [bo])
                            ST(s_ke[jj * 128:(jj + 1) * 128, g * 512:(g + 1) * 512], o[:], [bo], bo)

                norm_part1(0)
                norm_part2(H, b_H, *XNs[0])
                for g in range(NG):
                    XN, b_XN = XNs[g % 2]

                    def proj_fm(c0):
                        P, bP = PS.next()
                        for c in range(8):
                            MM(P[:], Wb[:, c, c0:c0 + 128], XN[:, c, :], c == 0, c == 7,
                               [b_Wbs[c0 // 128], b_XN], [bP])
                        return P, bP

                    def store_fm(dst, j, t, b):
                        ST(dst[j * 128:(j + 1) * 128, g * 512:(g + 1) * 512], t[:], [b], b)

                    def qk_tail(j, P, bP, s2, bs2):
                        isk = j >= 2
                        Bp, bB = BS.next()
                        MM(Bp[:], blk[:], s2[:], True, True, [b_blk, bs2], [bB])
                        r1, br1 = f32r.next()
                        ACT(r1[:], Bp[:], AF.Ln, [bB], [br1], bias=EPS, scale=1.0 / 64)
                        ACT(r1[:], r1[:], AF.Exp, [br1], [br1], scale=-0.5)
                        o, bo = stage.next()
                        gg = gk if isk else gq
                        STT("dve", o[:], P[:], gg[:, l:l + 1], r1[:], ALU.mult, ALU.mult,
                            [bP, b_gk if isk else b_gq, br1], [bo])
                        if isk:
                            Sc.op("dve", lambda e, o=o, j=j, g=g: e.tensor_reduce(
                                out=KS[:, j - 2, 2 * g:2 * g + 2],
                                in_=o[:].rearrange("p (b t) -> p b t", t=256),
                                axis=AX.X, op=ALU.add), [bo], [b_KS])
                        store_fm(s_kn if isk else s_qn, j % 2, o, bo)

                    pend = None
                    for j in range(4):
                        P, bP = proj_fm(j * 128)
                        s2, bs2 = sqq.next()
                        ACT(s2[:], P[:], AF.Square, [bP], [bs2])
                        if pend is not None:
                            qk_tail(*pend)
                        pend = (j, P, bP, s2, bs2)
                    firstv = True
                    for (c0, dst) in ((512, s_v), (1536, s_vb)):
                        for tt in range(4):
                            P, bP = PS.next()
                            for c in range(8):
                                MM(P[:, 0:256], XN[:, c, tt * 128:(tt + 1) * 128], Wb[:, c, c0:c0 + 256],
                                   c == 0, c == 7, [b_XN, b_Wbs[c0 // 128], b_Wbs[c0 // 128 + 1]], [bP])
                            if firstv:
                                qk_tail(*pend)
                                firstv = False
                            o, bo = stage.next()
                            CP("act", o[:, 0:256], P[:, 0:256], [bP], [bo])
                            ST(dst[g * 512 + tt * 128:g * 512 + (tt + 1) * 128, :], o[:, 0:256], [bo], bo)
                    if g > 0:
                        decay_chain(g - 1)
                    if g + 1 < NG:
                        norm_part1(g + 1)
                    for jj in range(HB):
                        P, bP = proj_fm(1280 + jj * 128)
                        ACT(SG[jj][0][:], P[:], AF.Sigmoid, [bP], [SG[jj][1]])
                    if g + 1 < NG:
                        norm_part2(H, b_H, *XNs[(g + 1) % 2])
                    for (c00, dst) in ((768, s_sag), (1792, s_sbg)):
                        for jj in range(2):
                            P, bP = proj_fm(c00 + jj * 128)
                            o, bo = stage.next()
                            ACT(o[:], P[:], AF.Silu, [bP], [bo])
                            store_fm(dst, jj, o, bo)
                    for jj in range(HB):
                        P, bP = proj_fm(1024 + jj * 128)
                        ACT(QB[jj][0][:], P[:], AF.Silu, [bP], [QB[jj][1]])
                decay_chain(NG - 1)
                hov = h_own.rearrange("(c p) t -> p c t", p=128)
                def gate_norm(g):
                    XN, b_XN = XNs[g % 2]
                    LD(H2[:], hov[:, :, g * 512:(g + 1) * 512], [b_H2], b_H2)
                    ACT(SQ[:].rearrange("p c t -> p (c t)"), H2[:].rearrange("p c t -> p (c t)"),
                        AF.Square, [b_H2], [b_SQ])
                    norm_part2(H2, b_H2, XN, b_XN)

                gate_norm(0)
                for g in range(NGO):
                    XN, b_XN = XNs[g % 2]
                    for jj in range(16):
                        if jj == 6 and g + 1 < NGO:
                            gate_norm(g + 1)
                        P, bP = PS.next()
                        for c in range(8):
                            MM(P[:], Wb[:, c, 2048 + jj * 128:2048 + (jj + 1) * 128], XN[:, c, :], c == 0, c == 7,
                               [b_Wbs[16 + jj], b_XN], [bP])
                        o, bo = stage.next()
                        ACT(o[:], P[:], AF.Sigmoid, [bP], [bo])
                        dst = s_sga if jj < 8 else s_sgb
                        ST(dst[(jj % 8) * 128:(jj % 8 + 1) * 128, g * 512:(g + 1) * 512], o[:], [bo], bo)
                Sc.emit()

            with ExitStack() as st:
                sb, ps = mk(st)
                QAs = [(sb(f"QA{i}", [128, S], BF16), Buf(f"QA{i}")) for i in range(2)]
                KAs = [(sb(f"KA{i}", [128, S], BF16), Buf(f"KA{i}")) for i in range(2)]
                VAs = [(sb(f"VA{i}", [128, NT, 128], BF16), Buf(f"VA{i}")) for i in range(2)]
                SAGr = Ring([(sb(f"SAG{i}", [64, 512], BF16), Buf(f"SAG{i}")) for i in range(3)])
                YGr = Ring([(sb(f"YG{i}", [64, 512], BF16), Buf(f"YG{i}")) for i in range(3)])
                FUT = sb("FUT", [128, NT, 32], BF16); b_FUT = Buf("FUT")
                NEGP = sb("NEGP", [128, NT, 32], BF16); b_NEGP = Buf("NEGP")
                KMh = sb("KMh", [64, 32], BF16); b_KMh = Buf("KMh")
                NTB = min(NT, 16)
                Gs = sb("Gs", [128, NTB, 32]); b_Gs = Buf("Gs")
                thr = sb("thr", [128, NTB, 8]); b_thr = Buf("thr")
                nsel = sb("nsel", [128, NTB, 32]); b_nsel = Buf("nsel")
                MBp = sb("MBp", [128, NTB, 128], BF16); b_MBp = Buf("MBp")
                PT = Ring([(sb(f"PT{i}", [128, 1024], BF16), Buf(f"PT{i}")) for i in range(3)])
                rden = sb("rden", [64, 512]); b_rden = Buf("rden")
                yh = sb("yh", [64, 512]); b_yh = Buf("yh")
                STp = Ring([(ps(f"STp{i}", [128, 1024]), Buf(f"STp{i}")) for i in range(2)])
                Op = Ring([(ps(f"Op{i}", [128, 512]), Buf(f"Op{i}")) for i in range(2)])
                Gp = ps("Gp", [128, 16, 32]); b_Gp = Buf("Gp")
                MTp = ps("MTp", [128, 512]); b_MTp = Buf("MTp")
                LD(FUT[:], c_fut, [b_FUT], b_FUT)
                LD(NEGP[:], c_neg, [b_NEGP], b_NEGP)
                for i in range(2):
                    LD(KAs[i][0][64:96, :], c_onehot, [KAs[i][1]], KAs[i][1])
                    Sc.op("pool", lambda e, i=i: e.memset(VAs[i][0][:, :, 64:128], 1.0), (), [VAs[i][1]])
                Sc.op("pool", lambda e: e.memset(MBp[:], 0.0), (), [b_MBp])
                TS_ = sb("TS", [128, HA, STRIP], BF16); b_TS = Buf("TS")
                c31s = sb("c31s", [128, HA]); b_c31 = Buf("c31s")
                LD(c31s[:], c31, [b_c31], b_c31)
                SPC = STRIP // 4
                stgs = [(sb(f"stripstg{i}", [128, SPC]), Buf(f"stripstg{i}")) for i in range(2)]
                for h in range(HA):
                    for q4 in range(4):
                        stg, b_stg = stgs[(h * 4 + q4) % 2]
                        LD(stg[:], strip_raw[:, h, q4 * SPC:(q4 + 1) * SPC], [b_stg], b_stg)
                        TSC("dve", TS_[:, h, q4 * SPC:(q4 + 1) * SPC], stg[:], c31s[:, h:h + 1], None, ALU.subtract,
                            None, [b_stg, b_c31], [b_TS])

                def head_loads(h):
                    sl = h % 2
                    QA, b_QA = QAs[sl]; KA, b_KA = KAs[sl]; VA, b_VA = VAs[sl]
                    LD(QA[0:64, :], s_qn[h * 64:(h + 1) * 64, :], [b_QA], b_QA)
                    LD(KA[0:64, :], s_kn[h * 64:(h + 1) * 64, :], [b_KA], b_KA)
                    vsrc = s_v[:, h * 64:(h + 1) * 64].rearrange("(t p) d -> p t d", p=128)
                    nvs = max(1, NT // 8)
                    for i in range(0, NT, nvs):
                        LD(VA[:, i:i + nvs, 0:64], vsrc[:, i:i + nvs, :], [b_VA], b_VA)

                def gate_stage(h, tb, stage_):
                    sl = h % 2
                    QA, b_QA = QAs[sl]
                    cq, po = h // 2, 64 * (h % 2)
                    if stage_ == 0:
                        if tb == 0:
                            TSC("dve", KMh[0:64, 0:NB], KS[po:po + 64, cq, :], 1.0 / 256, None, ALU.mult, None,
                                [b_KS], [b_KMh])
                        for t in range(NTB):
                            MM(Gp[:, t, 0:NB], QA[0:64, (tb + t) * 128:(tb + t + 1) * 128], KMh[0:64, 0:NB],
                               True, True, [b_QA, b_KMh], [b_Gp])
                    elif stage_ == 1:
                        if NB < 32:
                            Sc.op("dve", lambda e: e.memset(Gs[:], NEG), (), [b_Gs])
                        TT("dve", Gs[:, :, 0:NB], Gp[:, 0:NTB, 0:NB], FUT[:, tb:tb + NTB, 0:NB], ALU.add,
                           [b_Gp, b_FUT], [b_Gs])
                        for t in range(NTB):
                            Sc.op("dve", lambda e, t=t: e.max(out=thr[:, t, :], in_=Gs[:, t, :]), [b_Gs], [b_thr])
                        TT("dve", nsel[:], Gs[:], thr[:, :, 2:3].to_broadcast([128, NTB, 32]), ALU.is_lt,
                           [b_Gs, b_thr], [b_nsel])
                        TT("pool", MBp[:, :, 64:96], nsel[:], NEGP[:, tb:tb + NTB, :], ALU.mult,
                           [b_nsel, b_NEGP], [b_MBp])
                    else:
                        t4 = (stage_ - 2) * 4
                        for t in range(4):
                            MM(MTp[:, t * 128:(t + 1) * 128], MBp[:, t4 + t, :], ident[:], True, True,
                               [b_MBp, b_ident], [b_MTp])
                        c0 = (tb + t4) * 128
                        CP("act", QA[64:96, c0:c0 + 512], MTp[64:96, :], [b_MTp], [b_QA])

                NST = 2 + NTB // 4
                gate_sched = [(tb, st_) for tb in range(0, NT, NTB) for st_ in range(NST)]

                def head_gate(h):
                    for (tb, st_) in gate_sched:
                        gate_stage(h, tb, st_)

                head_loads(0)
                head_gate(0)
                for h in range(HA):
                    sl = h % 2
                    QA, b_QA = QAs[sl]; KA, b_KA = KAs[sl]; VA, b_VA = VAs[sl]
                    pairs = [(g, kp) for g in range(NG) for kp in range(2 * g + 2)]
                    slots = {}

                    def emit_qk(i):
                        g, kp = pairs[i]
                        Sp, bS = STp.next()
                        for u in range(2):
                            kt = 2 * kp + u
                            delta = 512 * g - 128 * kt
                            near = delta <= 1536
                            MM(Sp[:, u * 512:(u + 1) * 512], KA[0:96, kt * 128:(kt + 1) * 128],
                               QA[0:96, g * 512:(g + 1) * 512], True, not near, [b_KA, b_QA], [bS])
                            if near:
                                MM(Sp[:, u * 512:(u + 1) * 512], ident[:],
                                   TS_[:, h, delta + 384:delta + 384 + 512], False, True, [b_ident, b_TS], [bS])
                        slots[i] = (Sp, bS)

                    emit_qk(0)
                    O, bO = None, None
                    gate_i0 = len(pairs) // 4
                    gate_step = max(1, (len(pairs) - gate_i0 - 2) // len(gate_sched))
                    for i, (g, kp) in enumerate(pairs):
                        npair = 2 * g + 2
                        if kp == 0:
                            O, bO = Op.next()
                            SAG, b_SAG = SAGr.next()
                            LD(SAG[:], s_sag[h * 64:(h + 1) * 64, g * 512:(g + 1) * 512], [b_SAG], b_SAG)
                        if i + 1 < len(pairs):
                            emit_qk(i + 1)
                        if i == 0 and h + 1 < HA:
                            head_loads(h + 1)
                        if h + 1 < HA and i >= gate_i0 and (i - gate_i0) % gate_step == 0:
                            kq = (i - gate_i0) // gate_step
                            if kq < len(gate_sched):
                                gate_stage(h + 1, *gate_sched[kq])
                        Sp, bS = slots.pop(i)
                        P_, bPt = PT.next()
                        ACT(P_[:], Sp[:], AF.Exp, [bS], [bPt])
                        for u in range(2):
                            kt = 2 * kp + u
                            MM(O[:], VA[:, kt, :], P_[:, u * 512:(u + 1) * 512], kt == 0, kt == 2 * npair - 1,
                               [b_VA, bPt], [bO])
                        if kp == npair - 1:
                            Sc.op("dve", lambda e, O=O: e.reciprocal(out=rden[:], in_=O[64:128, :]), [bO], [b_rden])
                            TT("dve", yh[:], O[0:64, :], rden[:], ALU.mult, [bO, b_rden], [b_yh])
                            YG, b_YG = YGr.next()
                            TT("pool", YG[:], yh[:], SAG[:], ALU.mult, [b_yh, b_SAG], [b_YG])
                            ST(XS[h // 2, (h % 2) * 64:(h % 2) * 64 + 64, g * 512:(g + 1) * 512], YG[:], [b_YG], b_YG)
                Sc.emit()

            with ExitStack() as st:
                sb, ps = mk(st)
                QDs = [(sb(f"QD{i}", [128, S], BF16), Buf(f"QD{i}")) for i in range(HB)]
                KDs = [(sb(f"KD{i}", [128, S], BF16), Buf(f"KD{i}")) for i in range(HB)]
                KEr = Ring([(sb(f"KE{i}", [128, 512], BF16), Buf(f"KE{i}")) for i in range(4)])
                VBr = Ring([(sb(f"VB{i}", [64, 8, 128], BF16), Buf(f"VB{i}")) for i in range(4)])
                SBGr = Ring([(sb(f"SBG{i}", [128, 512], BF16), Buf(f"SBG{i}")) for i in range(4)])
                YBr = Ring([(sb(f"YB{i}", [128, 512], BF16), Buf(f"YB{i}")) for i in range(4)])
                KTr = Ring([(sb(f"KT{i}", [64, 8, 128], BF16), Buf(f"KT{i}")) for i in range(4)])
                ATs = Ring([(sb(f"ATs{i}", [64, 64], BF16), Buf(f"ATs{i}")) for i in range(4)])
                Sbs = [[(sb(f"Sb{hh}_{i}", [128, 128], BF16), Buf(f"Sb{hh}_{i}")) for i in range(2)] for hh in range(HB)]
                osq = sb("osq", [128, 512], BF16); b_osq = Buf("osq")
                ort = sb("ort", [128, 512]); b_ort = Buf("ort")
                ors = sb("ors", [128, 512]); b_ors = Buf("ors")
                ybf = sb("ybf", [128, 512]); b_ybf = Buf("ybf")
                KT1 = (ps("KTp", [64, 8, 128], BF16), Buf("KTp"))
                KTp = [KT1] * HB
                OHp = [Ring([(ps(f"OHp{hh}_{i}", [128, 512]), Buf(f"OHp{hh}_{i}")) for i in range(1)]) for hh in range(HB)]
                Abk = [ps(f"Abk{hh}", [128, 512]) for hh in range(HB)]
                dSbk = [ps(f"dSbk{hh}", [128, 512]) for hh in range(HB)]
                ATp = [(Abk[hh][0:64, 0:64], Buf(f"ATp{hh}")) for hh in range(HB)]
                dSp = [(dSbk[hh][:, 0:128], Buf(f"dSp{hh}")) for hh in range(HB)]
                NSp = ps("NSp", [128, 512]); b_NSp = Buf("NSp")
                for k in range(2):
                    AG(XG[k], XS[k], Buf(f"agx{k}"))
                for hh in range(HB):
                    rows = slice(hh * 128, (hh + 1) * 128)
                    LD(QDs[hh][0][:], s_qd[rows, :], [QDs[hh][1]], QDs[hh][1])
                    LD(KDs[hh][0][:], s_kd[rows, :], [KDs[hh][1]], KDs[hh][1])
                    Sc.op("dve", lambda e, hh=hh: e.memset(Sbs[hh][0][0][:], 0.0), (), [Sbs[hh][0][1]])
                cur = [0] * HB

                def group_loads(g):
                    out = []
                    for hh in range(HB):
                        rows = slice(hh * 128, (hh + 1) * 128)
                        KE, bKE = KEr.next()
                        LD(KE[:], s_ke[rows, g * 512:(g + 1) * 512], [bKE], bKE)
                        VB, bVB = VBr.next()
                        vsrc = s_vb[g * 512:(g + 1) * 512, hh * 128:(hh + 1) * 128].rearrange("(c s) v -> s c v", s=64)
                        LD(VB[:], vsrc, [bVB], bVB)
                        SBG, bSBG = SBGr.next()
                        LD(SBG[:], s_sbg[rows, g * 512:(g + 1) * 512], [bSBG], bSBG)
                        out.append((KE, bKE, VB, bVB, SBG, bSBG))
                    return out

                nxt = group_loads(0)
                for g in range(NG):
                    gl_ = nxt
                    if g + 1 < NG:
                        nxt = group_loads(g + 1)
                    KTl, OHl = [], []
                    for hh in range(HB):
                        KE, bKE, VB, bVB, SBG, bSBG = gl_[hh]
                        KTps, bKTp = KTp[hh]
                        for c in range(8):
                            Sc.op("pe", lambda e, KTps=KTps, c=c, KE=KE: e.transpose(
                                out=KTps[:, c, :], in_=KE[:, c * 64:(c + 1) * 64], identity=ident[:]),
                                [bKE, b_ident], [bKTp])
                        KTs, bKT = KTr.next()
                        CP("act", KTs[:], KTps[:], [bKTp], [bKT])
                        KTl.append((KTs, bKT))
                        OHl.append(OHp[hh].next())
                    for c in range(8):
                        ch = g * 8 + c
                        cs = slice(ch * 64, (ch + 1) * 64)
                        Asl = []
                        for hh in range(HB):
                            QD, b_QD = QDs[hh]; KD, b_KD = KDs[hh]
                            A, bA = ATp[hh]
                            MM(A, KD[:, cs], QD[:, cs], True, True, [b_KD, b_QD], [bA])
                            As, bAs = ATs.next()
                            TT("dve", As[:], A, caus[:], ALU.mult, [bA, b_caus], [bAs])
                            Asl.append((As, bAs))
                        for hh in range(HB):
                            KE, bKE, VB, bVB, SBG, bSBG = gl_[hh]
                            QD, b_QD = QDs[hh]
                            KTs, bKT = KTl[hh]; OH, bOH = OHl[hh]
                            As, bAs = Asl[hh]
                            S0, bS0 = Sbs[hh][cur[hh]]
                            S1, bS1 = Sbs[hh][1 - cur[hh]]
                            MM(OH[:, c * 64:(c + 1) * 64], VB[:, c, :], As[:], True, False, [bVB, bAs], [bOH])
                            MM(OH[:, c * 64:(c + 1) * 64], S0[:], QD[:, cs], False, True, [bS0, b_QD], [bOH])
                            dS, bdS = dSp[hh]
                            MM(dS, KTs[:, c, :], VB[:, c, :], True, True, [bKT, bVB], [bdS])
                            STT("dve", S1[:], S0[:], EL[:, hh, ch:ch + 1], dS, ALU.mult, ALU.add,
                                [bS0, b_EL, bdS], [bS1])
                            cur[hh] = 1 - cur[hh]
                    for hh in range(HB):
                        KE, bKE, VB, bVB, SBG, bSBG = gl_[hh]
                        OH, bOH = OHl[hh]
                        ACT(osq[:], OH[:], AF.Square, [bOH], [b_osq])
                        MM(NSp[:], ones[:], osq[:], True, True, [b_ones, b_osq], [b_NSp])
                        ACT(ort[:], NSp[:], AF.Ln, [b_NSp], [b_ort], bias=EPS, scale=1.0 / 128)
                        ACT(ors[:], ort[:], AF.Exp, [b_ort], [b_ors], scale=-0.5)
                        STT("dve", ybf[:], OH[:], go[:, l, hh:hh + 1], ors[:], ALU.mult, ALU.mult,
                            [bOH, b_go, b_ors], [b_ybf])
                        YB, bYB = YBr.next()
                        TT("pool", YB[:], ybf[:], SBG[:], ALU.mult, [b_ybf, bSBG], [bYB])
                        ST(XS[2 + hh, :, g * 512:(g + 1) * 512], YB[:], [bYB], bYB)
                Sc.emit()

            with ExitStack() as st:
                sb, ps = mk(st)
                WA = sb("WA", [128, 4, D], BF16)
                WB_ = sb("WB", [128, 4, D], BF16)
                WO = sb("WO", [128, 8, D], BF16)
                WP = sb("WP", [128, 2, D], BF16)
                WG = sb("WG", [128, 8, D], BF16)
                bW = {}

                def wtok(name, c, j):
                    return bW[(name, (c // 2) * 2, (j // 4) * 512)]
                wst = [(sb(f"wstc{i}", [128, 2, 512]), Buf(f"wstc{i}")) for i in range(2)]
                k = 0
                for (dst, nm, src, nch) in ((WA, "WA", w_up_a[l], 4), (WB_, "WB", w_up_b[l], 4),
                                            (WO, "WO", w_out[l], 8), (WP, "WP", w_ple[l], 2),
                                            (WG, "WG", w_pg[l], 8)):
                    sv = src.rearrange("(c p) n -> p c n", p=128)
                    for c2 in range(0, nch, 2):
                        for n2 in range(0, D, 512):
                            t, b = wst[k % 2]
                            bW[(nm, c2, n2)] = Buf(f"{nm}_{c2}_{n2}")
                            LD(t[:], sv[:, c2:c2 + 2, n2:n2 + 512], [b], b)
                            CP(("act", "dve", "pool")[k % 3], dst[:, c2:c2 + 2, n2:n2 + 512], t[:], [b],
                               [bW[(nm, c2, n2)]])
                            k += 1
                INS = []
                for i in range(2):
                    INS.append(dict(
                        YAG=(sb(f"YAG{i}", [128, 4, 512], BF16), Buf(f"YAG{i}")),
                        YBG=(sb(f"YBG{i}", [128, 4, 512], BF16), Buf(f"YBG{i}")),
                        SGA=(sb(f"SGA{i}", [128, 8, 512], BF16), Buf(f"SGA{i}")),
                        SGB=(sb(f"SGB{i}", [128, 8, 512], BF16), Buf(f"SGB{i}")),
                        PB=(sb(f"PB{i}", [128, 2, 512], BF16), Buf(f"PB{i}"))))
                H = sb("Hc", [128, 8, 512]); b_H = Buf("Hc")
                PF = sb("PF", [128, 2, 512]); b_PF = Buf("PF")
                MG = sb("MG", [128, 8, 512], BF16); b_MG = Buf("MG")
                HM = sb("HM", [128, 8, 512]); b_HM = Buf("HM")
                HMb = sb("HMb", [128, 8, 512], BF16); b_HMb = Buf("HMb")
                HN = Ring([(sb(f"HN{i}", [128, 512]), Buf(f"HN{i}")) for i in range(2)])
                HNb = Ring([(sb(f"HNb{i}", [128, 512], BF16), Buf(f"HNb{i}")) for i in range(2)])
                tmp = Ring([(sb(f"tmp{i}", [128, 512]), Buf(f"tmp{i}")) for i in range(4)])
                PS = Ring([(ps(f"PSc{i}", [128, 512]), Buf(f"PSc{i}")) for i in range(8)])
                hv = h_own.rearrange("(c p) t -> p c t", p=128)
                hd = h_dst.rearrange("(c p) t -> p c t", p=128)
                pv = pT[l].rearrange("(c p) t -> p c t", p=128)
                b_XG = [Buf(f"XG{k}") for k in range(4)]
                for k in (2, 3):
                    AG(XG[k], XS[k], Buf(f"agx{k}"), w=[b_XG[k]])
                NXP = 4 if SH >= 2048 else 1
                XPW = SH // NXP
                b_XOs = [Buf(f"XO{i}") for i in range(NXP)]
                for xp in range(NXP):
                    for kx_ in range(4):
                        LD(XO[kx_, :, xp * XPW:(xp + 1) * XPW], XG[kx_, :, bass.ds(off_own + xp * XPW, XPW)],
                           [b_XOs[xp]], b_XOs[xp], r=[b_XG[kx_]])
                b_HS = [Buf(f"HSp{q}") for q in range(NPC)]

                def loads(g):
                    I = INS[g % 2]
                    ts_ = slice(g * 512, (g + 1) * 512)
                    for c in range(4):
                        rk, kk = c // 2, c % 2
                        bxo = b_XOs[(g * 512) // XPW]
                        LD(I["YAG"][0][:, c, :], XO[kk, rk * 128:(rk + 1) * 128, ts_], [I["YAG"][1]], I["YAG"][1], r=[bxo])
                        LD(I["YBG"][0][:, c, :], XO[2 + kk, rk * 128:(rk + 1) * 128, ts_], [I["YBG"][1]], I["YBG"][1], r=[bxo])
                    LD(I["SGA"][0][:], s_sga.rearrange("(c p) t -> p c t", p=128)[:, :, ts_], [I["SGA"][1]], I["SGA"][1])
                    LD(I["SGB"][0][:], s_sgb.rearrange("(c p) t -> p c t", p=128)[:, :, ts_], [I["SGB"][1]], I["SGB"][1])
                    LD(PF[:], pv[:, :, ts_], [b_PF], b_PF)
                    CP("act", I["PB"][0][:], PF[:], [b_PF], [I["PB"][1]])

                loads(0)
                LD(H[:], hv[:, :, 0:512], [b_H], b_H)
                for g in range(NGO):
                    ts_ = slice(g * 512, (g + 1) * 512)
                    I = INS[g % 2]
                    YAG, b_YAG = I["YAG"]; YBG, b_YBG = I["YBG"]; SGA, b_SGA = I["SGA"]; SGB, b_SGB = I["SGB"]
                    PB, b_PB = I["PB"]
                    if g + 1 < NGO:
                        loads(g + 1)
                    for j in range(8):
                        Pa, bPa = PS.next()
                        for c in range(4):
                            MM(Pa[:], WA[:, c, j * 128:(j + 1) * 128], YAG[:, c, :], c == 0, c == 3,
                               [wtok("WA", c, j), b_YAG], [bPa])
                        Pb, bPb = PS.next()
                        for c in range(4):
                            MM(Pb[:], WB_[:, c, j * 128:(j + 1) * 128], YBG[:, c, :], c == 0, c == 3,
                               [wtok("WB", c, j), b_YBG], [bPb])
                        t1, bt1 = tmp.next()
                        TT("dve", t1[:], Pa[:], SGA[:, j, :], ALU.mult, [bPa, b_SGA], [bt1])
                        t2, bt2 = tmp.next()
                        TT("dve", t2[:], Pb[:], SGB[:, j, :], ALU.mult, [bPb, b_SGB], [bt2])
                        TT("pool", MG[:, j, :], t1[:], t2[:], ALU.add, [bt1, bt2], [b_MG])
                    for j in range(8):
                        Po, bPo = PS.next()
                        for c in range(8):
                            MM(Po[:], WO[:, c, j * 128:(j + 1) * 128], MG[:, c, :], c == 0, c == 7,
                               [wtok("WO", c, j), b_MG], [bPo])
                        TT("dve", HM[:, j, :], Po[:], H[:, j, :], ALU.add, [bPo, b_H], [b_HM])
                        CP("act", HMb[:, j, :], HM[:, j, :], [b_HM], [b_HMb])
                    if g + 1 < NGO:
                        LD(H[:], hv[:, :, (g + 1) * 512:(g + 2) * 512], [b_H], b_H)
                    q, col = (g * 512) // PIECE, (g * 512) % PIECE
                    for j in range(8):
                        Pp, bPp = PS.next()
                        for c in range(2):
                            MM(Pp[:], WP[:, c, j * 128:(j + 1) * 128], PB[:, c, :], c == 0, c == 1,
                               [wtok("WP", c, j), b_PB], [bPp])
                        Pg, bPg = PS.next()
                        for c in range(8):
                            MM(Pg[:], WG[:, c, j * 128:(j + 1) * 128], HMb[:, c, :], c == 0, c == 7,
                               [wtok("WG", c, j), b_HMb], [bPg])
                        sg, bsg = tmp.next()
                        ACT(sg[:], Pg[:], AF.Sigmoid, [bPg], [bsg])
                        t1, bt1 = tmp.next()
                        TT("dve", t1[:], Pp[:], sg[:], ALU.mult, [bPp, bsg], [bt1])
                        hn, bhn = HN.next()
                        TT("pool", hn[:], t1[:], HM[:, j, :], ALU.add, [bt1, b_HM], [bhn])
                        ST(hd[:, j, ts_], hn[:], [bhn], bhn, q="sp")
                        if not last:
                            hb, bhb = HNb.next()
                            CP("act", hb[:], hn[:], [bhn], [bhb])
                            ST(HS[q, j * 128:(j + 1) * 128, col:col + 512], hb[:], [bhb], bhb, w=[b_HS[q]], q="sp")
                    if not last and (g + 1) * 512 % PIECE == 0:
                        AG(HG[q], HS[q], Buf(f"agh{q}"), r=[b_HS[q]])
                Sc.emit()

        print("total ops", Sc.total, "sems", Sc.nsem)
    return nc


def host_consts(S):
    NT = S // 128
    bf = ml_dtypes.bfloat16
    c = {}
    c["c_ident"] = np.eye(128, dtype=np.float32).astype(bf)
    blk = np.zeros((128, 128), np.float32)
    blk[:64, :64] = 1.0
    blk[64:, 64:] = 1.0
    c["c_blk"] = blk.astype(bf)
    oh = np.zeros((32, S), np.float32)
    for n in range(S // 256):
        oh[n, n * 256:(n + 1) * 256] = 1.0
    c["c_onehot"] = oh.astype(bf)
    c["c_causal"] = np.triu(np.ones((64, 64), np.float32))
    sm = np.ones((128, 512), np.float32)
    sm[:, ::64] = 0.0
    c["c_scan"] = sm
    fut = np.zeros((NT, 32), np.float32)
    neg = np.full((NT, 32), NEG, np.float32)
    for t in range(NT):
        b = t // 2
        fut[t, b:] = NEG
        neg[t, b] = 0.0
    c["c_fut"] = np.ascontiguousarray(np.broadcast_to(fut[None], (128, NT, 32))).astype(bf)
    c["c_neg"] = np.ascontiguousarray(np.broadcast_to(neg[None], (128, NT, 32))).astype(bf)
    return c


def host_strips(rel_bias, heads):
    i = np.arange(128)[:, None]
    u = np.arange(STRIP)[None, :]
    rel = u - 384 - i
    bucket = t5_bucket_np(rel)
    strip = np.empty((128, len(heads), STRIP), np.float32)
    for k, h in enumerate(heads):
        g = rel_bias[:, h][bucket]
        strip[:, k, :] = np.where(rel >= 0, g, np.float32(NEG))
    c31 = np.ascontiguousarray(np.broadcast_to(rel_bias[31, heads][None, :], (128, len(heads)))).astype(np.float32)
    return strip, c31


def host_inputs(b, r, S, x, p, norm_gain, w_in, q_norm_gain, k_norm_gain, rel_bias, hgrn_lb_logits,
                hgrn_out_gain, w_up_a, w_up_b, w_out, w_ple, w_ple_gate, consts):
    L = w_in.shape[0]
    SH = S // 2
    m = dict(consts)
    m["xT"] = np.ascontiguousarray(x[b, :S].T)
    m["xT_own"] = np.ascontiguousarray(x[b, r * SH:(r + 1) * SH].T)
    m["pT"] = np.ascontiguousarray(np.transpose(p[:, b, r * SH:(r + 1) * SH, :], (0, 2, 1)))
    cols = []
    for blk0 in range(0, 4096, 512):
        cols.append(np.arange(blk0 + r * 256, blk0 + (r + 1) * 256))
    cols.append(np.arange(4096, 6144))
    cols = np.concatenate(cols)
    m["w_in"] = np.ascontiguousarray(w_in[:, :, cols])
    m["w_up_a"] = w_up_a
    m["w_up_b"] = w_up_b
    m["w_out"] = w_out
    m["w_ple"] = w_ple
    m["w_pg"] = w_ple_gate
    m["g_norm"] = np.ascontiguousarray(np.transpose(norm_gain.reshape(L, 8, 128), (2, 0, 1)))
    m["g_q"] = np.ascontiguousarray(np.concatenate([q_norm_gain, q_norm_gain], axis=1).T)
    m["g_k"] = np.ascontiguousarray(np.concatenate([k_norm_gain, k_norm_gain], axis=1).T)
    m["g_o"] = np.ascontiguousarray(np.transpose(hgrn_out_gain.reshape(L, 4, 128)[:, 2 * r:2 * r + 2], (2, 0, 1)))
    m["lbl"] = np.ascontiguousarray(np.transpose(hgrn_lb_logits.reshape(L, 4, 128)[:, 2 * r:2 * r + 2], (2, 0, 1)))
    strip, c31 = host_strips(rel_bias, list(range(4 * r, 4 * r + 4)))
    m["strip_raw"] = strip
    m["c31"] = c31
    return m


_NC_CACHE = {}


def kernel(x, p, norm_gain, w_in, q_norm_gain, k_norm_gain, rel_bias, hgrn_lb_logits,
           hgrn_out_gain, w_up_a, w_up_b, w_out, w_ple, w_ple_gate):
    args = [np.asarray(a, dtype=np.float32) for a in (
        x, p, norm_gain, w_in, q_norm_gain, k_norm_gain, rel_bias, hgrn_lb_logits,
        hgrn_out_gain, w_up_a, w_up_b, w_out, w_ple, w_ple_gate)]
    x = args[0]
    B, S, _ = x.shape
    if S not in _NC_CACHE:
        _NC_CACHE[S] = build(S)
    nc = _NC_CACHE[S]
    consts = host_consts(S)
    in_maps = [host_inputs(i // 2, i % 2, S, *args, consts) for i in range(2 * B)]
    res = run_bass_kernel_spmd(nc, in_maps, core_ids=list(range(2 * B)))
    SH = S // 2
    out = np.empty((B, S, D), np.float32)
    for i in range(2 * B):
        out[i // 2, (i % 2) * SH:(i % 2 + 1) * SH, :] = res.results[i]["hT_out"].T
    return out
```

```python
import math
from contextlib import ExitStack

import numpy as np
import ml_dtypes
import concourse.bass as bass
import concourse.mybir as mybir
from concourse.bass_utils import run_bass_kernel_spmd

F32 = mybir.dt.float32
BF16 = mybir.dt.bfloat16
AF = mybir.ActivationFunctionType
ALU = mybir.AluOpType
AX = mybir.AxisListType

SEM_CHUNK = 20000
D = 1024
NEG = -30000.0
STRIP = 2432
EPS = 1e-6


class Buf:
    __slots__ = ("name", "last_writer", "dma_writers", "readers", "dma_readers")

    def __init__(self, name):
        self.name = name
        self.clear()

    def clear(self):
        self.last_writer = None
        self.dma_writers = []
        self.readers = {}
        self.dma_readers = []


class Sched:
    def __init__(self, nc, stack):
        self.nc = nc
        self.stack = stack
        self.ops = []
        self.eng = {"pe": nc.tensor, "act": nc.scalar, "dve": nc.vector,
                    "pool": nc.gpsimd, "sp": nc.sync}
        self.nsem = 0
        self.eng_count = {}
        self.eng_sems = {}
        self.dma_pool = []
        self.dma_free = []
        self.key_slot = {}
        self.waited = {}
        self.total = 0

    def new_sem(self, name):
        self.nsem += 1
        return self.stack.enter_context(self.nc.semaphore(name))

    def op(self, eng, fn, reads=(), writes=()):
        self.ops.append(["c", eng, fn, tuple(reads), tuple(writes), None, 1])

    def dma(self, queue, fn, reads=(), writes=(), key=None, inc=16):
        assert key is not None
        self.ops.append(["d", queue, fn, tuple(reads), tuple(writes), key, inc])

    def emit(self):
        ops = self.ops
        n = len(ops)
        deps = [None] * n
        signaling = [False] * n
        last_of_eng = {}
        for i, o in enumerate(ops):
            d = set()
            isd = o[0] == "d"
            for b in o[3]:
                if b.last_writer is not None:
                    d.add(b.last_writer)
                d.update(b.dma_writers)
            for b in o[4]:
                if b.last_writer is not None:
                    d.add(b.last_writer)
                d.update(b.readers.values())
                d.update(b.dma_readers)
                if not isd:
                    d.update(b.dma_writers)
            d.discard(i)
            for b in o[3]:
                if isd:
                    b.dma_readers.append(i)
                else:
                    b.readers[o[1]] = i
            for b in o[4]:
                if isd:
                    b.dma_writers.append(i)
                else:
                    b.last_writer = i
                    b.dma_writers = []
                    b.readers = {}
                    b.dma_readers = []
            deps[i] = d
            for j in d:
                signaling[j] = True
            if o[0] == "c":
                last_of_eng[o[1]] = i
        for e, i in last_of_eng.items():
            signaling[i] = True
        sig = [None] * n
        for i, o in enumerate(ops):
            if o[0] == "c":
                if not signaling[i]:
                    continue
                e = o[1]
                c = self.eng_count.get(e, 0)
                k = c // SEM_CHUNK
                if (e, k) not in self.eng_sems:
                    self.eng_sems[(e, k)] = self.new_sem(f"s_{e}_{k}")
                self.eng_count[e] = c + 1
                sig[i] = (self.eng_sems[(e, k)], c - k * SEM_CHUNK + 1, 1, ("e", e, k))
            else:
                key = o[5]
                if key not in self.key_slot:
                    if self.dma_free:
                        s = self.dma_free.pop()
                    else:
                        self.dma_pool.append([self.new_sem(f"d_{len(self.dma_pool)}"), 0])
                        s = len(self.dma_pool) - 1
                    self.key_slot[key] = s
                s = self.key_slot[key]
                self.dma_pool[s][1] += o[6]
                sig[i] = (self.dma_pool[s][0], self.dma_pool[s][1], o[6], ("k", s))
        waited = self.waited
        for i, o in enumerate(ops):
            e = o[1]
            engine = self.eng[e]
            w = waited.setdefault(e, {})
            for j in sorted(deps[i]):
                pj = ops[j]
                if pj[0] == "c" and pj[1] == "pe" and e == "pe" and o[0] == "c":
                    continue
                sem, val, _, sid = sig[j]
                if w.get(sid, 0) >= val:
                    continue
                w[sid] = val
                engine.wait_ge(sem, val)
            ins = o[2](engine)
            if sig[i] is not None:
                ins.then_inc(sig[i][0], sig[i][2])
        finals = []
        for (e, k), sem in self.eng_sems.items():
            c = self.eng_count.get(e, 0)
            if c // SEM_CHUNK == k and c - k * SEM_CHUNK > 0:
                finals.append((sem, c - k * SEM_CHUNK, ("e", e, k)))
            elif c // SEM_CHUNK > k:
                finals.append((sem, SEM_CHUNK, ("e", e, k)))
        for s, (sem, c) in enumerate(self.dma_pool):
            if c > 0:
                finals.append((sem, c, ("k", s)))
        for e in ("sp", "pe", "act", "dve", "pool"):
            w = waited.setdefault(e, {})
            for sem, val, sid in finals:
                if w.get(sid, 0) >= val:
                    continue
                w[sid] = val
                self.eng[e].wait_ge(sem, val)
        for o in ops:
            for b in o[3] + o[4]:
                b.clear()
        self.key_slot = {}
        self.dma_free = list(range(len(self.dma_pool)))
        self.total += n
        self.ops = []
        return n


class Ring:
    def __init__(self, items):
        self.items = items
        self.i = 0

    def next(self):
        it = self.items[self.i % len(self.items)]
        self.i += 1
        return it


def t5_bucket_np(rel):
    n = np.maximum(rel, 0)
    nf = np.maximum(n, 16).astype(np.float32)
    large = 16 + (np.log(nf / np.float32(16)) / np.float32(math.log(128)) * np.float32(16)).astype(np.int32)
    large = np.minimum(large, 31)
    return np.where(n < 16, n, large)


def build(S, L=2, debug=False, groups=None):
    NT, NB, NG, NCH = S // 128, S // 256, S // 512, S // 64
    SH, NGO = S // 2, S // 1024
    HA, HB = 4, 2
    if groups is None:
        groups = [[0, 1], [2, 3], [4, 5], [6, 7]]
    nc = bass.Bass("TRN2", target_bir_lowering=False)

    def din(name, shape, dt=F32):
        return nc.dram_tensor(name, list(shape), dt, kind="ExternalInput").ap()

    def dscr(name, shape, dt=BF16, out=False):
        kind = "Internal"
        return nc.dram_tensor(name, list(shape), dt, kind=kind).ap()

    xT = din("xT", [D, S])
    xT_own = din("xT_own", [D, SH])
    pT = din("pT", [L, 256, SH])
    w_in = din("w_in", [L, D, 4096])
    w_up_a = din("w_up_a", [L, 512, D])
    w_up_b = din("w_up_b", [L, 512, D])
    w_out = din("w_out", [L, D, D])
    w_ple = din("w_ple", [L, 256, D])
    w_pg = din("w_pg", [L, D, D])
    g_norm = din("g_norm", [128, L, 8])
    g_q = din("g_q", [128, L])
    g_k = din("g_k", [128, L])
    g_o = din("g_o", [128, L, HB])
    lbl = din("lbl", [128, L, HB])
    strip_raw = din("strip_raw", [128, HA, STRIP])
    c31 = din("c31", [128, HA])
    c_ident = din("c_ident", [128, 128], BF16)
    c_blk = din("c_blk", [128, 128], BF16)
    c_onehot = din("c_onehot", [32, S], BF16)
    c_causal = din("c_causal", [64, 64])
    c_scan = din("c_scan", [128, 512])
    c_fut = din("c_fut", [128, NT, 32], BF16)
    c_neg = din("c_neg", [128, NT, 32], BF16)

    hT_out = nc.dram_tensor("hT_out", [D, SH], F32, kind="ExternalOutput").ap()
    h_mid = dscr("h_mid", [D, SH], F32)
    s_qn = dscr("s_qn", [256, S])
    s_kn = dscr("s_kn", [256, S])
    s_v = dscr("s_v", [S, 256])
    s_sag = dscr("s_sag", [256, S])
    s_qd = dscr("s_qd", [256, S])
    s_kd = dscr("s_kd", [256, S])
    s_ke = dscr("s_ke", [256, S])
    s_vb = dscr("s_vb", [S, 256])
    s_sbg = dscr("s_sbg", [256, S])
    s_sga = dscr("s_sga", [D, SH])
    s_sgb = dscr("s_sgb", [D, SH])
    XS = dscr("XS", [4, 128, S], out=True)
    XG = dscr("XG", [4, 2 * 128, S], out=True)
    XO = dscr("XO", [4, 2 * 128, SH])
    PIECE = max(512, SH // 4)
    NPC = SH // PIECE
    HS = dscr("HS", [NPC, D, PIECE])
    HG = dscr("HG", [NPC, 2 * D, PIECE])

    off_own = (nc.sync.partition_id() % 2) * SH

    with ExitStack() as outer:
        Sc = Sched(nc, outer)
        uniq = [0]

        def mk(stack):
            uniq[0] += 1
            tag = f"u{uniq[0]}_"

            def sb(name, shape, dt=F32):
                return stack.enter_context(nc.sbuf_tensor(tag + name, list(shape), dt))

            def ps(name, shape, dt=F32):
                return stack.enter_context(nc.psum_tensor(tag + name, list(shape), dt))
            return sb, ps

        def MM(out, lhsT, rhs, start, stop, r, w):
            Sc.op("pe", lambda e: e.matmul(out, lhsT=lhsT, rhs=rhs, start=start, stop=stop), r, w)

        def ACT(out, in_, func, r, w, bias=0.0, scale=1.0, eng="act"):
            Sc.op(eng, lambda e: e.activation(out=out, in_=in_, func=func, bias=bias, scale=scale), r, w)

        def TT(eng, out, in0, in1, op, r, w):
            Sc.op(eng, lambda e: e.tensor_tensor(out=out, in0=in0, in1=in1, op=op), r, w)

        def TSC(eng, out, in0, s1, s2, op0, op1, r, w):
            if s2 is None:
                Sc.op(eng, lambda e: e.tensor_scalar(out=out, in0=in0, scalar1=s1, scalar2=None, op0=op0), r, w)
            else:
                Sc.op(eng, lambda e: e.tensor_scalar(out=out, in0=in0, scalar1=s1, scalar2=s2, op0=op0, op1=op1), r, w)

        def STT(eng, out, in0, scalar, in1, op0, op1, r, w):
            Sc.op(eng, lambda e: e.scalar_tensor_tensor(out=out, in0=in0, scalar=scalar, in1=in1, op0=op0, op1=op1), r, w)

        def CP(eng, out, in_, r, w):
            if eng == "act":
                Sc.op("act", lambda e: e.activation(out=out, in_=in_, func=AF.Copy), r, w)
            else:
                Sc.op(eng, lambda e: e.tensor_copy(out=out, in_=in_), r, w)

        def LD(out, in_, w, key, r=(), q="sp"):
            Sc.dma(q, lambda e: e.dma_start(out=out, in_=in_), r, w, key)

        def ST(out, in_, r, key, w=(), q="pool"):
            Sc.dma(q, lambda e: e.dma_start(out=out, in_=in_), r, w, key)

        def AG(out, in_, key, r=(), w=()):
            Sc.dma("pool", lambda e: e.collective_compute(
                "AllGather", ALU.bypass, replica_groups=groups, ins=[in_], outs=[out]), r, w, key, inc=1)

        sbP, _ = mk(outer)
        ident = sbP("ident", [128, 128], BF16); b_ident = Buf("ident")
        blk = sbP("blk", [128, 128], BF16); b_blk = Buf("blk")
        ones = sbP("ones", [128, 128], BF16); b_ones = Buf("ones")
        gn = sbP("gn", [128, L, 8]); b_gn = Buf("gn")
        gq = sbP("gq", [128, L]); b_gq = Buf("gq")
        gk = sbP("gk", [128, L]); b_gk = Buf("gk")
        go = sbP("go", [128, L, HB]); b_go = Buf("go")
        lb = sbP("lb", [128, L, HB]); b_lb = Buf("lb")
        oml = sbP("oml", [128, L, HB]); b_oml = Buf("oml")
        lbe = sbP("lbe", [128, L, HB]); b_lbe = Buf("lbe")
        lbs = sbP("lbs", [128, HB]); b_lbs = Buf("lbs")
        KS = sbP("KS", [128, 2, NB]); b_KS = Buf("KS")
        EL = sbP("EL", [128, HB, NCH]); b_EL = Buf("EL")
        caus = sbP("caus", [64, 64]); b_caus = Buf("caus")
        scanm = sbP("scanm", [128, 512]); b_scanm = Buf("scanm")

        with ExitStack() as st:
            sb, ps = mk(st)
            LD(ident[:], c_ident, [b_ident], b_ident)
            LD(blk[:], c_blk, [b_blk], b_blk)
            Sc.op("pool", lambda e: e.memset(ones[:], 1.0), (), [b_ones])
            LD(gn[:], g_norm, [b_gn], b_gn)
            LD(gq[:], g_q, [b_gq], b_gq)
            LD(gk[:], g_k, [b_gk], b_gk)
            LD(go[:], g_o, [b_go], b_go)
            LD(lbe[:], lbl, [b_lbe], b_lbe)
            LD(caus[:], c_causal, [b_caus], b_caus)
            LD(scanm[:], c_scan, [b_scanm], b_scanm)
            TSC("dve", gq[:], gq[:], 0.125, None, ALU.mult, None, [b_gq], [b_gq])
            ACT(lbe[:], lbe[:], AF.Exp, [b_lbe], [b_lbe])
            CP("dve", lbs[:], lbe[:, 0, :], [b_lbe], [b_lbs])
            for l in range(1, L):
                TT("dve", lbs[:], lbs[:], lbe[:, l, :], ALU.add, [b_lbs, b_lbe], [b_lbs])
            Sc.op("dve", lambda e: e.reciprocal(out=lbs[:], in_=lbs[:]), [b_lbs], [b_lbs])
            Sc.op("dve", lambda e: e.memset(lb[:, 0, :], 0.0), (), [b_lb])
            for l in range(1, L):
                TT("dve", lb[:, l, :], lb[:, l - 1, :], lbe[:, l, :], ALU.add, [b_lb, b_lbe], [b_lb])
            for l in range(1, L):
                TT("dve", lb[:, l, :], lb[:, l, :], lbs[:], ALU.mult, [b_lb, b_lbs], [b_lb])
            for l in range(L):
                TSC("dve", oml[:, l, :], lb[:, l, :], -1.0, 1.0, ALU.mult, ALU.add, [b_lb], [b_oml])
            Sc.emit()

        for l in range(L):
            first, last = (l == 0), (l == L - 1)
            h_own = xT_own if first else h_mid
            h_dst = hT_out if last else h_mid

            with ExitStack() as st:
                sb, ps = mk(st)
                Wb = sb("Wb", [128, 8, 4096], BF16); b_Wb = Buf("Wb")
                wst = [(sb(f"wst{i}", [128, 8, 128]), Buf(f"wst{i}")) for i in range(2)]
                wv = w_in[l].rearrange("(c p) n -> p c n", p=128)
                for i in range(32):
                    t, b = wst[i % 2]
                    LD(t[:], wv[:, :, i * 128:(i + 1) * 128], [b], b)
                    CP(("act", "dve", "pool")[i % 3], Wb[:, :, i * 128:(i + 1) * 128], t[:], [b], [b_Wb])
                HDT = F32 if first else BF16
                H = sb("H", [128, 8, 512], HDT); b_H = Buf("H")
                H2 = sb("H2", [128, 8, 512]); b_H2 = Buf("H2")
                SQ = sb("SQ", [128, 8, 512], BF16); b_SQ = Buf("SQ")
                XNs = [(sb(f"XN{i}", [128, 8, 512], BF16), Buf(f"XN{i}")) for i in range(2)]
                rstd = sb("rstd", [128, 512]); b_rstd = Buf("rstd")
                stage = Ring([(sb(f"stg{i}", [128, 512], BF16), Buf(f"stg{i}")) for i in range(6)])
                f32r = Ring([(sb(f"f32r{i}", [128, 512]), Buf(f"f32r{i}")) for i in range(8)])
                QB = [(sb(f"QB{i}", [128, 512]), Buf(f"QB{i}")) for i in range(HB)]
                SG = [(sb(f"SG{i}", [128, 512]), Buf(f"SG{i}")) for i in range(HB)]
                sqq = Ring([(sb(f"sqq{i}", [128, 512], BF16), Buf(f"sqq{i}")) for i in range(2)])
                p_ss = ps("p_ss", [128, 512]); b_pss = Buf("p_ss")
                PS = Ring([(ps(f"PSa{i}", [128, 512]), Buf(f"PSa{i}")) for i in range(5)])
                BS = Ring([(ps(f"BSa{i}", [128, 512]), Buf(f"BSa{i}")) for i in range(2)])
                Sc.op("dve", lambda e: e.memset(KS[:], 0.0), (), [b_KS])

                def load_h(g):
                    if first:
                        LD(H[:], xT.rearrange("(c p) t -> p c t", p=128)[:, :, g * 512:(g + 1) * 512], [b_H], b_H)
                    else:
                        half, tok = g // NGO, (g % NGO) * 512
                        q, col = tok // PIECE, tok % PIECE
                        src = HG[q, half * D:(half + 1) * D, col:col + 512].rearrange("(c p) t -> p c t", p=128)
                        LD(H[:], src, [b_H], b_H)

                def norm_part1(g):
                    load_h(g)
                    ACT(SQ[:].rearrange("p c t -> p (c t)"), H[:].rearrange("p c t -> p (c t)"),
                        AF.Square, [b_H], [b_SQ])

                def norm_part2(Hs, bH, XN, b_XN):
                    for c in range(8):
                        MM(p_ss[:], ones[:], SQ[:, c, :], c == 0, c == 7, [b_ones, b_SQ], [b_pss])
                    ACT(rstd[:], p_ss[:], AF.Ln, [b_pss], [b_rstd], bias=EPS, scale=1.0 / D)
                    ACT(rstd[:], rstd[:], AF.Exp, [b_rstd], [b_rstd], scale=-0.5)
                    for c in range(8):
                        STT("dve", XN[:, c, :], Hs[:, c, :], gn[:, l, c:c + 1], rstd[:],
                            ALU.mult, ALU.mult, [bH, b_gn, b_rstd], [b_XN])

                norm_part1(0)
                norm_part2(H, b_H, *XNs[0])
                for g in range(NG):
                    XN, b_XN = XNs[g % 2]

                    def proj_fm(c0):
                        P, bP = PS.next()
                        for c in range(8):
                            MM(P[:], Wb[:, c, c0:c0 + 128], XN[:, c, :], c == 0, c == 7, [b_Wb, b_XN], [bP])
                        return P, bP

                    def store_fm(dst, j, t, b):
                        ST(dst[j * 128:(j + 1) * 128, g * 512:(g + 1) * 512], t[:], [b], b)

                    def qk_tail(j, P, bP, s2, bs2):
                        isk = j >= 2
                        Bp, bB = BS.next()
                        MM(Bp[:], blk[:], s2[:], True, True, [b_blk, bs2], [bB])
                        r1, br1 = f32r.next()
                        ACT(r1[:], Bp[:], AF.Ln, [bB], [br1], bias=EPS, scale=1.0 / 64)
                        ACT(r1[:], r1[:], AF.Exp, [br1], [br1], scale=-0.5)
                        o, bo = stage.next()
                        gg = gk if isk else gq
                        STT("dve", o[:], P[:], gg[:, l:l + 1], r1[:], ALU.mult, ALU.mult,
                            [bP, b_gk if isk else b_gq, br1], [bo])
                        if isk:
                            Sc.op("dve", lambda e, o=o, j=j, g=g: e.tensor_reduce(
                                out=KS[:, j - 2, 2 * g:2 * g + 2],
                                in_=o[:].rearrange("p (b t) -> p b t", t=256),
                                axis=AX.X, op=ALU.add), [bo], [b_KS])
                        store_fm(s_kn if isk else s_qn, j % 2, o, bo)

                    pend = None
                    for j in range(4):
                        P, bP = proj_fm(j * 128)
                        s2, bs2 = sqq.next()
                        ACT(s2[:], P[:], AF.Square, [bP], [bs2])
                        if pend is not None:
                            qk_tail(*pend)
                        pend = (j, P, bP, s2, bs2)
                    firstv = True
                    for (c0, dst) in ((512, s_v), (1536, s_vb)):
                        for tt in range(4):
                            P, bP = PS.next()
                            for c in range(8):
                                MM(P[:, 0:256], XN[:, c, tt * 128:(tt + 1) * 128], Wb[:, c, c0:c0 + 256],
                                   c == 0, c == 7, [b_XN, b_Wb], [bP])
                            if firstv:
                                qk_tail(*pend)
                                firstv = False
                            o, bo = stage.next()
                            CP("dve", o[:, 0:256], P[:, 0:256], [bP], [bo])
                            ST(dst[g * 512 + tt * 128:g * 512 + (tt + 1) * 128, :], o[:, 0:256], [bo], bo)
                    if g + 1 < NG:
                        norm_part1(g + 1)
                    for jj in range(HB):
                        P, bP = proj_fm(1280 + jj * 128)
                        ACT(SG[jj][0][:], P[:], AF.Sigmoid, [bP], [SG[jj][1]])
                    if g + 1 < NG:
                        norm_part2(H, b_H, *XNs[(g + 1) % 2])
                    for (c00, dst) in ((768, s_sag), (1792, s_sbg)):
                        for jj in range(2):
                            P, bP = proj_fm(c00 + jj * 128)
                            o, bo = stage.next()
                            ACT(o[:], P[:], AF.Silu, [bP], [bo])
                            store_fm(dst, jj, o, bo)
                    for jj in range(HB):
                        P, bP = proj_fm(1024 + jj * 128)
                        ACT(QB[jj][0][:], P[:], AF.Silu, [bP], [QB[jj][1]])
                    for jj in range(HB):
                        sg, bsg = SG[jj]
                        f, bf_ = f32r.next()
                        TSC("dve", f[:], sg[:], oml[:, l, jj:jj + 1], lb[:, l, jj:jj + 1], ALU.mult, ALU.add,
                            [bsg, b_oml, b_lb], [bf_])
                        gl, bgl = f32r.next()
                        ACT(gl[:], f[:], AF.Ln, [bf_], [bgl])
                        cum, bcum = f32r.next()
                        Sc.op("dve", lambda e, cum=cum, gl=gl: e.tensor_tensor_scan(
                            out=cum[:], data0=scanm[:], data1=gl[:], initial=0.0,
                            op0=ALU.mult, op1=ALU.add), [bgl, b_scanm], [bcum])
                        ec, bec = f32r.next()
                        ACT(ec[:], cum[:], AF.Exp, [bcum], [bec])
                        ACT(gl[:], cum[:], AF.Exp, [bcum], [bgl], scale=-1.0)
                        CP("pool", EL[:, jj, g * 8:(g + 1) * 8],
                           ec[:].rearrange("p (c t) -> p c t", t=64)[:, :, 63], [bec], [b_EL])
                        o, bo = stage.next()
                        TT("pool", o[:], QB[jj][0][:], ec[:], ALU.mult, [QB[jj][1], bec], [bo])
                        store_fm(s_qd, jj, o, bo)
                        TSC("dve", f[:], f[:], -1.0, 1.0, ALU.mult, ALU.add, [bf_], [bf_])
                        TT("dve", f[:], f[:], gl[:], ALU.mult, [bf_, bgl], [bf_])
                        o, bo = stage.next()
                        CP("pool", o[:], f[:], [bf_], [bo])
                        store_fm(s_kd, jj, o, bo)
                        o, bo = stage.next()
                        TT("pool", o[:].rearrange("p (c t) -> p c t", t=64),
                           f[:].rearrange("p (c t) -> p c t", t=64),
                           EL[:, jj, g * 8:(g + 1) * 8].unsqueeze(2).to_broadcast([128, 8, 64]),
                           ALU.mult, [bf_, b_EL], [bo])
                        store_fm(s_ke, jj, o, bo)
                hov = h_own.rearrange("(c p) t -> p c t", p=128)
                def gate_norm(g):
                    XN, b_XN = XNs[g % 2]
                    LD(H2[:], hov[:, :, g * 512:(g + 1) * 512], [b_H2], b_H2)
                    ACT(SQ[:].rearrange("p c t -> p (c t)"), H2[:].rearrange("p c t -> p (c t)"),
                        AF.Square, [b_H2], [b_SQ])
                    norm_part2(H2, b_H2, XN, b_XN)

                gate_norm(0)
                for g in range(NGO):
                    XN, b_XN = XNs[g % 2]
                    for jj in range(16):
                        if jj == 6 and g + 1 < NGO:
                            gate_norm(g + 1)
                        P, bP = PS.next()
                        for c in range(8):
                            MM(P[:], Wb[:, c, 2048 + jj * 128:2048 + (jj + 1) * 128], XN[:, c, :], c == 0, c == 7,
                               [b_Wb, b_XN], [bP])
                        o, bo = stage.next()
                        ACT(o[:], P[:], AF.Sigmoid, [bP], [bo])
                        dst = s_sga if jj < 8 else s_sgb
                        ST(dst[(jj % 8) * 128:(jj % 8 + 1) * 128, g * 512:(g + 1) * 512], o[:], [bo], bo)
                Sc.emit()

            with ExitStack() as st:
                sb, ps = mk(st)
                QAs = [(sb(f"QA{i}", [128, S], BF16), Buf(f"QA{i}")) for i in range(2)]
                KAs = [(sb(f"KA{i}", [128, S], BF16), Buf(f"KA{i}")) for i in range(2)]
                VAs = [(sb(f"VA{i}", [128, NT, 128], BF16), Buf(f"VA{i}")) for i in range(2)]
                SAGr = Ring([(sb(f"SAG{i}", [64, 512], BF16), Buf(f"SAG{i}")) for i in range(3)])
                YGr = Ring([(sb(f"YG{i}", [64, 512], BF16), Buf(f"YG{i}")) for i in range(3)])
                FUT = sb("FUT", [128, NT, 32], BF16); b_FUT = Buf("FUT")
                NEGP = sb("NEGP", [128, NT, 32], BF16); b_NEGP = Buf("NEGP")
                KMh = sb("KMh", [64, 32], BF16); b_KMh = Buf("KMh")
                NTB = min(NT, 16)
                Gs = sb("Gs", [128, NTB, 32]); b_Gs = Buf("Gs")
                thr = sb("thr", [128, NTB, 8]); b_thr = Buf("thr")
                nsel = sb("nsel", [128, NTB, 32]); b_nsel = Buf("nsel")
                MBp = sb("MBp", [128, NTB, 128], BF16); b_MBp = Buf("MBp")
                PT = Ring([(sb(f"PT{i}", [128, 1024], BF16), Buf(f"PT{i}")) for i in range(3)])
                rden = sb("rden", [64, 512]); b_rden = Buf("rden")
                yh = sb("yh", [64, 512]); b_yh = Buf("yh")
                STp = Ring([(ps(f"STp{i}", [128, 1024]), Buf(f"STp{i}")) for i in range(2)])
                Op = Ring([(ps(f"Op{i}", [128, 512]), Buf(f"Op{i}")) for i in range(2)])
                Gp = ps("Gp", [128, 16, 32]); b_Gp = Buf("Gp")
                MTp = ps("MTp", [128, 512]); b_MTp = Buf("MTp")
                LD(FUT[:], c_fut, [b_FUT], b_FUT)
                LD(NEGP[:], c_neg, [b_NEGP], b_NEGP)
                for i in range(2):
                    LD(KAs[i][0][64:96, :], c_onehot, [KAs[i][1]], KAs[i][1])
                    Sc.op("pool", lambda e, i=i: e.memset(VAs[i][0][:, :, 64:128], 1.0), (), [VAs[i][1]])
                Sc.op("pool", lambda e: e.memset(MBp[:], 0.0), (), [b_MBp])
                TS_ = sb("TS", [128, HA, STRIP], BF16); b_TS = Buf("TS")
                c31s = sb("c31s", [128, HA]); b_c31 = Buf("c31s")
                LD(c31s[:], c31, [b_c31], b_c31)
                SPC = STRIP // 4
                stgs = [(sb(f"stripstg{i}", [128, SPC]), Buf(f"stripstg{i}")) for i in range(2)]
                for h in range(HA):
                    for q4 in range(4):
                        stg, b_stg = stgs[(h * 4 + q4) % 2]
                        LD(stg[:], strip_raw[:, h, q4 * SPC:(q4 + 1) * SPC], [b_stg], b_stg)
                        TSC("dve", TS_[:, h, q4 * SPC:(q4 + 1) * SPC], stg[:], c31s[:, h:h + 1], None, ALU.subtract,
                            None, [b_stg, b_c31], [b_TS])

                def head_loads(h):
                    sl = h % 2
                    QA, b_QA = QAs[sl]; KA, b_KA = KAs[sl]; VA, b_VA = VAs[sl]
                    LD(QA[0:64, :], s_qn[h * 64:(h + 1) * 64, :], [b_QA], b_QA)
                    LD(KA[0:64, :], s_kn[h * 64:(h + 1) * 64, :], [b_KA], b_KA)
                    vsrc = s_v[:, h * 64:(h + 1) * 64].rearrange("(t p) d -> p t d", p=128)
                    nvs = max(1, NT // 8)
                    for i in range(0, NT, nvs):
                        LD(VA[:, i:i + nvs, 0:64], vsrc[:, i:i + nvs, :], [b_VA], b_VA)

                def gate_stage(h, tb, stage_):
                    sl = h % 2
                    QA, b_QA = QAs[sl]
                    cq, po = h // 2, 64 * (h % 2)
                    if stage_ == 0:
                        if tb == 0:
                            TSC("dve", KMh[0:64, 0:NB], KS[po:po + 64, cq, :], 1.0 / 256, None, ALU.mult, None,
                                [b_KS], [b_KMh])
                        for t in range(NTB):
                            MM(Gp[:, t, 0:NB], QA[0:64, (tb + t) * 128:(tb + t + 1) * 128], KMh[0:64, 0:NB],
                               True, True, [b_QA, b_KMh], [b_Gp])
                    elif stage_ == 1:
                        if NB < 32:
                            Sc.op("dve", lambda e: e.memset(Gs[:], NEG), (), [b_Gs])
                        TT("dve", Gs[:, :, 0:NB], Gp[:, 0:NTB, 0:NB], FUT[:, tb:tb + NTB, 0:NB], ALU.add,
                           [b_Gp, b_FUT], [b_Gs])
                        for t in range(NTB):
                            Sc.op("dve", lambda e, t=t: e.max(out=thr[:, t, :], in_=Gs[:, t, :]), [b_Gs], [b_thr])
                        TT("dve", nsel[:], Gs[:], thr[:, :, 2:3].to_broadcast([128, NTB, 32]), ALU.is_lt,
                           [b_Gs, b_thr], [b_nsel])
                        TT("pool", MBp[:, :, 64:96], nsel[:], NEGP[:, tb:tb + NTB, :], ALU.mult,
                           [b_nsel, b_NEGP], [b_MBp])
                    else:
                        t4 = (stage_ - 2) * 4
                        for t in range(4):
                            MM(MTp[:, t * 128:(t + 1) * 128], MBp[:, t4 + t, :], ident[:], True, True,
                               [b_MBp, b_ident], [b_MTp])
                        c0 = (tb + t4) * 128
                        CP("act", QA[64:96, c0:c0 + 512], MTp[64:96, :], [b_MTp], [b_QA])

                NST = 2 + NTB // 4
                gate_sched = [(tb, st_) for tb in range(0, NT, NTB) for st_ in range(NST)]

                def head_gate(h):
                    for (tb, st_) in gate_sched:
                        gate_stage(h, tb, st_)

                head_loads(0)
                head_gate(0)
                for h in range(HA):
                    sl = h % 2
                    QA, b_QA = QAs[sl]; KA, b_KA = KAs[sl]; VA, b_VA = VAs[sl]
                    pairs = [(g, kp) for g in range(NG) for kp in range(2 * g + 2)]
                    slots = {}

                    def emit_qk(i):
                        g, kp = pairs[i]
                        Sp, bS = STp.next()
                        for u in range(2):
                            kt = 2 * kp + u
                            delta = 512 * g - 128 * kt
                            near = delta <= 1536
                            MM(Sp[:, u * 512:(u + 1) * 512], KA[0:96, kt * 128:(kt + 1) * 128],
                               QA[0:96, g * 512:(g + 1) * 512], True, not near, [b_KA, b_QA], [bS])
                            if near:
                                MM(Sp[:, u * 512:(u + 1) * 512], ident[:],
                                   TS_[:, h, delta + 384:delta + 384 + 512], False, True, [b_ident, b_TS], [bS])
                        slots[i] = (Sp, bS)

                    emit_qk(0)
                    O, bO = None, None
                    gate_i0 = len(pairs) // 4
                    gate_step = max(1, (len(pairs) - gate_i0 - 2) // len(gate_sched))
                    for i, (g, kp) in enumerate(pairs):
                        npair = 2 * g + 2
                        if kp == 0:
                            O, bO = Op.next()
                            SAG, b_SAG = SAGr.next()
                            LD(SAG[:], s_sag[h * 64:(h + 1) * 64, g * 512:(g + 1) * 512], [b_SAG], b_SAG)
                        if i + 1 < len(pairs):
                            emit_qk(i + 1)
                        if i == 0 and h + 1 < HA:
                            head_loads(h + 1)
                        if h + 1 < HA and i >= gate_i0 and (i - gate_i0) % gate_step == 0:
                            kq = (i - gate_i0) // gate_step
                            if kq < len(gate_sched):
                                gate_stage(h + 1, *gate_sched[kq])
                        Sp, bS = slots.pop(i)
                        P_, bPt = PT.next()
                        ACT(P_[:], Sp[:], AF.Exp, [bS], [bPt])
                        for u in range(2):
                            kt = 2 * kp + u
                            MM(O[:], VA[:, kt, :], P_[:, u * 512:(u + 1) * 512], kt == 0, kt == 2 * npair - 1,
                               [b_VA, bPt], [bO])
                        if kp == npair - 1:
                            Sc.op("dve", lambda e, O=O: e.reciprocal(out=rden[:], in_=O[64:128, :]), [bO], [b_rden])
                            TT("dve", yh[:], O[0:64, :], rden[:], ALU.mult, [bO, b_rden], [b_yh])
                            YG, b_YG = YGr.next()
                            TT("pool", YG[:], yh[:], SAG[:], ALU.mult, [b_yh, b_SAG], [b_YG])
                            ST(XS[h // 2, (h % 2) * 64:(h % 2) * 64 + 64, g * 512:(g + 1) * 512], YG[:], [b_YG], b_YG)
                Sc.emit()

            with ExitStack() as st:
                sb, ps = mk(st)
                QDs = [(sb(f"QD{i}", [128, S], BF16), Buf(f"QD{i}")) for i in range(HB)]
                KDs = [(sb(f"KD{i}", [128, S], BF16), Buf(f"KD{i}")) for i in range(HB)]
                KEr = Ring([(sb(f"KE{i}", [128, 512], BF16), Buf(f"KE{i}")) for i in range(4)])
                VBr = Ring([(sb(f"VB{i}", [64, 8, 128], BF16), Buf(f"VB{i}")) for i in range(4)])
                SBGr = Ring([(sb(f"SBG{i}", [128, 512], BF16), Buf(f"SBG{i}")) for i in range(4)])
                YBr = Ring([(sb(f"YB{i}", [128, 512], BF16), Buf(f"YB{i}")) for i in range(4)])
                KTr = Ring([(sb(f"KT{i}", [64, 8, 128], BF16), Buf(f"KT{i}")) for i in range(4)])
                ATs = Ring([(sb(f"ATs{i}", [64, 64], BF16), Buf(f"ATs{i}")) for i in range(4)])
                Sbs = [[(sb(f"Sb{hh}_{i}", [128, 128], BF16), Buf(f"Sb{hh}_{i}")) for i in range(2)] for hh in range(HB)]
                osq = sb("osq", [128, 512], BF16); b_osq = Buf("osq")
                ort = sb("ort", [128, 512]); b_ort = Buf("ort")
                ors = sb("ors", [128, 512]); b_ors = Buf("ors")
                ybf = sb("ybf", [128, 512]); b_ybf = Buf("ybf")
                KT1 = (ps("KTp", [64, 8, 128], BF16), Buf("KTp"))
                KTp = [KT1] * HB
                OHp = [Ring([(ps(f"OHp{hh}_{i}", [128, 512]), Buf(f"OHp{hh}_{i}")) for i in range(1)]) for hh in range(HB)]
                Abk = [ps(f"Abk{hh}", [128, 512]) for hh in range(HB)]
                dSbk = [ps(f"dSbk{hh}", [128, 512]) for hh in range(HB)]
                ATp = [(Abk[hh][0:64, 0:64], Buf(f"ATp{hh}")) for hh in range(HB)]
                dSp = [(dSbk[hh][:, 0:128], Buf(f"dSp{hh}")) for hh in range(HB)]
                NSp = ps("NSp", [128, 512]); b_NSp = Buf("NSp")
                for k in range(2):
                    AG(XG[k], XS[k], Buf(f"agx{k}"))
                for hh in range(HB):
                    rows = slice(hh * 128, (hh + 1) * 128)
                    LD(QDs[hh][0][:], s_qd[rows, :], [QDs[hh][1]], QDs[hh][1])
                    LD(KDs[hh][0][:], s_kd[rows, :], [KDs[hh][1]], KDs[hh][1])
                    Sc.op("dve", lambda e, hh=hh: e.memset(Sbs[hh][0][0][:], 0.0), (), [Sbs[hh][0][1]])
                cur = [0] * HB

                def group_loads(g):
                    out = []
                    for hh in range(HB):
                        rows = slice(hh * 128, (hh + 1) * 128)
                        KE, bKE = KEr.next()
                        LD(KE[:], s_ke[rows, g * 512:(g + 1) * 512], [bKE], bKE)
                        VB, bVB = VBr.next()
                        vsrc = s_vb[g * 512:(g + 1) * 512, hh * 128:(hh + 1) * 128].rearrange("(c s) v -> s c v", s=64)
                        LD(VB[:], vsrc, [bVB], bVB)
                        SBG, bSBG = SBGr.next()
                        LD(SBG[:], s_sbg[rows, g * 512:(g + 1) * 512], [bSBG], bSBG)
                        out.append((KE, bKE, VB, bVB, SBG, bSBG))
                    return out

                nxt = group_loads(0)
                for g in range(NG):
                    gl_ = nxt
                    if g + 1 < NG:
                        nxt = group_loads(g + 1)
                    KTl, OHl = [], []
                    for hh in range(HB):
                        KE, bKE, VB, bVB, SBG, bSBG = gl_[hh]
                        KTps, bKTp = KTp[hh]
                        for c in range(8):
                            Sc.op("pe", lambda e, KTps=KTps, c=c, KE=KE: e.transpose(
                                out=KTps[:, c, :], in_=KE[:, c * 64:(c + 1) * 64], identity=ident[:]),
                                [bKE, b_ident], [bKTp])
                        KTs, bKT = KTr.next()
                        CP("act", KTs[:], KTps[:], [bKTp], [bKT])
                        KTl.append((KTs, bKT))
                        OHl.append(OHp[hh].next())
                    for c in range(8):
                        ch = g * 8 + c
                        cs = slice(ch * 64, (ch + 1) * 64)
                        Asl = []
                        for hh in range(HB):
                            QD, b_QD = QDs[hh]; KD, b_KD = KDs[hh]
                            A, bA = ATp[hh]
                            MM(A, KD[:, cs], QD[:, cs], True, True, [b_KD, b_QD], [bA])
                            As, bAs = ATs.next()
                            TT("dve", As[:], A, caus[:], ALU.mult, [bA, b_caus], [bAs])
                            Asl.append((As, bAs))
                        for hh in range(HB):
                            KE, bKE, VB, bVB, SBG, bSBG = gl_[hh]
                            QD, b_QD = QDs[hh]
                            KTs, bKT = KTl[hh]; OH, bOH = OHl[hh]
                            As, bAs = Asl[hh]
                            S0, bS0 = Sbs[hh][cur[hh]]
                            S1, bS1 = Sbs[hh][1 - cur[hh]]
                            MM(OH[:, c * 64:(c + 1) * 64], VB[:, c, :], As[:], True, False, [bVB, bAs], [bOH])
                            MM(OH[:, c * 64:(c + 1) * 64], S0[:], QD[:, cs], False, True, [bS0, b_QD], [bOH])
                            dS, bdS = dSp[hh]
                            MM(dS, KTs[:, c, :], VB[:, c, :], True, True, [bKT, bVB], [bdS])
                            STT("dve", S1[:], S0[:], EL[:, hh, ch:ch + 1], dS, ALU.mult, ALU.add,
                                [bS0, b_EL, bdS], [bS1])
                            cur[hh] = 1 - cur[hh]
                    for hh in range(HB):
                        KE, bKE, VB, bVB, SBG, bSBG = gl_[hh]
                        OH, bOH = OHl[hh]
                        ACT(osq[:], OH[:], AF.Square, [bOH], [b_osq])
                        MM(NSp[:], ones[:], osq[:], True, True, [b_ones, b_osq], [b_NSp])
                        ACT(ort[:], NSp[:], AF.Ln, [b_NSp], [b_ort], bias=EPS, scale=1.0 / 128)
                        ACT(ors[:], ort[:], AF.Exp, [b_ort], [b_ors], scale=-0.5)
                        STT("dve", ybf[:], OH[:], go[:, l, hh:hh + 1], ors[:], ALU.mult, ALU.mult,
                            [bOH, b_go, b_ors], [b_ybf])
                        YB, bYB = YBr.next()
                        TT("pool", YB[:], ybf[:], SBG[:], ALU.mult, [b_ybf, bSBG], [bYB])
                        ST(XS[2 + hh, :, g * 512:(g + 1) * 512], YB[:], [bYB], bYB)
                Sc.emit()

            with ExitStack() as st:
                sb, ps = mk(st)
                WA = sb("WA", [128, 4, D], BF16); b_WA = Buf("WA")
                WB_ = sb("WB", [128, 4, D], BF16); b_WB = Buf("WB")
                WO = sb("WO", [128, 8, D], BF16); b_WO = Buf("WO")
                WP = sb("WP", [128, 2, D], BF16); b_WP = Buf("WP")
                WG = sb("WG", [128, 8, D], BF16); b_WG = Buf("WG")
                wst = [(sb(f"wstc{i}", [128, 2, 512]), Buf(f"wstc{i}")) for i in range(2)]
                k = 0
                for (dst, bd, src, nch) in ((WA, b_WA, w_up_a[l], 4), (WB_, b_WB, w_up_b[l], 4),
                                            (WO, b_WO, w_out[l], 8), (WP, b_WP, w_ple[l], 2),
                                            (WG, b_WG, w_pg[l], 8)):
                    sv = src.rearrange("(c p) n -> p c n", p=128)
                    for c2 in range(0, nch, 2):
                        for n2 in range(0, D, 512):
                            t, b = wst[k % 2]
                            LD(t[:], sv[:, c2:c2 + 2, n2:n2 + 512], [b], b)
                            CP(("act", "dve", "pool")[k % 3], dst[:, c2:c2 + 2, n2:n2 + 512], t[:], [b], [bd])
                            k += 1
                INS = []
                for i in range(2):
                    INS.append(dict(
                        YAG=(sb(f"YAG{i}", [128, 4, 512], BF16), Buf(f"YAG{i}")),
                        YBG=(sb(f"YBG{i}", [128, 4, 512], BF16), Buf(f"YBG{i}")),
                        SGA=(sb(f"SGA{i}", [128, 8, 512], BF16), Buf(f"SGA{i}")),
                        SGB=(sb(f"SGB{i}", [128, 8, 512], BF16), Buf(f"SGB{i}")),
                        PB=(sb(f"PB{i}", [128, 2, 512], BF16), Buf(f"PB{i}"))))
                H = sb("Hc", [128, 8, 512]); b_H = Buf("Hc")
                PF = sb("PF", [128, 2, 512]); b_PF = Buf("PF")
                MG = sb("MG", [128, 8, 512], BF16); b_MG = Buf("MG")
                HM = sb("HM", [128, 8, 512]); b_HM = Buf("HM")
                HMb = sb("HMb", [128, 8, 512], BF16); b_HMb = Buf("HMb")
                HN = Ring([(sb(f"HN{i}", [128, 512]), Buf(f"HN{i}")) for i in range(2)])
                HNb = Ring([(sb(f"HNb{i}", [128, 512], BF16), Buf(f"HNb{i}")) for i in range(2)])
                tmp = Ring([(sb(f"tmp{i}", [128, 512]), Buf(f"tmp{i}")) for i in range(4)])
                PS = Ring([(ps(f"PSc{i}", [128, 512]), Buf(f"PSc{i}")) for i in range(8)])
                hv = h_own.rearrange("(c p) t -> p c t", p=128)
                hd = h_dst.rearrange("(c p) t -> p c t", p=128)
                pv = pT[l].rearrange("(c p) t -> p c t", p=128)
                b_XG = [Buf(f"XG{k}") for k in range(4)]
                for k in (2, 3):
                    AG(XG[k], XS[k], Buf(f"agx{k}"), w=[b_XG[k]])
                b_XO = Buf("XO")
                for kx_ in range(4):
                    for rk in range(2):
                        LD(XO[kx_, rk * 128:(rk + 1) * 128, :], XG[kx_, rk * 128:(rk + 1) * 128, bass.ds(off_own, SH)],
                           [b_XO], b_XO, r=[b_XG[kx_]])
                b_HS = [Buf(f"HSp{q}") for q in range(NPC)]

                def loads(g):
                    I = INS[g % 2]
                    ts_ = slice(g * 512, (g + 1) * 512)
                    for c in range(4):
                        rk, kk = c // 2, c % 2
                        LD(I["YAG"][0][:, c, :], XO[kk, rk * 128:(rk + 1) * 128, ts_], [I["YAG"][1]], I["YAG"][1], r=[b_XO])
                        LD(I["YBG"][0][:, c, :], XO[2 + kk, rk * 128:(rk + 1) * 128, ts_], [I["YBG"][1]], I["YBG"][1], r=[b_XO])
                    LD(I["SGA"][0][:], s_sga.rearrange("(c p) t -> p c t", p=128)[:, :, ts_], [I["SGA"][1]], I["SGA"][1])
                    LD(I["SGB"][0][:], s_sgb.rearrange("(c p) t -> p c t", p=128)[:, :, ts_], [I["SGB"][1]], I["SGB"][1])
                    LD(PF[:], pv[:, :, ts_], [b_PF], b_PF)
                    CP("act", I["PB"][0][:], PF[:], [b_PF], [I["PB"][1]])

                loads(0)
                LD(H[:], hv[:, :, 0:512], [b_H], b_H)
                for g in range(NGO):
                    ts_ = slice(g * 512, (g + 1) * 512)
                    I = INS[g % 2]
                    YAG, b_YAG = I["YAG"]; YBG, b_YBG = I["YBG"]; SGA, b_SGA = I["SGA"]; SGB, b_SGB = I["SGB"]
                    PB, b_PB = I["PB"]
                    if g + 1 < NGO:
                        loads(g + 1)
                    for j in range(8):
                        Pa, bPa = PS.next()
                        for c in range(4):
                            MM(Pa[:], WA[:, c, j * 128:(j + 1) * 128], YAG[:, c, :], c == 0, c == 3,
                               [b_WA, b_YAG], [bPa])
                        Pb, bPb = PS.next()
                        for c in range(4):
                            MM(Pb[:], WB_[:, c, j * 128:(j + 1) * 128], YBG[:, c, :], c == 0, c == 3,
                               [b_WB, b_YBG], [bPb])
                        t1, bt1 = tmp.next()
                        TT("dve", t1[:], Pa[:], SGA[:, j, :], ALU.mult, [bPa, b_SGA], [bt1])
                        t2, bt2 = tmp.next()
                        TT("dve", t2[:], Pb[:], SGB[:, j, :], ALU.mult, [bPb, b_SGB], [bt2])
                        TT("pool", MG[:, j, :], t1[:], t2[:], ALU.add, [bt1, bt2], [b_MG])
                    for j in range(8):
                        Po, bPo = PS.next()
                        for c in range(8):
                            MM(Po[:], WO[:, c, j * 128:(j + 1) * 128], MG[:, c, :], c == 0, c == 7,
                               [b_WO, b_MG], [bPo])
                        TT("dve", HM[:, j, :], Po[:], H[:, j, :], ALU.add, [bPo, b_H], [b_HM])
                        CP("act", HMb[:, j, :], HM[:, j, :], [b_HM], [b_HMb])
                    if g + 1 < NGO:
                        LD(H[:], hv[:, :, (g + 1) * 512:(g + 2) * 512], [b_H], b_H)
                    q, col = (g * 512) // PIECE, (g * 512) % PIECE
                    for j in range(8):
                        Pp, bPp = PS.next()
                        for c in range(2):
                            MM(Pp[:], WP[:, c, j * 128:(j + 1) * 128], PB[:, c, :], c == 0, c == 1,
                               [b_WP, b_PB], [bPp])
                        Pg, bPg = PS.next()
                        for c in range(8):
                            MM(Pg[:], WG[:, c, j * 128:(j + 1) * 128], HMb[:, c, :], c == 0, c == 7,
                               [b_WG, b_HMb], [bPg])
                        sg, bsg = tmp.next()
                        ACT(sg[:], Pg[:], AF.Sigmoid, [bPg], [bsg])
                        t1, bt1 = tmp.next()
                        TT("dve", t1[:], Pp[:], sg[:], ALU.mult, [bPp, bsg], [bt1])
                        hn, bhn = HN.next()
                        TT("pool", hn[:], t1[:], HM[:, j, :], ALU.add, [bt1, b_HM], [bhn])
                        ST(hd[:, j, ts_], hn[:], [bhn], bhn, q="sp")
                        if not last:
                            hb, bhb = HNb.next()
                            CP("act", hb[:], hn[:], [bhn], [bhb])
                            ST(HS[q, j * 128:(j + 1) * 128, col:col + 512], hb[:], [bhb], bhb, w=[b_HS[q]], q="sp")
                    if not last and (g + 1) * 512 % PIECE == 0:
                        AG(HG[q], HS[q], Buf(f"agh{q}"), r=[b_HS[q]])
                Sc.emit()

        print("total ops", Sc.total, "sems", Sc.nsem)
    return nc


def host_consts(S):
    NT = S // 128
    bf = ml_dtypes.bfloat16
    c = {}
    c["c_ident"] = np.eye(128, dtype=np.float32).astype(bf)
    blk = np.zeros((128, 128), np.float32)
    blk[:64, :64] = 1.0
    blk[64:, 64:] = 1.0
    c["c_blk"] = blk.astype(bf)
    oh = np.zeros((32, S), np.float32)
    for n in range(S // 256):
        oh[n, n * 256:(n + 1) * 256] = 1.0
    c["c_onehot"] = oh.astype(bf)
    c["c_causal"] = np.triu(np.ones((64, 64), np.float32))
    sm = np.ones((128, 512), np.float32)
    sm[:, ::64] = 0.0
    c["c_scan"] = sm
    fut = np.zeros((NT, 32), np.float32)
    neg = np.full((NT, 32), NEG, np.float32)
    for t in range(NT):
        b = t // 2
        fut[t, b:] = NEG
        neg[t, b] = 0.0
    c["c_fut"] = np.ascontiguousarray(np.broadcast_to(fut[None], (128, NT, 32))).astype(bf)
    c["c_neg"] = np.ascontiguousarray(np.broadcast_to(neg[None], (128, NT, 32))).astype(bf)
    return c


def host_strips(rel_bias, heads):
    i = np.arange(128)[:, None]
    u = np.arange(STRIP)[None, :]
    rel = u - 384 - i
    bucket = t5_bucket_np(rel)
    strip = np.empty((128, len(heads), STRIP), np.float32)
    for k, h in enumerate(heads):
        g = rel_bias[:, h][bucket]
        strip[:, k, :] = np.where(rel >= 0, g, np.float32(NEG))
    c31 = np.ascontiguousarray(np.broadcast_to(rel_bias[31, heads][None, :], (128, len(heads)))).astype(np.float32)
    return strip, c31


def host_inputs(b, r, S, x, p, norm_gain, w_in, q_norm_gain, k_norm_gain, rel_bias, hgrn_lb_logits,
                hgrn_out_gain, w_up_a, w_up_b, w_out, w_ple, w_ple_gate, consts):
    L = w_in.shape[0]
    SH = S // 2
    m = dict(consts)
    m["xT"] = np.ascontiguousarray(x[b, :S].T)
    m["xT_own"] = np.ascontiguousarray(x[b, r * SH:(r + 1) * SH].T)
    m["pT"] = np.ascontiguousarray(np.transpose(p[:, b, r * SH:(r + 1) * SH, :], (0, 2, 1)))
    cols = []
    for blk0 in range(0, 4096, 512):
        cols.append(np.arange(blk0 + r * 256, blk0 + (r + 1) * 256))
    cols.append(np.arange(4096, 6144))
    cols = np.concatenate(cols)
    m["w_in"] = np.ascontiguousarray(w_in[:, :, cols])
    m["w_up_a"] = w_up_a
    m["w_up_b"] = w_up_b
    m["w_out"] = w_out
    m["w_ple"] = w_ple
    m["w_pg"] = w_ple_gate
    m["g_norm"] = np.ascontiguousarray(np.transpose(norm_gain.reshape(L, 8, 128), (2, 0, 1)))
    m["g_q"] = np.ascontiguousarray(np.concatenate([q_norm_gain, q_norm_gain], axis=1).T)
    m["g_k"] = np.ascontiguousarray(np.concatenate([k_norm_gain, k_norm_gain], axis=1).T)
    m["g_o"] = np.ascontiguousarray(np.transpose(hgrn_out_gain.reshape(L, 4, 128)[:, 2 * r:2 * r + 2], (2, 0, 1)))
    m["lbl"] = np.ascontiguousarray(np.transpose(hgrn_lb_logits.reshape(L, 4, 128)[:, 2 * r:2 * r + 2], (2, 0, 1)))
    strip, c31 = host_strips(rel_bias, list(range(4 * r, 4 * r + 4)))
    m["strip_raw"] = strip
    m["c31"] = c31
    return m


_NC_CACHE = {}


def kernel(x, p, norm_gain, w_in, q_norm_gain, k_norm_gain, rel_bias, hgrn_lb_logits,
           hgrn_out_gain, w_up_a, w_up_b, w_out, w_ple, w_ple_gate):
    args = [np.asarray(a, dtype=np.float32) for a in (
        x, p, norm_gain, w_in, q_norm_gain, k_norm_gain, rel_bias, hgrn_lb_logits,
        hgrn_out_gain, w_up_a, w_up_b, w_out, w_ple, w_ple_gate)]
    x = args[0]
    B, S, _ = x.shape
    if S not in _NC_CACHE:
        _NC_CACHE[S] = build(S)
    nc = _NC_CACHE[S]
    consts = host_consts(S)
    in_maps = [host_inputs(i // 2, i % 2, S, *args, consts) for i in range(2 * B)]
    res = run_bass_kernel_spmd(nc, in_maps, core_ids=list(range(2 * B)))
    SH = S // 2
    out = np.empty((B, S, D), np.float32)
    for i in range(2 * B):
        out[i // 2, (i % 2) * SH:(i % 2 + 1) * SH, :] = res.results[i]["hT_out"].T
    return out
```

```python
import math
from contextlib import ExitStack

import numpy as np
import ml_dtypes
import concourse.bass as bass
import concourse.mybir as mybir
from concourse.bass_utils import run_bass_kernel_spmd

F32 = mybir.dt.float32
BF16 = mybir.dt.bfloat16
AF = mybir.ActivationFunctionType
ALU = mybir.AluOpType
AX = mybir.AxisListType

SEM_CHUNK = 20000
D = 1024
NEG = -30000.0
STRIP = 2432
EPS = 1e-6


class Buf:
    __slots__ = ("name", "last_writer", "dma_writers", "readers", "dma_readers")

    def __init__(self, name):
        self.name = name
        self.clear()

    def clear(self):
        self.last_writer = None
        self.dma_writers = []
        self.readers = {}
        self.dma_readers = []


class Sched:
    def __init__(self, nc, stack):
        self.nc = nc
        self.stack = stack
        self.ops = []
        self.eng = {"pe": nc.tensor, "act": nc.scalar, "dve": nc.vector,
                    "pool": nc.gpsimd, "sp": nc.sync}
        self.nsem = 0
        self.eng_count = {}
        self.eng_sems = {}
        self.dma_pool = []
        self.dma_free = []
        self.key_slot = {}
        self.waited = {}
        self.total = 0

    def new_sem(self, name):
        self.nsem += 1
        return self.stack.enter_context(self.nc.semaphore(name))

    def op(self, eng, fn, reads=(), writes=()):
        self.ops.append(["c", eng, fn, tuple(reads), tuple(writes), None, 1])

    def dma(self, queue, fn, reads=(), writes=(), key=None, inc=16):
        assert key is not None
        self.ops.append(["d", queue, fn, tuple(reads), tuple(writes), key, inc])

    def emit(self):
        ops = self.ops
        n = len(ops)
        deps = [None] * n
        signaling = [False] * n
        last_of_eng = {}
        for i, o in enumerate(ops):
            d = set()
            isd = o[0] == "d"
            for b in o[3]:
                if b.last_writer is not None:
                    d.add(b.last_writer)
                d.update(b.dma_writers)
            for b in o[4]:
                if b.last_writer is not None:
                    d.add(b.last_writer)
                d.update(b.readers.values())
                d.update(b.dma_readers)
                if not isd:
                    d.update(b.dma_writers)
            d.discard(i)
            for b in o[3]:
                if isd:
                    b.dma_readers.append(i)
                else:
                    b.readers[o[1]] = i
            for b in o[4]:
                if isd:
                    b.dma_writers.append(i)
                else:
                    b.last_writer = i
                    b.dma_writers = []
                    b.readers = {}
                    b.dma_readers = []
            deps[i] = d
            for j in d:
                signaling[j] = True
            if o[0] == "c":
                last_of_eng[o[1]] = i
        for e, i in last_of_eng.items():
            signaling[i] = True
        sig = [None] * n
        for i, o in enumerate(ops):
            if o[0] == "c":
                if not signaling[i]:
                    continue
                e = o[1]
                c = self.eng_count.get(e, 0)
                k = c // SEM_CHUNK
                if (e, k) not in self.eng_sems:
                    self.eng_sems[(e, k)] = self.new_sem(f"s_{e}_{k}")
                self.eng_count[e] = c + 1
                sig[i] = (self.eng_sems[(e, k)], c - k * SEM_CHUNK + 1, 1, ("e", e, k))
            else:
                key = o[5]
                if key not in self.key_slot:
                    if self.dma_free:
                        s = self.dma_free.pop()
                    else:
                        self.dma_pool.append([self.new_sem(f"d_{len(self.dma_pool)}"), 0])
                        s = len(self.dma_pool) - 1
                    self.key_slot[key] = s
                s = self.key_slot[key]
                self.dma_pool[s][1] += o[6]
                sig[i] = (self.dma_pool[s][0], self.dma_pool[s][1], o[6], ("k", s))
        waited = self.waited
        for i, o in enumerate(ops):
            e = o[1]
            engine = self.eng[e]
            w = waited.setdefault(e, {})
            for j in sorted(deps[i]):
                pj = ops[j]
                if pj[0] == "c" and pj[1] == "pe" and e == "pe" and o[0] == "c":
                    continue
                sem, val, _, sid = sig[j]
                if w.get(sid, 0) >= val:
                    continue
                w[sid] = val
                engine.wait_ge(sem, val)
            ins = o[2](engine)
            if sig[i] is not None:
                ins.then_inc(sig[i][0], sig[i][2])
        finals = []
        for (e, k), sem in self.eng_sems.items():
            c = self.eng_count.get(e, 0)
            if c // SEM_CHUNK == k and c - k * SEM_CHUNK > 0:
                finals.append((sem, c - k * SEM_CHUNK, ("e", e, k)))
            elif c // SEM_CHUNK > k:
                finals.append((sem, SEM_CHUNK, ("e", e, k)))
        for s, (sem, c) in enumerate(self.dma_pool):
            if c > 0:
                finals.append((sem, c, ("k", s)))
        for e in ("sp", "pe", "act", "dve", "pool"):
            w = waited.setdefault(e, {})
            for sem, val, sid in finals:
                if w.get(sid, 0) >= val:
                    continue
                w[sid] = val
                self.eng[e].wait_ge(sem, val)
        for o in ops:
            for b in o[3] + o[4]:
                b.clear()
        self.key_slot = {}
        self.dma_free = list(range(len(self.dma_pool)))
        self.total += n
        self.ops = []
        return n


class Ring:
    def __init__(self, items):
        self.items = items
        self.i = 0

    def next(self):
        it = self.items[self.i % len(self.items)]
        self.i += 1
        return it


def t5_bucket_np(rel):
    n = np.maximum(rel, 0)
    nf = np.maximum(n, 16).astype(np.float32)
    large = 16 + (np.log(nf / np.float32(16)) / np.float32(math.log(128)) * np.float32(16)).astype(np.int32)
    large = np.minimum(large, 31)
    return np.where(n < 16, n, large)


def build(S, L=2, debug=False, groups=None):
    NT, NB, NG, NCH = S // 128, S // 256, S // 512, S // 64
    SH, NGO = S // 2, S // 1024
    HA, HB = 4, 2
    if groups is None:
        groups = [[0, 1], [2, 3], [4, 5], [6, 7]]
    nc = bass.Bass("TRN2", target_bir_lowering=False)

    def din(name, shape, dt=F32):
        return nc.dram_tensor(name, list(shape), dt, kind="ExternalInput").ap()

    def dscr(name, shape, dt=BF16, out=False):
        kind = "Internal"
        return nc.dram_tensor(name, list(shape), dt, kind=kind).ap()

    xT = din("xT", [D, S])
    xT_own = din("xT_own", [D, SH])
    pT = din("pT", [L, 256, SH])
    w_in = din("w_in", [L, D, 4096])
    w_up_a = din("w_up_a", [L, 512, D])
    w_up_b = din("w_up_b", [L, 512, D])
    w_out = din("w_out", [L, D, D])
    w_ple = din("w_ple", [L, 256, D])
    w_pg = din("w_pg", [L, D, D])
    g_norm = din("g_norm", [128, L, 8])
    g_q = din("g_q", [128, L])
    g_k = din("g_k", [128, L])
    g_o = din("g_o", [128, L, HB])
    lbl = din("lbl", [128, L, HB])
    strip_raw = din("strip_raw", [128, HA, STRIP])
    c31 = din("c31", [128, HA])
    c_ident = din("c_ident", [128, 128], BF16)
    c_blk = din("c_blk", [128, 128], BF16)
    c_onehot = din("c_onehot", [32, S], BF16)
    c_causal = din("c_causal", [64, 64])
    c_scan = din("c_scan", [128, 512])
    c_fut = din("c_fut", [128, NT, 32], BF16)
    c_neg = din("c_neg", [128, NT, 32], BF16)

    hT_out = nc.dram_tensor("hT_out", [D, SH], F32, kind="ExternalOutput").ap()
    h_mid = dscr("h_mid", [D, SH], F32)
    s_qn = dscr("s_qn", [256, S])
    s_kn = dscr("s_kn", [256, S])
    s_v = dscr("s_v", [S, 256])
    s_sag = dscr("s_sag", [256, S])
    s_qd = dscr("s_qd", [256, S])
    s_kd = dscr("s_kd", [256, S])
    s_ke = dscr("s_ke", [256, S])
    s_vb = dscr("s_vb", [S, 256])
    s_sbg = dscr("s_sbg", [256, S])
    s_sga = dscr("s_sga", [D, SH])
    s_sgb = dscr("s_sgb", [D, SH])
    XS = dscr("XS", [4, 128, S], out=True)
    XG = dscr("XG", [4, 2 * 128, S], out=True)
    XO = dscr("XO", [4, 2 * 128, SH])
    PIECE = max(512, SH // 4)
    NPC = SH // PIECE
    HS = dscr("HS", [NPC, D, PIECE])
    HG = dscr("HG", [NPC, 2 * D, PIECE])

    off_own = (nc.sync.partition_id() % 2) * SH

    with ExitStack() as outer:
        Sc = Sched(nc, outer)
        uniq = [0]

        def mk(stack):
            uniq[0] += 1
            tag = f"u{uniq[0]}_"

            def sb(name, shape, dt=F32):
                return stack.enter_context(nc.sbuf_tensor(tag + name, list(shape), dt))

            def ps(name, shape, dt=F32):
                return stack.enter_context(nc.psum_tensor(tag + name, list(shape), dt))
            return sb, ps

        def MM(out, lhsT, rhs, start, stop, r, w):
            Sc.op("pe", lambda e: e.matmul(out, lhsT=lhsT, rhs=rhs, start=start, stop=stop), r, w)

        def ACT(out, in_, func, r, w, bias=0.0, scale=1.0, eng="act"):
            Sc.op(eng, lambda e: e.activation(out=out, in_=in_, func=func, bias=bias, scale=scale), r, w)

        def TT(eng, out, in0, in1, op, r, w):
            Sc.op(eng, lambda e: e.tensor_tensor(out=out, in0=in0, in1=in1, op=op), r, w)

        def TSC(eng, out, in0, s1, s2, op0, op1, r, w):
            if s2 is None:
                Sc.op(eng, lambda e: e.tensor_scalar(out=out, in0=in0, scalar1=s1, scalar2=None, op0=op0), r, w)
            else:
                Sc.op(eng, lambda e: e.tensor_scalar(out=out, in0=in0, scalar1=s1, scalar2=s2, op0=op0, op1=op1), r, w)

        def STT(eng, out, in0, scalar, in1, op0, op1, r, w):
            Sc.op(eng, lambda e: e.scalar_tensor_tensor(out=out, in0=in0, scalar=scalar, in1=in1, op0=op0, op1=op1), r, w)

        def CP(eng, out, in_, r, w):
            if eng == "act":
                Sc.op("act", lambda e: e.activation(out=out, in_=in_, func=AF.Copy), r, w)
            else:
                Sc.op(eng, lambda e: e.tensor_copy(out=out, in_=in_), r, w)

        def LD(out, in_, w, key, r=(), q="sp"):
            Sc.dma(q, lambda e: e.dma_start(out=out, in_=in_), r, w, key)

        def ST(out, in_, r, key, w=(), q="pool"):
            Sc.dma(q, lambda e: e.dma_start(out=out, in_=in_), r, w, key)

        def AG(out, in_, key, r=(), w=()):
            Sc.dma("pool", lambda e: e.collective_compute(
                "AllGather", ALU.bypass, replica_groups=groups, ins=[in_], outs=[out]), r, w, key, inc=1)

        sbP, _ = mk(outer)
        ident = sbP("ident", [128, 128], BF16); b_ident = Buf("ident")
        blk = sbP("blk", [128, 128], BF16); b_blk = Buf("blk")
        ones = sbP("ones", [128, 128], BF16); b_ones = Buf("ones")
        gn = sbP("gn", [128, L, 8]); b_gn = Buf("gn")
        gq = sbP("gq", [128, L]); b_gq = Buf("gq")
        gk = sbP("gk", [128, L]); b_gk = Buf("gk")
        go = sbP("go", [128, L, HB]); b_go = Buf("go")
        lb = sbP("lb", [128, L, HB]); b_lb = Buf("lb")
        oml = sbP("oml", [128, L, HB]); b_oml = Buf("oml")
        lbe = sbP("lbe", [128, L, HB]); b_lbe = Buf("lbe")
        lbs = sbP("lbs", [128, HB]); b_lbs = Buf("lbs")
        KS = sbP("KS", [128, 2, NB]); b_KS = Buf("KS")
        EL = sbP("EL", [128, HB, NCH]); b_EL = Buf("EL")
        caus = sbP("caus", [64, 64]); b_caus = Buf("caus")
        scanm = sbP("scanm", [128, 512]); b_scanm = Buf("scanm")

        with ExitStack() as st:
            sb, ps = mk(st)
            LD(ident[:], c_ident, [b_ident], b_ident)
            LD(blk[:], c_blk, [b_blk], b_blk)
            Sc.op("pool", lambda e: e.memset(ones[:], 1.0), (), [b_ones])
            LD(gn[:], g_norm, [b_gn], b_gn)
            LD(gq[:], g_q, [b_gq], b_gq)
            LD(gk[:], g_k, [b_gk], b_gk)
            LD(go[:], g_o, [b_go], b_go)
            LD(lbe[:], lbl, [b_lbe], b_lbe)
            LD(caus[:], c_causal, [b_caus], b_caus)
            LD(scanm[:], c_scan, [b_scanm], b_scanm)
            TSC("dve", gq[:], gq[:], 0.125, None, ALU.mult, None, [b_gq], [b_gq])
            ACT(lbe[:], lbe[:], AF.Exp, [b_lbe], [b_lbe])
            CP("dve", lbs[:], lbe[:, 0, :], [b_lbe], [b_lbs])
            for l in range(1, L):
                TT("dve", lbs[:], lbs[:], lbe[:, l, :], ALU.add, [b_lbs, b_lbe], [b_lbs])
            Sc.op("dve", lambda e: e.reciprocal(out=lbs[:], in_=lbs[:]), [b_lbs], [b_lbs])
            Sc.op("dve", lambda e: e.memset(lb[:, 0, :], 0.0), (), [b_lb])
            for l in range(1, L):
                TT("dve", lb[:, l, :], lb[:, l - 1, :], lbe[:, l, :], ALU.add, [b_lb, b_lbe], [b_lb])
            for l in range(1, L):
                TT("dve", lb[:, l, :], lb[:, l, :], lbs[:], ALU.mult, [b_lb, b_lbs], [b_lb])
            for l in range(L):
                TSC("dve", oml[:, l, :], lb[:, l, :], -1.0, 1.0, ALU.mult, ALU.add, [b_lb], [b_oml])
            Sc.emit()

        for l in range(L):
            first, last = (l == 0), (l == L - 1)
            h_own = xT_own if first else h_mid
            h_dst = hT_out if last else h_mid

            with ExitStack() as st:
                sb, ps = mk(st)
                Wb = sb("Wb", [128, 8, 4096], BF16)
                b_Wbs = [Buf(f"Wb{i}") for i in range(32)]
                wst = [(sb(f"wst{i}", [128, 8, 128]), Buf(f"wst{i}")) for i in range(4)]
                wv = w_in[l].rearrange("(c p) n -> p c n", p=128)
                for i in range(32):
                    t, b = wst[i % 4]
                    LD(t[:], wv[:, :, i * 128:(i + 1) * 128], [b], b)
                    CP(("act", "dve")[i % 2], Wb[:, :, i * 128:(i + 1) * 128], t[:], [b], [b_Wbs[i]])
                HDT = F32 if first else BF16
                H = sb("H", [128, 8, 512], HDT); b_H = Buf("H")
                H2 = sb("H2", [128, 8, 512]); b_H2 = Buf("H2")
                SQ = sb("SQ", [128, 8, 512], BF16); b_SQ = Buf("SQ")
                XNs = [(sb(f"XN{i}", [128, 8, 512], BF16), Buf(f"XN{i}")) for i in range(2)]
                rstd = sb("rstd", [128, 512]); b_rstd = Buf("rstd")
                stage = Ring([(sb(f"stg{i}", [128, 512], BF16), Buf(f"stg{i}")) for i in range(6)])
                f32r = Ring([(sb(f"f32r{i}", [128, 512]), Buf(f"f32r{i}")) for i in range(8)])
                QB = [(sb(f"QB{i}", [128, 512]), Buf(f"QB{i}")) for i in range(HB)]
                SG = [(sb(f"SG{i}", [128, 512]), Buf(f"SG{i}")) for i in range(HB)]
                sqq = Ring([(sb(f"sqq{i}", [128, 512], BF16), Buf(f"sqq{i}")) for i in range(2)])
                p_ss = ps("p_ss", [128, 512]); b_pss = Buf("p_ss")
                PS = Ring([(ps(f"PSa{i}", [128, 512]), Buf(f"PSa{i}")) for i in range(5)])
                BS = Ring([(ps(f"BSa{i}", [128, 512]), Buf(f"BSa{i}")) for i in range(2)])
                Sc.op("dve", lambda e: e.memset(KS[:], 0.0), (), [b_KS])

                def load_h(g):
                    if first:
                        LD(H[:], xT.rearrange("(c p) t -> p c t", p=128)[:, :, g * 512:(g + 1) * 512], [b_H], b_H)
                    else:
                        half, tok = g // NGO, (g % NGO) * 512
                        q, col = tok // PIECE, tok % PIECE
                        src = HG[q, half * D:(half + 1) * D, col:col + 512].rearrange("(c p) t -> p c t", p=128)
                        LD(H[:], src, [b_H], b_H)

                def norm_part1(g):
                    load_h(g)
                    ACT(SQ[:].rearrange("p c t -> p (c t)"), H[:].rearrange("p c t -> p (c t)"),
                        AF.Square, [b_H], [b_SQ])

                def norm_part2(Hs, bH, XN, b_XN):
                    for c in range(8):
                        MM(p_ss[:], ones[:], SQ[:, c, :], c == 0, c == 7, [b_ones, b_SQ], [b_pss])
                    ACT(rstd[:], p_ss[:], AF.Ln, [b_pss], [b_rstd], bias=EPS, scale=1.0 / D)
                    ACT(rstd[:], rstd[:], AF.Exp, [b_rstd], [b_rstd], scale=-0.5)
                    for c in range(8):
                        STT("dve", XN[:, c, :], Hs[:, c, :], gn[:, l, c:c + 1], rstd[:],
                            ALU.mult, ALU.mult, [bH, b_gn, b_rstd], [b_XN])

                norm_part1(0)
                norm_part2(H, b_H, *XNs[0])
                for g in range(NG):
                    XN, b_XN = XNs[g % 2]

                    def proj_fm(c0):
                        P, bP = PS.next()
                        for c in range(8):
                            MM(P[:], Wb[:, c, c0:c0 + 128], XN[:, c, :], c == 0, c == 7,
                               [b_Wbs[c0 // 128], b_XN], [bP])
                        return P, bP

                    def store_fm(dst, j, t, b):
                        ST(dst[j * 128:(j + 1) * 128, g * 512:(g + 1) * 512], t[:], [b], b)

                    def qk_tail(j, P, bP, s2, bs2):
                        isk = j >= 2
                        Bp, bB = BS.next()
                        MM(Bp[:], blk[:], s2[:], True, True, [b_blk, bs2], [bB])
                        r1, br1 = f32r.next()
                        ACT(r1[:], Bp[:], AF.Ln, [bB], [br1], bias=EPS, scale=1.0 / 64)
                        ACT(r1[:], r1[:], AF.Exp, [br1], [br1], scale=-0.5)
                        o, bo = stage.next()
                        gg = gk if isk else gq
                        STT("dve", o[:], P[:], gg[:, l:l + 1], r1[:], ALU.mult, ALU.mult,
                            [bP, b_gk if isk else b_gq, br1], [bo])
                        if isk:
                            Sc.op("dve", lambda e, o=o, j=j, g=g: e.tensor_reduce(
                                out=KS[:, j - 2, 2 * g:2 * g + 2],
                                in_=o[:].rearrange("p (b t) -> p b t", t=256),
                                axis=AX.X, op=ALU.add), [bo], [b_KS])
                        store_fm(s_kn if isk else s_qn, j % 2, o, bo)

                    pend = None
                    for j in range(4):
                        P, bP = proj_fm(j * 128)
                        s2, bs2 = sqq.next()
                        ACT(s2[:], P[:], AF.Square, [bP], [bs2])
                        if pend is not None:
                            qk_tail(*pend)
                        pend = (j, P, bP, s2, bs2)
                    firstv = True
                    for (c0, dst) in ((512, s_v), (1536, s_vb)):
                        for tt in range(4):
                            P, bP = PS.next()
                            for c in range(8):
                                MM(P[:, 0:256], XN[:, c, tt * 128:(tt + 1) * 128], Wb[:, c, c0:c0 + 256],
                                   c == 0, c == 7, [b_XN, b_Wbs[c0 // 128], b_Wbs[c0 // 128 + 1]], [bP])
                            if firstv:
                                qk_tail(*pend)
                                firstv = False
                            o, bo = stage.next()
                            CP("act", o[:, 0:256], P[:, 0:256], [bP], [bo])
                            ST(dst[g * 512 + tt * 128:g * 512 + (tt + 1) * 128, :], o[:, 0:256], [bo], bo)
                    if g + 1 < NG:
                        norm_part1(g + 1)
                    for jj in range(HB):
                        P, bP = proj_fm(1280 + jj * 128)
                        ACT(SG[jj][0][:], P[:], AF.Sigmoid, [bP], [SG[jj][1]])
                    if g + 1 < NG:
                        norm_part2(H, b_H, *XNs[(g + 1) % 2])
                    for (c00, dst) in ((768, s_sag), (1792, s_sbg)):
                        for jj in range(2):
                            P, bP = proj_fm(c00 + jj * 128)
                            o, bo = stage.next()
                            ACT(o[:], P[:], AF.Silu, [bP], [bo])
                            store_fm(dst, jj, o, bo)
                    for jj in range(HB):
                        P, bP = proj_fm(1024 + jj * 128)
                        ACT(QB[jj][0][:], P[:], AF.Silu, [bP], [QB[jj][1]])
                    for jj in range(HB):
                        sg, bsg = SG[jj]
                        f, bf_ = f32r.next()
                        TSC("dve", f[:], sg[:], oml[:, l, jj:jj + 1], lb[:, l, jj:jj + 1], ALU.mult, ALU.add,
                            [bsg, b_oml, b_lb], [bf_])
                        gl, bgl = f32r.next()
                        ACT(gl[:], f[:], AF.Ln, [bf_], [bgl])
                        cum, bcum = f32r.next()
                        Sc.op("dve", lambda e, cum=cum, gl=gl: e.tensor_tensor_scan(
                            out=cum[:], data0=scanm[:], data1=gl[:], initial=0.0,
                            op0=ALU.mult, op1=ALU.add), [bgl, b_scanm], [bcum])
                        ec, bec = f32r.next()
                        ACT(ec[:], cum[:], AF.Exp, [bcum], [bec])
                        ACT(gl[:], cum[:], AF.Exp, [bcum], [bgl], scale=-1.0)
                        CP("pool", EL[:, jj, g * 8:(g + 1) * 8],
                           ec[:].rearrange("p (c t) -> p c t", t=64)[:, :, 63], [bec], [b_EL])
                        o, bo = stage.next()
                        TT("pool", o[:], QB[jj][0][:], ec[:], ALU.mult, [QB[jj][1], bec], [bo])
                        store_fm(s_qd, jj, o, bo)
                        TSC("dve", f[:], f[:], -1.0, 1.0, ALU.mult, ALU.add, [bf_], [bf_])
                        TT("dve", f[:], f[:], gl[:], ALU.mult, [bf_, bgl], [bf_])
                        o, bo = stage.next()
                        CP("act", o[:], f[:], [bf_], [bo])
                        store_fm(s_kd, jj, o, bo)
                        o, bo = stage.next()
                        TT("pool", o[:].rearrange("p (c t) -> p c t", t=64),
                           f[:].rearrange("p (c t) -> p c t", t=64),
                           EL[:, jj, g * 8:(g + 1) * 8].unsqueeze(2).to_broadcast([128, 8, 64]),
                           ALU.mult, [bf_, b_EL], [bo])
                        store_fm(s_ke, jj, o, bo)
                hov = h_own.rearrange("(c p) t -> p c t", p=128)
                def gate_norm(g):
                    XN, b_XN = XNs[g % 2]
                    LD(H2[:], hov[:, :, g * 512:(g + 1) * 512], [b_H2], b_H2)
                    ACT(SQ[:].rearrange("p c t -> p (c t)"), H2[:].rearrange("p c t -> p (c t)"),
                        AF.Square, [b_H2], [b_SQ])
                    norm_part2(H2, b_H2, XN, b_XN)

                gate_norm(0)
                for g in range(NGO):
                    XN, b_XN = XNs[g % 2]
                    for jj in range(16):
                        if jj == 6 and g + 1 < NGO:
                            gate_norm(g + 1)
                        P, bP = PS.next()
                        for c in range(8):
                            MM(P[:], Wb[:, c, 2048 + jj * 128:2048 + (jj + 1) * 128], XN[:, c, :], c == 0, c == 7,
                               [b_Wbs[16 + jj], b_XN], [bP])
                        o, bo = stage.next()
                        ACT(o[:], P[:], AF.Sigmoid, [bP], [bo])
                        dst = s_sga if jj < 8 else s_sgb
                        ST(dst[(jj % 8) * 128:(jj % 8 + 1) * 128, g * 512:(g + 1) * 512], o[:], [bo], bo)
                Sc.emit()

            with ExitStack() as st:
                sb, ps = mk(st)
                QAs = [(sb(f"QA{i}", [128, S], BF16), Buf(f"QA{i}")) for i in range(2)]
                KAs = [(sb(f"KA{i}", [128, S], BF16), Buf(f"KA{i}")) for i in range(2)]
                VAs = [(sb(f"VA{i}", [128, NT, 128], BF16), Buf(f"VA{i}")) for i in range(2)]
                SAGr = Ring([(sb(f"SAG{i}", [64, 512], BF16), Buf(f"SAG{i}")) for i in range(3)])
                YGr = Ring([(sb(f"YG{i}", [64, 512], BF16), Buf(f"YG{i}")) for i in range(3)])
                FUT = sb("FUT", [128, NT, 32], BF16); b_FUT = Buf("FUT")
                NEGP = sb("NEGP", [128, NT, 32], BF16); b_NEGP = Buf("NEGP")
                KMh = sb("KMh", [64, 32], BF16); b_KMh = Buf("KMh")
                NTB = min(NT, 16)
                Gs = sb("Gs", [128, NTB, 32]); b_Gs = Buf("Gs")
                thr = sb("thr", [128, NTB, 8]); b_thr = Buf("thr")
                nsel = sb("nsel", [128, NTB, 32]); b_nsel = Buf("nsel")
                MBp = sb("MBp", [128, NTB, 128], BF16); b_MBp = Buf("MBp")
                PT = Ring([(sb(f"PT{i}", [128, 1024], BF16), Buf(f"PT{i}")) for i in range(3)])
                rden = sb("rden", [64, 512]); b_rden = Buf("rden")
                yh = sb("yh", [64, 512]); b_yh = Buf("yh")
                STp = Ring([(ps(f"STp{i}", [128, 1024]), Buf(f"STp{i}")) for i in range(2)])
                Op = Ring([(ps(f"Op{i}", [128, 512]), Buf(f"Op{i}")) for i in range(2)])
                Gp = ps("Gp", [128, 16, 32]); b_Gp = Buf("Gp")
                MTp = ps("MTp", [128, 512]); b_MTp = Buf("MTp")
                LD(FUT[:], c_fut, [b_FUT], b_FUT)
                LD(NEGP[:], c_neg, [b_NEGP], b_NEGP)
                for i in range(2):
                    LD(KAs[i][0][64:96, :], c_onehot, [KAs[i][1]], KAs[i][1])
                    Sc.op("pool", lambda e, i=i: e.memset(VAs[i][0][:, :, 64:128], 1.0), (), [VAs[i][1]])
                Sc.op("pool", lambda e: e.memset(MBp[:], 0.0), (), [b_MBp])
                TS_ = sb("TS", [128, HA, STRIP], BF16); b_TS = Buf("TS")
                c31s = sb("c31s", [128, HA]); b_c31 = Buf("c31s")
                LD(c31s[:], c31, [b_c31], b_c31)
                SPC = STRIP // 4
                stgs = [(sb(f"stripstg{i}", [128, SPC]), Buf(f"stripstg{i}")) for i in range(2)]
                for h in range(HA):
                    for q4 in range(4):
                        stg, b_stg = stgs[(h * 4 + q4) % 2]
                        LD(stg[:], strip_raw[:, h, q4 * SPC:(q4 + 1) * SPC], [b_stg], b_stg)
                        TSC("dve", TS_[:, h, q4 * SPC:(q4 + 1) * SPC], stg[:], c31s[:, h:h + 1], None, ALU.subtract,
                            None, [b_stg, b_c31], [b_TS])

                def head_loads(h):
                    sl = h % 2
                    QA, b_QA = QAs[sl]; KA, b_KA = KAs[sl]; VA, b_VA = VAs[sl]
                    LD(QA[0:64, :], s_qn[h * 64:(h + 1) * 64, :], [b_QA], b_QA)
                    LD(KA[0:64, :], s_kn[h * 64:(h + 1) * 64, :], [b_KA], b_KA)
                    vsrc = s_v[:, h * 64:(h + 1) * 64].rearrange("(t p) d -> p t d", p=128)
                    nvs = max(1, NT // 8)
                    for i in range(0, NT, nvs):
                        LD(VA[:, i:i + nvs, 0:64], vsrc[:, i:i + nvs, :], [b_VA], b_VA)

                def gate_stage(h, tb, stage_):
                    sl = h % 2
                    QA, b_QA = QAs[sl]
                    cq, po = h // 2, 64 * (h % 2)
                    if stage_ == 0:
                        if tb == 0:
                            TSC("dve", KMh[0:64, 0:NB], KS[po:po + 64, cq, :], 1.0 / 256, None, ALU.mult, None,
                                [b_KS], [b_KMh])
                        for t in range(NTB):
                            MM(Gp[:, t, 0:NB], QA[0:64, (tb + t) * 128:(tb + t + 1) * 128], KMh[0:64, 0:NB],
                               True, True, [b_QA, b_KMh], [b_Gp])
                    elif stage_ == 1:
                        if NB < 32:
                            Sc.op("dve", lambda e: e.memset(Gs[:], NEG), (), [b_Gs])
                        TT("dve", Gs[:, :, 0:NB], Gp[:, 0:NTB, 0:NB], FUT[:, tb:tb + NTB, 0:NB], ALU.add,
                           [b_Gp, b_FUT], [b_Gs])
                        for t in range(NTB):
                            Sc.op("dve", lambda e, t=t: e.max(out=thr[:, t, :], in_=Gs[:, t, :]), [b_Gs], [b_thr])
                        TT("dve", nsel[:], Gs[:], thr[:, :, 2:3].to_broadcast([128, NTB, 32]), ALU.is_lt,
                           [b_Gs, b_thr], [b_nsel])
                        TT("pool", MBp[:, :, 64:96], nsel[:], NEGP[:, tb:tb + NTB, :], ALU.mult,
                           [b_nsel, b_NEGP], [b_MBp])
                    else:
                        t4 = (stage_ - 2) * 4
                        for t in range(4):
                            MM(MTp[:, t * 128:(t + 1) * 128], MBp[:, t4 + t, :], ident[:], True, True,
                               [b_MBp, b_ident], [b_MTp])
                        c0 = (tb + t4) * 128
                        CP("act", QA[64:96, c0:c0 + 512], MTp[64:96, :], [b_MTp], [b_QA])

                NST = 2 + NTB // 4
                gate_sched = [(tb, st_) for tb in range(0, NT, NTB) for st_ in range(NST)]

                def head_gate(h):
                    for (tb, st_) in gate_sched:
                        gate_stage(h, tb, st_)

                head_loads(0)
                head_gate(0)
                for h in range(HA):
                    sl = h % 2
                    QA, b_QA = QAs[sl]; KA, b_KA = KAs[sl]; VA, b_VA = VAs[sl]
                    pairs = [(g, kp) for g in range(NG) for kp in range(2 * g + 2)]
                    slots = {}

                    def emit_qk(i):
                        g, kp = pairs[i]
                        Sp, bS = STp.next()
                        for u in range(2):
                            kt = 2 * kp + u
                            delta = 512 * g - 128 * kt
                            near = delta <= 1536
                            MM(Sp[:, u * 512:(u + 1) * 512], KA[0:96, kt * 128:(kt + 1) * 128],
                               QA[0:96, g * 512:(g + 1) * 512], True, not near, [b_KA, b_QA], [bS])
                            if near:
                                MM(Sp[:, u * 512:(u + 1) * 512], ident[:],
                                   TS_[:, h, delta + 384:delta + 384 + 512], False, True, [b_ident, b_TS], [bS])
                        slots[i] = (Sp, bS)

                    emit_qk(0)
                    O, bO = None, None
                    gate_i0 = len(pairs) // 4
                    gate_step = max(1, (len(pairs) - gate_i0 - 2) // len(gate_sched))
                    for i, (g, kp) in enumerate(pairs):
                        npair = 2 * g + 2
                        if kp == 0:
                            O, bO = Op.next()
                            SAG, b_SAG = SAGr.next()
                            LD(SAG[:], s_sag[h * 64:(h + 1) * 64, g * 512:(g + 1) * 512], [b_SAG], b_SAG)
                        if i + 1 < len(pairs):
                            emit_qk(i + 1)
                        if i == 0 and h + 1 < HA:
                            head_loads(h + 1)
                        if h + 1 < HA and i >= gate_i0 and (i - gate_i0) % gate_step == 0:
                            kq = (i - gate_i0) // gate_step
                            if kq < len(gate_sched):
                                gate_stage(h + 1, *gate_sched[kq])
                        Sp, bS = slots.pop(i)
                        P_, bPt = PT.next()
                        ACT(P_[:], Sp[:], AF.Exp, [bS], [bPt])
                        for u in range(2):
                            kt = 2 * kp + u
                            MM(O[:], VA[:, kt, :], P_[:, u * 512:(u + 1) * 512], kt == 0, kt == 2 * npair - 1,
                               [b_VA, bPt], [bO])
                        if kp == npair - 1:
                            Sc.op("dve", lambda e, O=O: e.reciprocal(out=rden[:], in_=O[64:128, :]), [bO], [b_rden])
                            TT("dve", yh[:], O[0:64, :], rden[:], ALU.mult, [bO, b_rden], [b_yh])
                            YG, b_YG = YGr.next()
                            TT("pool", YG[:], yh[:], SAG[:], ALU.mult, [b_yh, b_SAG], [b_YG])
                            ST(XS[h // 2, (h % 2) * 64:(h % 2) * 64 + 64, g * 512:(g + 1) * 512], YG[:], [b_YG], b_YG)
                Sc.emit()

            with ExitStack() as st:
                sb, ps = mk(st)
                QDs = [(sb(f"QD{i}", [128, S], BF16), Buf(f"QD{i}")) for i in range(HB)]
                KDs = [(sb(f"KD{i}", [128, S], BF16), Buf(f"KD{i}")) for i in range(HB)]
                KEr = Ring([(sb(f"KE{i}", [128, 512], BF16), Buf(f"KE{i}")) for i in range(4)])
                VBr = Ring([(sb(f"VB{i}", [64, 8, 128], BF16), Buf(f"VB{i}")) for i in range(4)])
                SBGr = Ring([(sb(f"SBG{i}", [128, 512], BF16), Buf(f"SBG{i}")) for i in range(4)])
                YBr = Ring([(sb(f"YB{i}", [128, 512], BF16), Buf(f"YB{i}")) for i in range(4)])
                KTr = Ring([(sb(f"KT{i}", [64, 8, 128], BF16), Buf(f"KT{i}")) for i in range(4)])
                ATs = Ring([(sb(f"ATs{i}", [64, 64], BF16), Buf(f"ATs{i}")) for i in range(4)])
                Sbs = [[(sb(f"Sb{hh}_{i}", [128, 128], BF16), Buf(f"Sb{hh}_{i}")) for i in range(2)] for hh in range(HB)]
                osq = sb("osq", [128, 512], BF16); b_osq = Buf("osq")
                ort = sb("ort", [128, 512]); b_ort = Buf("ort")
                ors = sb("ors", [128, 512]); b_ors = Buf("ors")
                ybf = sb("ybf", [128, 512]); b_ybf = Buf("ybf")
                KT1 = (ps("KTp", [64, 8, 128], BF16), Buf("KTp"))
                KTp = [KT1] * HB
                OHp = [Ring([(ps(f"OHp{hh}_{i}", [128, 512]), Buf(f"OHp{hh}_{i}")) for i in range(1)]) for hh in range(HB)]
                Abk = [ps(f"Abk{hh}", [128, 512]) for hh in range(HB)]
                dSbk = [ps(f"dSbk{hh}", [128, 512]) for hh in range(HB)]
                ATp = [(Abk[hh][0:64, 0:64], Buf(f"ATp{hh}")) for hh in range(HB)]
                dSp = [(dSbk[hh][:, 0:128], Buf(f"dSp{hh}")) for hh in range(HB)]
                NSp = ps("NSp", [128, 512]); b_NSp = Buf("NSp")
                for k in range(2):
                    AG(XG[k], XS[k], Buf(f"agx{k}"))
                for hh in range(HB):
                    rows = slice(hh * 128, (hh + 1) * 128)
                    LD(QDs[hh][0][:], s_qd[rows, :], [QDs[hh][1]], QDs[hh][1])
                    LD(KDs[hh][0][:], s_kd[rows, :], [KDs[hh][1]], KDs[hh][1])
                    Sc.op("dve", lambda e, hh=hh: e.memset(Sbs[hh][0][0][:], 0.0), (), [Sbs[hh][0][1]])
                cur = [0] * HB

                def group_loads(g):
                    out = []
                    for hh in range(HB):
                        rows = slice(hh * 128, (hh + 1) * 128)
                        KE, bKE = KEr.next()
                        LD(KE[:], s_ke[rows, g * 512:(g + 1) * 512], [bKE], bKE)
                        VB, bVB = VBr.next()
                        vsrc = s_vb[g * 512:(g + 1) * 512, hh * 128:(hh + 1) * 128].rearrange("(c s) v -> s c v", s=64)
                        LD(VB[:], vsrc, [bVB], bVB)
                        SBG, bSBG = SBGr.next()
                        LD(SBG[:], s_sbg[rows, g * 512:(g + 1) * 512], [bSBG], bSBG)
                        out.append((KE, bKE, VB, bVB, SBG, bSBG))
                    return out

                nxt = group_loads(0)
                for g in range(NG):
                    gl_ = nxt
                    if g + 1 < NG:
                        nxt = group_loads(g + 1)
                    KTl, OHl = [], []
                    for hh in range(HB):
                        KE, bKE, VB, bVB, SBG, bSBG = gl_[hh]
                        KTps, bKTp = KTp[hh]
                        for c in range(8):
                            Sc.op("pe", lambda e, KTps=KTps, c=c, KE=KE: e.transpose(
                                out=KTps[:, c, :], in_=KE[:, c * 64:(c + 1) * 64], identity=ident[:]),
                                [bKE, b_ident], [bKTp])
                        KTs, bKT = KTr.next()
                        CP("act", KTs[:], KTps[:], [bKTp], [bKT])
                        KTl.append((KTs, bKT))
                        OHl.append(OHp[hh].next())
                    for c in range(8):
                        ch = g * 8 + c
                        cs = slice(ch * 64, (ch + 1) * 64)
                        Asl = []
                        for hh in range(HB):
                            QD, b_QD = QDs[hh]; KD, b_KD = KDs[hh]
                            A, bA = ATp[hh]
                            MM(A, KD[:, cs], QD[:, cs], True, True, [b_KD, b_QD], [bA])
                            As, bAs = ATs.next()
                            TT("dve", As[:], A, caus[:], ALU.mult, [bA, b_caus], [bAs])
                            Asl.append((As, bAs))
                        for hh in range(HB):
                            KE, bKE, VB, bVB, SBG, bSBG = gl_[hh]
                            QD, b_QD = QDs[hh]
                            KTs, bKT = KTl[hh]; OH, bOH = OHl[hh]
                            As, bAs = Asl[hh]
                            S0, bS0 = Sbs[hh][cur[hh]]
                            S1, bS1 = Sbs[hh][1 - cur[hh]]
                            MM(OH[:, c * 64:(c + 1) * 64], VB[:, c, :], As[:], True, False, [bVB, bAs], [bOH])
                            MM(OH[:, c * 64:(c + 1) * 64], S0[:], QD[:, cs], False, True, [bS0, b_QD], [bOH])
                            dS, bdS = dSp[hh]
                            MM(dS, KTs[:, c, :], VB[:, c, :], True, True, [bKT, bVB], [bdS])
                            STT("dve", S1[:], S0[:], EL[:, hh, ch:ch + 1], dS, ALU.mult, ALU.add,
                                [bS0, b_EL, bdS], [bS1])
                            cur[hh] = 1 - cur[hh]
                    for hh in range(HB):
                        KE, bKE, VB, bVB, SBG, bSBG = gl_[hh]
                        OH, bOH = OHl[hh]
                        ACT(osq[:], OH[:], AF.Square, [bOH], [b_osq])
                        MM(NSp[:], ones[:], osq[:], True, True, [b_ones, b_osq], [b_NSp])
                        ACT(ort[:], NSp[:], AF.Ln, [b_NSp], [b_ort], bias=EPS, scale=1.0 / 128)
                        ACT(ors[:], ort[:], AF.Exp, [b_ort], [b_ors], scale=-0.5)
                        STT("dve", ybf[:], OH[:], go[:, l, hh:hh + 1], ors[:], ALU.mult, ALU.mult,
                            [bOH, b_go, b_ors], [b_ybf])
                        YB, bYB = YBr.next()
                        TT("pool", YB[:], ybf[:], SBG[:], ALU.mult, [b_ybf, bSBG], [bYB])
                        ST(XS[2 + hh, :, g * 512:(g + 1) * 512], YB[:], [bYB], bYB)
                Sc.emit()

            with ExitStack() as st:
                sb, ps = mk(st)
                WA = sb("WA", [128, 4, D], BF16)
                WB_ = sb("WB", [128, 4, D], BF16)
                WO = sb("WO", [128, 8, D], BF16)
                WP = sb("WP", [128, 2, D], BF16)
                WG = sb("WG", [128, 8, D], BF16)
                bW = {}

                def wtok(name, c, j):
                    return bW[(name, (c // 2) * 2, (j // 2) * 256)]
                wst = [(sb(f"wstc{i}", [128, 2, 256]), Buf(f"wstc{i}")) for i in range(4)]
                k = 0
                for (dst, nm, src, nch) in ((WA, "WA", w_up_a[l], 4), (WB_, "WB", w_up_b[l], 4),
                                            (WO, "WO", w_out[l], 8), (WP, "WP", w_ple[l], 2),
                                            (WG, "WG", w_pg[l], 8)):
                    sv = src.rearrange("(c p) n -> p c n", p=128)
                    for c2 in range(0, nch, 2):
                        for n2 in range(0, D, 256):
                            t, b = wst[k % 4]
                            bW[(nm, c2, n2)] = Buf(f"{nm}_{c2}_{n2}")
                            LD(t[:], sv[:, c2:c2 + 2, n2:n2 + 256], [b], b)
                            CP(("act", "dve")[k % 2], dst[:, c2:c2 + 2, n2:n2 + 256], t[:], [b], [bW[(nm, c2, n2)]])
                            k += 1
                INS = []
                for i in range(2):
                    INS.append(dict(
                        YAG=(sb(f"YAG{i}", [128, 4, 512], BF16), Buf(f"YAG{i}")),
                        YBG=(sb(f"YBG{i}", [128, 4, 512], BF16), Buf(f"YBG{i}")),
                        SGA=(sb(f"SGA{i}", [128, 8, 512], BF16), Buf(f"SGA{i}")),
                        SGB=(sb(f"SGB{i}", [128, 8, 512], BF16), Buf(f"SGB{i}")),
                        PB=(sb(f"PB{i}", [128, 2, 512], BF16), Buf(f"PB{i}"))))
                H = sb("Hc", [128, 8, 512]); b_H = Buf("Hc")
                PF = sb("PF", [128, 2, 512]); b_PF = Buf("PF")
                MG = sb("MG", [128, 8, 512], BF16); b_MG = Buf("MG")
                HM = sb("HM", [128, 8, 512]); b_HM = Buf("HM")
                HMb = sb("HMb", [128, 8, 512], BF16); b_HMb = Buf("HMb")
                HN = Ring([(sb(f"HN{i}", [128, 512]), Buf(f"HN{i}")) for i in range(2)])
                HNb = Ring([(sb(f"HNb{i}", [128, 512], BF16), Buf(f"HNb{i}")) for i in range(2)])
                tmp = Ring([(sb(f"tmp{i}", [128, 512]), Buf(f"tmp{i}")) for i in range(4)])
                PS = Ring([(ps(f"PSc{i}", [128, 512]), Buf(f"PSc{i}")) for i in range(8)])
                hv = h_own.rearrange("(c p) t -> p c t", p=128)
                hd = h_dst.rearrange("(c p) t -> p c t", p=128)
                pv = pT[l].rearrange("(c p) t -> p c t", p=128)
                b_XG = [Buf(f"XG{k}") for k in range(4)]
                for k in (2, 3):
                    AG(XG[k], XS[k], Buf(f"agx{k}"), w=[b_XG[k]])
                b_XO = Buf("XO")
                for kx_ in range(4):
                    for rk in range(2):
                        LD(XO[kx_, rk * 128:(rk + 1) * 128, :], XG[kx_, rk * 128:(rk + 1) * 128, bass.ds(off_own, SH)],
                           [b_XO], b_XO, r=[b_XG[kx_]])
                b_HS = [Buf(f"HSp{q}") for q in range(NPC)]

                def loads(g):
                    I = INS[g % 2]
                    ts_ = slice(g * 512, (g + 1) * 512)
                    for c in range(4):
                        rk, kk = c // 2, c % 2
                        LD(I["YAG"][0][:, c, :], XO[kk, rk * 128:(rk + 1) * 128, ts_], [I["YAG"][1]], I["YAG"][1], r=[b_XO])
                        LD(I["YBG"][0][:, c, :], XO[2 + kk, rk * 128:(rk + 1) * 128, ts_], [I["YBG"][1]], I["YBG"][1], r=[b_XO])
                    LD(I["SGA"][0][:], s_sga.rearrange("(c p) t -> p c t", p=128)[:, :, ts_], [I["SGA"][1]], I["SGA"][1])
                    LD(I["SGB"][0][:], s_sgb.rearrange("(c p) t -> p c t", p=128)[:, :, ts_], [I["SGB"][1]], I["SGB"][1])
                    LD(PF[:], pv[:, :, ts_], [b_PF], b_PF)
                    CP("act", I["PB"][0][:], PF[:], [b_PF], [I["PB"][1]])

                loads(0)
                LD(H[:], hv[:, :, 0:512], [b_H], b_H)
                for g in range(NGO):
                    ts_ = slice(g * 512, (g + 1) * 512)
                    I = INS[g % 2]
                    YAG, b_YAG = I["YAG"]; YBG, b_YBG = I["YBG"]; SGA, b_SGA = I["SGA"]; SGB, b_SGB = I["SGB"]
                    PB, b_PB = I["PB"]
                    if g + 1 < NGO:
                        loads(g + 1)
                    for j in range(8):
                        Pa, bPa = PS.next()
                        for c in range(4):
                            MM(Pa[:], WA[:, c, j * 128:(j + 1) * 128], YAG[:, c, :], c == 0, c == 3,
                               [wtok("WA", c, j), b_YAG], [bPa])
                        Pb, bPb = PS.next()
                        for c in range(4):
                            MM(Pb[:], WB_[:, c, j * 128:(j + 1) * 128], YBG[:, c, :], c == 0, c == 3,
                               [wtok("WB", c, j), b_YBG], [bPb])
                        t1, bt1 = tmp.next()
                        TT("dve", t1[:], Pa[:], SGA[:, j, :], ALU.mult, [bPa, b_SGA], [bt1])
                        t2, bt2 = tmp.next()
                        TT("dve", t2[:], Pb[:], SGB[:, j, :], ALU.mult, [bPb, b_SGB], [bt2])
                        TT("pool", MG[:, j, :], t1[:], t2[:], ALU.add, [bt1, bt2], [b_MG])
                    for j in range(8):
                        Po, bPo = PS.next()
                        for c in range(8):
                            MM(Po[:], WO[:, c, j * 128:(j + 1) * 128], MG[:, c, :], c == 0, c == 7,
                               [wtok("WO", c, j), b_MG], [bPo])
                        TT("dve", HM[:, j, :], Po[:], H[:, j, :], ALU.add, [bPo, b_H], [b_HM])
                        CP("act", HMb[:, j, :], HM[:, j, :], [b_HM], [b_HMb])
                    if g + 1 < NGO:
                        LD(H[:], hv[:, :, (g + 1) * 512:(g + 2) * 512], [b_H], b_H)
                    q, col = (g * 512) // PIECE, (g * 512) % PIECE
                    for j in range(8):
                        Pp, bPp = PS.next()
                        for c in range(2):
                            MM(Pp[:], WP[:, c, j * 128:(j + 1) * 128], PB[:, c, :], c == 0, c == 1,
                               [wtok("WP", c, j), b_PB], [bPp])
                        Pg, bPg = PS.next()
                        for c in range(8):
                            MM(Pg[:], WG[:, c, j * 128:(j + 1) * 128], HMb[:, c, :], c == 0, c == 7,
                               [wtok("WG", c, j), b_HMb], [bPg])
                        sg, bsg = tmp.next()
                        ACT(sg[:], Pg[:], AF.Sigmoid, [bPg], [bsg])
                        t1, bt1 = tmp.next()
                        TT("dve", t1[:], Pp[:], sg[:], ALU.mult, [bPp, bsg], [bt1])
                        hn, bhn = HN.next()
                        TT("pool", hn[:], t1[:], HM[:, j, :], ALU.add, [bt1, b_HM], [bhn])
                        ST(hd[:, j, ts_], hn[:], [bhn], bhn, q="sp")
                        if not last:
                            hb, bhb = HNb.next()
                            CP("act", hb[:], hn[:], [bhn], [bhb])
                            ST(HS[q, j * 128:(j + 1) * 128, col:col + 512], hb[:], [bhb], bhb, w=[b_HS[q]], q="sp")
                    if not last and (g + 1) * 512 % PIECE == 0:
                        AG(HG[q], HS[q], Buf(f"agh{q}"), r=[b_HS[q]])
                Sc.emit()

        print("total ops", Sc.total, "sems", Sc.nsem)
    return nc


def host_consts(S):
    NT = S // 128
    bf = ml_dtypes.bfloat16
    c = {}
    c["c_ident"] = np.eye(128, dtype=np.float32).astype(bf)
    blk = np.zeros((128, 128), np.float32)
    blk[:64, :64] = 1.0
    blk[64:, 64:] = 1.0
    c["c_blk"] = blk.astype(bf)
    oh = np.zeros((32, S), np.float32)
    for n in range(S // 256):
        oh[n, n * 256:(n + 1) * 256] = 1.0
    c["c_onehot"] = oh.astype(bf)
    c["c_causal"] = np.triu(np.ones((64, 64), np.float32))
    sm = np.ones((128, 512), np.float32)
    sm[:, ::64] = 0.0
    c["c_scan"] = sm
    fut = np.zeros((NT, 32), np.float32)
    neg = np.full((NT, 32), NEG, np.float32)
    for t in range(NT):
        b = t // 2
        fut[t, b:] = NEG
        neg[t, b] = 0.0
    c["c_fut"] = np.ascontiguousarray(np.broadcast_to(fut[None], (128, NT, 32))).astype(bf)
    c["c_neg"] = np.ascontiguousarray(np.broadcast_to(neg[None], (128, NT, 32))).astype(bf)
    return c


def host_strips(rel_bias, heads):
    i = np.arange(128)[:, None]
    u = np.arange(STRIP)[None, :]
    rel = u - 384 - i
    bucket = t5_bucket_np(rel)
    strip = np.empty((128, len(heads), STRIP), np.float32)
    for k, h in enumerate(heads):
        g = rel_bias[:, h][bucket]
        strip[:, k, :] = np.where(rel >= 0, g, np.float32(NEG))
    c31 = np.ascontiguousarray(np.broadcast_to(rel_bias[31, heads][None, :], (128, len(heads)))).astype(np.float32)
    return strip, c31


def host_inputs(b, r, S, x, p, norm_gain, w_in, q_norm_gain, k_norm_gain, rel_bias, hgrn_lb_logits,
                hgrn_out_gain, w_up_a, w_up_b, w_out, w_ple, w_ple_gate, consts):
    L = w_in.shape[0]
    SH = S // 2
    m = dict(consts)
    m["xT"] = np.ascontiguousarray(x[b, :S].T)
    m["xT_own"] = np.ascontiguousarray(x[b, r * SH:(r + 1) * SH].T)
    m["pT"] = np.ascontiguousarray(np.transpose(p[:, b, r * SH:(r + 1) * SH, :], (0, 2, 1)))
    cols = []
    for blk0 in range(0, 4096, 512):
        cols.append(np.arange(blk0 + r * 256, blk0 + (r + 1) * 256))
    cols.append(np.arange(4096, 6144))
    cols = np.concatenate(cols)
    m["w_in"] = np.ascontiguousarray(w_in[:, :, cols])
    m["w_up_a"] = w_up_a
    m["w_up_b"] = w_up_b
    m["w_out"] = w_out
    m["w_ple"] = w_ple
    m["w_pg"] = w_ple_gate
    m["g_norm"] = np.ascontiguousarray(np.transpose(norm_gain.reshape(L, 8, 128), (2, 0, 1)))
    m["g_q"] = np.ascontiguousarray(np.concatenate([q_norm_gain, q_norm_gain], axis=1).T)
    m["g_k"] = np.ascontiguousarray(np.concatenate([k_norm_gain, k_norm_gain], axis=1).T)
    m["g_o"] = np.ascontiguousarray(np.transpose(hgrn_out_gain.reshape(L, 4, 128)[:, 2 * r:2 * r + 2], (2, 0, 1)))
    m["lbl"] = np.ascontiguousarray(np.transpose(hgrn_lb_logits.reshape(L, 4, 128)[:, 2 * r:2 * r + 2], (2, 0, 1)))
    strip, c31 = host_strips(rel_bias, list(range(4 * r, 4 * r + 4)))
    m["strip_raw"] = strip
    m["c31"] = c31
    return m


_NC_CACHE = {}


def kernel(x, p, norm_gain, w_in, q_norm_gain, k_norm_gain, rel_bias, hgrn_lb_logits,
           hgrn_out_gain, w_up_a, w_up_b, w_out, w_ple, w_ple_gate):
    args = [np.asarray(a, dtype=np.float32) for a in (
        x, p, norm_gain, w_in, q_norm_gain, k_norm_gain, rel_bias, hgrn_lb_logits,
        hgrn_out_gain, w_up_a, w_up_b, w_out, w_ple, w_ple_gate)]
    x = args[0]
    B, S, _ = x.shape
    if S not in _NC_CACHE:
        _NC_CACHE[S] = build(S)
    nc = _NC_CACHE[S]
    consts = host_consts(S)
    in_maps = [host_inputs(i // 2, i % 2, S, *args, consts) for i in range(2 * B)]
    res = run_bass_kernel_spmd(nc, in_maps, core_ids=list(range(2 * B)))
    SH = S // 2
    out = np.empty((B, S, D), np.float32)
    for i in range(2 * B):
        out[i // 2, (i % 2) * SH:(i % 2 + 1) * SH, :] = res.results[i]["hT_out"].T
    return out
```

```python
import math
from contextlib import ExitStack

import numpy as np
import ml_dtypes
import concourse.bass as bass
import concourse.mybir as mybir
from concourse.bass_utils import run_bass_kernel_spmd

F32 = mybir.dt.float32
BF16 = mybir.dt.bfloat16
AF = mybir.ActivationFunctionType
ALU = mybir.AluOpType
AX = mybir.AxisListType

SEM_CHUNK = 20000
D = 1024
NEG = -30000.0
STRIP = 2432
EPS = 1e-6


class Buf:
    __slots__ = ("name", "last_writer", "dma_writers", "readers", "dma_readers")

    def __init__(self, name):
        self.name = name
        self.clear()

    def clear(self):
        self.last_writer = None
        self.dma_writers = []
        self.readers = {}
        self.dma_readers = []


class Sched:
    def __init__(self, nc, stack):
        self.nc = nc
        self.stack = stack
        self.ops = []
        self.eng = {"pe": nc.tensor, "act": nc.scalar, "dve": nc.vector,
                    "pool": nc.gpsimd, "sp": nc.sync}
        self.nsem = 0
        self.eng_count = {}
        self.eng_sems = {}
        self.dma_pool = []
        self.dma_free = []
        self.key_slot = {}
        self.waited = {}
        self.total = 0

    def new_sem(self, name):
        self.nsem += 1
        return self.stack.enter_context(self.nc.semaphore(name))

    def op(self, eng, fn, reads=(), writes=()):
        self.ops.append(["c", eng, fn, tuple(reads), tuple(writes), None, 1])

    def dma(self, queue, fn, reads=(), writes=(), key=None, inc=16):
        assert key is not None
        self.ops.append(["d", queue, fn, tuple(reads), tuple(writes), key, inc])

    def emit(self):
        ops = self.ops
        n = len(ops)
        deps = [None] * n
        signaling = [False] * n
        last_of_eng = {}
        for i, o in enumerate(ops):
            d = set()
            isd = o[0] == "d"
            for b in o[3]:
                if b.last_writer is not None:
                    d.add(b.last_writer)
                d.update(b.dma_writers)
            for b in o[4]:
                if b.last_writer is not None:
                    d.add(b.last_writer)
                d.update(b.readers.values())
                d.update(b.dma_readers)
                if not isd:
                    d.update(b.dma_writers)
            d.discard(i)
            for b in o[3]:
                if isd:
                    b.dma_readers.append(i)
                else:
                    b.readers[o[1]] = i
            for b in o[4]:
                if isd:
                    b.dma_writers.append(i)
                else:
                    b.last_writer = i
                    b.dma_writers = []
                    b.readers = {}
                    b.dma_readers = []
            deps[i] = d
            for j in d:
                signaling[j] = True
            if o[0] == "c":
                last_of_eng[o[1]] = i
        for e, i in last_of_eng.items():
            signaling[i] = True
        sig = [None] * n
        for i, o in enumerate(ops):
            if o[0] == "c":
                if not signaling[i]:
                    continue
                e = o[1]
                c = self.eng_count.get(e, 0)
                k = c // SEM_CHUNK
                if (e, k) not in self.eng_sems:
                    self.eng_sems[(e, k)] = self.new_sem(f"s_{e}_{k}")
                self.eng_count[e] = c + 1
                sig[i] = (self.eng_sems[(e, k)], c - k * SEM_CHUNK + 1, 1, ("e", e, k))
            else:
                key = o[5]
                if key not in self.key_slot:
                    if self.dma_free:
                        s = self.dma_free.pop()
                    else:
                        self.dma_pool.append([self.new_sem(f"d_{len(self.dma_pool)}"), 0])
                        s = len(self.dma_pool) - 1
                    self.key_slot[key] = s
                s = self.key_slot[key]
                self.dma_pool[s][1] += o[6]
                sig[i] = (self.dma_pool[s][0], self.dma_pool[s][1], o[6], ("k", s))
        waited = self.waited
        for i, o in enumerate(ops):
            e = o[1]
            engine = self.eng[e]
            w = waited.setdefault(e, {})
            for j in sorted(deps[i]):
                pj = ops[j]
                if pj[0] == "c" and pj[1] == "pe" and e == "pe" and o[0] == "c":
                    continue
                sem, val, _, sid = sig[j]
                if w.get(sid, 0) >= val:
                    continue
                w[sid] = val
                engine.wait_ge(sem, val)
            ins = o[2](engine)
            if sig[i] is not None:
                ins.then_inc(sig[i][0], sig[i][2])
        finals = []
        for (e, k), sem in self.eng_sems.items():
            c = self.eng_count.get(e, 0)
            if c // SEM_CHUNK == k and c - k * SEM_CHUNK > 0:
                finals.append((sem, c - k * SEM_CHUNK, ("e", e, k)))
            elif c // SEM_CHUNK > k:
                finals.append((sem, SEM_CHUNK, ("e", e, k)))
        for s, (sem, c) in enumerate(self.dma_pool):
            if c > 0:
                finals.append((sem, c, ("k", s)))
        for e in ("sp", "pe", "act", "dve", "pool"):
            w = waited.setdefault(e, {})
            for sem, val, sid in finals:
                if w.get(sid, 0) >= val:
                    continue
                w[sid] = val
                self.eng[e].wait_ge(sem, val)
        for o in ops:
            for b in o[3] + o[4]:
                b.clear()
        self.key_slot = {}
        self.dma_free = list(range(len(self.dma_pool)))
        self.total += n
        self.ops = []
        return n


class Ring:
    def __init__(self, items):
        self.items = items
        self.i = 0

    def next(self):
        it = self.items[self.i % len(self.items)]
        self.i += 1
        return it


def t5_bucket_np(rel):
    n = np.maximum(rel, 0)
    nf = np.maximum(n, 16).astype(np.float32)
    large = 16 + (np.log(nf / np.float32(16)) / np.float32(math.log(128)) * np.float32(16)).astype(np.int32)
    large = np.minimum(large, 31)
    return np.where(n < 16, n, large)


def build(S, L=2, debug=False, groups=None):
    NT, NB, NG, NCH = S // 128, S // 256, S // 512, S // 64
    SH, NGO = S // 2, S // 1024
    HA, HB = 4, 2
    if groups is None:
        groups = [[0, 1], [2, 3], [4, 5], [6, 7]]
    nc = bass.Bass("TRN2", target_bir_lowering=False)

    def din(name, shape, dt=F32):
        return nc.dram_tensor(name, list(shape), dt, kind="ExternalInput").ap()

    def dscr(name, shape, dt=BF16, out=False):
        kind = "Internal"
        return nc.dram_tensor(name, list(shape), dt, kind=kind).ap()

    xT = din("xT", [D, S])
    xT_own = din("xT_own", [D, SH])
    pT = din("pT", [L, 256, SH])
    w_in = din("w_in", [L, D, 4096])
    w_up_a = din("w_up_a", [L, 512, D])
    w_up_b = din("w_up_b", [L, 512, D])
    w_out = din("w_out", [L, D, D])
    w_ple = din("w_ple", [L, 256, D])
    w_pg = din("w_pg", [L, D, D])
    g_norm = din("g_norm", [128, L, 8])
    g_q = din("g_q", [128, L])
    g_k = din("g_k", [128, L])
    g_o = din("g_o", [128, L, HB])
    lbl = din("lbl", [128, L, HB])
    strip_raw = din("strip_raw", [128, HA, STRIP])
    c31 = din("c31", [128, HA])
    c_ident = din("c_ident", [128, 128], BF16)
    c_blk = din("c_blk", [128, 128], BF16)
    c_onehot = din("c_onehot", [32, S], BF16)
    c_causal = din("c_causal", [64, 64])
    c_scan = din("c_scan", [128, 512])
    c_fut = din("c_fut", [128, NT, 32], BF16)
    c_neg = din("c_neg", [128, NT, 32], BF16)

    hT_out = nc.dram_tensor("hT_out", [D, SH], F32, kind="ExternalOutput").ap()
    h_mid = dscr("h_mid", [D, SH], F32)
    s_qn = dscr("s_qn", [256, S])
    s_kn = dscr("s_kn", [256, S])
    s_v = dscr("s_v", [S, 256])
    s_sag = dscr("s_sag", [256, S])
    s_qd = dscr("s_qd", [256, S])
    s_kd = dscr("s_kd", [256, S])
    s_ke = dscr("s_ke", [256, S])
    s_vb = dscr("s_vb", [S, 256])
    s_sbg = dscr("s_sbg", [256, S])
    s_sga = dscr("s_sga", [D, SH])
    s_sgb = dscr("s_sgb", [D, SH])
    XS = dscr("XS", [4, 128, S], out=True)
    XG = dscr("XG", [4, 2 * 128, S], out=True)
    XO = dscr("XO", [4, 2 * 128, SH])
    NQ = 4
    QW = S // NQ
    XSH = dscr("XSH", [HB, NQ, 128, QW])
    XGH = dscr("XGH", [HB, NQ * 2 * 128, QW])
    XOH = dscr("XOH", [HB, 2, 128, SH])
    PIECE = max(512, SH // 4)
    NPC = SH // PIECE
    HS = dscr("HS", [NPC, D, PIECE])
    HG = dscr("HG", [NPC, 2 * D, PIECE])

    off_own = (nc.sync.partition_id() % 2) * SH

    with ExitStack() as outer:
        Sc = Sched(nc, outer)
        uniq = [0]

        def mk(stack):
            uniq[0] += 1
            tag = f"u{uniq[0]}_"

            def sb(name, shape, dt=F32):
                return stack.enter_context(nc.sbuf_tensor(tag + name, list(shape), dt))

            def ps(name, shape, dt=F32):
                return stack.enter_context(nc.psum_tensor(tag + name, list(shape), dt))
            return sb, ps

        def MM(out, lhsT, rhs, start, stop, r, w):
            Sc.op("pe", lambda e: e.matmul(out, lhsT=lhsT, rhs=rhs, start=start, stop=stop), r, w)

        def ACT(out, in_, func, r, w, bias=0.0, scale=1.0, eng="act"):
            Sc.op(eng, lambda e: e.activation(out=out, in_=in_, func=func, bias=bias, scale=scale), r, w)

        def TT(eng, out, in0, in1, op, r, w):
            Sc.op(eng, lambda e: e.tensor_tensor(out=out, in0=in0, in1=in1, op=op), r, w)

        def TSC(eng, out, in0, s1, s2, op0, op1, r, w):
            if s2 is None:
                Sc.op(eng, lambda e: e.tensor_scalar(out=out, in0=in0, scalar1=s1, scalar2=None, op0=op0), r, w)
            else:
                Sc.op(eng, lambda e: e.tensor_scalar(out=out, in0=in0, scalar1=s1, scalar2=s2, op0=op0, op1=op1), r, w)

        def STT(eng, out, in0, scalar, in1, op0, op1, r, w):
            Sc.op(eng, lambda e: e.scalar_tensor_tensor(out=out, in0=in0, scalar=scalar, in1=in1, op0=op0, op1=op1), r, w)

        def CP(eng, out, in_, r, w):
            if eng == "act":
                Sc.op("act", lambda e: e.activation(out=out, in_=in_, func=AF.Copy), r, w)
            else:
                Sc.op(eng, lambda e: e.tensor_copy(out=out, in_=in_), r, w)

        def LD(out, in_, w, key, r=(), q="sp"):
            Sc.dma(q, lambda e: e.dma_start(out=out, in_=in_), r, w, key)

        def ST(out, in_, r, key, w=(), q="pool"):
            Sc.dma(q, lambda e: e.dma_start(out=out, in_=in_), r, w, key)

        def AG(out, in_, key, r=(), w=()):
            Sc.dma("pool", lambda e: e.collective_compute(
                "AllGather", ALU.bypass, replica_groups=groups, ins=[in_], outs=[out]), r, w, key, inc=1)

        sbP, _ = mk(outer)
        ident = sbP("ident", [128, 128], BF16); b_ident = Buf("ident")
        blk = sbP("blk", [128, 128], BF16); b_blk = Buf("blk")
        ones = sbP("ones", [128, 128], BF16); b_ones = Buf("ones")
        gn = sbP("gn", [128, L, 8]); b_gn = Buf("gn")
        gq = sbP("gq", [128, L]); b_gq = Buf("gq")
        gk = sbP("gk", [128, L]); b_gk = Buf("gk")
        go = sbP("go", [128, L, HB]); b_go = Buf("go")
        lb = sbP("lb", [128, L, HB]); b_lb = Buf("lb")
        oml = sbP("oml", [128, L, HB]); b_oml = Buf("oml")
        lbe = sbP("lbe", [128, L, HB]); b_lbe = Buf("lbe")
        lbs = sbP("lbs", [128, HB]); b_lbs = Buf("lbs")
        KS = sbP("KS", [128, 2, NB]); b_KS = Buf("KS")
        EL = sbP("EL", [128, HB, NCH]); b_EL = Buf("EL")
        caus = sbP("caus", [64, 64]); b_caus = Buf("caus")
        scanm = sbP("scanm", [128, 512]); b_scanm = Buf("scanm")

        with ExitStack() as st:
            sb, ps = mk(st)
            LD(ident[:], c_ident, [b_ident], b_ident)
            LD(blk[:], c_blk, [b_blk], b_blk)
            Sc.op("pool", lambda e: e.memset(ones[:], 1.0), (), [b_ones])
            LD(gn[:], g_norm, [b_gn], b_gn)
            LD(gq[:], g_q, [b_gq], b_gq)
            LD(gk[:], g_k, [b_gk], b_gk)
            LD(go[:], g_o, [b_go], b_go)
            LD(lbe[:], lbl, [b_lbe], b_lbe)
            LD(caus[:], c_causal, [b_caus], b_caus)
            LD(scanm[:], c_scan, [b_scanm], b_scanm)
            TSC("dve", gq[:], gq[:], 0.125, None, ALU.mult, None, [b_gq], [b_gq])
            ACT(lbe[:], lbe[:], AF.Exp, [b_lbe], [b_lbe])
            CP("dve", lbs[:], lbe[:, 0, :], [b_lbe], [b_lbs])
            for l in range(1, L):
                TT("dve", lbs[:], lbs[:], lbe[:, l, :], ALU.add, [b_lbs, b_lbe], [b_lbs])
            Sc.op("dve", lambda e: e.reciprocal(out=lbs[:], in_=lbs[:]), [b_lbs], [b_lbs])
            Sc.op("dve", lambda e: e.memset(lb[:, 0, :], 0.0), (), [b_lb])
            for l in range(1, L):
                TT("dve", lb[:, l, :], lb[:, l - 1, :], lbe[:, l, :], ALU.add, [b_lb, b_lbe], [b_lb])
            for l in range(1, L):
                TT("dve", lb[:, l, :], lb[:, l, :], lbs[:], ALU.mult, [b_lb, b_lbs], [b_lb])
            for l in range(L):
                TSC("dve", oml[:, l, :], lb[:, l, :], -1.0, 1.0, ALU.mult, ALU.add, [b_lb], [b_oml])
            Sc.emit()

        for l in range(L):
            first, last = (l == 0), (l == L - 1)
            h_own = xT_own if first else h_mid
            h_dst = hT_out if last else h_mid

            with ExitStack() as st:
                sb, ps = mk(st)
                Wb = sb("Wb", [128, 8, 4096], BF16)
                b_Wbs = [Buf(f"Wb{i}") for i in range(32)]
                wst = [(sb(f"wst{i}", [128, 8, 128]), Buf(f"wst{i}")) for i in range(4)]
                wv = w_in[l].rearrange("(c p) n -> p c n", p=128)
                for i in range(32):
                    t, b = wst[i % 4]
                    LD(t[:], wv[:, :, i * 128:(i + 1) * 128], [b], b)
                    CP(("act", "dve")[i % 2], Wb[:, :, i * 128:(i + 1) * 128], t[:], [b], [b_Wbs[i]])
                HDT = F32 if first else BF16
                H = sb("H", [128, 8, 512], HDT); b_H = Buf("H")
                H2 = sb("H2", [128, 8, 512]); b_H2 = Buf("H2")
                SQ = sb("SQ", [128, 8, 512], BF16); b_SQ = Buf("SQ")
                XNs = [(sb(f"XN{i}", [128, 8, 512], BF16), Buf(f"XN{i}")) for i in range(2)]
                rstd = sb("rstd", [128, 512]); b_rstd = Buf("rstd")
                stage = Ring([(sb(f"stg{i}", [128, 512], BF16), Buf(f"stg{i}")) for i in range(6)])
                f32r = Ring([(sb(f"f32r{i}", [128, 512]), Buf(f"f32r{i}")) for i in range(8)])
                QB = [(sb(f"QB{i}", [128, 512]), Buf(f"QB{i}")) for i in range(HB)]
                SG = [(sb(f"SG{i}", [128, 512]), Buf(f"SG{i}")) for i in range(HB)]
                sqq = Ring([(sb(f"sqq{i}", [128, 512], BF16), Buf(f"sqq{i}")) for i in range(2)])
                p_ss = ps("p_ss", [128, 512]); b_pss = Buf("p_ss")
                PS = Ring([(ps(f"PSa{i}", [128, 512]), Buf(f"PSa{i}")) for i in range(5)])
                BS = Ring([(ps(f"BSa{i}", [128, 512]), Buf(f"BSa{i}")) for i in range(2)])
                Sc.op("dve", lambda e: e.memset(KS[:], 0.0), (), [b_KS])

                def load_h(g):
                    if first:
                        LD(H[:], xT.rearrange("(c p) t -> p c t", p=128)[:, :, g * 512:(g + 1) * 512], [b_H], b_H)
                    else:
                        half, tok = g // NGO, (g % NGO) * 512
                        q, col = tok // PIECE, tok % PIECE
                        src = HG[q, half * D:(half + 1) * D, col:col + 512].rearrange("(c p) t -> p c t", p=128)
                        LD(H[:], src, [b_H], b_H)

                def norm_part1(g):
                    load_h(g)
                    ACT(SQ[:].rearrange("p c t -> p (c t)"), H[:].rearrange("p c t -> p (c t)"),
                        AF.Square, [b_H], [b_SQ])

                def norm_part2(Hs, bH, XN, b_XN):
                    for c in range(8):
                        MM(p_ss[:], ones[:], SQ[:, c, :], c == 0, c == 7, [b_ones, b_SQ], [b_pss])
                    ACT(rstd[:], p_ss[:], AF.Ln, [b_pss], [b_rstd], bias=EPS, scale=1.0 / D)
                    ACT(rstd[:], rstd[:], AF.Exp, [b_rstd], [b_rstd], scale=-0.5)
                    for c in range(8):
                        STT("dve", XN[:, c, :], Hs[:, c, :], gn[:, l, c:c + 1], rstd[:],
                            ALU.mult, ALU.mult, [bH, b_gn, b_rstd], [b_XN])

                norm_part1(0)
                norm_part2(H, b_H, *XNs[0])
                for g in range(NG):
                    XN, b_XN = XNs[g % 2]

                    def proj_fm(c0):
                        P, bP = PS.next()
                        for c in range(8):
                            MM(P[:], Wb[:, c, c0:c0 + 128], XN[:, c, :], c == 0, c == 7,
                               [b_Wbs[c0 // 128], b_XN], [bP])
                        return P, bP

                    def store_fm(dst, j, t, b):
                        ST(dst[j * 128:(j + 1) * 128, g * 512:(g + 1) * 512], t[:], [b], b)

                    def qk_tail(j, P, bP, s2, bs2):
                        isk = j >= 2
                        Bp, bB = BS.next()
                        MM(Bp[:], blk[:], s2[:], True, True, [b_blk, bs2], [bB])
                        r1, br1 = f32r.next()
                        ACT(r1[:], Bp[:], AF.Ln, [bB], [br1], bias=EPS, scale=1.0 / 64)
                        ACT(r1[:], r1[:], AF.Exp, [br1], [br1], scale=-0.5)
                        o, bo = stage.next()
                        gg = gk if isk else gq
                        STT("dve", o[:], P[:], gg[:, l:l + 1], r1[:], ALU.mult, ALU.mult,
                            [bP, b_gk if isk else b_gq, br1], [bo])
                        if isk:
                            Sc.op("dve", lambda e, o=o, j=j, g=g: e.tensor_reduce(
                                out=KS[:, j - 2, 2 * g:2 * g + 2],
                                in_=o[:].rearrange("p (b t) -> p b t", t=256),
                                axis=AX.X, op=ALU.add), [bo], [b_KS])
                        store_fm(s_kn if isk else s_qn, j % 2, o, bo)

                    pend = None
                    for j in range(4):
                        P, bP = proj_fm(j * 128)
                        s2, bs2 = sqq.next()
                        ACT(s2[:], P[:], AF.Square, [bP], [bs2])
                        if pend is not None:
                            qk_tail(*pend)
                        pend = (j, P, bP, s2, bs2)
                    firstv = True
                    for (c0, dst) in ((512, s_v), (1536, s_vb)):
                        for tt in range(4):
                            P, bP = PS.next()
                            for c in range(8):
                                MM(P[:, 0:256], XN[:, c, tt * 128:(tt + 1) * 128], Wb[:, c, c0:c0 + 256],
                                   c == 0, c == 7, [b_XN, b_Wbs[c0 // 128], b_Wbs[c0 // 128 + 1]], [bP])
                            if firstv:
                                qk_tail(*pend)
                                firstv = False
                            o, bo = stage.next()
                            CP("act", o[:, 0:256], P[:, 0:256], [bP], [bo])
                            ST(dst[g * 512 + tt * 128:g * 512 + (tt + 1) * 128, :], o[:, 0:256], [bo], bo)
                    if g + 1 < NG:
                        norm_part1(g + 1)
                    for jj in range(HB):
                        P, bP = proj_fm(1280 + jj * 128)
                        ACT(SG[jj][0][:], P[:], AF.Sigmoid, [bP], [SG[jj][1]])
                    if g + 1 < NG:
                        norm_part2(H, b_H, *XNs[(g + 1) % 2])
                    for (c00, dst) in ((768, s_sag), (1792, s_sbg)):
                        for jj in range(2):
                            P, bP = proj_fm(c00 + jj * 128)
                            o, bo = stage.next()
                            ACT(o[:], P[:], AF.Silu, [bP], [bo])
                            store_fm(dst, jj, o, bo)
                    for jj in range(HB):
                        P, bP = proj_fm(1024 + jj * 128)
                        ACT(QB[jj][0][:], P[:], AF.Silu, [bP], [QB[jj][1]])
                    for jj in range(HB):
                        sg, bsg = SG[jj]
                        f, bf_ = f32r.next()
                        TSC("dve", f[:], sg[:], oml[:, l, jj:jj + 1], lb[:, l, jj:jj + 1], ALU.mult, ALU.add,
                            [bsg, b_oml, b_lb], [bf_])
                        gl, bgl = f32r.next()
                        ACT(gl[:], f[:], AF.Ln, [bf_], [bgl])
                        cum, bcum = f32r.next()
                        Sc.op("dve", lambda e, cum=cum, gl=gl: e.tensor_tensor_scan(
                            out=cum[:], data0=scanm[:], data1=gl[:], initial=0.0,
                            op0=ALU.mult, op1=ALU.add), [bgl, b_scanm], [bcum])
                        ec, bec = f32r.next()
                        ACT(ec[:], cum[:], AF.Exp, [bcum], [bec])
                        ACT(gl[:], cum[:], AF.Exp, [bcum], [bgl], scale=-1.0)
                        CP("pool", EL[:, jj, g * 8:(g + 1) * 8],
                           ec[:].rearrange("p (c t) -> p c t", t=64)[:, :, 63], [bec], [b_EL])
                        o, bo = stage.next()
                        TT("pool", o[:], QB[jj][0][:], ec[:], ALU.mult, [QB[jj][1], bec], [bo])
                        store_fm(s_qd, jj, o, bo)
                        TSC("dve", f[:], f[:], -1.0, 1.0, ALU.mult, ALU.add, [bf_], [bf_])
                        TT("dve", f[:], f[:], gl[:], ALU.mult, [bf_, bgl], [bf_])
                        o, bo = stage.next()
                        CP("act", o[:], f[:], [bf_], [bo])
                        store_fm(s_kd, jj, o, bo)
                        o, bo = stage.next()
                        TT("pool", o[:].rearrange("p (c t) -> p c t", t=64),
                           f[:].rearrange("p (c t) -> p c t", t=64),
                           EL[:, jj, g * 8:(g + 1) * 8].unsqueeze(2).to_broadcast([128, 8, 64]),
                           ALU.mult, [bf_, b_EL], [bo])
                        store_fm(s_ke, jj, o, bo)
                hov = h_own.rearrange("(c p) t -> p c t", p=128)
                def gate_norm(g):
                    XN, b_XN = XNs[g % 2]
                    LD(H2[:], hov[:, :, g * 512:(g + 1) * 512], [b_H2], b_H2)
                    ACT(SQ[:].rearrange("p c t -> p (c t)"), H2[:].rearrange("p c t -> p (c t)"),
                        AF.Square, [b_H2], [b_SQ])
                    norm_part2(H2, b_H2, XN, b_XN)

                gate_norm(0)
                for g in range(NGO):
                    XN, b_XN = XNs[g % 2]
                    for jj in range(16):
                        if jj == 6 and g + 1 < NGO:
                            gate_norm(g + 1)
                        P, bP = PS.next()
                        for c in range(8):
                            MM(P[:], Wb[:, c, 2048 + jj * 128:2048 + (jj + 1) * 128], XN[:, c, :], c == 0, c == 7,
                               [b_Wbs[16 + jj], b_XN], [bP])
                        o, bo = stage.next()
                        ACT(o[:], P[:], AF.Sigmoid, [bP], [bo])
                        dst = s_sga if jj < 8 else s_sgb
                        ST(dst[(jj % 8) * 128:(jj % 8 + 1) * 128, g * 512:(g + 1) * 512], o[:], [bo], bo)
                Sc.emit()

            with ExitStack() as st:
                sb, ps = mk(st)
                QAs = [(sb(f"QA{i}", [128, S], BF16), Buf(f"QA{i}")) for i in range(2)]
                KAs = [(sb(f"KA{i}", [128, S], BF16), Buf(f"KA{i}")) for i in range(2)]
                VAs = [(sb(f"VA{i}", [128, NT, 128], BF16), Buf(f"VA{i}")) for i in range(2)]
                SAGr = Ring([(sb(f"SAG{i}", [64, 512], BF16), Buf(f"SAG{i}")) for i in range(3)])
                YGr = Ring([(sb(f"YG{i}", [64, 512], BF16), Buf(f"YG{i}")) for i in range(3)])
                FUT = sb("FUT", [128, NT, 32], BF16); b_FUT = Buf("FUT")
                NEGP = sb("NEGP", [128, NT, 32], BF16); b_NEGP = Buf("NEGP")
                KMh = sb("KMh", [64, 32], BF16); b_KMh = Buf("KMh")
                NTB = min(NT, 16)
                Gs = sb("Gs", [128, NTB, 32]); b_Gs = Buf("Gs")
                thr = sb("thr", [128, NTB, 8]); b_thr = Buf("thr")
                nsel = sb("nsel", [128, NTB, 32]); b_nsel = Buf("nsel")
                MBp = sb("MBp", [128, NTB, 128], BF16); b_MBp = Buf("MBp")
                PT = Ring([(sb(f"PT{i}", [128, 1024], BF16), Buf(f"PT{i}")) for i in range(3)])
                rden = sb("rden", [64, 512]); b_rden = Buf("rden")
                yh = sb("yh", [64, 512]); b_yh = Buf("yh")
                STp = Ring([(ps(f"STp{i}", [128, 1024]), Buf(f"STp{i}")) for i in range(2)])
                Op = Ring([(ps(f"Op{i}", [128, 512]), Buf(f"Op{i}")) for i in range(2)])
                Gp = ps("Gp", [128, 16, 32]); b_Gp = Buf("Gp")
                MTp = ps("MTp", [128, 512]); b_MTp = Buf("MTp")
                LD(FUT[:], c_fut, [b_FUT], b_FUT)
                LD(NEGP[:], c_neg, [b_NEGP], b_NEGP)
                for i in range(2):
                    LD(KAs[i][0][64:96, :], c_onehot, [KAs[i][1]], KAs[i][1])
                    Sc.op("pool", lambda e, i=i: e.memset(VAs[i][0][:, :, 64:128], 1.0), (), [VAs[i][1]])
                Sc.op("pool", lambda e: e.memset(MBp[:], 0.0), (), [b_MBp])
                TS_ = sb("TS", [128, HA, STRIP], BF16); b_TS = Buf("TS")
                c31s = sb("c31s", [128, HA]); b_c31 = Buf("c31s")
                LD(c31s[:], c31, [b_c31], b_c31)
                SPC = STRIP // 4
                stgs = [(sb(f"stripstg{i}", [128, SPC]), Buf(f"stripstg{i}")) for i in range(2)]
                for h in range(HA):
                    for q4 in range(4):
                        stg, b_stg = stgs[(h * 4 + q4) % 2]
                        LD(stg[:], strip_raw[:, h, q4 * SPC:(q4 + 1) * SPC], [b_stg], b_stg)
                        TSC("dve", TS_[:, h, q4 * SPC:(q4 + 1) * SPC], stg[:], c31s[:, h:h + 1], None, ALU.subtract,
                            None, [b_stg, b_c31], [b_TS])

                def head_loads(h):
                    sl = h % 2
                    QA, b_QA = QAs[sl]; KA, b_KA = KAs[sl]; VA, b_VA = VAs[sl]
                    LD(QA[0:64, :], s_qn[h * 64:(h + 1) * 64, :], [b_QA], b_QA)
                    LD(KA[0:64, :], s_kn[h * 64:(h + 1) * 64, :], [b_KA], b_KA)
                    vsrc = s_v[:, h * 64:(h + 1) * 64].rearrange("(t p) d -> p t d", p=128)
                    nvs = max(1, NT // 8)
                    for i in range(0, NT, nvs):
                        LD(VA[:, i:i + nvs, 0:64], vsrc[:, i:i + nvs, :], [b_VA], b_VA)

                def gate_stage(h, tb, stage_):
                    sl = h % 2
                    QA, b_QA = QAs[sl]
                    cq, po = h // 2, 64 * (h % 2)
                    if stage_ == 0:
                        if tb == 0:
                            TSC("dve", KMh[0:64, 0:NB], KS[po:po + 64, cq, :], 1.0 / 256, None, ALU.mult, None,
                                [b_KS], [b_KMh])
                        for t in range(NTB):
                            MM(Gp[:, t, 0:NB], QA[0:64, (tb + t) * 128:(tb + t + 1) * 128], KMh[0:64, 0:NB],
                               True, True, [b_QA, b_KMh], [b_Gp])
                    elif stage_ == 1:
                        if NB < 32:
                            Sc.op("dve", lambda e: e.memset(Gs[:], NEG), (), [b_Gs])
                        TT("dve", Gs[:, :, 0:NB], Gp[:, 0:NTB, 0:NB], FUT[:, tb:tb + NTB, 0:NB], ALU.add,
                           [b_Gp, b_FUT], [b_Gs])
                        for t in range(NTB):
                            Sc.op("dve", lambda e, t=t: e.max(out=thr[:, t, :], in_=Gs[:, t, :]), [b_Gs], [b_thr])
                        TT("dve", nsel[:], Gs[:], thr[:, :, 2:3].to_broadcast([128, NTB, 32]), ALU.is_lt,
                           [b_Gs, b_thr], [b_nsel])
                        TT("pool", MBp[:, :, 64:96], nsel[:], NEGP[:, tb:tb + NTB, :], ALU.mult,
                           [b_nsel, b_NEGP], [b_MBp])
                    else:
                        t4 = (stage_ - 2) * 4
                        for t in range(4):
                            MM(MTp[:, t * 128:(t + 1) * 128], MBp[:, t4 + t, :], ident[:], True, True,
                               [b_MBp, b_ident], [b_MTp])
                        c0 = (tb + t4) * 128
                        CP("act", QA[64:96, c0:c0 + 512], MTp[64:96, :], [b_MTp], [b_QA])

                NST = 2 + NTB // 4
                gate_sched = [(tb, st_) for tb in range(0, NT, NTB) for st_ in range(NST)]

                def head_gate(h):
                    for (tb, st_) in gate_sched:
                        gate_stage(h, tb, st_)

                head_loads(0)
                head_gate(0)
                for h in range(HA):
                    sl = h % 2
                    QA, b_QA = QAs[sl]; KA, b_KA = KAs[sl]; VA, b_VA = VAs[sl]
                    pairs = [(g, kp) for g in range(NG) for kp in range(2 * g + 2)]
                    slots = {}

                    def emit_qk(i):
                        g, kp = pairs[i]
                        Sp, bS = STp.next()
                        for u in range(2):
                            kt = 2 * kp + u
                            delta = 512 * g - 128 * kt
                            near = delta <= 1536
                            MM(Sp[:, u * 512:(u + 1) * 512], KA[0:96, kt * 128:(kt + 1) * 128],
                               QA[0:96, g * 512:(g + 1) * 512], True, not near, [b_KA, b_QA], [bS])
                            if near:
                                MM(Sp[:, u * 512:(u + 1) * 512], ident[:],
                                   TS_[:, h, delta + 384:delta + 384 + 512], False, True, [b_ident, b_TS], [bS])
                        slots[i] = (Sp, bS)

                    emit_qk(0)
                    O, bO = None, None
                    gate_i0 = len(pairs) // 4
                    gate_step = max(1, (len(pairs) - gate_i0 - 2) // len(gate_sched))
                    for i, (g, kp) in enumerate(pairs):
                        npair = 2 * g + 2
                        if kp == 0:
                            O, bO = Op.next()
                            SAG, b_SAG = SAGr.next()
                            LD(SAG[:], s_sag[h * 64:(h + 1) * 64, g * 512:(g + 1) * 512], [b_SAG], b_SAG)
                        if i + 1 < len(pairs):
                            emit_qk(i + 1)
                        if i == 0 and h + 1 < HA:
                            head_loads(h + 1)
                        if h + 1 < HA and i >= gate_i0 and (i - gate_i0) % gate_step == 0:
                            kq = (i - gate_i0) // gate_step
                            if kq < len(gate_sched):
                                gate_stage(h + 1, *gate_sched[kq])
                        Sp, bS = slots.pop(i)
                        P_, bPt = PT.next()
                        ACT(P_[:], Sp[:], AF.Exp, [bS], [bPt])
                        for u in range(2):
                            kt = 2 * kp + u
                            MM(O[:], VA[:, kt, :], P_[:, u * 512:(u + 1) * 512], kt == 0, kt == 2 * npair - 1,
                               [b_VA, bPt], [bO])
                        if kp == npair - 1:
                            Sc.op("dve", lambda e, O=O: e.reciprocal(out=rden[:], in_=O[64:128, :]), [bO], [b_rden])
                            TT("dve", yh[:], O[0:64, :], rden[:], ALU.mult, [bO, b_rden], [b_yh])
                            YG, b_YG = YGr.next()
                            TT("pool", YG[:], yh[:], SAG[:], ALU.mult, [b_yh, b_SAG], [b_YG])
                            ST(XS[h // 2, (h % 2) * 64:(h % 2) * 64 + 64, g * 512:(g + 1) * 512], YG[:], [b_YG], b_YG)
                Sc.emit()

            with ExitStack() as st:
                sb, ps = mk(st)
                QDs = [(sb(f"QD{i}", [128, S], BF16), Buf(f"QD{i}")) for i in range(HB)]
                KDs = [(sb(f"KD{i}", [128, S], BF16), Buf(f"KD{i}")) for i in range(HB)]
                KEr = Ring([(sb(f"KE{i}", [128, 512], BF16), Buf(f"KE{i}")) for i in range(4)])
                VBr = Ring([(sb(f"VB{i}", [64, 8, 128], BF16), Buf(f"VB{i}")) for i in range(4)])
                SBGr = Ring([(sb(f"SBG{i}", [128, 512], BF16), Buf(f"SBG{i}")) for i in range(4)])
                YBr = Ring([(sb(f"YB{i}", [128, 512], BF16), Buf(f"YB{i}")) for i in range(4)])
                KTr = Ring([(sb(f"KT{i}", [64, 8, 128], BF16), Buf(f"KT{i}")) for i in range(4)])
                ATs = Ring([(sb(f"ATs{i}", [64, 64], BF16), Buf(f"ATs{i}")) for i in range(4)])
                Sbs = [[(sb(f"Sb{hh}_{i}", [128, 128], BF16), Buf(f"Sb{hh}_{i}")) for i in range(2)] for hh in range(HB)]
                osq = sb("osq", [128, 512], BF16); b_osq = Buf("osq")
                ort = sb("ort", [128, 512]); b_ort = Buf("ort")
                ors = sb("ors", [128, 512]); b_ors = Buf("ors")
                ybf = sb("ybf", [128, 512]); b_ybf = Buf("ybf")
                KT1 = (ps("KTp", [64, 8, 128], BF16), Buf("KTp"))
                KTp = [KT1] * HB
                OHp = [Ring([(ps(f"OHp{hh}_{i}", [128, 512]), Buf(f"OHp{hh}_{i}")) for i in range(1)]) for hh in range(HB)]
                Abk = [ps(f"Abk{hh}", [128, 512]) for hh in range(HB)]
                dSbk = [ps(f"dSbk{hh}", [128, 512]) for hh in range(HB)]
                ATp = [(Abk[hh][0:64, 0:64], Buf(f"ATp{hh}")) for hh in range(HB)]
                dSp = [(dSbk[hh][:, 0:128], Buf(f"dSp{hh}")) for hh in range(HB)]
                NSp = ps("NSp", [128, 512]); b_NSp = Buf("NSp")
                b_XGm = [Buf(f"XGm{k}") for k in range(2)]
                b_XOm = Buf("XOm")
                for k in range(2):
                    AG(XG[k], XS[k], Buf(f"agx{k}"), w=[b_XGm[k]])
                b_XSH = [[Buf(f"XSH{hh}_{q}") for q in range(NQ)] for hh in range(HB)]
                for hh in range(HB):
                    rows = slice(hh * 128, (hh + 1) * 128)
                    LD(QDs[hh][0][:], s_qd[rows, :], [QDs[hh][1]], QDs[hh][1])
                    LD(KDs[hh][0][:], s_kd[rows, :], [KDs[hh][1]], KDs[hh][1])
                    Sc.op("dve", lambda e, hh=hh: e.memset(Sbs[hh][0][0][:], 0.0), (), [Sbs[hh][0][1]])
                cur = [0] * HB

                def group_loads(g):
                    out = []
                    for hh in range(HB):
                        rows = slice(hh * 128, (hh + 1) * 128)
                        KE, bKE = KEr.next()
                        LD(KE[:], s_ke[rows, g * 512:(g + 1) * 512], [bKE], bKE)
                        VB, bVB = VBr.next()
                        vsrc = s_vb[g * 512:(g + 1) * 512, hh * 128:(hh + 1) * 128].rearrange("(c s) v -> s c v", s=64)
                        LD(VB[:], vsrc, [bVB], bVB)
                        SBG, bSBG = SBGr.next()
                        LD(SBG[:], s_sbg[rows, g * 512:(g + 1) * 512], [bSBG], bSBG)
                        out.append((KE, bKE, VB, bVB, SBG, bSBG))
                    return out

                nxt = group_loads(0)
                for g in range(NG):
                    gl_ = nxt
                    if g + 1 < NG:
                        nxt = group_loads(g + 1)
                    KTl, OHl = [], []
                    for hh in range(HB):
                        KE, bKE, VB, bVB, SBG, bSBG = gl_[hh]
                        KTps, bKTp = KTp[hh]
                        for c in range(8):
                            Sc.op("pe", lambda e, KTps=KTps, c=c, KE=KE: e.transpose(
                                out=KTps[:, c, :], in_=KE[:, c * 64:(c + 1) * 64], identity=ident[:]),
                                [bKE, b_ident], [bKTp])
                        KTs, bKT = KTr.next()
                        CP("act", KTs[:], KTps[:], [bKTp], [bKT])
                        KTl.append((KTs, bKT))
                        OHl.append(OHp[hh].next())
                    for c in range(8):
                        ch = g * 8 + c
                        cs = slice(ch * 64, (ch + 1) * 64)
                        Asl = []
                        for hh in range(HB):
                            QD, b_QD = QDs[hh]; KD, b_KD = KDs[hh]
                            A, bA = ATp[hh]
                            MM(A, KD[:, cs], QD[:, cs], True, True, [b_KD, b_QD], [bA])
                            As, bAs = ATs.next()
                            TT("dve", As[:], A, caus[:], ALU.mult, [bA, b_caus], [bAs])
                            Asl.append((As, bAs))
                        for hh in range(HB):
                            KE, bKE, VB, bVB, SBG, bSBG = gl_[hh]
                            QD, b_QD = QDs[hh]
                            KTs, bKT = KTl[hh]; OH, bOH = OHl[hh]
                            As, bAs = Asl[hh]
                            S0, bS0 = Sbs[hh][cur[hh]]
                            S1, bS1 = Sbs[hh][1 - cur[hh]]
                            MM(OH[:, c * 64:(c + 1) * 64], VB[:, c, :], As[:], True, False, [bVB, bAs], [bOH])
                            MM(OH[:, c * 64:(c + 1) * 64], S0[:], QD[:, cs], False, True, [bS0, b_QD], [bOH])
                            dS, bdS = dSp[hh]
                            MM(dS, KTs[:, c, :], VB[:, c, :], True, True, [bKT, bVB], [bdS])
                            STT("dve", S1[:], S0[:], EL[:, hh, ch:ch + 1], dS, ALU.mult, ALU.add,
                                [bS0, b_EL, bdS], [bS1])
                            cur[hh] = 1 - cur[hh]
                    for hh in range(HB):
                        KE, bKE, VB, bVB, SBG, bSBG = gl_[hh]
                        OH, bOH = OHl[hh]
                        ACT(osq[:], OH[:], AF.Square, [bOH], [b_osq])
                        MM(NSp[:], ones[:], osq[:], True, True, [b_ones, b_osq], [b_NSp])
                        ACT(ort[:], NSp[:], AF.Ln, [b_NSp], [b_ort], bias=EPS, scale=1.0 / 128)
                        ACT(ors[:], ort[:], AF.Exp, [b_ort], [b_ors], scale=-0.5)
                        STT("dve", ybf[:], OH[:], go[:, l, hh:hh + 1], ors[:], ALU.mult, ALU.mult,
                            [bOH, b_go, b_ors], [b_ybf])
                        YB, bYB = YBr.next()
                        TT("pool", YB[:], ybf[:], SBG[:], ALU.mult, [b_ybf, bSBG], [bYB])
                        qd_, col_ = (g * 512) // QW, (g * 512) % QW
                        ST(XSH[hh, qd_, :, col_:col_ + 512], YB[:], [bYB], bYB, w=[b_XSH[hh][qd_]])
                    if g == NG // 2:
                        for k in range(2):
                            LD(XO[k, :, :], XG[k, :, bass.ds(off_own, SH)], [b_XOm], b_XOm, r=[b_XGm[k]])
                    if ((g + 1) * 512) % QW == 0 and (g * 512) // QW < NQ - 1:
                        qd_ = (g * 512) // QW
                        for hh in range(HB):
                            AG(XGH[hh, qd_ * 256:(qd_ + 1) * 256, :], XSH[hh, qd_], Buf(f"agh{hh}_{qd_}"),
                               r=[b_XSH[hh][qd_]])
                Sc.emit()

            with ExitStack() as st:
                sb, ps = mk(st)
                WA = sb("WA", [128, 4, D], BF16)
                WB_ = sb("WB", [128, 4, D], BF16)
                WO = sb("WO", [128, 8, D], BF16)
                WP = sb("WP", [128, 2, D], BF16)
                WG = sb("WG", [128, 8, D], BF16)
                bW = {}

                def wtok(name, c, j):
                    return bW[(name, (c // 2) * 2, (j // 2) * 256)]
                wst = [(sb(f"wstc{i}", [128, 2, 256]), Buf(f"wstc{i}")) for i in range(4)]
                k = 0
                for (dst, nm, src, nch) in ((WA, "WA", w_up_a[l], 4), (WB_, "WB", w_up_b[l], 4),
                                            (WO, "WO", w_out[l], 8), (WP, "WP", w_ple[l], 2),
                                            (WG, "WG", w_pg[l], 8)):
                    sv = src.rearrange("(c p) n -> p c n", p=128)
                    for c2 in range(0, nch, 2):
                        for n2 in range(0, D, 256):
                            t, b = wst[k % 4]
                            bW[(nm, c2, n2)] = Buf(f"{nm}_{c2}_{n2}")
                            LD(t[:], sv[:, c2:c2 + 2, n2:n2 + 256], [b], b)
                            CP(("act", "dve")[k % 2], dst[:, c2:c2 + 2, n2:n2 + 256], t[:], [b], [bW[(nm, c2, n2)]])
                            k += 1
                INS = []
                for i in range(2):
                    INS.append(dict(
                        YAG=(sb(f"YAG{i}", [128, 4, 512], BF16), Buf(f"YAG{i}")),
                        YBG=(sb(f"YBG{i}", [128, 4, 512], BF16), Buf(f"YBG{i}")),
                        SGA=(sb(f"SGA{i}", [128, 8, 512], BF16), Buf(f"SGA{i}")),
                        SGB=(sb(f"SGB{i}", [128, 8, 512], BF16), Buf(f"SGB{i}")),
                        PB=(sb(f"PB{i}", [128, 2, 512], BF16), Buf(f"PB{i}"))))
                H = sb("Hc", [128, 8, 512]); b_H = Buf("Hc")
                PF = sb("PF", [128, 2, 512]); b_PF = Buf("PF")
                MG = sb("MG", [128, 8, 512], BF16); b_MG = Buf("MG")
                HM = sb("HM", [128, 8, 512]); b_HM = Buf("HM")
                HMb = sb("HMb", [128, 8, 512], BF16); b_HMb = Buf("HMb")
                HN = Ring([(sb(f"HN{i}", [128, 512]), Buf(f"HN{i}")) for i in range(2)])
                HNb = Ring([(sb(f"HNb{i}", [128, 512], BF16), Buf(f"HNb{i}")) for i in range(2)])
                tmp = Ring([(sb(f"tmp{i}", [128, 512]), Buf(f"tmp{i}")) for i in range(4)])
                PS = Ring([(ps(f"PSc{i}", [128, 512]), Buf(f"PSc{i}")) for i in range(8)])
                hv = h_own.rearrange("(c p) t -> p c t", p=128)
                hd = h_dst.rearrange("(c p) t -> p c t", p=128)
                pv = pT[l].rearrange("(c p) t -> p c t", p=128)
                b_XGl = [Buf(f"XGl{hh}") for hh in range(HB)]
                for hh in range(HB):
                    AG(XGH[hh, (NQ - 1) * 256:NQ * 256, :], XSH[hh, NQ - 1], Buf(f"aghl{hh}"), w=[b_XGl[hh]])
                off_rows = (nc.sync.partition_id() % 2) * 512
                b_XOH = [Buf(f"XOH{i}") for i in range(2)]

                def xoh_copy(i):
                    for hh in range(HB):
                        LD(XOH[hh, :, :, i * QW:(i + 1) * QW].rearrange("r p t -> (r p) t"),
                           XGH[hh, (i * 256):, :][bass.ds(off_rows, 256), :],
                           [b_XOH[i]], b_XOH[i], r=([b_XGl[hh]] if i == 1 else []))

                xoh_copy(0)
                g_xoh1 = max(QW // 512 - 1, 0)
                b_HS = [Buf(f"HSp{q}") for q in range(NPC)]

                def loads(g):
                    I = INS[g % 2]
                    ts_ = slice(g * 512, (g + 1) * 512)
                    for c in range(4):
                        rk, kk = c // 2, c % 2
                        LD(I["YAG"][0][:, c, :], XO[kk, rk * 128:(rk + 1) * 128, ts_], [I["YAG"][1]], I["YAG"][1])
                        LD(I["YBG"][0][:, c, :], XOH[kk, rk, :, ts_], [I["YBG"][1]], I["YBG"][1],
                           r=[b_XOH[(g * 512) // QW]])
                    LD(I["SGA"][0][:], s_sga.rearrange("(c p) t -> p c t", p=128)[:, :, ts_], [I["SGA"][1]], I["SGA"][1])
                    LD(I["SGB"][0][:], s_sgb.rearrange("(c p) t -> p c t", p=128)[:, :, ts_], [I["SGB"][1]], I["SGB"][1])
                    LD(PF[:], pv[:, :, ts_], [b_PF], b_PF)
                    CP("act", I["PB"][0][:], PF[:], [b_PF], [I["PB"][1]])

                loads(0)
                LD(H[:], hv[:, :, 0:512], [b_H], b_H)
                for g in range(NGO):
                    ts_ = slice(g * 512, (g + 1) * 512)
                    I = INS[g % 2]
                    YAG, b_YAG = I["YAG"]; YBG, b_YBG = I["YBG"]; SGA, b_SGA = I["SGA"]; SGB, b_SGB = I["SGB"]
                    PB, b_PB = I["PB"]
                    if g == g_xoh1:
                        xoh_copy(1)
                    if g + 1 < NGO:
                        loads(g + 1)
                    for j in range(8):
                        Pa, bPa = PS.next()
                        for c in range(4):
                            MM(Pa[:], WA[:, c, j * 128:(j + 1) * 128], YAG[:, c, :], c == 0, c == 3,
                               [wtok("WA", c, j), b_YAG], [bPa])
                        Pb, bPb = PS.next()
                        for c in range(4):
                            MM(Pb[:], WB_[:, c, j * 128:(j + 1) * 128], YBG[:, c, :], c == 0, c == 3,
                               [wtok("WB", c, j), b_YBG], [bPb])
                        t1, bt1 = tmp.next()
                        TT("dve", t1[:], Pa[:], SGA[:, j, :], ALU.mult, [bPa, b_SGA], [bt1])
                        t2, bt2 = tmp.next()
                        TT("dve", t2[:], Pb[:], SGB[:, j, :], ALU.mult, [bPb, b_SGB], [bt2])
                        TT("pool", MG[:, j, :], t1[:], t2[:], ALU.add, [bt1, bt2], [b_MG])
                    for j in range(8):
                        Po, bPo = PS.next()
                        for c in range(8):
                            MM(Po[:], WO[:, c, j * 128:(j + 1) * 128], MG[:, c, :], c == 0, c == 7,
                               [wtok("WO", c, j), b_MG], [bPo])
                        TT("dve", HM[:, j, :], Po[:], H[:, j, :], ALU.add, [bPo, b_H], [b_HM])
                        CP("act", HMb[:, j, :], HM[:, j, :], [b_HM], [b_HMb])
                    if g + 1 < NGO:
                        LD(H[:], hv[:, :, (g + 1) * 512:(g + 2) * 512], [b_H], b_H)
                    q, col = (g * 512) // PIECE, (g * 512) % PIECE
                    for j in range(8):
                        Pp, bPp = PS.next()
                        for c in range(2):
                            MM(Pp[:], WP[:, c, j * 128:(j + 1) * 128], PB[:, c, :], c == 0, c == 1,
                               [wtok("WP", c, j), b_PB], [bPp])
                        Pg, bPg = PS.next()
                        for c in range(8):
                            MM(Pg[:], WG[:, c, j * 128:(j + 1) * 128], HMb[:, c, :], c == 0, c == 7,
                               [wtok("WG", c, j), b_HMb], [bPg])
                        sg, bsg = tmp.next()
                        ACT(sg[:], Pg[:], AF.Sigmoid, [bPg], [bsg])
                        t1, bt1 = tmp.next()
                        TT("dve", t1[:], Pp[:], sg[:], ALU.mult, [bPp, bsg], [bt1])
                        hn, bhn = HN.next()
                        TT("pool", hn[:], t1[:], HM[:, j, :], ALU.add, [bt1, b_HM], [bhn])
                        ST(hd[:, j, ts_], hn[:], [bhn], bhn, q="sp")
                        if not last:
                            hb, bhb = HNb.next()
                            CP("act", hb[:], hn[:], [bhn], [bhb])
                            ST(HS[q, j * 128:(j + 1) * 128, col:col + 512], hb[:], [bhb], bhb, w=[b_HS[q]], q="sp")
                    if not last and (g + 1) * 512 % PIECE == 0:
                        AG(HG[q], HS[q], Buf(f"agh{q}"), r=[b_HS[q]])
                Sc.emit()

        print("total ops", Sc.total, "sems", Sc.nsem)
    return nc


def host_consts(S):
    NT = S // 128
    bf = ml_dtypes.bfloat16
    c = {}
    c["c_ident"] = np.eye(128, dtype=np.float32).astype(bf)
    blk = np.zeros((128, 128), np.float32)
    blk[:64, :64] = 1.0
    blk[64:, 64:] = 1.0
    c["c_blk"] = blk.astype(bf)
    oh = np.zeros((32, S), np.float32)
    for n in range(S // 256):
        oh[n, n * 256:(n + 1) * 256] = 1.0
    c["c_onehot"] = oh.astype(bf)
    c["c_causal"] = np.triu(np.ones((64, 64), np.float32))
    sm = np.ones((128, 512), np.float32)
    sm[:, ::64] = 0.0
    c["c_scan"] = sm
    fut = np.zeros((NT, 32), np.float32)
    neg = np.full((NT, 32), NEG, np.float32)
    for t in range(NT):
        b = t // 2
        fut[t, b:] = NEG
        neg[t, b] = 0.0
    c["c_fut"] = np.ascontiguousarray(np.broadcast_to(fut[None], (128, NT, 32))).astype(bf)
    c["c_neg"] = np.ascontiguousarray(np.broadcast_to(neg[None], (128, NT, 32))).astype(bf)
    return c


def host_strips(rel_bias, heads):
    i = np.arange(128)[:, None]
    u = np.arange(STRIP)[None, :]
    rel = u - 384 - i
    bucket = t5_bucket_np(rel)
    strip = np.empty((128, len(heads), STRIP), np.float32)
    for k, h in enumerate(heads):
        g = rel_bias[:, h][bucket]
        strip[:, k, :] = np.where(rel >= 0, g, np.float32(NEG))
    c31 = np.ascontiguousarray(np.broadcast_to(rel_bias[31, heads][None, :], (128, len(heads)))).astype(np.float32)
    return strip, c31


def host_inputs(b, r, S, x, p, norm_gain, w_in, q_norm_gain, k_norm_gain, rel_bias, hgrn_lb_logits,
                hgrn_out_gain, w_up_a, w_up_b, w_out, w_ple, w_ple_gate, consts):
    L = w_in.shape[0]
    SH = S // 2
    m = dict(consts)
    m["xT"] = np.ascontiguousarray(x[b, :S].T)
    m["xT_own"] = np.ascontiguousarray(x[b, r * SH:(r + 1) * SH].T)
    m["pT"] = np.ascontiguousarray(np.transpose(p[:, b, r * SH:(r + 1) * SH, :], (0, 2, 1)))
    cols = []
    for blk0 in range(0, 4096, 512):
        cols.append(np.arange(blk0 + r * 256, blk0 + (r + 1) * 256))
    cols.append(np.arange(4096, 6144))
    cols = np.concatenate(cols)
    m["w_in"] = np.ascontiguousarray(w_in[:, :, cols])
    m["w_up_a"] = w_up_a
    m["w_up_b"] = w_up_b
    m["w_out"] = w_out
    m["w_ple"] = w_ple
    m["w_pg"] = w_ple_gate
    m["g_norm"] = np.ascontiguousarray(np.transpose(norm_gain.reshape(L, 8, 128), (2, 0, 1)))
    m["g_q"] = np.ascontiguousarray(np.concatenate([q_norm_gain, q_norm_gain], axis=1).T)
    m["g_k"] = np.ascontiguousarray(np.concatenate([k_norm_gain, k_norm_gain], axis=1).T)
    m["g_o"] = np.ascontiguousarray(np.transpose(hgrn_out_gain.reshape(L, 4, 128)[:, 2 * r:2 * r + 2], (2, 0, 1)))
    m["lbl"] = np.ascontiguousarray(np.transpose(hgrn_lb_logits.reshape(L, 4, 128)[:, 2 * r:2 * r + 2], (2, 0, 1)))
    strip, c31 = host_strips(rel_bias, list(range(4 * r, 4 * r + 4)))
    m["strip_raw"] = strip
    m["c31"] = c31
    return m


_NC_CACHE = {}


def kernel(x, p, norm_gain, w_in, q_norm_gain, k_norm_gain, rel_bias, hgrn_lb_logits,
           hgrn_out_gain, w_up_a, w_up_b, w_out, w_ple, w_ple_gate):
    args = [np.asarray(a, dtype=np.float32) for a in (
        x, p, norm_gain, w_in, q_norm_gain, k_norm_gain, rel_bias, hgrn_lb_logits,
        hgrn_out_gain, w_up_a, w_up_b, w_out, w_ple, w_ple_gate)]
    x = args[0]
    B, S, _ = x.shape
    if S not in _NC_CACHE:
        _NC_CACHE[S] = build(S)
    nc = _NC_CACHE[S]
    consts = host_consts(S)
    in_maps = [host_inputs(i // 2, i % 2, S, *args, consts) for i in range(2 * B)]
    res = run_bass_kernel_spmd(nc, in_maps, core_ids=list(range(2 * B)))
    SH = S // 2
    out = np.empty((B, S, D), np.float32)
    for i in range(2 * B):
        out[i // 2, (i % 2) * SH:(i % 2 + 1) * SH, :] = res.results[i]["hT_out"].T
    return out
```

```python
import math
from contextlib import ExitStack

import numpy as np
import ml_dtypes
import concourse.bass as bass
import concourse.mybir as mybir
from concourse.bass_utils import run_bass_kernel_spmd

F32 = mybir.dt.float32
BF16 = mybir.dt.bfloat16
AF = mybir.ActivationFunctionType
ALU = mybir.AluOpType
AX = mybir.AxisListType

SEM_CHUNK = 20000
D = 1024
NEG = -30000.0
STRIP = 2432
EPS = 1e-6


class Buf:
    __slots__ = ("name", "last_writer", "dma_writers", "readers", "dma_readers")

    def __init__(self, name):
        self.name = name
        self.clear()

    def clear(self):
        self.last_writer = None
        self.dma_writers = []
        self.readers = {}
        self.dma_readers = []


class Sched:
    def __init__(self, nc, stack):
        self.nc = nc
        self.stack = stack
        self.ops = []
        self.eng = {"pe": nc.tensor, "act": nc.scalar, "dve": nc.vector,
                    "pool": nc.gpsimd, "sp": nc.sync}
        self.nsem = 0
        self.eng_count = {}
        self.eng_sems = {}
        self.dma_pool = []
        self.dma_free = []
        self.key_slot = {}
        self.waited = {}
        self.total = 0

    def new_sem(self, name):
        self.nsem += 1
        return self.stack.enter_context(self.nc.semaphore(name))

    def op(self, eng, fn, reads=(), writes=()):
        self.ops.append(["c", eng, fn, tuple(reads), tuple(writes), None, 1])

    def dma(self, queue, fn, reads=(), writes=(), key=None, inc=16):
        assert key is not None
        self.ops.append(["d", queue, fn, tuple(reads), tuple(writes), key, inc])

    def emit(self):
        ops = self.ops
        n = len(ops)
        deps = [None] * n
        signaling = [False] * n
        last_of_eng = {}
        for i, o in enumerate(ops):
            d = set()
            isd = o[0] == "d"
            for b in o[3]:
                if b.last_writer is not None:
                    d.add(b.last_writer)
                d.update(b.dma_writers)
            for b in o[4]:
                if b.last_writer is not None:
                    d.add(b.last_writer)
                d.update(b.readers.values())
                d.update(b.dma_readers)
                if not isd:
                    d.update(b.dma_writers)
            d.discard(i)
            for b in o[3]:
                if isd:
                    b.dma_readers.append(i)
                else:
                    b.readers[o[1]] = i
            for b in o[4]:
                if isd:
                    b.dma_writers.append(i)
                else:
                    b.last_writer = i
                    b.dma_writers = []
                    b.readers = {}
                    b.dma_readers = []
            deps[i] = d
            for j in d:
                signaling[j] = True
            if o[0] == "c":
                last_of_eng[o[1]] = i
        for e, i in last_of_eng.items():
            signaling[i] = True
        sig = [None] * n
        for i, o in enumerate(ops):
            if o[0] == "c":
                if not signaling[i]:
                    continue
                e = o[1]
                c = self.eng_count.get(e, 0)
                k = c // SEM_CHUNK
                if (e, k) not in self.eng_sems:
                    self.eng_sems[(e, k)] = self.new_sem(f"s_{e}_{k}")
                self.eng_count[e] = c + 1
                sig[i] = (self.eng_sems[(e, k)], c - k * SEM_CHUNK + 1, 1, ("e", e, k))
            else:
                key = o[5]
                if key not in self.key_slot:
                    if self.dma_free:
                        s = self.dma_free.pop()
                    else:
                        self.dma_pool.append([self.new_sem(f"d_{len(self.dma_pool)}"), 0])
                        s = len(self.dma_pool) - 1
                    self.key_slot[key] = s
                s = self.key_slot[key]
                self.dma_pool[s][1] += o[6]
                sig[i] = (self.dma_pool[s][0], self.dma_pool[s][1], o[6], ("k", s))
        waited = self.waited
        for i, o in enumerate(ops):
            e = o[1]
            engine = self.eng[e]
            w = waited.setdefault(e, {})
            for j in sorted(deps[i]):
                pj = ops[j]
                if pj[0] == "c" and pj[1] == "pe" and e == "pe" and o[0] == "c":
                    continue
                sem, val, _, sid = sig[j]
                if w.get(sid, 0) >= val:
                    continue
                w[sid] = val
                engine.wait_ge(sem, val)
            ins = o[2](engine)
            if sig[i] is not None:
                ins.then_inc(sig[i][0], sig[i][2])
        finals = []
        for (e, k), sem in self.eng_sems.items():
            c = self.eng_count.get(e, 0)
            if c // SEM_CHUNK == k and c - k * SEM_CHUNK > 0:
                finals.append((sem, c - k * SEM_CHUNK, ("e", e, k)))
            elif c // SEM_CHUNK > k:
                finals.append((sem, SEM_CHUNK, ("e", e, k)))
        for s, (sem, c) in enumerate(self.dma_pool):
            if c > 0:
                finals.append((sem, c, ("k", s)))
        for e in ("sp", "pe", "act", "dve", "pool"):
            w = waited.setdefault(e, {})
            for sem, val, sid in finals:
                if w.get(sid, 0) >= val:
                    continue
                w[sid] = val
                self.eng[e].wait_ge(sem, val)
        for o in ops:
            for b in o[3] + o[4]:
                b.clear()
        self.key_slot = {}
        self.dma_free = list(range(len(self.dma_pool)))
        self.total += n
        self.ops = []
        return n


class Ring:
    def __init__(self, items):
        self.items = items
        self.i = 0

    def next(self):
        it = self.items[self.i % len(self.items)]
        self.i += 1
        return it


def t5_bucket_np(rel):
    n = np.maximum(rel, 0)
    nf = np.maximum(n, 16).astype(np.float32)
    large = 16 + (np.log(nf / np.float32(16)) / np.float32(math.log(128)) * np.float32(16)).astype(np.int32)
    large = np.minimum(large, 31)
    return np.where(n < 16, n, large)


def build(S, L=2, debug=False, groups=None):
    NT, NB, NG, NCH = S // 128, S // 256, S // 512, S // 64
    SH, NGO = S // 2, S // 1024
    HA, HB = 4, 2
    if groups is None:
        groups = [[0, 1], [2, 3], [4, 5], [6, 7]]
    nc = bass.Bass("TRN2", target_bir_lowering=False)

    def din(name, shape, dt=F32):
        return nc.dram_tensor(name, list(shape), dt, kind="ExternalInput").ap()

    def dscr(name, shape, dt=BF16, out=False):
        kind = "Internal"
        return nc.dram_tensor(name, list(shape), dt, kind=kind).ap()

    xT = din("xT", [D, S])
    xT_own = din("xT_own", [D, SH])
    pT = din("pT", [L, 256, SH])
    w_in = din("w_in", [L, D, 4096])
    w_up_a = din("w_up_a", [L, 512, D])
    w_up_b = din("w_up_b", [L, 512, D])
    w_out = din("w_out", [L, D, D])
    w_ple = din("w_ple", [L, 256, D])
    w_pg = din("w_pg", [L, D, D])
    g_norm = din("g_norm", [128, L, 8])
    g_q = din("g_q", [128, L])
    g_k = din("g_k", [128, L])
    g_o = din("g_o", [128, L, HB])
    lbl = din("lbl", [128, L, HB])
    strip_raw = din("strip_raw", [128, HA, STRIP])
    c31 = din("c31", [128, HA])
    c_ident = din("c_ident", [128, 128], BF16)
    c_blk = din("c_blk", [128, 128], BF16)
    c_onehot = din("c_onehot", [32, S], BF16)
    c_causal = din("c_causal", [64, 64])
    c_scan = din("c_scan", [128, 512])
    c_fut = din("c_fut", [128, NT, 32], BF16)
    c_neg = din("c_neg", [128, NT, 32], BF16)

    hT_out = nc.dram_tensor("hT_out", [D, SH], F32, kind="ExternalOutput").ap()
    h_mid = dscr("h_mid", [D, SH], F32)
    s_qn = dscr("s_qn", [256, S])
    s_kn = dscr("s_kn", [256, S])
    s_v = dscr("s_v", [S, 256])
    s_sag = dscr("s_sag", [256, S])
    s_qd = dscr("s_qd", [256, S])
    s_kd = dscr("s_kd", [256, S])
    s_ke = dscr("s_ke", [256, S])
    s_vb = dscr("s_vb", [S, 256])
    s_sbg = dscr("s_sbg", [256, S])
    s_sga = dscr("s_sga", [D, SH])
    s_sgb = dscr("s_sgb", [D, SH])
    XS = dscr("XS", [4, 128, S], out=True)
    XG = dscr("XG", [4, 2 * 128, S], out=True)
    XO = dscr("XO", [4, 2 * 128, SH])
    NQ = 4
    QW = S // NQ
    XSH = dscr("XSH", [HB, NQ, 128, QW])
    XGH = dscr("XGH", [HB, NQ * 2 * 128, QW])
    XOH = dscr("XOH", [HB, 2, 128, SH])
    PIECE = max(512, SH // 4)
    NPC = SH // PIECE
    HS = dscr("HS", [NPC, D, PIECE])
    HG = dscr("HG", [NPC, 2 * D, PIECE])

    off_own = (nc.sync.partition_id() % 2) * SH

    with ExitStack() as outer:
        Sc = Sched(nc, outer)
        uniq = [0]

        def mk(stack):
            uniq[0] += 1
            tag = f"u{uniq[0]}_"

            def sb(name, shape, dt=F32):
                return stack.enter_context(nc.sbuf_tensor(tag + name, list(shape), dt))

            def ps(name, shape, dt=F32):
                return stack.enter_context(nc.psum_tensor(tag + name, list(shape), dt))
            return sb, ps

        def MM(out, lhsT, rhs, start, stop, r, w):
            Sc.op("pe", lambda e: e.matmul(out, lhsT=lhsT, rhs=rhs, start=start, stop=stop), r, w)

        def ACT(out, in_, func, r, w, bias=0.0, scale=1.0, eng="act"):
            Sc.op(eng, lambda e: e.activation(out=out, in_=in_, func=func, bias=bias, scale=scale), r, w)

        def TT(eng, out, in0, in1, op, r, w):
            Sc.op(eng, lambda e: e.tensor_tensor(out=out, in0=in0, in1=in1, op=op), r, w)

        def TSC(eng, out, in0, s1, s2, op0, op1, r, w):
            if s2 is None:
                Sc.op(eng, lambda e: e.tensor_scalar(out=out, in0=in0, scalar1=s1, scalar2=None, op0=op0), r, w)
            else:
                Sc.op(eng, lambda e: e.tensor_scalar(out=out, in0=in0, scalar1=s1, scalar2=s2, op0=op0, op1=op1), r, w)

        def STT(eng, out, in0, scalar, in1, op0, op1, r, w):
            Sc.op(eng, lambda e: e.scalar_tensor_tensor(out=out, in0=in0, scalar=scalar, in1=in1, op0=op0, op1=op1), r, w)

        def CP(eng, out, in_, r, w):
            if eng == "act":
                Sc.op("act", lambda e: e.activation(out=out, in_=in_, func=AF.Copy), r, w)
            else:
                Sc.op(eng, lambda e: e.tensor_copy(out=out, in_=in_), r, w)

        def LD(out, in_, w, key, r=(), q="sp"):
            Sc.dma(q, lambda e: e.dma_start(out=out, in_=in_), r, w, key)

        def ST(out, in_, r, key, w=(), q="pool"):
            Sc.dma(q, lambda e: e.dma_start(out=out, in_=in_), r, w, key)

        def AG(out, in_, key, r=(), w=()):
            Sc.dma("pool", lambda e: e.collective_compute(
                "AllGather", ALU.bypass, replica_groups=groups, ins=[in_], outs=[out]), r, w, key, inc=1)

        sbP, _ = mk(outer)
        ident = sbP("ident", [128, 128], BF16); b_ident = Buf("ident")
        blk = sbP("blk", [128, 128], BF16); b_blk = Buf("blk")
        ones = sbP("ones", [128, 128], BF16); b_ones = Buf("ones")
        gn = sbP("gn", [128, L, 8]); b_gn = Buf("gn")
        gq = sbP("gq", [128, L]); b_gq = Buf("gq")
        gk = sbP("gk", [128, L]); b_gk = Buf("gk")
        go = sbP("go", [128, L, HB]); b_go = Buf("go")
        lb = sbP("lb", [128, L, HB]); b_lb = Buf("lb")
        oml = sbP("oml", [128, L, HB]); b_oml = Buf("oml")
        lbe = sbP("lbe", [128, L, HB]); b_lbe = Buf("lbe")
        lbs = sbP("lbs", [128, HB]); b_lbs = Buf("lbs")
        KS = sbP("KS", [128, 2, NB]); b_KS = Buf("KS")
        EL = sbP("EL", [128, HB, NCH]); b_EL = Buf("EL")
        caus = sbP("caus", [64, 64]); b_caus = Buf("caus")
        scanm = sbP("scanm", [128, 512]); b_scanm = Buf("scanm")

        with ExitStack() as st:
            sb, ps = mk(st)
            LD(ident[:], c_ident, [b_ident], b_ident)
            LD(blk[:], c_blk, [b_blk], b_blk)
            Sc.op("pool", lambda e: e.memset(ones[:], 1.0), (), [b_ones])
            LD(gn[:], g_norm, [b_gn], b_gn)
            LD(gq[:], g_q, [b_gq], b_gq)
            LD(gk[:], g_k, [b_gk], b_gk)
            LD(go[:], g_o, [b_go], b_go)
            LD(lbe[:], lbl, [b_lbe], b_lbe)
            LD(caus[:], c_causal, [b_caus], b_caus)
            LD(scanm[:], c_scan, [b_scanm], b_scanm)
            TSC("dve", gq[:], gq[:], 0.125, None, ALU.mult, None, [b_gq], [b_gq])
            ACT(lbe[:], lbe[:], AF.Exp, [b_lbe], [b_lbe])
            CP("dve", lbs[:], lbe[:, 0, :], [b_lbe], [b_lbs])
            for l in range(1, L):
                TT("dve", lbs[:], lbs[:], lbe[:, l, :], ALU.add, [b_lbs, b_lbe], [b_lbs])
            Sc.op("dve", lambda e: e.reciprocal(out=lbs[:], in_=lbs[:]), [b_lbs], [b_lbs])
            Sc.op("dve", lambda e: e.memset(lb[:, 0, :], 0.0), (), [b_lb])
            for l in range(1, L):
                TT("dve", lb[:, l, :], lb[:, l - 1, :], lbe[:, l, :], ALU.add, [b_lb, b_lbe], [b_lb])
            for l in range(1, L):
                TT("dve", lb[:, l, :], lb[:, l, :], lbs[:], ALU.mult, [b_lb, b_lbs], [b_lb])
            for l in range(L):
                TSC("dve", oml[:, l, :], lb[:, l, :], -1.0, 1.0, ALU.mult, ALU.add, [b_lb], [b_oml])
            Sc.emit()

        for l in range(L):
            first, last = (l == 0), (l == L - 1)
            h_own = xT_own if first else h_mid
            h_dst = hT_out if last else h_mid

            with ExitStack() as st:
                sb, ps = mk(st)
                Wb = sb("Wb", [128, 8, 4096], BF16)
                b_Wbs = [Buf(f"Wb{i}") for i in range(32)]
                wst = [(sb(f"wst{i}", [128, 8, 128]), Buf(f"wst{i}")) for i in range(4)]
                wv = w_in[l].rearrange("(c p) n -> p c n", p=128)
                for i in range(32):
                    t, b = wst[i % 4]
                    LD(t[:], wv[:, :, i * 128:(i + 1) * 128], [b], b)
                    CP(("act", "dve")[i % 2], Wb[:, :, i * 128:(i + 1) * 128], t[:], [b], [b_Wbs[i]])
                HDT = F32 if first else BF16
                H = sb("H", [128, 8, 512], HDT); b_H = Buf("H")
                H2 = sb("H2", [128, 8, 512]); b_H2 = Buf("H2")
                SQ = sb("SQ", [128, 8, 512], BF16); b_SQ = Buf("SQ")
                XNs = [(sb(f"XN{i}", [128, 8, 512], BF16), Buf(f"XN{i}")) for i in range(2)]
                rstd = sb("rstd", [128, 512]); b_rstd = Buf("rstd")
                stage = Ring([(sb(f"stg{i}", [128, 512], BF16), Buf(f"stg{i}")) for i in range(6)])
                f32r = Ring([(sb(f"f32r{i}", [128, 512]), Buf(f"f32r{i}")) for i in range(8)])
                QB = [(sb(f"QB{i}", [128, 512]), Buf(f"QB{i}")) for i in range(HB)]
                SG = [(sb(f"SG{i}", [128, 512]), Buf(f"SG{i}")) for i in range(HB)]
                sqq = Ring([(sb(f"sqq{i}", [128, 512], BF16), Buf(f"sqq{i}")) for i in range(2)])
                p_ss = ps("p_ss", [128, 512]); b_pss = Buf("p_ss")
                PS = Ring([(ps(f"PSa{i}", [128, 512]), Buf(f"PSa{i}")) for i in range(5)])
                BS = Ring([(ps(f"BSa{i}", [128, 512]), Buf(f"BSa{i}")) for i in range(2)])
                Sc.op("dve", lambda e: e.memset(KS[:], 0.0), (), [b_KS])

                b_HGl = Buf("HGlast")
                if not first:
                    AG(HG[NPC - 1], HS[NPC - 1], Buf("aghl"), w=[b_HGl])

                def load_h(g):
                    if first:
                        LD(H[:], xT.rearrange("(c p) t -> p c t", p=128)[:, :, g * 512:(g + 1) * 512], [b_H], b_H)
                    else:
                        half, tok = g // NGO, (g % NGO) * 512
                        q, col = tok // PIECE, tok % PIECE
                        src = HG[q, half * D:(half + 1) * D, col:col + 512].rearrange("(c p) t -> p c t", p=128)
                        LD(H[:], src, [b_H], b_H, r=([b_HGl] if q == NPC - 1 else []))

                def norm_part1(g):
                    load_h(g)
                    ACT(SQ[:].rearrange("p c t -> p (c t)"), H[:].rearrange("p c t -> p (c t)"),
                        AF.Square, [b_H], [b_SQ])

                def norm_part2(Hs, bH, XN, b_XN):
                    for c in range(8):
                        MM(p_ss[:], ones[:], SQ[:, c, :], c == 0, c == 7, [b_ones, b_SQ], [b_pss])
                    ACT(rstd[:], p_ss[:], AF.Ln, [b_pss], [b_rstd], bias=EPS, scale=1.0 / D)
                    ACT(rstd[:], rstd[:], AF.Exp, [b_rstd], [b_rstd], scale=-0.5)
                    for c in range(8):
                        STT("dve", XN[:, c, :], Hs[:, c, :], gn[:, l, c:c + 1], rstd[:],
                            ALU.mult, ALU.mult, [bH, b_gn, b_rstd], [b_XN])

                norm_part1(0)
                norm_part2(H, b_H, *XNs[0])
                for g in range(NG):
                    XN, b_XN = XNs[g % 2]

                    def proj_fm(c0):
                        P, bP = PS.next()
                        for c in range(8):
                            MM(P[:], Wb[:, c, c0:c0 + 128], XN[:, c, :], c == 0, c == 7,
                               [b_Wbs[c0 // 128], b_XN], [bP])
                        return P, bP

                    def store_fm(dst, j, t, b):
                        ST(dst[j * 128:(j + 1) * 128, g * 512:(g + 1) * 512], t[:], [b], b)

                    def qk_tail(j, P, bP, s2, bs2):
                        isk = j >= 2
                        Bp, bB = BS.next()
                        MM(Bp[:], blk[:], s2[:], True, True, [b_blk, bs2], [bB])
                        r1, br1 = f32r.next()
                        ACT(r1[:], Bp[:], AF.Ln, [bB], [br1], bias=EPS, scale=1.0 / 64)
                        ACT(r1[:], r1[:], AF.Exp, [br1], [br1], scale=-0.5)
                        o, bo = stage.next()
                        gg = gk if isk else gq
                        STT("dve", o[:], P[:], gg[:, l:l + 1], r1[:], ALU.mult, ALU.mult,
                            [bP, b_gk if isk else b_gq, br1], [bo])
                        if isk:
                            Sc.op("dve", lambda e, o=o, j=j, g=g: e.tensor_reduce(
                                out=KS[:, j - 2, 2 * g:2 * g + 2],
                                in_=o[:].rearrange("p (b t) -> p b t", t=256),
                                axis=AX.X, op=ALU.add), [bo], [b_KS])
                        store_fm(s_kn if isk else s_qn, j % 2, o, bo)

                    pend = None
                    for j in range(4):
                        P, bP = proj_fm(j * 128)
                        s2, bs2 = sqq.next()
                        ACT(s2[:], P[:], AF.Square, [bP], [bs2])
                        if pend is not None:
                            qk_tail(*pend)
                        pend = (j, P, bP, s2, bs2)
                    firstv = True
                    for (c0, dst) in ((512, s_v), (1536, s_vb)):
                        for tt in range(4):
                            P, bP = PS.next()
                            for c in range(8):
                                MM(P[:, 0:256], XN[:, c, tt * 128:(tt + 1) * 128], Wb[:, c, c0:c0 + 256],
                                   c == 0, c == 7, [b_XN, b_Wbs[c0 // 128], b_Wbs[c0 // 128 + 1]], [bP])
                            if firstv:
                                qk_tail(*pend)
                                firstv = False
                            o, bo = stage.next()
                            CP("act", o[:, 0:256], P[:, 0:256], [bP], [bo])
                            ST(dst[g * 512 + tt * 128:g * 512 + (tt + 1) * 128, :], o[:, 0:256], [bo], bo)
                    if g + 1 < NG:
                        norm_part1(g + 1)
                    for jj in range(HB):
                        P, bP = proj_fm(1280 + jj * 128)
                        ACT(SG[jj][0][:], P[:], AF.Sigmoid, [bP], [SG[jj][1]])
                    if g + 1 < NG:
                        norm_part2(H, b_H, *XNs[(g + 1) % 2])
                    for (c00, dst) in ((768, s_sag), (1792, s_sbg)):
                        for jj in range(2):
                            P, bP = proj_fm(c00 + jj * 128)
                            o, bo = stage.next()
                            ACT(o[:], P[:], AF.Silu, [bP], [bo])
                            store_fm(dst, jj, o, bo)
                    for jj in range(HB):
                        P, bP = proj_fm(1024 + jj * 128)
                        ACT(QB[jj][0][:], P[:], AF.Silu, [bP], [QB[jj][1]])
                    for jj in range(HB):
                        sg, bsg = SG[jj]
                        f, bf_ = f32r.next()
                        TSC("dve", f[:], sg[:], oml[:, l, jj:jj + 1], lb[:, l, jj:jj + 1], ALU.mult, ALU.add,
                            [bsg, b_oml, b_lb], [bf_])
                        gl, bgl = f32r.next()
                        ACT(gl[:], f[:], AF.Ln, [bf_], [bgl])
                        cum, bcum = f32r.next()
                        Sc.op("dve", lambda e, cum=cum, gl=gl: e.tensor_tensor_scan(
                            out=cum[:], data0=scanm[:], data1=gl[:], initial=0.0,
                            op0=ALU.mult, op1=ALU.add), [bgl, b_scanm], [bcum])
                        ec, bec = f32r.next()
                        ACT(ec[:], cum[:], AF.Exp, [bcum], [bec])
                        ACT(gl[:], cum[:], AF.Exp, [bcum], [bgl], scale=-1.0)
                        CP("pool", EL[:, jj, g * 8:(g + 1) * 8],
                           ec[:].rearrange("p (c t) -> p c t", t=64)[:, :, 63], [bec], [b_EL])
                        o, bo = stage.next()
                        TT("pool", o[:], QB[jj][0][:], ec[:], ALU.mult, [QB[jj][1], bec], [bo])
                        store_fm(s_qd, jj, o, bo)
                        TSC("dve", f[:], f[:], -1.0, 1.0, ALU.mult, ALU.add, [bf_], [bf_])
                        TT("dve", f[:], f[:], gl[:], ALU.mult, [bf_, bgl], [bf_])
                        o, bo = stage.next()
                        CP("act", o[:], f[:], [bf_], [bo])
                        store_fm(s_kd, jj, o, bo)
                        o, bo = stage.next()
                        TT("pool", o[:].rearrange("p (c t) -> p c t", t=64),
                           f[:].rearrange("p (c t) -> p c t", t=64),
                           EL[:, jj, g * 8:(g + 1) * 8].unsqueeze(2).to_broadcast([128, 8, 64]),
                           ALU.mult, [bf_, b_EL], [bo])
                        store_fm(s_ke, jj, o, bo)
                hov = h_own.rearrange("(c p) t -> p c t", p=128)
                def gate_norm(g):
                    XN, b_XN = XNs[g % 2]
                    LD(H2[:], hov[:, :, g * 512:(g + 1) * 512], [b_H2], b_H2)
                    ACT(SQ[:].rearrange("p c t -> p (c t)"), H2[:].rearrange("p c t -> p (c t)"),
                        AF.Square, [b_H2], [b_SQ])
                    norm_part2(H2, b_H2, XN, b_XN)

                gate_norm(0)
                for g in range(NGO):
                    XN, b_XN = XNs[g % 2]
                    for jj in range(16):
                        if jj == 6 and g + 1 < NGO:
                            gate_norm(g + 1)
                        P, bP = PS.next()
                        for c in range(8):
                            MM(P[:], Wb[:, c, 2048 + jj * 128:2048 + (jj + 1) * 128], XN[:, c, :], c == 0, c == 7,
                               [b_Wbs[16 + jj], b_XN], [bP])
                        o, bo = stage.next()
                        ACT(o[:], P[:], AF.Sigmoid, [bP], [bo])
                        dst = s_sga if jj < 8 else s_sgb
                        ST(dst[(jj % 8) * 128:(jj % 8 + 1) * 128, g * 512:(g + 1) * 512], o[:], [bo], bo)
                Sc.emit()

            with ExitStack() as st:
                sb, ps = mk(st)
                QAs = [(sb(f"QA{i}", [128, S], BF16), Buf(f"QA{i}")) for i in range(2)]
                KAs = [(sb(f"KA{i}", [128, S], BF16), Buf(f"KA{i}")) for i in range(2)]
                VAs = [(sb(f"VA{i}", [128, NT, 128], BF16), Buf(f"VA{i}")) for i in range(2)]
                SAGr = Ring([(sb(f"SAG{i}", [64, 512], BF16), Buf(f"SAG{i}")) for i in range(3)])
                YGr = Ring([(sb(f"YG{i}", [64, 512], BF16), Buf(f"YG{i}")) for i in range(3)])
                FUT = sb("FUT", [128, NT, 32], BF16); b_FUT = Buf("FUT")
                NEGP = sb("NEGP", [128, NT, 32], BF16); b_NEGP = Buf("NEGP")
                KMh = sb("KMh", [64, 32], BF16); b_KMh = Buf("KMh")
                NTB = min(NT, 16)
                Gs = sb("Gs", [128, NTB, 32]); b_Gs = Buf("Gs")
                thr = sb("thr", [128, NTB, 8]); b_thr = Buf("thr")
                nsel = sb("nsel", [128, NTB, 32]); b_nsel = Buf("nsel")
                MBp = sb("MBp", [128, NTB, 128], BF16); b_MBp = Buf("MBp")
                PT = Ring([(sb(f"PT{i}", [128, 1024], BF16), Buf(f"PT{i}")) for i in range(3)])
                rden = sb("rden", [64, 512]); b_rden = Buf("rden")
                yh = sb("yh", [64, 512]); b_yh = Buf("yh")
                STp = Ring([(ps(f"STp{i}", [128, 1024]), Buf(f"STp{i}")) for i in range(2)])
                Op = Ring([(ps(f"Op{i}", [128, 512]), Buf(f"Op{i}")) for i in range(2)])
                Gp = ps("Gp", [128, 16, 32]); b_Gp = Buf("Gp")
                MTp = ps("MTp", [128, 512]); b_MTp = Buf("MTp")
                LD(FUT[:], c_fut, [b_FUT], b_FUT)
                LD(NEGP[:], c_neg, [b_NEGP], b_NEGP)
                for i in range(2):
                    LD(KAs[i][0][64:96, :], c_onehot, [KAs[i][1]], KAs[i][1])
                    Sc.op("pool", lambda e, i=i: e.memset(VAs[i][0][:, :, 64:128], 1.0), (), [VAs[i][1]])
                Sc.op("pool", lambda e: e.memset(MBp[:], 0.0), (), [b_MBp])
                TS_ = sb("TS", [128, HA, STRIP], BF16); b_TS = Buf("TS")
                c31s = sb("c31s", [128, HA]); b_c31 = Buf("c31s")
                LD(c31s[:], c31, [b_c31], b_c31)
                SPC = STRIP // 4
                stgs = [(sb(f"stripstg{i}", [128, SPC]), Buf(f"stripstg{i}")) for i in range(2)]
                for h in range(HA):
                    for q4 in range(4):
                        stg, b_stg = stgs[(h * 4 + q4) % 2]
                        LD(stg[:], strip_raw[:, h, q4 * SPC:(q4 + 1) * SPC], [b_stg], b_stg)
                        TSC("dve", TS_[:, h, q4 * SPC:(q4 + 1) * SPC], stg[:], c31s[:, h:h + 1], None, ALU.subtract,
                            None, [b_stg, b_c31], [b_TS])

                def head_loads(h):
                    sl = h % 2
                    QA, b_QA = QAs[sl]; KA, b_KA = KAs[sl]; VA, b_VA = VAs[sl]
                    LD(QA[0:64, :], s_qn[h * 64:(h + 1) * 64, :], [b_QA], b_QA)
                    LD(KA[0:64, :], s_kn[h * 64:(h + 1) * 64, :], [b_KA], b_KA)
                    vsrc = s_v[:, h * 64:(h + 1) * 64].rearrange("(t p) d -> p t d", p=128)
                    nvs = max(1, NT // 8)
                    for i in range(0, NT, nvs):
                        LD(VA[:, i:i + nvs, 0:64], vsrc[:, i:i + nvs, :], [b_VA], b_VA)

                def gate_stage(h, tb, stage_):
                    sl = h % 2
                    QA, b_QA = QAs[sl]
                    cq, po = h // 2, 64 * (h % 2)
                    if stage_ == 0:
                        if tb == 0:
                            TSC("dve", KMh[0:64, 0:NB], KS[po:po + 64, cq, :], 1.0 / 256, None, ALU.mult, None,
                                [b_KS], [b_KMh])
                        for t in range(NTB):
                            MM(Gp[:, t, 0:NB], QA[0:64, (tb + t) * 128:(tb + t + 1) * 128], KMh[0:64, 0:NB],
                               True, True, [b_QA, b_KMh], [b_Gp])
                    elif stage_ == 1:
                        if NB < 32:
                            Sc.op("dve", lambda e: e.memset(Gs[:], NEG), (), [b_Gs])
                        TT("dve", Gs[:, :, 0:NB], Gp[:, 0:NTB, 0:NB], FUT[:, tb:tb + NTB, 0:NB], ALU.add,
                           [b_Gp, b_FUT], [b_Gs])
                        for t in range(NTB):
                            Sc.op("dve", lambda e, t=t: e.max(out=thr[:, t, :], in_=Gs[:, t, :]), [b_Gs], [b_thr])
                        TT("dve", nsel[:], Gs[:], thr[:, :, 2:3].to_broadcast([128, NTB, 32]), ALU.is_lt,
                           [b_Gs, b_thr], [b_nsel])
                        TT("pool", MBp[:, :, 64:96], nsel[:], NEGP[:, tb:tb + NTB, :], ALU.mult,
                           [b_nsel, b_NEGP], [b_MBp])
                    else:
                        t4 = (stage_ - 2) * 4
                        for t in range(4):
                            MM(MTp[:, t * 128:(t + 1) * 128], MBp[:, t4 + t, :], ident[:], True, True,
                               [b_MBp, b_ident], [b_MTp])
                        c0 = (tb + t4) * 128
                        CP("act", QA[64:96, c0:c0 + 512], MTp[64:96, :], [b_MTp], [b_QA])

                NST = 2 + NTB // 4
                gate_sched = [(tb, st_) for tb in range(0, NT, NTB) for st_ in range(NST)]

                def head_gate(h):
                    for (tb, st_) in gate_sched:
                        gate_stage(h, tb, st_)

                head_loads(0)
                head_gate(0)
                for h in range(HA):
                    sl = h % 2
                    QA, b_QA = QAs[sl]; KA, b_KA = KAs[sl]; VA, b_VA = VAs[sl]
                    pairs = [(g, kp) for g in range(NG) for kp in range(2 * g + 2)]
                    slots = {}

                    def emit_qk(i):
                        g, kp = pairs[i]
                        Sp, bS = STp.next()
                        for u in range(2):
                            kt = 2 * kp + u
                            delta = 512 * g - 128 * kt
                            near = delta <= 1536
                            MM(Sp[:, u * 512:(u + 1) * 512], KA[0:96, kt * 128:(kt + 1) * 128],
                               QA[0:96, g * 512:(g + 1) * 512], True, not near, [b_KA, b_QA], [bS])
                            if near:
                                MM(Sp[:, u * 512:(u + 1) * 512], ident[:],
                                   TS_[:, h, delta + 384:delta + 384 + 512], False, True, [b_ident, b_TS], [bS])
                        slots[i] = (Sp, bS)

                    emit_qk(0)
                    O, bO = None, None
                    gate_i0 = len(pairs) // 4
                    gate_step = max(1, (len(pairs) - gate_i0 - 2) // len(gate_sched))
                    for i, (g, kp) in enumerate(pairs):
                        npair = 2 * g + 2
                        if kp == 0:
                            O, bO = Op.next()
                            SAG, b_SAG = SAGr.next()
                            LD(SAG[:], s_sag[h * 64:(h + 1) * 64, g * 512:(g + 1) * 512], [b_SAG], b_SAG)
                        if i + 1 < len(pairs):
                            emit_qk(i + 1)
                        if i == 0 and h + 1 < HA:
                            head_loads(h + 1)
                        if h + 1 < HA and i >= gate_i0 and (i - gate_i0) % gate_step == 0:
                            kq = (i - gate_i0) // gate_step
                            if kq < len(gate_sched):
                                gate_stage(h + 1, *gate_sched[kq])
                        Sp, bS = slots.pop(i)
                        P_, bPt = PT.next()
                        ACT(P_[:], Sp[:], AF.Exp, [bS], [bPt])
                        for u in range(2):
                            kt = 2 * kp + u
                            MM(O[:], VA[:, kt, :], P_[:, u * 512:(u + 1) * 512], kt == 0, kt == 2 * npair - 1,
                               [b_VA, bPt], [bO])
                        if kp == npair - 1:
                            Sc.op("dve", lambda e, O=O: e.reciprocal(out=rden[:], in_=O[64:128, :]), [bO], [b_rden])
                            TT("dve", yh[:], O[0:64, :], rden[:], ALU.mult, [bO, b_rden], [b_yh])
                            YG, b_YG = YGr.next()
                            TT("pool", YG[:], yh[:], SAG[:], ALU.mult, [b_yh, b_SAG], [b_YG])
                            ST(XS[h // 2, (h % 2) * 64:(h % 2) * 64 + 64, g * 512:(g + 1) * 512], YG[:], [b_YG], b_YG)
                Sc.emit()

            with ExitStack() as st:
                sb, ps = mk(st)
                QDs = [(sb(f"QD{i}", [128, S], BF16), Buf(f"QD{i}")) for i in range(HB)]
                KDs = [(sb(f"KD{i}", [128, S], BF16), Buf(f"KD{i}")) for i in range(HB)]
                KEr = Ring([(sb(f"KE{i}", [128, 512], BF16), Buf(f"KE{i}")) for i in range(4)])
                VBr = Ring([(sb(f"VB{i}", [64, 8, 128], BF16), Buf(f"VB{i}")) for i in range(4)])
                SBGr = Ring([(sb(f"SBG{i}", [128, 512], BF16), Buf(f"SBG{i}")) for i in range(4)])
                YBr = Ring([(sb(f"YB{i}", [128, 512], BF16), Buf(f"YB{i}")) for i in range(4)])
                KTr = Ring([(sb(f"KT{i}", [64, 8, 128], BF16), Buf(f"KT{i}")) for i in range(4)])
                ATs = Ring([(sb(f"ATs{i}", [64, 64], BF16), Buf(f"ATs{i}")) for i in range(4)])
                Sbs = [[(sb(f"Sb{hh}_{i}", [128, 128], BF16), Buf(f"Sb{hh}_{i}")) for i in range(2)] for hh in range(HB)]
                osq = sb("osq", [128, 512], BF16); b_osq = Buf("osq")
                ort = sb("ort", [128, 512]); b_ort = Buf("ort")
                ors = sb("ors", [128, 512]); b_ors = Buf("ors")
                ybf = sb("ybf", [128, 512]); b_ybf = Buf("ybf")
                KT1 = (ps("KTp", [64, 8, 128], BF16), Buf("KTp"))
                KTp = [KT1] * HB
                OHp = [Ring([(ps(f"OHp{hh}_{i}", [128, 512]), Buf(f"OHp{hh}_{i}")) for i in range(1)]) for hh in range(HB)]
                Abk = [ps(f"Abk{hh}", [128, 512]) for hh in range(HB)]
                dSbk = [ps(f"dSbk{hh}", [128, 512]) for hh in range(HB)]
                ATp = [(Abk[hh][0:64, 0:64], Buf(f"ATp{hh}")) for hh in range(HB)]
                dSp = [(dSbk[hh][:, 0:128], Buf(f"dSp{hh}")) for hh in range(HB)]
                NSp = ps("NSp", [128, 512]); b_NSp = Buf("NSp")
                b_XGm = [Buf(f"XGm{k}") for k in range(2)]
                b_XOm = Buf("XOm")
                for k in range(2):
                    AG(XG[k], XS[k], Buf(f"agx{k}"), w=[b_XGm[k]])
                b_XSH = [[Buf(f"XSH{hh}_{q}") for q in range(NQ)] for hh in range(HB)]
                for hh in range(HB):
                    rows = slice(hh * 128, (hh + 1) * 128)
                    LD(QDs[hh][0][:], s_qd[rows, :], [QDs[hh][1]], QDs[hh][1])
                    LD(KDs[hh][0][:], s_kd[rows, :], [KDs[hh][1]], KDs[hh][1])
                    Sc.op("dve", lambda e, hh=hh: e.memset(Sbs[hh][0][0][:], 0.0), (), [Sbs[hh][0][1]])
                cur = [0] * HB

                def group_loads(g):
                    out = []
                    for hh in range(HB):
                        rows = slice(hh * 128, (hh + 1) * 128)
                        KE, bKE = KEr.next()
                        LD(KE[:], s_ke[rows, g * 512:(g + 1) * 512], [bKE], bKE)
                        VB, bVB = VBr.next()
                        vsrc = s_vb[g * 512:(g + 1) * 512, hh * 128:(hh + 1) * 128].rearrange("(c s) v -> s c v", s=64)
                        LD(VB[:], vsrc, [bVB], bVB)
                        SBG, bSBG = SBGr.next()
                        LD(SBG[:], s_sbg[rows, g * 512:(g + 1) * 512], [bSBG], bSBG)
                        out.append((KE, bKE, VB, bVB, SBG, bSBG))
                    return out

                nxt = group_loads(0)
                for g in range(NG):
                    gl_ = nxt
                    if g + 1 < NG:
                        nxt = group_loads(g + 1)
                    KTl, OHl = [], []
                    for hh in range(HB):
                        KE, bKE, VB, bVB, SBG, bSBG = gl_[hh]
                        KTps, bKTp = KTp[hh]
                        for c in range(8):
                            Sc.op("pe", lambda e, KTps=KTps, c=c, KE=KE: e.transpose(
                                out=KTps[:, c, :], in_=KE[:, c * 64:(c + 1) * 64], identity=ident[:]),
                                [bKE, b_ident], [bKTp])
                        KTs, bKT = KTr.next()
                        CP("act", KTs[:], KTps[:], [bKTp], [bKT])
                        KTl.append((KTs, bKT))
                        OHl.append(OHp[hh].next())
                    for c in range(8):
                        ch = g * 8 + c
                        cs = slice(ch * 64, (ch + 1) * 64)
                        Asl = []
                        for hh in range(HB):
                            QD, b_QD = QDs[hh]; KD, b_KD = KDs[hh]
                            A, bA = ATp[hh]
                            MM(A, KD[:, cs], QD[:, cs], True, True, [b_KD, b_QD], [bA])
                            As, bAs = ATs.next()
                            TT("dve", As[:], A, caus[:], ALU.mult, [bA, b_caus], [bAs])
                            Asl.append((As, bAs))
                        for hh in range(HB):
                            KE, bKE, VB, bVB, SBG, bSBG = gl_[hh]
                            QD, b_QD = QDs[hh]
                            KTs, bKT = KTl[hh]; OH, bOH = OHl[hh]
                            As, bAs = Asl[hh]
                            S0, bS0 = Sbs[hh][cur[hh]]
                            S1, bS1 = Sbs[hh][1 - cur[hh]]
                            MM(OH[:, c * 64:(c + 1) * 64], VB[:, c, :], As[:], True, False, [bVB, bAs], [bOH])
                            MM(OH[:, c * 64:(c + 1) * 64], S0[:], QD[:, cs], False, True, [bS0, b_QD], [bOH])
                            dS, bdS = dSp[hh]
                            MM(dS, KTs[:, c, :], VB[:, c, :], True, True, [bKT, bVB], [bdS])
                            STT("dve", S1[:], S0[:], EL[:, hh, ch:ch + 1], dS, ALU.mult, ALU.add,
                                [bS0, b_EL, bdS], [bS1])
                            cur[hh] = 1 - cur[hh]
                    for hh in range(HB):
                        KE, bKE, VB, bVB, SBG, bSBG = gl_[hh]
                        OH, bOH = OHl[hh]
                        ACT(osq[:], OH[:], AF.Square, [bOH], [b_osq])
                        MM(NSp[:], ones[:], osq[:], True, True, [b_ones, b_osq], [b_NSp])
                        ACT(ort[:], NSp[:], AF.Ln, [b_NSp], [b_ort], bias=EPS, scale=1.0 / 128)
                        ACT(ors[:], ort[:], AF.Exp, [b_ort], [b_ors], scale=-0.5)
                        STT("dve", ybf[:], OH[:], go[:, l, hh:hh + 1], ors[:], ALU.mult, ALU.mult,
                            [bOH, b_go, b_ors], [b_ybf])
                        YB, bYB = YBr.next()
                        TT("pool", YB[:], ybf[:], SBG[:], ALU.mult, [b_ybf, bSBG], [bYB])
                        qd_, col_ = (g * 512) // QW, (g * 512) % QW
                        ST(XSH[hh, qd_, :, col_:col_ + 512], YB[:], [bYB], bYB, w=[b_XSH[hh][qd_]])
                    if g == NG // 2:
                        for k in range(2):
                            LD(XO[k, :, :], XG[k, :, bass.ds(off_own, SH)], [b_XOm], b_XOm, r=[b_XGm[k]])
                    if ((g + 1) * 512) % QW == 0 and (g * 512) // QW < NQ - 1:
                        qd_ = (g * 512) // QW
                        for hh in range(HB):
                            AG(XGH[hh, qd_ * 256:(qd_ + 1) * 256, :], XSH[hh, qd_], Buf(f"agh{hh}_{qd_}"),
                               r=[b_XSH[hh][qd_]])
                Sc.emit()

            with ExitStack() as st:
                sb, ps = mk(st)
                WA = sb("WA", [128, 4, D], BF16)
                WB_ = sb("WB", [128, 4, D], BF16)
                WO = sb("WO", [128, 8, D], BF16)
                WP = sb("WP", [128, 2, D], BF16)
                WG = sb("WG", [128, 8, D], BF16)
                bW = {}

                def wtok(name, c, j):
                    return bW[(name, (c // 2) * 2, (j // 2) * 256)]
                wst = [(sb(f"wstc{i}", [128, 2, 256]), Buf(f"wstc{i}")) for i in range(4)]
                k = 0
                for (dst, nm, src, nch) in ((WA, "WA", w_up_a[l], 4), (WB_, "WB", w_up_b[l], 4),
                                            (WO, "WO", w_out[l], 8), (WP, "WP", w_ple[l], 2),
                                            (WG, "WG", w_pg[l], 8)):
                    sv = src.rearrange("(c p) n -> p c n", p=128)
                    for c2 in range(0, nch, 2):
                        for n2 in range(0, D, 256):
                            t, b = wst[k % 4]
                            bW[(nm, c2, n2)] = Buf(f"{nm}_{c2}_{n2}")
                            LD(t[:], sv[:, c2:c2 + 2, n2:n2 + 256], [b], b)
                            CP(("act", "dve")[k % 2], dst[:, c2:c2 + 2, n2:n2 + 256], t[:], [b], [bW[(nm, c2, n2)]])
                            k += 1
                INS = []
                for i in range(2):
                    INS.append(dict(
                        YAG=(sb(f"YAG{i}", [128, 4, 512], BF16), Buf(f"YAG{i}")),
                        YBG=(sb(f"YBG{i}", [128, 4, 512], BF16), Buf(f"YBG{i}")),
                        SGA=(sb(f"SGA{i}", [128, 8, 512], BF16), Buf(f"SGA{i}")),
                        SGB=(sb(f"SGB{i}", [128, 8, 512], BF16), Buf(f"SGB{i}")),
                        PB=(sb(f"PB{i}", [128, 2, 512], BF16), Buf(f"PB{i}"))))
                H = sb("Hc", [128, 8, 512]); b_H = Buf("Hc")
                PF = sb("PF", [128, 2, 512]); b_PF = Buf("PF")
                MG = sb("MG", [128, 8, 512], BF16); b_MG = Buf("MG")
                HM = sb("HM", [128, 8, 512]); b_HM = Buf("HM")
                HMb = sb("HMb", [128, 8, 512], BF16); b_HMb = Buf("HMb")
                HN = Ring([(sb(f"HN{i}", [128, 512]), Buf(f"HN{i}")) for i in range(2)])
                HNb = Ring([(sb(f"HNb{i}", [128, 512], BF16), Buf(f"HNb{i}")) for i in range(2)])
                tmp = Ring([(sb(f"tmp{i}", [128, 512]), Buf(f"tmp{i}")) for i in range(4)])
                PS = Ring([(ps(f"PSc{i}", [128, 512]), Buf(f"PSc{i}")) for i in range(8)])
                hv = h_own.rearrange("(c p) t -> p c t", p=128)
                hd = h_dst.rearrange("(c p) t -> p c t", p=128)
                pv = pT[l].rearrange("(c p) t -> p c t", p=128)
                b_XGl = [Buf(f"XGl{hh}") for hh in range(HB)]
                for hh in range(HB):
                    AG(XGH[hh, (NQ - 1) * 256:NQ * 256, :], XSH[hh, NQ - 1], Buf(f"aghl{hh}"), w=[b_XGl[hh]])
                off_rows = (nc.sync.partition_id() % 2) * 512
                b_XOH = [Buf(f"XOH{i}") for i in range(2)]

                def xoh_copy(i):
                    for hh in range(HB):
                        LD(XOH[hh, :, :, i * QW:(i + 1) * QW].rearrange("r p t -> (r p) t"),
                           XGH[hh, (i * 256):, :][bass.ds(off_rows, 256), :],
                           [b_XOH[i]], b_XOH[i], r=([b_XGl[hh]] if i == 1 else []))

                xoh_copy(0)
                g_xoh1 = max(QW // 512 - 1, 0)
                b_HS = [Buf(f"HSp{q}") for q in range(NPC)]

                def loads(g):
                    I = INS[g % 2]
                    ts_ = slice(g * 512, (g + 1) * 512)
                    for c in range(4):
                        rk, kk = c // 2, c % 2
                        LD(I["YAG"][0][:, c, :], XO[kk, rk * 128:(rk + 1) * 128, ts_], [I["YAG"][1]], I["YAG"][1])
                        LD(I["YBG"][0][:, c, :], XOH[kk, rk, :, ts_], [I["YBG"][1]], I["YBG"][1],
                           r=[b_XOH[(g * 512) // QW]])
                    LD(I["SGA"][0][:], s_sga.rearrange("(c p) t -> p c t", p=128)[:, :, ts_], [I["SGA"][1]], I["SGA"][1])
                    LD(I["SGB"][0][:], s_sgb.rearrange("(c p) t -> p c t", p=128)[:, :, ts_], [I["SGB"][1]], I["SGB"][1])
                    LD(PF[:], pv[:, :, ts_], [b_PF], b_PF)
                    CP("act", I["PB"][0][:], PF[:], [b_PF], [I["PB"][1]])

                loads(0)
                LD(H[:], hv[:, :, 0:512], [b_H], b_H)
                for g in range(NGO):
                    ts_ = slice(g * 512, (g + 1) * 512)
                    I = INS[g % 2]
                    YAG, b_YAG = I["YAG"]; YBG, b_YBG = I["YBG"]; SGA, b_SGA = I["SGA"]; SGB, b_SGB = I["SGB"]
                    PB, b_PB = I["PB"]
                    if g == g_xoh1:
                        xoh_copy(1)
                    if g + 1 < NGO:
                        loads(g + 1)
                    for j in range(8):
                        Pa, bPa = PS.next()
                        for c in range(4):
                            MM(Pa[:], WA[:, c, j * 128:(j + 1) * 128], YAG[:, c, :], c == 0, c == 3,
                               [wtok("WA", c, j), b_YAG], [bPa])
                        Pb, bPb = PS.next()
                        for c in range(4):
                            MM(Pb[:], WB_[:, c, j * 128:(j + 1) * 128], YBG[:, c, :], c == 0, c == 3,
                               [wtok("WB", c, j), b_YBG], [bPb])
                        t1, bt1 = tmp.next()
                        TT("dve", t1[:], Pa[:], SGA[:, j, :], ALU.mult, [bPa, b_SGA], [bt1])
                        t2, bt2 = tmp.next()
                        TT("dve", t2[:], Pb[:], SGB[:, j, :], ALU.mult, [bPb, b_SGB], [bt2])
                        TT("pool", MG[:, j, :], t1[:], t2[:], ALU.add, [bt1, bt2], [b_MG])
                    for j in range(8):
                        Po, bPo = PS.next()
                        for c in range(8):
                            MM(Po[:], WO[:, c, j * 128:(j + 1) * 128], MG[:, c, :], c == 0, c == 7,
                               [wtok("WO", c, j), b_MG], [bPo])
                        TT("dve", HM[:, j, :], Po[:], H[:, j, :], ALU.add, [bPo, b_H], [b_HM])
                        CP("act", HMb[:, j, :], HM[:, j, :], [b_HM], [b_HMb])
                    if g + 1 < NGO:
                        LD(H[:], hv[:, :, (g + 1) * 512:(g + 2) * 512], [b_H], b_H)
                    q, col = (g * 512) // PIECE, (g * 512) % PIECE
                    for j in range(8):
                        Pp, bPp = PS.next()
                        for c in range(2):
                            MM(Pp[:], WP[:, c, j * 128:(j + 1) * 128], PB[:, c, :], c == 0, c == 1,
                               [wtok("WP", c, j), b_PB], [bPp])
                        Pg, bPg = PS.next()
                        for c in range(8):
                            MM(Pg[:], WG[:, c, j * 128:(j + 1) * 128], HMb[:, c, :], c == 0, c == 7,
                               [wtok("WG", c, j), b_HMb], [bPg])
                        sg, bsg = tmp.next()
                        ACT(sg[:], Pg[:], AF.Sigmoid, [bPg], [bsg])
                        t1, bt1 = tmp.next()
                        TT("dve", t1[:], Pp[:], sg[:], ALU.mult, [bPp, bsg], [bt1])
                        hn, bhn = HN.next()
                        TT("pool", hn[:], t1[:], HM[:, j, :], ALU.add, [bt1, b_HM], [bhn])
                        ST(hd[:, j, ts_], hn[:], [bhn], bhn, q="sp")
                        if not last:
                            hb, bhb = HNb.next()
                            CP("act", hb[:], hn[:], [bhn], [bhb])
                            ST(HS[q, j * 128:(j + 1) * 128, col:col + 512], hb[:], [bhb], bhb, w=[b_HS[q]], q="sp")
                    if not last and (g + 1) * 512 % PIECE == 0 and q < NPC - 1:
                        AG(HG[q], HS[q], Buf(f"agh{q}"), r=[b_HS[q]])
                Sc.emit()

        print("total ops", Sc.total, "sems", Sc.nsem)
    return nc


def host_consts(S):
    NT = S // 128
    bf = ml_dtypes.bfloat16
    c = {}
    c["c_ident"] = np.eye(128, dtype=np.float32).astype(bf)
    blk = np.zeros((128, 128), np.float32)
    blk[:64, :64] = 1.0
    blk[64:, 64:] = 1.0
    c["c_blk"] = blk.astype(bf)
    oh = np.zeros((32, S), np.float32)
    for n in range(S // 256):
        oh[n, n * 256:(n + 1) * 256] = 1.0
    c["c_onehot"] = oh.astype(bf)
    c["c_causal"] = np.triu(np.ones((64, 64), np.float32))
    sm = np.ones((128, 512), np.float32)
    sm[:, ::64] = 0.0
    c["c_scan"] = sm
    fut = np.zeros((NT, 32), np.float32)
    neg = np.full((NT, 32), NEG, np.float32)
    for t in range(NT):
        b = t // 2
        fut[t, b:] = NEG
        neg[t, b] = 0.0
    c["c_fut"] = np.ascontiguousarray(np.broadcast_to(fut[None], (128, NT, 32))).astype(bf)
    c["c_neg"] = np.ascontiguousarray(np.broadcast_to(neg[None], (128, NT, 32))).astype(bf)
    return c


def host_strips(rel_bias, heads):
    i = np.arange(128)[:, None]
    u = np.arange(STRIP)[None, :]
    rel = u - 384 - i
    bucket = t5_bucket_np(rel)
    strip = np.empty((128, len(heads), STRIP), np.float32)
    for k, h in enumerate(heads):
        g = rel_bias[:, h][bucket]
        strip[:, k, :] = np.where(rel >= 0, g, np.float32(NEG))
    c31 = np.ascontiguousarray(np.broadcast_to(rel_bias[31, heads][None, :], (128, len(heads)))).astype(np.float32)
    return strip, c31


def host_inputs(b, r, S, x, p, norm_gain, w_in, q_norm_gain, k_norm_gain, rel_bias, hgrn_lb_logits,
                hgrn_out_gain, w_up_a, w_up_b, w_out, w_ple, w_ple_gate, consts):
    L = w_in.shape[0]
    SH = S // 2
    m = dict(consts)
    m["xT"] = np.ascontiguousarray(x[b, :S].T)
    m["xT_own"] = np.ascontiguousarray(x[b, r * SH:(r + 1) * SH].T)
    m["pT"] = np.ascontiguousarray(np.transpose(p[:, b, r * SH:(r + 1) * SH, :], (0, 2, 1)))
    cols = []
    for blk0 in range(0, 4096, 512):
        cols.append(np.arange(blk0 + r * 256, blk0 + (r + 1) * 256))
    cols.append(np.arange(4096, 6144))
    cols = np.concatenate(cols)
    m["w_in"] = np.ascontiguousarray(w_in[:, :, cols])
    m["w_up_a"] = w_up_a
    m["w_up_b"] = w_up_b
    m["w_out"] = w_out
    m["w_ple"] = w_ple
    m["w_pg"] = w_ple_gate
    m["g_norm"] = np.ascontiguousarray(np.transpose(norm_gain.reshape(L, 8, 128), (2, 0, 1)))
    m["g_q"] = np.ascontiguousarray(np.concatenate([q_norm_gain, q_norm_gain], axis=1).T)
    m["g_k"] = np.ascontiguousarray(np.concatenate([k_norm_gain, k_norm_gain], axis=1).T)
    m["g_o"] = np.ascontiguousarray(np.transpose(hgrn_out_gain.reshape(L, 4, 128)[:, 2 * r:2 * r + 2], (2, 0, 1)))
    m["lbl"] = np.ascontiguousarray(np.transpose(hgrn_lb_logits.reshape(L, 4, 128)[:, 2 * r:2 * r + 2], (2, 0, 1)))
    strip, c31 = host_strips(rel_bias, list(range(4 * r, 4 * r + 4)))
    m["strip_raw"] = strip
    m["c31"] = c31
    return m


_NC_CACHE = {}


def kernel(x, p, norm_gain, w_in, q_norm_gain, k_norm_gain, rel_bias, hgrn_lb_logits,
           hgrn_out_gain, w_up_a, w_up_b, w_out, w_ple, w_ple_gate):
    args = [np.asarray(a, dtype=np.float32) for a in (
        x, p, norm_gain, w_in, q_norm_gain, k_norm_gain, rel_bias, hgrn_lb_logits,
        hgrn_out_gain, w_up_a, w_up_b, w_out, w_ple, w_ple_gate)]
    x = args[0]
    B, S, _ = x.shape
    if S not in _NC_CACHE:
        _NC_CACHE[S] = build(S)
    nc = _NC_CACHE[S]
    consts = host_consts(S)
    in_maps = [host_inputs(i // 2, i % 2, S, *args, consts) for i in range(2 * B)]
    res = run_bass_kernel_spmd(nc, in_maps, core_ids=list(range(2 * B)))
    SH = S // 2
    out = np.empty((B, S, D), np.float32)
    for i in range(2 * B):
        out[i // 2, (i % 2) * SH:(i % 2 + 1) * SH, :] = res.results[i]["hT_out"].T
    return out
```

```python
import math
from contextlib import ExitStack

import numpy as np
import ml_dtypes
import concourse.bass as bass
import concourse.mybir as mybir
from concourse.bass_utils import run_bass_kernel_spmd

F32 = mybir.dt.float32
BF16 = mybir.dt.bfloat16
AF = mybir.ActivationFunctionType
ALU = mybir.AluOpType
AX = mybir.AxisListType

SEM_CHUNK = 20000
D = 1024
NEG = -30000.0
STRIP = 2432
EPS = 1e-6


class Buf:
    __slots__ = ("name", "last_writer", "dma_writers", "readers", "dma_readers")

    def __init__(self, name):
        self.name = name
        self.clear()

    def clear(self):
        self.last_writer = None
        self.dma_writers = []
        self.readers = {}
        self.dma_readers = []


class Sched:
    def __init__(self, nc, stack):
        self.nc = nc
        self.stack = stack
        self.ops = []
        self.eng = {"pe": nc.tensor, "act": nc.scalar, "dve": nc.vector,
                    "pool": nc.gpsimd, "sp": nc.sync}
        self.nsem = 0
        self.eng_count = {}
        self.eng_sems = {}
        self.dma_pool = []
        self.dma_free = []
        self.key_slot = {}
        self.waited = {}
        self.total = 0

    def new_sem(self, name):
        self.nsem += 1
        return self.stack.enter_context(self.nc.semaphore(name))

    def op(self, eng, fn, reads=(), writes=()):
        self.ops.append(["c", eng, fn, tuple(reads), tuple(writes), None, 1])

    def dma(self, queue, fn, reads=(), writes=(), key=None, inc=16):
        assert key is not None
        self.ops.append(["d", queue, fn, tuple(reads), tuple(writes), key, inc])

    def emit(self):
        ops = self.ops
        n = len(ops)
        deps = [None] * n
        signaling = [False] * n
        last_of_eng = {}
        for i, o in enumerate(ops):
            d = set()
            isd = o[0] == "d"
            for b in o[3]:
                if b.last_writer is not None:
                    d.add(b.last_writer)
                d.update(b.dma_writers)
            for b in o[4]:
                if b.last_writer is not None:
                    d.add(b.last_writer)
                d.update(b.readers.values())
                d.update(b.dma_readers)
                if not isd:
                    d.update(b.dma_writers)
            d.discard(i)
            for b in o[3]:
                if isd:
                    b.dma_readers.append(i)
                else:
                    b.readers[o[1]] = i
            for b in o[4]:
                if isd:
                    b.dma_writers.append(i)
                else:
                    b.last_writer = i
                    b.dma_writers = []
                    b.readers = {}
                    b.dma_readers = []
            deps[i] = d
            for j in d:
                signaling[j] = True
            if o[0] == "c":
                last_of_eng[o[1]] = i
        for e, i in last_of_eng.items():
            signaling[i] = True
        sig = [None] * n
        for i, o in enumerate(ops):
            if o[0] == "c":
                if not signaling[i]:
                    continue
                e = o[1]
                c = self.eng_count.get(e, 0)
                k = c // SEM_CHUNK
                if (e, k) not in self.eng_sems:
                    self.eng_sems[(e, k)] = self.new_sem(f"s_{e}_{k}")
                self.eng_count[e] = c + 1
                sig[i] = (self.eng_sems[(e, k)], c - k * SEM_CHUNK + 1, 1, ("e", e, k))
            else:
                key = o[5]
                if key not in self.key_slot:
                    if self.dma_free:
                        s = self.dma_free.pop()
                    else:
                        self.dma_pool.append([self.new_sem(f"d_{len(self.dma_pool)}"), 0])
                        s = len(self.dma_pool) - 1
                    self.key_slot[key] = s
                s = self.key_slot[key]
                self.dma_pool[s][1] += o[6]
                sig[i] = (self.dma_pool[s][0], self.dma_pool[s][1], o[6], ("k", s))
        waited = self.waited
        for i, o in enumerate(ops):
            e = o[1]
            engine = self.eng[e]
            w = waited.setdefault(e, {})
            for j in sorted(deps[i]):
                pj = ops[j]
                if pj[0] == "c" and pj[1] == "pe" and e == "pe" and o[0] == "c":
                    continue
                sem, val, _, sid = sig[j]
                if w.get(sid, 0) >= val:
                    continue
                w[sid] = val
                engine.wait_ge(sem, val)
            ins = o[2](engine)
            if sig[i] is not None:
                ins.then_inc(sig[i][0], sig[i][2])
        finals = []
        for (e, k), sem in self.eng_sems.items():
            c = self.eng_count.get(e, 0)
            if c // SEM_CHUNK == k and c - k * SEM_CHUNK > 0:
                finals.append((sem, c - k * SEM_CHUNK, ("e", e, k)))
            elif c // SEM_CHUNK > k:
                finals.append((sem, SEM_CHUNK, ("e", e, k)))
        for s, (sem, c) in enumerate(self.dma_pool):
            if c > 0:
                finals.append((sem, c, ("k", s)))
        for e in ("sp", "pe", "act", "dve", "pool"):
            w = waited.setdefault(e, {})
            for sem, val, sid in finals:
                if w.get(sid, 0) >= val:
                    continue
                w[sid] = val
                self.eng[e].wait_ge(sem, val)
        for o in ops:
            for b in o[3] + o[4]:
                b.clear()
        self.key_slot = {}
        self.dma_free = list(range(len(self.dma_pool)))
        self.total += n
        self.ops = []
        return n


class Ring:
    def __init__(self, items):
        self.items = items
        self.i = 0

    def next(self):
        it = self.items[self.i % len(self.items)]
        self.i += 1
        return it


def t5_bucket_np(rel):
    n = np.maximum(rel, 0)
    nf = np.maximum(n, 16).astype(np.float32)
    large = 16 + (np.log(nf / np.float32(16)) / np.float32(math.log(128)) * np.float32(16)).astype(np.int32)
    large = np.minimum(large, 31)
    return np.where(n < 16, n, large)


def build(S, L=2, debug=False, groups=None):
    NT, NB, NG, NCH = S // 128, S // 256, S // 512, S // 64
    SH, NGO = S // 2, S // 1024
    HA, HB = 4, 2
    if groups is None:
        groups = [[0, 1], [2, 3], [4, 5], [6, 7]]
    nc = bass.Bass("TRN2", target_bir_lowering=False)

    def din(name, shape, dt=F32):
        return nc.dram_tensor(name, list(shape), dt, kind="ExternalInput").ap()

    def dscr(name, shape, dt=BF16, out=False):
        kind = "Internal"
        return nc.dram_tensor(name, list(shape), dt, kind=kind).ap()

    xT = din("xT", [D, S])
    xT_own = din("xT_own", [D, SH])
    pT = din("pT", [L, 256, SH])
    w_in = din("w_in", [L, D, 4096])
    w_up_a = din("w_up_a", [L, 512, D])
    w_up_b = din("w_up_b", [L, 512, D])
    w_out = din("w_out", [L, D, D])
    w_ple = din("w_ple", [L, 256, D])
    w_pg = din("w_pg", [L, D, D])
    g_norm = din("g_norm", [128, L, 8])
    g_q = din("g_q", [128, L])
    g_k = din("g_k", [128, L])
    g_o = din("g_o", [128, L, HB])
    lbl = din("lbl", [128, L, HB])
    strip_raw = din("strip_raw", [128, HA, STRIP])
    c31 = din("c31", [128, HA])
    c_ident = din("c_ident", [128, 128], BF16)
    c_blk = din("c_blk", [128, 128], BF16)
    c_onehot = din("c_onehot", [32, S], BF16)
    c_causal = din("c_causal", [64, 64])
    c_scan = din("c_scan", [128, 512])
    c_fut = din("c_fut", [128, NT, 32], BF16)
    c_neg = din("c_neg", [128, NT, 32], BF16)

    hT_out = nc.dram_tensor("hT_out", [D, SH], F32, kind="ExternalOutput").ap()
    h_mid = dscr("h_mid", [D, SH], F32)
    s_qn = dscr("s_qn", [256, S])
    s_kn = dscr("s_kn", [256, S])
    s_v = dscr("s_v", [S, 256])
    s_sag = dscr("s_sag", [256, S])
    s_qd = dscr("s_qd", [256, S])
    s_kd = dscr("s_kd", [256, S])
    s_ke = dscr("s_ke", [256, S])
    s_vb = dscr("s_vb", [S, 256])
    s_sbg = dscr("s_sbg", [256, S])
    s_sga = dscr("s_sga", [D, SH])
    s_sgb = dscr("s_sgb", [D, SH])
    XS = dscr("XS", [4, 128, S], out=True)
    XG = dscr("XG", [4, 2 * 128, S], out=True)
    XO = dscr("XO", [4, 2 * 128, SH])
    NQ = 4
    QW = S // NQ
    XSH = dscr("XSH", [HB, NQ, 128, QW])
    XGH = dscr("XGH", [HB, NQ * 2 * 128, QW])
    XOH = dscr("XOH", [HB, 2, 128, SH])
    PIECE = max(512, SH // 4)
    NPC = SH // PIECE
    HS = dscr("HS", [NPC, D, PIECE])
    HG = dscr("HG", [NPC, 2 * D, PIECE])

    off_own = (nc.sync.partition_id() % 2) * SH

    with ExitStack() as outer:
        Sc = Sched(nc, outer)
        uniq = [0]

        def mk(stack):
            uniq[0] += 1
            tag = f"u{uniq[0]}_"

            def sb(name, shape, dt=F32):
                return stack.enter_context(nc.sbuf_tensor(tag + name, list(shape), dt))

            def ps(name, shape, dt=F32):
                return stack.enter_context(nc.psum_tensor(tag + name, list(shape), dt))
            return sb, ps

        def MM(out, lhsT, rhs, start, stop, r, w):
            Sc.op("pe", lambda e: e.matmul(out, lhsT=lhsT, rhs=rhs, start=start, stop=stop), r, w)

        def ACT(out, in_, func, r, w, bias=0.0, scale=1.0, eng="act"):
            Sc.op(eng, lambda e: e.activation(out=out, in_=in_, func=func, bias=bias, scale=scale), r, w)

        def TT(eng, out, in0, in1, op, r, w):
            Sc.op(eng, lambda e: e.tensor_tensor(out=out, in0=in0, in1=in1, op=op), r, w)

        def TSC(eng, out, in0, s1, s2, op0, op1, r, w):
            if s2 is None:
                Sc.op(eng, lambda e: e.tensor_scalar(out=out, in0=in0, scalar1=s1, scalar2=None, op0=op0), r, w)
            else:
                Sc.op(eng, lambda e: e.tensor_scalar(out=out, in0=in0, scalar1=s1, scalar2=s2, op0=op0, op1=op1), r, w)

        def STT(eng, out, in0, scalar, in1, op0, op1, r, w):
            Sc.op(eng, lambda e: e.scalar_tensor_tensor(out=out, in0=in0, scalar=scalar, in1=in1, op0=op0, op1=op1), r, w)

        def CP(eng, out, in_, r, w):
            if eng == "act":
                Sc.op("act", lambda e: e.activation(out=out, in_=in_, func=AF.Copy), r, w)
            else:
                Sc.op(eng, lambda e: e.tensor_copy(out=out, in_=in_), r, w)

        def LD(out, in_, w, key, r=(), q="sp"):
            Sc.dma(q, lambda e: e.dma_start(out=out, in_=in_), r, w, key)

        def ST(out, in_, r, key, w=(), q="pool"):
            Sc.dma(q, lambda e: e.dma_start(out=out, in_=in_), r, w, key)

        def AG(out, in_, key, r=(), w=()):
            Sc.dma("pool", lambda e: e.collective_compute(
                "AllGather", ALU.bypass, replica_groups=groups, ins=[in_], outs=[out]), r, w, key, inc=1)

        sbP, _ = mk(outer)
        ident = sbP("ident", [128, 128], BF16); b_ident = Buf("ident")
        blk = sbP("blk", [128, 128], BF16); b_blk = Buf("blk")
        ones = sbP("ones", [128, 128], BF16); b_ones = Buf("ones")
        gn = sbP("gn", [128, L, 8]); b_gn = Buf("gn")
        gq = sbP("gq", [128, L]); b_gq = Buf("gq")
        gk = sbP("gk", [128, L]); b_gk = Buf("gk")
        go = sbP("go", [128, L, HB]); b_go = Buf("go")
        lb = sbP("lb", [128, L, HB]); b_lb = Buf("lb")
        oml = sbP("oml", [128, L, HB]); b_oml = Buf("oml")
        lbe = sbP("lbe", [128, L, HB]); b_lbe = Buf("lbe")
        lbs = sbP("lbs", [128, HB]); b_lbs = Buf("lbs")
        KS = sbP("KS", [128, 2, NB]); b_KS = Buf("KS")
        EL = sbP("EL", [128, HB, NCH]); b_EL = Buf("EL")
        caus = sbP("caus", [64, 64]); b_caus = Buf("caus")
        scanm = sbP("scanm", [128, 512]); b_scanm = Buf("scanm")

        with ExitStack() as st:
            sb, ps = mk(st)
            LD(ident[:], c_ident, [b_ident], b_ident)
            LD(blk[:], c_blk, [b_blk], b_blk)
            Sc.op("pool", lambda e: e.memset(ones[:], 1.0), (), [b_ones])
            LD(gn[:], g_norm, [b_gn], b_gn)
            LD(gq[:], g_q, [b_gq], b_gq)
            LD(gk[:], g_k, [b_gk], b_gk)
            LD(go[:], g_o, [b_go], b_go)
            LD(lbe[:], lbl, [b_lbe], b_lbe)
            LD(caus[:], c_causal, [b_caus], b_caus)
            LD(scanm[:], c_scan, [b_scanm], b_scanm)
            TSC("dve", gq[:], gq[:], 0.125, None, ALU.mult, None, [b_gq], [b_gq])
            ACT(lbe[:], lbe[:], AF.Exp, [b_lbe], [b_lbe])
            CP("dve", lbs[:], lbe[:, 0, :], [b_lbe], [b_lbs])
            for l in range(1, L):
                TT("dve", lbs[:], lbs[:], lbe[:, l, :], ALU.add, [b_lbs, b_lbe], [b_lbs])
            Sc.op("dve", lambda e: e.reciprocal(out=lbs[:], in_=lbs[:]), [b_lbs], [b_lbs])
            Sc.op("dve", lambda e: e.memset(lb[:, 0, :], 0.0), (), [b_lb])
            for l in range(1, L):
                TT("dve", lb[:, l, :], lb[:, l - 1, :], lbe[:, l, :], ALU.add, [b_lb, b_lbe], [b_lb])
            for l in range(1, L):
                TT("dve", lb[:, l, :], lb[:, l, :], lbs[:], ALU.mult, [b_lb, b_lbs], [b_lb])
            for l in range(L):
                TSC("dve", oml[:, l, :], lb[:, l, :], -1.0, 1.0, ALU.mult, ALU.add, [b_lb], [b_oml])
            Sc.emit()

        for l in range(L):
            first, last = (l == 0), (l == L - 1)
            h_own = xT_own if first else h_mid
            h_dst = hT_out if last else h_mid

            with ExitStack() as st:
                sb, ps = mk(st)
                Wb = sb("Wb", [128, 8, 4096], BF16)
                b_Wbs = [Buf(f"Wb{i}") for i in range(32)]
                wst = [(sb(f"wst{i}", [128, 8, 128]), Buf(f"wst{i}")) for i in range(4)]
                wv = w_in[l].rearrange("(c p) n -> p c n", p=128)
                for i in range(32):
                    t, b = wst[i % 4]
                    LD(t[:], wv[:, :, i * 128:(i + 1) * 128], [b], b, q="pool")
                    CP(("act", "dve")[i % 2], Wb[:, :, i * 128:(i + 1) * 128], t[:], [b], [b_Wbs[i]])
                HDT = F32 if first else BF16
                H = sb("H", [128, 8, 512], HDT); b_H = Buf("H")
                H2 = sb("H2", [128, 8, 512]); b_H2 = Buf("H2")
                SQ = sb("SQ", [128, 8, 512], BF16); b_SQ = Buf("SQ")
                XNs = [(sb(f"XN{i}", [128, 8, 512], BF16), Buf(f"XN{i}")) for i in range(2)]
                rstd = sb("rstd", [128, 512]); b_rstd = Buf("rstd")
                stage = Ring([(sb(f"stg{i}", [128, 512], BF16), Buf(f"stg{i}")) for i in range(6)])
                f32r = Ring([(sb(f"f32r{i}", [128, 512]), Buf(f"f32r{i}")) for i in range(8)])
                QB = [(sb(f"QB{i}", [128, 512]), Buf(f"QB{i}")) for i in range(HB)]
                SG = [(sb(f"SG{i}", [128, 512]), Buf(f"SG{i}")) for i in range(HB)]
                sqq = Ring([(sb(f"sqq{i}", [128, 512], BF16), Buf(f"sqq{i}")) for i in range(2)])
                p_ss = ps("p_ss", [128, 512]); b_pss = Buf("p_ss")
                PS = Ring([(ps(f"PSa{i}", [128, 512]), Buf(f"PSa{i}")) for i in range(5)])
                BS = Ring([(ps(f"BSa{i}", [128, 512]), Buf(f"BSa{i}")) for i in range(2)])
                Sc.op("dve", lambda e: e.memset(KS[:], 0.0), (), [b_KS])

                b_HGl = Buf("HGlast")
                if not first:
                    AG(HG[NPC - 1], HS[NPC - 1], Buf("aghl"), w=[b_HGl])

                def load_h(g):
                    if first:
                        LD(H[:], xT.rearrange("(c p) t -> p c t", p=128)[:, :, g * 512:(g + 1) * 512], [b_H], b_H)
                    else:
                        half, tok = g // NGO, (g % NGO) * 512
                        q, col = tok // PIECE, tok % PIECE
                        src = HG[q, half * D:(half + 1) * D, col:col + 512].rearrange("(c p) t -> p c t", p=128)
                        LD(H[:], src, [b_H], b_H, r=([b_HGl] if q == NPC - 1 else []))

                def norm_part1(g):
                    load_h(g)
                    ACT(SQ[:].rearrange("p c t -> p (c t)"), H[:].rearrange("p c t -> p (c t)"),
                        AF.Square, [b_H], [b_SQ])

                def norm_part2(Hs, bH, XN, b_XN):
                    for c in range(8):
                        MM(p_ss[:], ones[:], SQ[:, c, :], c == 0, c == 7, [b_ones, b_SQ], [b_pss])
                    ACT(rstd[:], p_ss[:], AF.Ln, [b_pss], [b_rstd], bias=EPS, scale=1.0 / D)
                    ACT(rstd[:], rstd[:], AF.Exp, [b_rstd], [b_rstd], scale=-0.5)
                    for c in range(8):
                        STT("dve", XN[:, c, :], Hs[:, c, :], gn[:, l, c:c + 1], rstd[:],
                            ALU.mult, ALU.mult, [bH, b_gn, b_rstd], [b_XN])

                norm_part1(0)
                norm_part2(H, b_H, *XNs[0])
                for g in range(NG):
                    XN, b_XN = XNs[g % 2]

                    def proj_fm(c0):
                        P, bP = PS.next()
                        for c in range(8):
                            MM(P[:], Wb[:, c, c0:c0 + 128], XN[:, c, :], c == 0, c == 7,
                               [b_Wbs[c0 // 128], b_XN], [bP])
                        return P, bP

                    def store_fm(dst, j, t, b):
                        ST(dst[j * 128:(j + 1) * 128, g * 512:(g + 1) * 512], t[:], [b], b)

                    def qk_tail(j, P, bP, s2, bs2):
                        isk = j >= 2
                        Bp, bB = BS.next()
                        MM(Bp[:], blk[:], s2[:], True, True, [b_blk, bs2], [bB])
                        r1, br1 = f32r.next()
                        ACT(r1[:], Bp[:], AF.Ln, [bB], [br1], bias=EPS, scale=1.0 / 64)
                        ACT(r1[:], r1[:], AF.Exp, [br1], [br1], scale=-0.5)
                        o, bo = stage.next()
                        gg = gk if isk else gq
                        STT("dve", o[:], P[:], gg[:, l:l + 1], r1[:], ALU.mult, ALU.mult,
                            [bP, b_gk if isk else b_gq, br1], [bo])
                        if isk:
                            Sc.op("dve", lambda e, o=o, j=j, g=g: e.tensor_reduce(
                                out=KS[:, j - 2, 2 * g:2 * g + 2],
                                in_=o[:].rearrange("p (b t) -> p b t", t=256),
                                axis=AX.X, op=ALU.add), [bo], [b_KS])
                        store_fm(s_kn if isk else s_qn, j % 2, o, bo)

                    pend = None
                    for j in range(4):
                        P, bP = proj_fm(j * 128)
                        s2, bs2 = sqq.next()
                        ACT(s2[:], P[:], AF.Square, [bP], [bs2])
                        if pend is not None:
                            qk_tail(*pend)
                        pend = (j, P, bP, s2, bs2)
                    firstv = True
                    for (c0, dst) in ((512, s_v), (1536, s_vb)):
                        for tt in range(4):
                            P, bP = PS.next()
                            for c in range(8):
                                MM(P[:, 0:256], XN[:, c, tt * 128:(tt + 1) * 128], Wb[:, c, c0:c0 + 256],
                                   c == 0, c == 7, [b_XN, b_Wbs[c0 // 128], b_Wbs[c0 // 128 + 1]], [bP])
                            if firstv:
                                qk_tail(*pend)
                                firstv = False
                            o, bo = stage.next()
                            CP("act", o[:, 0:256], P[:, 0:256], [bP], [bo])
                            ST(dst[g * 512 + tt * 128:g * 512 + (tt + 1) * 128, :], o[:, 0:256], [bo], bo)
                    if g + 1 < NG:
                        norm_part1(g + 1)
                    for jj in range(HB):
                        P, bP = proj_fm(1280 + jj * 128)
                        ACT(SG[jj][0][:], P[:], AF.Sigmoid, [bP], [SG[jj][1]])
                    if g + 1 < NG:
                        norm_part2(H, b_H, *XNs[(g + 1) % 2])
                    for (c00, dst) in ((768, s_sag), (1792, s_sbg)):
                        for jj in range(2):
                            P, bP = proj_fm(c00 + jj * 128)
                            o, bo = stage.next()
                            ACT(o[:], P[:], AF.Silu, [bP], [bo])
                            store_fm(dst, jj, o, bo)
                    for jj in range(HB):
                        P, bP = proj_fm(1024 + jj * 128)
                        ACT(QB[jj][0][:], P[:], AF.Silu, [bP], [QB[jj][1]])
                    for jj in range(HB):
                        sg, bsg = SG[jj]
                        f, bf_ = f32r.next()
                        TSC("dve", f[:], sg[:], oml[:, l, jj:jj + 1], lb[:, l, jj:jj + 1], ALU.mult, ALU.add,
                            [bsg, b_oml, b_lb], [bf_])
                        gl, bgl = f32r.next()
                        ACT(gl[:], f[:], AF.Ln, [bf_], [bgl])
                        cum, bcum = f32r.next()
                        Sc.op("dve", lambda e, cum=cum, gl=gl: e.tensor_tensor_scan(
                            out=cum[:], data0=scanm[:], data1=gl[:], initial=0.0,
                            op0=ALU.mult, op1=ALU.add), [bgl, b_scanm], [bcum])
                        ec, bec = f32r.next()
                        ACT(ec[:], cum[:], AF.Exp, [bcum], [bec])
                        ACT(gl[:], cum[:], AF.Exp, [bcum], [bgl], scale=-1.0)
                        CP("pool", EL[:, jj, g * 8:(g + 1) * 8],
                           ec[:].rearrange("p (c t) -> p c t", t=64)[:, :, 63], [bec], [b_EL])
                        o, bo = stage.next()
                        TT("pool", o[:], QB[jj][0][:], ec[:], ALU.mult, [QB[jj][1], bec], [bo])
                        store_fm(s_qd, jj, o, bo)
                        TSC("dve", f[:], f[:], -1.0, 1.0, ALU.mult, ALU.add, [bf_], [bf_])
                        TT("dve", f[:], f[:], gl[:], ALU.mult, [bf_, bgl], [bf_])
                        o, bo = stage.next()
                        CP("act", o[:], f[:], [bf_], [bo])
                        store_fm(s_kd, jj, o, bo)
                        o, bo = stage.next()
                        TT("pool", o[:].rearrange("p (c t) -> p c t", t=64),
                           f[:].rearrange("p (c t) -> p c t", t=64),
                           EL[:, jj, g * 8:(g + 1) * 8].unsqueeze(2).to_broadcast([128, 8, 64]),
                           ALU.mult, [bf_, b_EL], [bo])
                        store_fm(s_ke, jj, o, bo)
                hov = h_own.rearrange("(c p) t -> p c t", p=128)
                def gate_norm(g):
                    XN, b_XN = XNs[g % 2]
                    LD(H2[:], hov[:, :, g * 512:(g + 1) * 512], [b_H2], b_H2)
                    ACT(SQ[:].rearrange("p c t -> p (c t)"), H2[:].rearrange("p c t -> p (c t)"),
                        AF.Square, [b_H2], [b_SQ])
                    norm_part2(H2, b_H2, XN, b_XN)

                gate_norm(0)
                for g in range(NGO):
                    XN, b_XN = XNs[g % 2]
                    for jj in range(16):
                        if jj == 6 and g + 1 < NGO:
                            gate_norm(g + 1)
                        P, bP = PS.next()
                        for c in range(8):
                            MM(P[:], Wb[:, c, 2048 + jj * 128:2048 + (jj + 1) * 128], XN[:, c, :], c == 0, c == 7,
                               [b_Wbs[16 + jj], b_XN], [bP])
                        o, bo = stage.next()
                        ACT(o[:], P[:], AF.Sigmoid, [bP], [bo])
                        dst = s_sga if jj < 8 else s_sgb
                        ST(dst[(jj % 8) * 128:(jj % 8 + 1) * 128, g * 512:(g + 1) * 512], o[:], [bo], bo)
                Sc.emit()

            with ExitStack() as st:
                sb, ps = mk(st)
                QAs = [(sb(f"QA{i}", [128, S], BF16), Buf(f"QA{i}")) for i in range(2)]
                KAs = [(sb(f"KA{i}", [128, S], BF16), Buf(f"KA{i}")) for i in range(2)]
                VAs = [(sb(f"VA{i}", [128, NT, 128], BF16), Buf(f"VA{i}")) for i in range(2)]
                SAGr = Ring([(sb(f"SAG{i}", [64, 512], BF16), Buf(f"SAG{i}")) for i in range(3)])
                YGr = Ring([(sb(f"YG{i}", [64, 512], BF16), Buf(f"YG{i}")) for i in range(3)])
                FUT = sb("FUT", [128, NT, 32], BF16); b_FUT = Buf("FUT")
                NEGP = sb("NEGP", [128, NT, 32], BF16); b_NEGP = Buf("NEGP")
                KMh = sb("KMh", [64, 32], BF16); b_KMh = Buf("KMh")
                NTB = min(NT, 16)
                Gs = sb("Gs", [128, NTB, 32]); b_Gs = Buf("Gs")
                thr = sb("thr", [128, NTB, 8]); b_thr = Buf("thr")
                nsel = sb("nsel", [128, NTB, 32]); b_nsel = Buf("nsel")
                MBp = sb("MBp", [128, NTB, 128], BF16); b_MBp = Buf("MBp")
                PT = Ring([(sb(f"PT{i}", [128, 1024], BF16), Buf(f"PT{i}")) for i in range(3)])
                rden = sb("rden", [64, 512]); b_rden = Buf("rden")
                yh = sb("yh", [64, 512]); b_yh = Buf("yh")
                STp = Ring([(ps(f"STp{i}", [128, 1024]), Buf(f"STp{i}")) for i in range(2)])
                Op = Ring([(ps(f"Op{i}", [128, 512]), Buf(f"Op{i}")) for i in range(2)])
                Gp = ps("Gp", [128, 16, 32]); b_Gp = Buf("Gp")
                MTp = ps("MTp", [128, 512]); b_MTp = Buf("MTp")
                LD(FUT[:], c_fut, [b_FUT], b_FUT)
                LD(NEGP[:], c_neg, [b_NEGP], b_NEGP)
                for i in range(2):
                    LD(KAs[i][0][64:96, :], c_onehot, [KAs[i][1]], KAs[i][1])
                    Sc.op("pool", lambda e, i=i: e.memset(VAs[i][0][:, :, 64:128], 1.0), (), [VAs[i][1]])
                Sc.op("pool", lambda e: e.memset(MBp[:], 0.0), (), [b_MBp])
                TS_ = sb("TS", [128, HA, STRIP], BF16); b_TS = Buf("TS")
                c31s = sb("c31s", [128, HA]); b_c31 = Buf("c31s")
                LD(c31s[:], c31, [b_c31], b_c31)
                SPC = STRIP // 4
                stgs = [(sb(f"stripstg{i}", [128, SPC]), Buf(f"stripstg{i}")) for i in range(2)]
                for h in range(HA):
                    for q4 in range(4):
                        stg, b_stg = stgs[(h * 4 + q4) % 2]
                        LD(stg[:], strip_raw[:, h, q4 * SPC:(q4 + 1) * SPC], [b_stg], b_stg)
                        TSC("dve", TS_[:, h, q4 * SPC:(q4 + 1) * SPC], stg[:], c31s[:, h:h + 1], None, ALU.subtract,
                            None, [b_stg, b_c31], [b_TS])

                def head_loads(h):
                    sl = h % 2
                    QA, b_QA = QAs[sl]; KA, b_KA = KAs[sl]; VA, b_VA = VAs[sl]
                    LD(QA[0:64, :], s_qn[h * 64:(h + 1) * 64, :], [b_QA], b_QA)
                    LD(KA[0:64, :], s_kn[h * 64:(h + 1) * 64, :], [b_KA], b_KA)
                    vsrc = s_v[:, h * 64:(h + 1) * 64].rearrange("(t p) d -> p t d", p=128)
                    nvs = max(1, NT // 8)
                    for i in range(0, NT, nvs):
                        LD(VA[:, i:i + nvs, 0:64], vsrc[:, i:i + nvs, :], [b_VA], b_VA)

                def gate_stage(h, tb, stage_):
                    sl = h % 2
                    QA, b_QA = QAs[sl]
                    cq, po = h // 2, 64 * (h % 2)
                    if stage_ == 0:
                        if tb == 0:
                            TSC("dve", KMh[0:64, 0:NB], KS[po:po + 64, cq, :], 1.0 / 256, None, ALU.mult, None,
                                [b_KS], [b_KMh])
                        for t in range(NTB):
                            MM(Gp[:, t, 0:NB], QA[0:64, (tb + t) * 128:(tb + t + 1) * 128], KMh[0:64, 0:NB],
                               True, True, [b_QA, b_KMh], [b_Gp])
                    elif stage_ == 1:
                        if NB < 32:
                            Sc.op("dve", lambda e: e.memset(Gs[:], NEG), (), [b_Gs])
                        TT("dve", Gs[:, :, 0:NB], Gp[:, 0:NTB, 0:NB], FUT[:, tb:tb + NTB, 0:NB], ALU.add,
                           [b_Gp, b_FUT], [b_Gs])
                        for t in range(NTB):
                            Sc.op("dve", lambda e, t=t: e.max(out=thr[:, t, :], in_=Gs[:, t, :]), [b_Gs], [b_thr])
                        TT("dve", nsel[:], Gs[:], thr[:, :, 2:3].to_broadcast([128, NTB, 32]), ALU.is_lt,
                           [b_Gs, b_thr], [b_nsel])
                        TT("pool", MBp[:, :, 64:96], nsel[:], NEGP[:, tb:tb + NTB, :], ALU.mult,
                           [b_nsel, b_NEGP], [b_MBp])
                    else:
                        t4 = (stage_ - 2) * 4
                        for t in range(4):
                            MM(MTp[:, t * 128:(t + 1) * 128], MBp[:, t4 + t, :], ident[:], True, True,
                               [b_MBp, b_ident], [b_MTp])
                        c0 = (tb + t4) * 128
                        CP("act", QA[64:96, c0:c0 + 512], MTp[64:96, :], [b_MTp], [b_QA])

                NST = 2 + NTB // 4
                gate_sched = [(tb, st_) for tb in range(0, NT, NTB) for st_ in range(NST)]

                def head_gate(h):
                    for (tb, st_) in gate_sched:
                        gate_stage(h, tb, st_)

                head_loads(0)
                head_gate(0)
                for h in range(HA):
                    sl = h % 2
                    QA, b_QA = QAs[sl]; KA, b_KA = KAs[sl]; VA, b_VA = VAs[sl]
                    pairs = [(g, kp) for g in range(NG) for kp in range(2 * g + 2)]
                    slots = {}

                    def emit_qk(i):
                        g, kp = pairs[i]
                        Sp, bS = STp.next()
                        for u in range(2):
                            kt = 2 * kp + u
                            delta = 512 * g - 128 * kt
                            near = delta <= 1536
                            MM(Sp[:, u * 512:(u + 1) * 512], KA[0:96, kt * 128:(kt + 1) * 128],
                               QA[0:96, g * 512:(g + 1) * 512], True, not near, [b_KA, b_QA], [bS])
                            if near:
                                MM(Sp[:, u * 512:(u + 1) * 512], ident[:],
                                   TS_[:, h, delta + 384:delta + 384 + 512], False, True, [b_ident, b_TS], [bS])
                        slots[i] = (Sp, bS)

                    emit_qk(0)
                    O, bO = None, None
                    gate_i0 = len(pairs) // 4
                    gate_step = max(1, (len(pairs) - gate_i0 - 2) // len(gate_sched))
                    for i, (g, kp) in enumerate(pairs):
                        npair = 2 * g + 2
                        if kp == 0:
                            O, bO = Op.next()
                            SAG, b_SAG = SAGr.next()
                            LD(SAG[:], s_sag[h * 64:(h + 1) * 64, g * 512:(g + 1) * 512], [b_SAG], b_SAG)
                        if i + 1 < len(pairs):
                            emit_qk(i + 1)
                        if i == 0 and h + 1 < HA:
                            head_loads(h + 1)
                        if h + 1 < HA and i >= gate_i0 and (i - gate_i0) % gate_step == 0:
                            kq = (i - gate_i0) // gate_step
                            if kq < len(gate_sched):
                                gate_stage(h + 1, *gate_sched[kq])
                        Sp, bS = slots.pop(i)
                        P_, bPt = PT.next()
                        ACT(P_[:], Sp[:], AF.Exp, [bS], [bPt])
                        for u in range(2):
                            kt = 2 * kp + u
                            MM(O[:], VA[:, kt, :], P_[:, u * 512:(u + 1) * 512], kt == 0, kt == 2 * npair - 1,
                               [b_VA, bPt], [bO])
                        if kp == npair - 1:
                            Sc.op("dve", lambda e, O=O: e.reciprocal(out=rden[:], in_=O[64:128, :]), [bO], [b_rden])
                            TT("dve", yh[:], O[0:64, :], rden[:], ALU.mult, [bO, b_rden], [b_yh])
                            YG, b_YG = YGr.next()
                            TT("pool", YG[:], yh[:], SAG[:], ALU.mult, [b_yh, b_SAG], [b_YG])
                            ST(XS[h // 2, (h % 2) * 64:(h % 2) * 64 + 64, g * 512:(g + 1) * 512], YG[:], [b_YG], b_YG)
                Sc.emit()

            with ExitStack() as st:
                sb, ps = mk(st)
                QDs = [(sb(f"QD{i}", [128, S], BF16), Buf(f"QD{i}")) for i in range(HB)]
                KDs = [(sb(f"KD{i}", [128, S], BF16), Buf(f"KD{i}")) for i in range(HB)]
                KEr = Ring([(sb(f"KE{i}", [128, 512], BF16), Buf(f"KE{i}")) for i in range(4)])
                VBr = Ring([(sb(f"VB{i}", [64, 8, 128], BF16), Buf(f"VB{i}")) for i in range(4)])
                SBGr = Ring([(sb(f"SBG{i}", [128, 512], BF16), Buf(f"SBG{i}")) for i in range(4)])
                YBr = Ring([(sb(f"YB{i}", [128, 512], BF16), Buf(f"YB{i}")) for i in range(4)])
                KTr = Ring([(sb(f"KT{i}", [64, 8, 128], BF16), Buf(f"KT{i}")) for i in range(4)])
                ATs = Ring([(sb(f"ATs{i}", [64, 64], BF16), Buf(f"ATs{i}")) for i in range(4)])
                Sbs = [[(sb(f"Sb{hh}_{i}", [128, 128], BF16), Buf(f"Sb{hh}_{i}")) for i in range(2)] for hh in range(HB)]
                osq = sb("osq", [128, 512], BF16); b_osq = Buf("osq")
                ort = sb("ort", [128, 512]); b_ort = Buf("ort")
                ors = sb("ors", [128, 512]); b_ors = Buf("ors")
                ybf = sb("ybf", [128, 512]); b_ybf = Buf("ybf")
                KT1 = (ps("KTp", [64, 8, 128], BF16), Buf("KTp"))
                KTp = [KT1] * HB
                OHp = [Ring([(ps(f"OHp{hh}_{i}", [128, 512]), Buf(f"OHp{hh}_{i}")) for i in range(1)]) for hh in range(HB)]
                Abk = [ps(f"Abk{hh}", [128, 512]) for hh in range(HB)]
                dSbk = [ps(f"dSbk{hh}", [128, 512]) for hh in range(HB)]
                ATp = [(Abk[hh][0:64, 0:64], Buf(f"ATp{hh}")) for hh in range(HB)]
                dSp = [(dSbk[hh][:, 0:128], Buf(f"dSp{hh}")) for hh in range(HB)]
                NSp = ps("NSp", [128, 512]); b_NSp = Buf("NSp")
                b_XGm = [Buf(f"XGm{k}") for k in range(2)]
                b_XOm = Buf("XOm")
                for k in range(2):
                    AG(XG[k], XS[k], Buf(f"agx{k}"), w=[b_XGm[k]])
                b_XSH = [[Buf(f"XSH{hh}_{q}") for q in range(NQ)] for hh in range(HB)]
                for hh in range(HB):
                    rows = slice(hh * 128, (hh + 1) * 128)
                    LD(QDs[hh][0][:], s_qd[rows, :], [QDs[hh][1]], QDs[hh][1])
                    LD(KDs[hh][0][:], s_kd[rows, :], [KDs[hh][1]], KDs[hh][1])
                    Sc.op("dve", lambda e, hh=hh: e.memset(Sbs[hh][0][0][:], 0.0), (), [Sbs[hh][0][1]])
                cur = [0] * HB

                def group_loads(g):
                    out = []
                    for hh in range(HB):
                        rows = slice(hh * 128, (hh + 1) * 128)
                        KE, bKE = KEr.next()
                        LD(KE[:], s_ke[rows, g * 512:(g + 1) * 512], [bKE], bKE)
                        VB, bVB = VBr.next()
                        vsrc = s_vb[g * 512:(g + 1) * 512, hh * 128:(hh + 1) * 128].rearrange("(c s) v -> s c v", s=64)
                        LD(VB[:], vsrc, [bVB], bVB)
                        SBG, bSBG = SBGr.next()
                        LD(SBG[:], s_sbg[rows, g * 512:(g + 1) * 512], [bSBG], bSBG)
                        out.append((KE, bKE, VB, bVB, SBG, bSBG))
                    return out

                nxt = group_loads(0)
                for g in range(NG):
                    gl_ = nxt
                    if g + 1 < NG:
                        nxt = group_loads(g + 1)
                    KTl, OHl = [], []
                    for hh in range(HB):
                        KE, bKE, VB, bVB, SBG, bSBG = gl_[hh]
                        KTps, bKTp = KTp[hh]
                        for c in range(8):
                            Sc.op("pe", lambda e, KTps=KTps, c=c, KE=KE: e.transpose(
                                out=KTps[:, c, :], in_=KE[:, c * 64:(c + 1) * 64], identity=ident[:]),
                                [bKE, b_ident], [bKTp])
                        KTs, bKT = KTr.next()
                        CP("act", KTs[:], KTps[:], [bKTp], [bKT])
                        KTl.append((KTs, bKT))
                        OHl.append(OHp[hh].next())
                    for c in range(8):
                        ch = g * 8 + c
                        cs = slice(ch * 64, (ch + 1) * 64)
                        Asl = []
                        for hh in range(HB):
                            QD, b_QD = QDs[hh]; KD, b_KD = KDs[hh]
                            A, bA = ATp[hh]
                            MM(A, KD[:, cs], QD[:, cs], True, True, [b_KD, b_QD], [bA])
                            As, bAs = ATs.next()
                            TT("dve", As[:], A, caus[:], ALU.mult, [bA, b_caus], [bAs])
                            Asl.append((As, bAs))
                        for hh in range(HB):
                            KE, bKE, VB, bVB, SBG, bSBG = gl_[hh]
                            QD, b_QD = QDs[hh]
                            KTs, bKT = KTl[hh]; OH, bOH = OHl[hh]
                            As, bAs = Asl[hh]
                            S0, bS0 = Sbs[hh][cur[hh]]
                            S1, bS1 = Sbs[hh][1 - cur[hh]]
                            MM(OH[:, c * 64:(c + 1) * 64], VB[:, c, :], As[:], True, False, [bVB, bAs], [bOH])
                            MM(OH[:, c * 64:(c + 1) * 64], S0[:], QD[:, cs], False, True, [bS0, b_QD], [bOH])
                            dS, bdS = dSp[hh]
                            MM(dS, KTs[:, c, :], VB[:, c, :], True, True, [bKT, bVB], [bdS])
                            STT("dve", S1[:], S0[:], EL[:, hh, ch:ch + 1], dS, ALU.mult, ALU.add,
                                [bS0, b_EL, bdS], [bS1])
                            cur[hh] = 1 - cur[hh]
                    for hh in range(HB):
                        KE, bKE, VB, bVB, SBG, bSBG = gl_[hh]
                        OH, bOH = OHl[hh]
                        ACT(osq[:], OH[:], AF.Square, [bOH], [b_osq])
                        MM(NSp[:], ones[:], osq[:], True, True, [b_ones, b_osq], [b_NSp])
                        ACT(ort[:], NSp[:], AF.Ln, [b_NSp], [b_ort], bias=EPS, scale=1.0 / 128)
                        ACT(ors[:], ort[:], AF.Exp, [b_ort], [b_ors], scale=-0.5)
                        STT("dve", ybf[:], OH[:], go[:, l, hh:hh + 1], ors[:], ALU.mult, ALU.mult,
                            [bOH, b_go, b_ors], [b_ybf])
                        YB, bYB = YBr.next()
                        TT("pool", YB[:], ybf[:], SBG[:], ALU.mult, [b_ybf, bSBG], [bYB])
                        qd_, col_ = (g * 512) // QW, (g * 512) % QW
                        ST(XSH[hh, qd_, :, col_:col_ + 512], YB[:], [bYB], bYB, w=[b_XSH[hh][qd_]])
                    if g == NG // 2:
                        for k in range(2):
                            LD(XO[k, :, :], XG[k, :, bass.ds(off_own, SH)], [b_XOm], b_XOm, r=[b_XGm[k]])
                    if ((g + 1) * 512) % QW == 0 and (g * 512) // QW < NQ - 1:
                        qd_ = (g * 512) // QW
                        for hh in range(HB):
                            AG(XGH[hh, qd_ * 256:(qd_ + 1) * 256, :], XSH[hh, qd_], Buf(f"agh{hh}_{qd_}"),
                               r=[b_XSH[hh][qd_]])
                Sc.emit()

            with ExitStack() as st:
                sb, ps = mk(st)
                WA = sb("WA", [128, 4, D], BF16)
                WB_ = sb("WB", [128, 4, D], BF16)
                WO = sb("WO", [128, 8, D], BF16)
                WP = sb("WP", [128, 2, D], BF16)
                WG = sb("WG", [128, 8, D], BF16)
                bW = {}

                def wtok(name, c, j):
                    return bW[(name, (c // 2) * 2, (j // 2) * 256)]
                wst = [(sb(f"wstc{i}", [128, 2, 256]), Buf(f"wstc{i}")) for i in range(4)]
                k = 0
                for (dst, nm, src, nch) in ((WA, "WA", w_up_a[l], 4), (WB_, "WB", w_up_b[l], 4),
                                            (WO, "WO", w_out[l], 8), (WP, "WP", w_ple[l], 2),
                                            (WG, "WG", w_pg[l], 8)):
                    sv = src.rearrange("(c p) n -> p c n", p=128)
                    for c2 in range(0, nch, 2):
                        for n2 in range(0, D, 256):
                            t, b = wst[k % 4]
                            bW[(nm, c2, n2)] = Buf(f"{nm}_{c2}_{n2}")
                            LD(t[:], sv[:, c2:c2 + 2, n2:n2 + 256], [b], b, q="pool")
                            CP(("act", "dve")[k % 2], dst[:, c2:c2 + 2, n2:n2 + 256], t[:], [b], [bW[(nm, c2, n2)]])
                            k += 1
                INS = []
                for i in range(2):
                    INS.append(dict(
                        YAG=(sb(f"YAG{i}", [128, 4, 512], BF16), Buf(f"YAG{i}")),
                        YBG=(sb(f"YBG{i}", [128, 4, 512], BF16), Buf(f"YBG{i}")),
                        SGA=(sb(f"SGA{i}", [128, 8, 512], BF16), Buf(f"SGA{i}")),
                        SGB=(sb(f"SGB{i}", [128, 8, 512], BF16), Buf(f"SGB{i}")),
                        PB=(sb(f"PB{i}", [128, 2, 512], BF16), Buf(f"PB{i}"))))
                H = sb("Hc", [128, 8, 512]); b_H = Buf("Hc")
                PF = sb("PF", [128, 2, 512]); b_PF = Buf("PF")
                MG = sb("MG", [128, 8, 512], BF16); b_MG = Buf("MG")
                HM = sb("HM", [128, 8, 512]); b_HM = Buf("HM")
                HMb = sb("HMb", [128, 8, 512], BF16); b_HMb = Buf("HMb")
                HN = Ring([(sb(f"HN{i}", [128, 512]), Buf(f"HN{i}")) for i in range(2)])
                HNb = Ring([(sb(f"HNb{i}", [128, 512], BF16), Buf(f"HNb{i}")) for i in range(2)])
                tmp = Ring([(sb(f"tmp{i}", [128, 512]), Buf(f"tmp{i}")) for i in range(4)])
                PS = Ring([(ps(f"PSc{i}", [128, 512]), Buf(f"PSc{i}")) for i in range(8)])
                hv = h_own.rearrange("(c p) t -> p c t", p=128)
                hd = h_dst.rearrange("(c p) t -> p c t", p=128)
                pv = pT[l].rearrange("(c p) t -> p c t", p=128)
                b_XGl = [Buf(f"XGl{hh}") for hh in range(HB)]
                for hh in range(HB):
                    AG(XGH[hh, (NQ - 1) * 256:NQ * 256, :], XSH[hh, NQ - 1], Buf(f"aghl{hh}"), w=[b_XGl[hh]])
                off_rows = (nc.sync.partition_id() % 2) * 512
                b_XOH = [Buf(f"XOH{i}") for i in range(2)]

                def xoh_copy(i):
                    for hh in range(HB):
                        LD(XOH[hh, :, :, i * QW:(i + 1) * QW].rearrange("r p t -> (r p) t"),
                           XGH[hh, (i * 256):, :][bass.ds(off_rows, 256), :],
                           [b_XOH[i]], b_XOH[i], r=([b_XGl[hh]] if i == 1 else []))

                xoh_copy(0)
                g_xoh1 = max(QW // 512 - 1, 0)
                b_HS = [Buf(f"HSp{q}") for q in range(NPC)]

                def loads(g):
                    I = INS[g % 2]
                    ts_ = slice(g * 512, (g + 1) * 512)
                    for c in range(4):
                        rk, kk = c // 2, c % 2
                        LD(I["YAG"][0][:, c, :], XO[kk, rk * 128:(rk + 1) * 128, ts_], [I["YAG"][1]], I["YAG"][1])
                        LD(I["YBG"][0][:, c, :], XOH[kk, rk, :, ts_], [I["YBG"][1]], I["YBG"][1],
                           r=[b_XOH[(g * 512) // QW]])
                    LD(I["SGA"][0][:], s_sga.rearrange("(c p) t -> p c t", p=128)[:, :, ts_], [I["SGA"][1]], I["SGA"][1])
                    LD(I["SGB"][0][:], s_sgb.rearrange("(c p) t -> p c t", p=128)[:, :, ts_], [I["SGB"][1]], I["SGB"][1])
                    LD(PF[:], pv[:, :, ts_], [b_PF], b_PF)
                    CP("act", I["PB"][0][:], PF[:], [b_PF], [I["PB"][1]])

                loads(0)
                LD(H[:], hv[:, :, 0:512], [b_H], b_H)
                for g in range(NGO):
                    ts_ = slice(g * 512, (g + 1) * 512)
                    I = INS[g % 2]
                    YAG, b_YAG = I["YAG"]; YBG, b_YBG = I["YBG"]; SGA, b_SGA = I["SGA"]; SGB, b_SGB = I["SGB"]
                    PB, b_PB = I["PB"]
                    if g == g_xoh1:
                        xoh_copy(1)
                    if g + 1 < NGO:
                        loads(g + 1)
                    for j in range(8):
                        Pa, bPa = PS.next()
                        for c in range(4):
                            MM(Pa[:], WA[:, c, j * 128:(j + 1) * 128], YAG[:, c, :], c == 0, c == 3,
                               [wtok("WA", c, j), b_YAG], [bPa])
                        Pb, bPb = PS.next()
                        for c in range(4):
                            MM(Pb[:], WB_[:, c, j * 128:(j + 1) * 128], YBG[:, c, :], c == 0, c == 3,
                               [wtok("WB", c, j), b_YBG], [bPb])
                        t1, bt1 = tmp.next()
                        TT("dve", t1[:], Pa[:], SGA[:, j, :], ALU.mult, [bPa, b_SGA], [bt1])
                        t2, bt2 = tmp.next()
                        TT("dve", t2[:], Pb[:], SGB[:, j, :], ALU.mult, [bPb, b_SGB], [bt2])
                        TT("pool", MG[:, j, :], t1[:], t2[:], ALU.add, [bt1, bt2], [b_MG])
                    for j in range(8):
                        Po, bPo = PS.next()
                        for c in range(8):
                            MM(Po[:], WO[:, c, j * 128:(j + 1) * 128], MG[:, c, :], c == 0, c == 7,
                               [wtok("WO", c, j), b_MG], [bPo])
                        TT("dve", HM[:, j, :], Po[:], H[:, j, :], ALU.add, [bPo, b_H], [b_HM])
                        CP("act", HMb[:, j, :], HM[:, j, :], [b_HM], [b_HMb])
                    if g + 1 < NGO:
                        LD(H[:], hv[:, :, (g + 1) * 512:(g + 2) * 512], [b_H], b_H)
                    q, col = (g * 512) // PIECE, (g * 512) % PIECE
                    for j in range(8):
                        Pp, bPp = PS.next()
                        for c in range(2):
                            MM(Pp[:], WP[:, c, j * 128:(j + 1) * 128], PB[:, c, :], c == 0, c == 1,
                               [wtok("WP", c, j), b_PB], [bPp])
                        Pg, bPg = PS.next()
                        for c in range(8):
                            MM(Pg[:], WG[:, c, j * 128:(j + 1) * 128], HMb[:, c, :], c == 0, c == 7,
                               [wtok("WG", c, j), b_HMb], [bPg])
                        sg, bsg = tmp.next()
                        ACT(sg[:], Pg[:], AF.Sigmoid, [bPg], [bsg])
                        t1, bt1 = tmp.next()
                        TT("dve", t1[:], Pp[:], sg[:], ALU.mult, [bPp, bsg], [bt1])
                        hn, bhn = HN.next()
                        TT("pool", hn[:], t1[:], HM[:, j, :], ALU.add, [bt1, b_HM], [bhn])
                        ST(hd[:, j, ts_], hn[:], [bhn], bhn, q="sp")
                        if not last:
                            hb, bhb = HNb.next()
                            CP("act", hb[:], hn[:], [bhn], [bhb])
                            ST(HS[q, j * 128:(j + 1) * 128, col:col + 512], hb[:], [bhb], bhb, w=[b_HS[q]], q="sp")
                    if not last and (g + 1) * 512 % PIECE == 0 and q < NPC - 1:
                        AG(HG[q], HS[q], Buf(f"agh{q}"), r=[b_HS[q]])
                Sc.emit()

        print("total ops", Sc.total, "sems", Sc.nsem)
    return nc


def host_consts(S):
    NT = S // 128
    bf = ml_dtypes.bfloat16
    c = {}
    c["c_ident"] = np.eye(128, dtype=np.float32).astype(bf)
    blk = np.zeros((128, 128), np.float32)
    blk[:64, :64] = 1.0
    blk[64:, 64:] = 1.0
    c["c_blk"] = blk.astype(bf)
    oh = np.zeros((32, S), np.float32)
    for n in range(S // 256):
        oh[n, n * 256:(n + 1) * 256] = 1.0
    c["c_onehot"] = oh.astype(bf)
    c["c_causal"] = np.triu(np.ones((64, 64), np.float32))
    sm = np.ones((128, 512), np.float32)
    sm[:, ::64] = 0.0
    c["c_scan"] = sm
    fut = np.zeros((NT, 32), np.float32)
    neg = np.full((NT, 32), NEG, np.float32)
    for t in range(NT):
        b = t // 2
        fut[t, b:] = NEG
        neg[t, b] = 0.0
    c["c_fut"] = np.ascontiguousarray(np.broadcast_to(fut[None], (128, NT, 32))).astype(bf)
    c["c_neg"] = np.ascontiguousarray(np.broadcast_to(neg[None], (128, NT, 32))).astype(bf)
    return c


def host_strips(rel_bias, heads):
    i = np.arange(128)[:, None]
    u = np.arange(STRIP)[None, :]
    rel = u - 384 - i
    bucket = t5_bucket_np(rel)
    strip = np.empty((128, len(heads), STRIP), np.float32)
    for k, h in enumerate(heads):
        g = rel_bias[:, h][bucket]
        strip[:, k, :] = np.where(rel >= 0, g, np.float32(NEG))
    c31 = np.ascontiguousarray(np.broadcast_to(rel_bias[31, heads][None, :], (128, len(heads)))).astype(np.float32)
    return strip, c31


def host_inputs(b, r, S, x, p, norm_gain, w_in, q_norm_gain, k_norm_gain, rel_bias, hgrn_lb_logits,
                hgrn_out_gain, w_up_a, w_up_b, w_out, w_ple, w_ple_gate, consts):
    L = w_in.shape[0]
    SH = S // 2
    m = dict(consts)
    m["xT"] = np.ascontiguousarray(x[b, :S].T)
    m["xT_own"] = np.ascontiguousarray(x[b, r * SH:(r + 1) * SH].T)
    m["pT"] = np.ascontiguousarray(np.transpose(p[:, b, r * SH:(r + 1) * SH, :], (0, 2, 1)))
    cols = []
    for blk0 in range(0, 4096, 512):
        cols.append(np.arange(blk0 + r * 256, blk0 + (r + 1) * 256))
    cols.append(np.arange(4096, 6144))
    cols = np.concatenate(cols)
    m["w_in"] = np.ascontiguousarray(w_in[:, :, cols])
    m["w_up_a"] = w_up_a
    m["w_up_b"] = w_up_b
    m["w_out"] = w_out
    m["w_ple"] = w_ple
    m["w_pg"] = w_ple_gate
    m["g_norm"] = np.ascontiguousarray(np.transpose(norm_gain.reshape(L, 8, 128), (2, 0, 1)))
    m["g_q"] = np.ascontiguousarray(np.concatenate([q_norm_gain, q_norm_gain], axis=1).T)
    m["g_k"] = np.ascontiguousarray(np.concatenate([k_norm_gain, k_norm_gain], axis=1).T)
    m["g_o"] = np.ascontiguousarray(np.transpose(hgrn_out_gain.reshape(L, 4, 128)[:, 2 * r:2 * r + 2], (2, 0, 1)))
    m["lbl"] = np.ascontiguousarray(np.transpose(hgrn_lb_logits.reshape(L, 4, 128)[:, 2 * r:2 * r + 2], (2, 0, 1)))
    strip, c31 = host_strips(rel_bias, list(range(4 * r, 4 * r + 4)))
    m["strip_raw"] = strip
    m["c31"] = c31
    return m


_NC_CACHE = {}


def kernel(x, p, norm_gain, w_in, q_norm_gain, k_norm_gain, rel_bias, hgrn_lb_logits,
           hgrn_out_gain, w_up_a, w_up_b, w_out, w_ple, w_ple_gate):
    args = [np.asarray(a, dtype=np.float32) for a in (
        x, p, norm_gain, w_in, q_norm_gain, k_norm_gain, rel_bias, hgrn_lb_logits,
        hgrn_out_gain, w_up_a, w_up_b, w_out, w_ple, w_ple_gate)]
    x = args[0]
    B, S, _ = x.shape
    if S not in _NC_CACHE:
        _NC_CACHE[S] = build(S)
    nc = _NC_CACHE[S]
    consts = host_consts(S)
    in_maps = [host_inputs(i // 2, i % 2, S, *args, consts) for i in range(2 * B)]
    res = run_bass_kernel_spmd(nc, in_maps, core_ids=list(range(2 * B)))
    SH = S // 2
    out = np.empty((B, S, D), np.float32)
    for i in range(2 * B):
        out[i // 2, (i % 2) * SH:(i % 2 + 1) * SH, :] = res.results[i]["hT_out"].T
    return out
```
